# Optimizing a Trainium2 kernel written in Bass

```python
import math
import jax, jax.numpy as jnp
from jax import lax
import numpy as np

D_MODEL = 1024
BATCH = 8
SEQ = 2048
DEPTH = 4
DEC_BATCH = 128
DEC_SEQ = 4
PAST_LEN = 16384
PAGE_SIZE = 128

N_MIXERS = 2
N_MAMBA = (DEPTH + 1) // 2
N_RWKV = DEPTH // 2
N_VRES = max(N_RWKV - 1, 0)

D_FF = 2816
PLE_DIM = 256
NORM_EPS = 1e-6

M_D_INNER = 2 * D_MODEL
M_HEADDIM = 64
M_HEADS = M_D_INNER // M_HEADDIM
M_GROUPS = 8
M_HEADS_PER_GROUP = M_HEADS // M_GROUPS
M_STATE = 128
M_CONV_W = 4
M_CONV_DIM = M_D_INNER + 2 * M_GROUPS * M_STATE
M_IN_DIM = 2 * M_D_INNER + 2 * M_GROUPS * M_STATE + M_HEADS
M_CHUNK = 128
M_NORM_EPS = 1e-5

R_HEADDIM = 64
R_HEADS = D_MODEL // R_HEADDIM
R_DECAY_LORA = 64
R_AAA_LORA = 64
R_MV_LORA = 32
R_GATE_LORA = 160
R_GN_EPS = 64e-5

kernel_name = 'mamba2_rwkv7_macaron_ple_step'

RWKV_KEYS = ('r_mu', 'r_wr', 'r_wk', 'r_wv', 'r_wo', 'r_w0', 'r_w1', 'r_w2', 'r_a0', 'r_a1', 'r_a2',
             'r_g1', 'r_g2', 'r_k_k', 'r_k_a', 'r_r_k', 'r_gn_w', 'r_gn_b')


def rmsnorm(x, g, eps=NORM_EPS):
    xf = x.astype(jnp.float32)
    ms = jnp.mean(xf * xf, axis=-1, keepdims=True)
    return (xf * lax.rsqrt(ms + eps)).astype(x.dtype) * g


def swiglu(u, w_gate_up, w_down):
    gate, up = jnp.split(u @ w_gate_up, 2, axis=-1)
    return (jax.nn.silu(gate) * up) @ w_down


def causal_depthwise_conv(x, buf, w, b):
    L = x.shape[1]
    full = jnp.concatenate([buf.astype(x.dtype), x], axis=1)
    out = full[:, 0:L] * w[0]
    for k in range(1, M_CONV_W):
        out = out + full[:, k:k + L] * w[k]
    return out + b, full[:, L:]


def ssd_scan(x, dt, A, B, C, h0):
    bsz, L = x.shape[0], x.shape[1]
    Q = min(M_CHUNK, L)
    nc = -(-L // Q)
    pad = nc * Q - L
    if pad:
        padf = lambda t: jnp.pad(t, [(0, 0), (0, pad)] + [(0, 0)] * (t.ndim - 2))
        x, dt, B, C = padf(x), padf(dt), padf(B), padf(C)
    G, R = M_GROUPS, M_HEADS_PER_GROUP
    x = x.reshape(bsz, nc, Q, G, R, M_HEADDIM)
    dt = dt.reshape(bsz, nc, Q, G, R)
    B = B.reshape(bsz, nc, Q, G, M_STATE)
    C = C.reshape(bsz, nc, Q, G, M_STATE)
    a_cum = jnp.cumsum(dt * A.reshape(G, R), axis=2)
    seg = a_cum[:, :, :, None] - a_cum[:, :, None, :]
    causal = jnp.tril(jnp.ones((Q, Q), bool))[:, :, None, None]
    decay_ij = jnp.exp(jnp.where(causal, seg, -jnp.inf))
    xdt = x * dt[..., None]
    cb = jnp.einsum('bcign,bcjgn->bcijg', C, B)
    y_intra = jnp.einsum('bcijg,bcijgr,bcjgrp->bcigrp', cb, decay_ij, xdt)
    decay_end = jnp.exp(a_cum[:, :, -1:] - a_cum)
    chunk_states = jnp.einsum('bcjgn,bcjgr,bcjgrp->bcgrpn', B, decay_end, xdt)
    chunk_decay = jnp.exp(a_cum[:, :, -1])

    def step(h, inp):
        s, d = inp
        return h * d[..., None, None] + s, h

    h_last, h_starts = lax.scan(step, h0.reshape(bsz, G, R, M_HEADDIM, M_STATE),
                                (jnp.moveaxis(chunk_states, 1, 0), jnp.moveaxis(chunk_decay, 1, 0)))
    h_starts = jnp.moveaxis(h_starts, 0, 1)
    y_inter = jnp.einsum('bcign,bcigr,bcgrpn->bcigrp', C, jnp.exp(a_cum), h_starts)
    y = (y_intra + y_inter).reshape(bsz, nc * Q, M_HEADS, M_HEADDIM)[:, :L]
    return y, h_last.reshape(bsz, M_HEADS, M_HEADDIM, M_STATE)


def mamba2_mixer(u, h0, conv_buf, w_in, conv_w, conv_b, dt_bias, A_log, D_skip, norm_w, w_out):
    f32 = jnp.float32
    bsz, L, _ = u.shape
    z, xbc, dt = jnp.split(u @ w_in, [M_D_INNER, M_D_INNER + M_CONV_DIM], axis=-1)
    xbc, new_buf = causal_depthwise_conv(xbc, conv_buf, conv_w, conv_b)
    xbc = jax.nn.silu(xbc)
    xs, Bm, Cm = jnp.split(xbc, [M_D_INNER, M_D_INNER + M_GROUPS * M_STATE], axis=-1)
    dt = jax.nn.softplus((dt + dt_bias).astype(f32))
    A = -jnp.exp(A_log.astype(f32))
    xh = xs.reshape(bsz, L, M_HEADS, M_HEADDIM).astype(f32)
    y, h_new = ssd_scan(xh, dt, A,
                        Bm.reshape(bsz, L, M_GROUPS, M_STATE).astype(f32),
                        Cm.reshape(bsz, L, M_GROUPS, M_STATE).astype(f32),
                        h0.astype(f32))
    y = y + D_skip.astype(f32)[:, None] * xh
    y = y.reshape(bsz, L, M_D_INNER) * jax.nn.silu(z.astype(f32))
    yg = y.reshape(bsz, L, M_GROUPS, M_D_INNER // M_GROUPS)
    yg = yg * lax.rsqrt(jnp.mean(yg * yg, axis=-1, keepdims=True) + M_NORM_EPS)
    y = yg.reshape(bsz, L, M_D_INNER).astype(u.dtype) * norm_w
    return y @ w_out, h_new.astype(h0.dtype), new_buf.astype(conv_buf.dtype)


def rwkv7_mixer(u, S0, shift0, v_first, vres, mu, w_r, w_k, w_v, w_o, w0, w1, w2, a0, a1, a2,
                g1, g2, k_k, k_a, r_k, gn_w, gn_b):
    f32 = jnp.float32
    bsz, L, _ = u.shape
    u_prev = jnp.concatenate([shift0[:, None].astype(u.dtype), u[:, :-1]], axis=1)
    xx = u_prev - u
    mixed = u[None] + xx[None] * mu[:, None, None, :]
    xr, xw, xk, xv, xa, xg = mixed[0], mixed[1], mixed[2], mixed[3], mixed[4], mixed[5]
    r = xr @ w_r
    k = xk @ w_k
    v = xv @ w_v
    w = -jax.nn.softplus(-(w0 + jnp.tanh(xw @ w1) @ w2)) - 0.5
    a = jax.nn.sigmoid(a0 + (xa @ a1) @ a2)
    g = jax.nn.sigmoid(xg @ g1) @ g2
    if vres is None:
        v_first = v
    else:
        v0, v1, v2 = vres
        v = v + (v_first - v) * jax.nn.sigmoid(v0 + (xv @ v1) @ v2)
    heads = lambda t: t.reshape(bsz, L, R_HEADS, R_HEADDIM).astype(f32)
    kk = heads(k * k_k)
    kk = kk * lax.rsqrt(jnp.maximum(jnp.sum(kk * kk, axis=-1, keepdims=True), 1e-24))
    k = k * (1 + (a - 1) * k_a)
    rh, kh, vh, ah = heads(r), heads(k), heads(v), heads(a)
    decay = jnp.exp(-jnp.exp(heads(w)))

    def step(S, inp):
        r_t, d_t, k_t, v_t, kk_t, a_t = inp
        sa = jnp.einsum('bhij,bhj->bhi', S, -kk_t)
        S = (S * d_t[:, :, None, :] + sa[..., None] * (kk_t * a_t)[:, :, None, :]
             + v_t[..., None] * k_t[:, :, None, :])
        return S, jnp.einsum('bhij,bhj->bhi', S, r_t)

    tm = lambda t: jnp.moveaxis(t, 1, 0)
    S_new, y = lax.scan(step, S0.astype(f32), (tm(rh), tm(decay), tm(kh), tm(vh), tm(kk), tm(ah)))
    y = jnp.moveaxis(y, 0, 1)
    mean = jnp.mean(y, axis=-1, keepdims=True)
    var = jnp.mean(jnp.square(y - mean), axis=-1, keepdims=True)
    y = ((y - mean) * lax.rsqrt(var + R_GN_EPS)).reshape(bsz, L, D_MODEL) * gn_w + gn_b
    bonus = jnp.sum(rh * kh * r_k, axis=-1, keepdims=True) * vh
    y = y + bonus.reshape(bsz, L, D_MODEL)
    out = (y.astype(u.dtype) * g) @ w_o
    return out, S_new.astype(S0.dtype), u[:, -1].astype(shift0.dtype), v_first


def trunk(x, p, ssm0, conv0, wkv0, shift0, prm):
    h = x
    v_first = None
    ssm_new, conv_new, wkv_new, shift_new = [], [], [], []
    for i in range(DEPTH):
        j = i // N_MIXERS
        h = h + 0.5 * swiglu(rmsnorm(h, prm['norm_ffn1'][i]), prm['ffn1_gate_up'][i], prm['ffn1_down'][i])
        u = rmsnorm(h, prm['norm_mix'][i])
        if i % N_MIXERS == 0:
            mix, s_new, c_new = mamba2_mixer(u, ssm0[j], conv0[j], prm['m_in_proj'][j], prm['m_conv_w'][j],
                                             prm['m_conv_b'][j], prm['m_dt_bias'][j], prm['m_A_log'][j],
                                             prm['m_D'][j], prm['m_norm'][j], prm['m_out_proj'][j])
            ssm_new.append(s_new)
            conv_new.append(c_new)
        else:
            vres = None if j == 0 else (prm['r_v0'][j - 1], prm['r_v1'][j - 1], prm['r_v2'][j - 1])
            mix, s_new, sh_new, v_first = rwkv7_mixer(u, wkv0[j], shift0[j], v_first, vres,
                                                      *[prm[n][j] for n in RWKV_KEYS])
            wkv_new.append(s_new)
            shift_new.append(sh_new)
        h = h + mix
        h = h + 0.5 * swiglu(rmsnorm(h, prm['norm_ffn2'][i]), prm['ffn2_gate_up'][i], prm['ffn2_down'][i])
        gate = jax.nn.sigmoid(rmsnorm(h, prm['norm_ple'][i]) @ prm['ple_gate'][i])
        h = h + (p[i] @ prm['ple_in'][i]) * gate
    y = rmsnorm(h, prm['norm_final'])
    return y, jnp.stack(ssm_new), jnp.stack(conv_new), jnp.stack(wkv_new), jnp.stack(shift_new)


def setup_inputs(seed: int = 0) -> dict:
    key = jax.random.key(seed)
    ks = iter(jax.random.split(key, 64))
    f32 = jnp.float32

    def nrm(shape, scale=1.0):
        return jax.random.normal(next(ks), shape, f32) * scale

    def gain(shape):
        return 1.0 + nrm(shape, 0.02)

    def unif(shape, lo, hi):
        return jax.random.uniform(next(ks), shape, f32, lo, hi)

    NM, NR, NV = N_MAMBA, N_RWKV, N_VRES
    D = D_MODEL
    return {
        'x_prompt': nrm((BATCH, SEQ, D)),
        'x_sample': nrm((DEC_BATCH, DEC_SEQ, D)),
        'p_prompt': nrm((DEPTH, BATCH, SEQ, PLE_DIM)),
        'p_sample': nrm((DEPTH, DEC_BATCH, DEC_SEQ, PLE_DIM)),
        'state_ssm': nrm((NM, DEC_BATCH, M_HEADS, M_HEADDIM, M_STATE), 0.1),
        'state_conv': nrm((NM, DEC_BATCH, M_CONV_W - 1, M_CONV_DIM)),
        'state_wkv': nrm((NR, DEC_BATCH, R_HEADS, R_HEADDIM, R_HEADDIM), 0.1),
        'state_shift': nrm((NR, DEC_BATCH, D)),
        'norm_ffn1': gain((DEPTH, D)),
        'ffn1_gate_up': nrm((DEPTH, D, 2 * D_FF), D ** -0.5),
        'ffn1_down': nrm((DEPTH, D_FF, D), D_FF ** -0.5),
        'norm_mix': gain((DEPTH, D)),
        'norm_ffn2': gain((DEPTH, D)),
        'ffn2_gate_up': nrm((DEPTH, D, 2 * D_FF), D ** -0.5),
        'ffn2_down': nrm((DEPTH, D_FF, D), D_FF ** -0.5),
        'norm_ple': gain((DEPTH, D)),
        'ple_in': nrm((DEPTH, PLE_DIM, D), PLE_DIM ** -0.5),
        'ple_gate': nrm((DEPTH, D, D), D ** -0.5),
        'norm_final': gain((D,)),
        'm_in_proj': nrm((NM, D, M_IN_DIM), D ** -0.5),
        'm_conv_w': nrm((NM, M_CONV_W, M_CONV_DIM), M_CONV_W ** -0.5),
        'm_conv_b': nrm((NM, M_CONV_DIM), 0.02),
        'm_dt_bias': (lambda dt0: dt0 + jnp.log(-jnp.expm1(-dt0)))(
            jnp.exp(unif((NM, M_HEADS), math.log(1e-3), math.log(1e-1)))),
        'm_A_log': jnp.log(unif((NM, M_HEADS), 1.0, 16.0)),
        'm_D': gain((NM, M_HEADS)),
        'm_norm': gain((NM, M_D_INNER)),
        'm_out_proj': nrm((NM, M_D_INNER, D), M_D_INNER ** -0.5),
        'r_mu': unif((NR, 6, D), 0.0, 1.0),
        'r_wr': nrm((NR, D, D), D ** -0.5),
        'r_wk': nrm((NR, D, D), D ** -0.5),
        'r_wv': nrm((NR, D, D), D ** -0.5),
        'r_wo': nrm((NR, D, D), D ** -0.5),
        'r_w0': unif((NR, D), -6.0, -1.0),
        'r_w1': nrm((NR, D, R_DECAY_LORA), D ** -0.5),
        'r_w2': nrm((NR, R_DECAY_LORA, D), 0.1 * R_DECAY_LORA ** -0.5),
        'r_a0': nrm((NR, D), 0.1),
        'r_a1': nrm((NR, D, R_AAA_LORA), D ** -0.5),
        'r_a2': nrm((NR, R_AAA_LORA, D), 0.1 * R_AAA_LORA ** -0.5),
        'r_g1': nrm((NR, D, R_GATE_LORA), D ** -0.5),
        'r_g2': nrm((NR, R_GATE_LORA, D), R_GATE_LORA ** -0.5),
        'r_k_k': 0.85 + nrm((NR, D), 0.02),
        'r_k_a': gain((NR, D)),
        'r_r_k': nrm((NR, R_HEADS, R_HEADDIM), 0.1),
        'r_gn_w': gain((NR, D)),
        'r_gn_b': nrm((NR, D), 0.02),
        'r_v0': gain((NV, D)),
        'r_v1': nrm((NV, D, R_MV_LORA), D ** -0.5),
        'r_v2': nrm((NV, R_MV_LORA, D), 0.1 * R_MV_LORA ** -0.5),
    }


def reference(x_prompt, x_sample, p_prompt, p_sample, state_ssm, state_conv, state_wkv, state_shift,
              norm_ffn1, ffn1_gate_up, ffn1_down, norm_mix, norm_ffn2, ffn2_gate_up, ffn2_down,
              norm_ple, ple_in, ple_gate, norm_final,
              m_in_proj, m_conv_w, m_conv_b, m_dt_bias, m_A_log, m_D, m_norm, m_out_proj,
              r_mu, r_wr, r_wk, r_wv, r_wo, r_w0, r_w1, r_w2, r_a0, r_a1, r_a2, r_g1, r_g2,
              r_k_k, r_k_a, r_r_k, r_gn_w, r_gn_b, r_v0, r_v1, r_v2):
    prm = dict(norm_ffn1=norm_ffn1, ffn1_gate_up=ffn1_gate_up, ffn1_down=ffn1_down, norm_mix=norm_mix,
               norm_ffn2=norm_ffn2, ffn2_gate_up=ffn2_gate_up, ffn2_down=ffn2_down,
               norm_ple=norm_ple, ple_in=ple_in, ple_gate=ple_gate, norm_final=norm_final,
               m_in_proj=m_in_proj, m_conv_w=m_conv_w, m_conv_b=m_conv_b, m_dt_bias=m_dt_bias,
               m_A_log=m_A_log, m_D=m_D, m_norm=m_norm, m_out_proj=m_out_proj,
               r_mu=r_mu, r_wr=r_wr, r_wk=r_wk, r_wv=r_wv, r_wo=r_wo, r_w0=r_w0, r_w1=r_w1, r_w2=r_w2,
               r_a0=r_a0, r_a1=r_a1, r_a2=r_a2, r_g1=r_g1, r_g2=r_g2, r_k_k=r_k_k, r_k_a=r_k_a,
               r_r_k=r_r_k, r_gn_w=r_gn_w, r_gn_b=r_gn_b, r_v0=r_v0, r_v1=r_v1, r_v2=r_v2)
    bp = x_prompt.shape[0]
    dtp = x_prompt.dtype
    ssm0 = jnp.zeros((state_ssm.shape[0], bp) + state_ssm.shape[2:], dtp)
    conv0 = jnp.zeros((state_conv.shape[0], bp) + state_conv.shape[2:], dtp)
    wkv0 = jnp.zeros((state_wkv.shape[0], bp) + state_wkv.shape[2:], dtp)
    shift0 = jnp.zeros((state_shift.shape[0], bp) + state_shift.shape[2:], dtp)
    y_prompt, ssm_p, conv_p, wkv_p, shift_p = trunk(x_prompt, p_prompt, ssm0, conv0, wkv0, shift0, prm)
    y_sample, ssm_s, conv_s, wkv_s, shift_s = trunk(x_sample, p_sample, state_ssm, state_conv,
                                                    state_wkv, state_shift, prm)
    return (y_prompt, y_sample, ssm_p, conv_p, wkv_p, shift_p, ssm_s, conv_s, wkv_s, shift_s)
```

```python
import contextlib
import numpy as np
import concourse.bass as bass
import concourse.mybir as mybir
from concourse.bass_utils import run_bass_kernel_spmd

F32 = mybir.dt.float32
BF16 = mybir.dt.bfloat16
AF = mybir.ActivationFunctionType
ALU = mybir.AluOpType
AX = mybir.AxisListType

ENGS = ['pe', 'act', 'dve', 'pool', 'sp']


class Op:
    __slots__ = ('eng', 'fn', 'deps', 'signal', 'is_dma', 'sem', 'val', 'prewait', 'barriered')

    def __init__(self, eng, fn, is_dma):
        self.eng = eng
        self.fn = fn
        self.deps = []
        self.signal = False
        self.is_dma = is_dma
        self.sem = None
        self.val = 0
        self.prewait = None
        self.barriered = False


class Sched:
    def __init__(self, nc, n_dma_sems=20):
        self.nc = nc
        self.ops = {e: [] for e in ENGS}
        self.res = {}
        self.n_dma_sems = n_dma_sems

    def _states(self, key):
        if isinstance(key, tuple):
            name, sub = key[0], key[1:]
            if len(sub) == 0:
                sub = None
        else:
            name, sub = key, None
        d = self.res.setdefault(name, {})
        if sub is None:
            if None not in d:
                d[None] = [None, []]
            return list(d.values()), d[None], True
        if sub not in d:
            d[sub] = [None, []]
        sts = [d[sub]]
        if None in d:
            sts.append(d[None])
        return sts, d[sub], False

    def add(self, eng, fn, reads=(), writes=(), dma=False):
        op = Op(eng, fn, dma)
        deps = []
        for k in reads:
            sts, own, whole = self._states(k)
            for st in sts:
                if st[0] is not None:
                    deps.append(st[0])
        for k in writes:
            sts, own, whole = self._states(k)
            for st in sts:
                if st[0] is not None:
                    deps.append(st[0])
                deps.extend(st[1])
        for k in reads:
            sts, own, whole = self._states(k)
            own[1].append(op)
        for k in writes:
            sts, own, whole = self._states(k)
            if whole:
                for st in sts:
                    st[0] = None
                    st[1] = []
            own[0] = op
            own[1] = []
        seen = set()
        for d in deps:
            if d is op or id(d) in seen:
                continue
            seen.add(id(d))
            if d.eng == eng and eng == 'pe' and not d.is_dma and not dma:
                continue
            op.deps.append(d)
            d.signal = True
        self.ops[eng].append(op)
        return op

    def barrier(self):
        last = []
        for e in ENGS:
            got = False
            for o in reversed(self.ops[e]):
                if o.barriered:
                    break
                if o.is_dma:
                    last.append(o)
                elif not got:
                    last.append(o)
                    got = True
        b = Op('sp', lambda e: e.nop(), False)
        for d in last:
            b.deps.append(d)
            d.signal = True
        for e in ENGS:
            for o in reversed(self.ops[e]):
                if o.barriered:
                    break
                o.barriered = True
        b.barriered = True
        self.ops['sp'].append(b)
        for e in ENGS:
            if e == 'sp':
                continue
            o = Op(e, None, False)
            o.barriered = True
            o.deps.append(b)
            b.signal = True
            self.ops[e].append(o)
        self.res = {}

    def emit(self):
        nc = self.nc
        with contextlib.ExitStack() as es:
            csem = {e: es.enter_context(nc.semaphore('c_' + e)) for e in ENGS}
            dsems = {e: [es.enter_context(nc.semaphore('d_%s_%d' % (e, i)))
                         for i in range(self.n_dma_sems)] for e in ('sp', 'pool', 'act')}
            for e in ENGS:
                cnt = 0
                dcnt = [0] * self.n_dma_sems
                rr = 0
                for op in self.ops[e]:
                    if op.is_dma:
                        s = rr % self.n_dma_sems
                        rr += 1
                        if dcnt[s] > 0:
                            op.prewait = (dsems[e][s], dcnt[s])
                        dcnt[s] += 16
                        op.sem = dsems[e][s]
                        op.val = dcnt[s]
                    elif op.signal:
                        cnt += 1
                        op.sem = csem[e]
                        op.val = cnt
            engobj = {'pe': 'tensor', 'act': 'scalar', 'dve': 'vector', 'pool': 'gpsimd', 'sp': 'sync'}

            def run(e, eng):
                seen = {}
                for op in self.ops[e]:
                    waits = []
                    if op.prewait is not None:
                        waits.append(op.prewait)
                    for d in op.deps:
                        waits.append((d.sem, d.val))
                    mx = {}
                    for (s, v) in waits:
                        k = id(s)
                        if v > mx.get(k, (None, 0))[1]:
                            mx[k] = (s, v)
                    for k, (s, v) in mx.items():
                        if seen.get(k, 0) >= v:
                            continue
                        seen[k] = v
                        eng.wait_ge(s, v)
                    if op.fn is None:
                        continue
                    ins = op.fn(eng)
                    if op.is_dma:
                        ins.then_inc(op.sem, 16)
                    elif op.signal:
                        ins.then_inc(op.sem, 1)

            with nc.Block() as block:
                for e in ENGS:
                    if not self.ops[e]:
                        continue
                    getattr(block, engobj[e])(lambda eng, e=e: run(e, eng))


class SB:
    def __init__(self, big, nbytes):
        self.big = big
        self.nbytes = nbytes
        self.off = 0
        self.views = {}

    def view(self, dtype):
        if dtype not in self.views:
            self.views[dtype] = self.big.bitcast(dtype) if dtype != F32 else self.big
        return self.views[dtype]

    def alloc(self, cols, dtype, parts=128):
        sz = mybir.dt.size(dtype)
        nb = (cols * sz + 63) // 64 * 64
        assert self.off + nb <= self.nbytes, ('SBUF overflow', self.off, nb, self.nbytes)
        o = self.off // sz
        self.off += nb
        return self.view(dtype)[0:parts, o:o + cols]

    def mark(self):
        return self.off

    def reset(self, m):
        self.off = m


D = 1024
KC = 8
NP = 2048
NS = 64
NT = NP + NS
TILES = [(0, 512), (512, 512), (1024, 512), (1536, 512), (2048, 64)]
DFF = 2816
FGROUPS = [(0, 4), (4, 4), (8, 4), (12, 4), (16, 3), (19, 3)]
DEPTH = 4
EPS = 1e-6

WSHAPES = {
    'ffn1_gate_up': (4, 1024, 5632), 'ffn1_down': (4, 2816, 1024),
    'ffn2_gate_up': (4, 1024, 5632), 'ffn2_down': (4, 2816, 1024),
    'ple_in': (4, 256, 1024), 'ple_gate': (4, 1024, 1024),
    'm_in_proj': (2, 1024, 6176), 'm_out_proj': (2, 2048, 1024),
    'r_wr': (2, 1024, 1024), 'r_wk': (2, 1024, 1024), 'r_wv': (2, 1024, 1024), 'r_wo': (2, 1024, 1024),
    'r_w1': (2, 1024, 64), 'r_w2': (2, 64, 1024), 'r_a1': (2, 1024, 64), 'r_a2': (2, 64, 1024),
    'r_g1': (2, 1024, 160), 'r_g2': (2, 160, 1024), 'r_v1': (1, 1024, 32), 'r_v2': (1, 32, 1024),
}
VECS = ['norm_ffn1', 'norm_mix', 'norm_ffn2', 'norm_ple', 'norm_final', 'm_conv_w', 'm_conv_b', 'm_norm',
        'r_mu', 'r_w0', 'r_a0', 'r_k_k', 'r_k_a', 'r_r_k', 'r_gn_w', 'r_gn_b', 'r_v0']
VSHAPES = {'norm_ffn1': (4, 1024), 'norm_mix': (4, 1024), 'norm_ffn2': (4, 1024), 'norm_ple': (4, 1024),
           'norm_final': (1024,), 'm_conv_w': (2, 4, 4096), 'm_conv_b': (2, 4096), 'm_norm': (2, 2048),
           'r_mu': (2, 6, 1024), 'r_w0': (2, 1024), 'r_a0': (2, 1024), 'r_k_k': (2, 1024), 'r_k_a': (2, 1024),
           'r_r_k': (2, 1024), 'r_gn_w': (2, 1024), 'r_gn_b': (2, 1024), 'r_v0': (1, 1024)}
VOFF = {}
_o = 0
for _n in VECS:
    VOFF[_n] = _o
    _o += int(np.prod(VSHAPES[_n])) // 128
NV = _o

C_ID, C_TRIP, C_ONES, C_TRIS, C_BLKS, C_BM, C_BMT = 0, 128, 256, 384, 448, 512, 528
C_STRIP = C_BMT + 1024
C_LOWP = C_STRIP + 128
C_STRIS = C_LOWP + 128
C_LOWS = C_STRIS + 64
C_MABP = C_LOWS + 64
C_MABS = C_MABP + 256
C_RSTP = C_MABS + 128
C_RSTS = C_RSTP + 128
C_BLK2 = C_RSTS + 64
NCC = C_BLK2 + 128

CFG = {'mixers': True, 'depth': DEPTH, 'cores': 8, 'rwkv': True}


class _Stop(Exception):
    pass


def _stg(n):
    if CFG.get('rstage', 99) < n:
        raise _Stop()


def make_consts():
    c = np.zeros((128, NCC), np.float32)
    i = np.arange(128)
    c[:, C_ID:C_ID + 128] = np.eye(128)
    c[:, C_TRIP:C_TRIP + 128] = (i[:, None] <= i[None, :])
    c[:, C_ONES:C_ONES + 128] = 1.0
    j = np.arange(64)
    same = (j[:, None] // 4) == (j[None, :] // 4)
    c[:64, C_TRIS:C_TRIS + 64] = same & (j[:, None] <= j[None, :])
    c[:64, C_BLKS:C_BLKS + 64] = same
    c[:64, C_BM:C_BM + 16] = (j[:, None] // 4) == np.arange(16)[None, :]
    bmt = ((j[None, :] // 4) == np.arange(16)[:, None]).astype(np.float32).reshape(1, 1024)
    c[:, C_BMT:C_BMT + 1024] = bmt
    c[:, C_STRIP:C_STRIP + 128] = (i[:, None] < i[None, :])
    c[:, C_LOWP:C_LOWP + 128] = (i[None, :] < i[:, None])
    c[:64, C_STRIS:C_STRIS + 64] = same & (j[:, None] < j[None, :])
    c[:64, C_LOWS:C_LOWS + 64] = same & (j[None, :] < j[:, None])
    c[:, C_MABP:C_MABP + 128] = c[:, C_STRIP:C_STRIP + 128]
    c[:, C_MABP + 128:C_MABP + 256] = c[:, C_TRIP:C_TRIP + 128]
    c[:64, C_MABS:C_MABS + 64] = c[:64, C_STRIS:C_STRIS + 64]
    c[:64, C_MABS + 64:C_MABS + 128] = c[:64, C_TRIS:C_TRIS + 64]
    c[:, C_RSTP:C_RSTP + 128] = 1.0
    c[:, C_RSTP] = 0.0
    c[:, C_RSTS:C_RSTS + 64] = (np.arange(64) % 4 != 0)[None, :]
    c[:, C_BLK2:C_BLK2 + 128] = (i[:, None] // 64) == (i[None, :] // 64)
    return c


def build_program():
    nc = bass.Bass("TRN2", target_bir_lowering=False)

    def din(name, shape):
        return nc.dram_tensor(name, list(shape), F32, kind="ExternalInput").ap()

    def dout(name, shape):
        return nc.dram_tensor(name, list(shape), F32, kind="ExternalOutput").ap()

    xT = din('xT', [D, NT])
    pT = din('pT', [4, 256, NT])
    vecs_d = din('vecs', [128, NV])
    Wd_ = {n: din(n, s) for n, s in WSHAPES.items()}
    yT = dout('yT', [D, NT])
    consts_d = din('consts', [128, NCC])
    hv_d = din('hv', [32, 4])
    dbc_d = din('dbc', [128, 64])
    ssm_in = din('ssm_in', [2, 16, 32, 64, 128])
    conv_in = din('conv_in', [2, 4096, 16, 3])
    wkv_in = din('wkv_in', [2, 16, 16, 64, 64])
    shift_in = din('shift_in', [2, 1024, 16])
    ssm_p = dout('ssm_p', [2, 32, 64, 128])
    conv_p = dout('conv_p', [2, 4096, 3])
    ssm_s = dout('ssm_s', [2, 16, 32, 64, 128])
    conv_s = dout('conv_s', [2, 4096, 16, 3])
    wkv_p = dout('wkv_p', [2, 16, 64, 64])
    shift_p = dout('shift_p', [2, 128, 8])
    wkv_s = dout('wkv_s', [2, 16, 16, 64, 64])
    shift_s = dout('shift_s', [2, 1024, 16])
    vfirst_d = dout('vfirst', [D, NT])

    with contextlib.ExitStack() as es:
        NB = 206 * 1024
        big = es.enter_context(nc.sbuf_tensor("big", [128, NB // 4], F32))
        psl = [es.enter_context(nc.psum_tensor("ps%d" % i, [128, 512], F32)) for i in range(8)]
        sb = SB(big, NB)
        S = Sched(nc)
        pctr = [0]

        def psum():
            i = pctr[0] % 8
            pctr[0] += 1
            return psl[i], ('ps', i)

        def MM(out, lhsT, rhs, start, stop, r, w):
            S.add('pe', lambda e: e.matmul(out, lhsT=lhsT, rhs=rhs, start=start, stop=stop), reads=r, writes=w)

        def TR(out, in_, ident, r, w):
            S.add('pe', lambda e: e.transpose(out, in_, ident), reads=r, writes=w)

        def ACT(out, in_, func, r, w, bias=None, scale=1.0):
            if bias is None:
                S.add('act', lambda e: e.activation(out=out, in_=in_, func=func, scale=scale), reads=r, writes=w)
            else:
                S.add('act', lambda e: e.activation(out=out, in_=in_, func=func, bias=bias, scale=scale), reads=r, writes=w)

        def TT(eng, out, in0, in1, op, r, w):
            S.add(eng, lambda e: e.tensor_tensor(out=out, in0=in0, in1=in1, op=op), reads=r, writes=w)

        def TS(eng, out, in0, s1, s2, op0, op1, r, w):
            if s2 is None:
                S.add(eng, lambda e: e.tensor_scalar(out=out, in0=in0, scalar1=s1, scalar2=None, op0=op0), reads=r, writes=w)
            else:
                S.add(eng, lambda e: e.tensor_scalar(out=out, in0=in0, scalar1=s1, scalar2=s2, op0=op0, op1=op1), reads=r, writes=w)

        def STT(eng, out, in0, scalar, in1, op0, op1, r, w):
            S.add(eng, lambda e: e.scalar_tensor_tensor(out=out, in0=in0, scalar=scalar, in1=in1, op0=op0, op1=op1), reads=r, writes=w)

        def CP(eng, out, in_, r, w):
            S.add(eng, lambda e: e.tensor_copy(out=out, in_=in_), reads=r, writes=w)

        def MSET(eng, ap, val, w):
            S.add(eng, lambda e: e.memset(ap, val), writes=w)

        def DMA(eng, out, in_, r, w):
            S.add(eng, lambda e: e.dma_start(out=out, in_=in_), reads=r, writes=w, dma=True)

        h = sb.alloc(KC * NT, F32).rearrange("p (k t) -> p k t", k=KC)
        vecs = sb.alloc(NV, F32)
        ones_bf = sb.alloc(128, BF16)
        epsc = sb.alloc(1, F32)
        cst = sb.alloc(NCC, F32)
        hv = sb.alloc(4, F32, parts=32)
        dbc = sb.alloc(64, F32)
        onec = sb.alloc(1, F32)
        eps5c = sb.alloc(1, F32)
        ident_b = sb.alloc(128, BF16)
        trip_b = sb.alloc(128, BF16)
        tris_b = sb.alloc(64, BF16)
        bmT_b = sb.alloc(1024, BF16)
        DMA('sp', vecs, vecs_d, [], ['vecs'])
        DMA('sp', cst, consts_d, [], ['cst'])
        DMA('sp', hv, hv_d, [], ['hv'])
        DMA('sp', dbc, dbc_d, [], ['dbc'])
        MSET('dve', onec, 1.0, ['onec'])
        MSET('dve', eps5c, 1e-5, ['eps5c'])
        CP('dve', ident_b, cst[:, C_ID:C_ID + 128], ['cst'], ['ident_b'])
        CP('dve', trip_b, cst[:, C_TRIP:C_TRIP + 128], ['cst'], ['trip_b'])
        CP('dve', tris_b, cst[:, C_TRIS:C_TRIS + 64], ['cst'], ['tris_b'])
        CP('dve', bmT_b, cst[:, C_BMT:C_BMT + 1024], ['cst'], ['bmT_b'])
        ident_f = cst[:, C_ID:C_ID + 128]
        DMA('sp', h, xT.rearrange("(k p) t -> p k t", p=128), [], ['h'])
        MSET('dve', ones_bf, 1.0, ['ones_bf'])
        MSET('dve', epsc, EPS, ['epsc'])

        def vcol(name, idx):
            o = VOFF[name] + idx
            return vecs[:, o:o + 1]

        scr0 = sb.mark()
        S.barrier()

        def rmsnorm(t0, T, gname, gbase, out, okey, sq, rs, ti, hkey=None, sqkey=('sq',)):
            hk = hkey if hkey is not None else ('h', ti)
            ACT(sq[:, :, :T], h[:, :, t0:t0 + T], AF.Square, [hk], [sqkey])
            ps, pk = psum()
            for k in range(KC):
                MM(ps[:, :T], ones_bf, sq[:, k, :T], k == 0, k == KC - 1, [sqkey, 'ones_bf'], [pk])
            ACT(rs[:, :T], ps[:, :T], AF.Ln, [pk, 'epsc'], [('rs',)], bias=epsc[:, 0:1], scale=1.0 / D)
            ACT(rs[:, :T], rs[:, :T], AF.Exp, [('rs',)], [('rs',)], scale=-0.5)
            for k in range(KC):
                STT('dve', out[:, k, :T], h[:, k, t0:t0 + T], vcol(gname, gbase + k), rs[:, :T], ALU.mult, ALU.mult,
                    [hk, ('rs',), 'vecs'], [okey])

        def ffn(l, wgu_d, wd_d, gname):
            m = sb.mark()
            xn = sb.alloc(KC * NT, BF16).rearrange("p (k t) -> p k t", k=KC)
            act = sb.alloc(4 * NT, BF16).rearrange("p (f t) -> p f t", f=4)
            wgu = [sb.alloc(KC * 2 * 512, BF16).rearrange("p (k g n) -> p k g n", k=KC, g=2) for _ in range(2)]
            wdb = [sb.alloc(4 * 1024, BF16).rearrange("p (f n) -> p f n", f=4) for _ in range(2)]
            sq = sb.alloc(KC * 512, BF16).rearrange("p (k t) -> p k t", k=KC)
            rs = sb.alloc(512, F32)
            sg = [sb.alloc(512, F32) for _ in range(2)]
            for ti, (t0, T) in enumerate(TILES):
                rmsnorm(t0, T, gname, l * KC, xn[:, :, t0:t0 + T], ('xn', ti), sq, rs, ti)
            wg_v = wgu_d[l].rearrange("(k p) n -> p k n", p=128)
            wd_v = wd_d[l].rearrange("(f p) n -> p f n", p=128)
            cnt = 0
            for gi, (f0, nf) in enumerate(FGROUPS):
                wb = wgu[gi % 2]
                wdd = wdb[gi % 2]
                DMA('pool', wb[:, :, 0, :nf * 128], wg_v[:, :, f0 * 128:(f0 + nf) * 128], [], [('wgu', gi % 2)])
                DMA('pool', wb[:, :, 1, :nf * 128], wg_v[:, :, DFF + f0 * 128:DFF + (f0 + nf) * 128], [], [('wgu', gi % 2)])
                DMA('pool', wdd[:, :nf, :], wd_v[:, f0:f0 + nf, :], [], [('wd', gi % 2)])
                for ti, (t0, T) in enumerate(TILES):
                    for f in range(nf):
                        pg, pgk = psum()
                        pu, puk = psum()
                        for k in range(KC):
                            MM(pg[:, :T], wb[:, k, 0, f * 128:(f + 1) * 128], xn[:, k, t0:t0 + T], k == 0, k == KC - 1,
                               [('wgu', gi % 2), ('xn', ti)], [pgk])
                        for k in range(KC):
                            MM(pu[:, :T], wb[:, k, 1, f * 128:(f + 1) * 128], xn[:, k, t0:t0 + T], k == 0, k == KC - 1,
                               [('wgu', gi % 2), ('xn', ti)], [puk])
                        sgb = sg[cnt % 2]
                        sk = ('sg', cnt % 2)
                        cnt += 1
                        ACT(sgb[:, :T], pg[:, :T], AF.Silu, [pgk], [sk])
                        TT('dve', act[:, f, t0:t0 + T], sgb[:, :T], pu[:, :T], ALU.mult, [sk, puk], [('act', f, ti)])
                for ti, (t0, T) in enumerate(TILES):
                    for d in range(KC):
                        po, pok = psum()
                        for f in range(nf):
                            MM(po[:, :T], wdd[:, f, d * 128:(d + 1) * 128], act[:, f, t0:t0 + T], f == 0, f == nf - 1,
                               [('wd', gi % 2), ('act', f, ti)], [pok])
                        STT('dve', h[:, d, t0:t0 + T], po[:, :T], 0.5, h[:, d, t0:t0 + T], ALU.mult, ALU.add,
                            [pok, ('h', ti)], [('h', ti)])
            S.barrier()
            sb.reset(m)

        def ple(l):
            m = sb.mark()
            xn = sb.alloc(KC * NT, BF16).rearrange("p (k t) -> p k t", k=KC)
            wg = sb.alloc(KC * 1024, BF16).rearrange("p (k n) -> p k n", k=KC)
            wpi = sb.alloc(2 * 1024, BF16).rearrange("p (k n) -> p k n", k=2)
            ptb = sb.alloc(2 * NT, BF16).rearrange("p (k t) -> p k t", k=2)
            sq = sb.alloc(KC * 512, BF16).rearrange("p (k t) -> p k t", k=KC)
            rs = sb.alloc(512, F32)
            sg = [sb.alloc(512, F32) for _ in range(2)]
            DMA('pool', wg, Wd_['ple_gate'][l].rearrange("(k p) n -> p k n", p=128), [], ['wg'])
            DMA('pool', wpi, Wd_['ple_in'][l].rearrange("(k p) n -> p k n", p=128), [], ['wpi'])
            DMA('pool', ptb, pT[l].rearrange("(k p) t -> p k t", p=128), [], ['ptb'])
            for ti, (t0, T) in enumerate(TILES):
                rmsnorm(t0, T, 'norm_ple', l * KC, xn[:, :, t0:t0 + T], ('xn', ti), sq, rs, ti)
            cnt = 0
            for ti, (t0, T) in enumerate(TILES):
                for d in range(KC):
                    p1, p1k = psum()
                    p2, p2k = psum()
                    for k in range(KC):
                        MM(p1[:, :T], wg[:, k, d * 128:(d + 1) * 128], xn[:, k, t0:t0 + T], k == 0, k == KC - 1,
                           ['wg', ('xn', ti)], [p1k])
                    for k in range(2):
                        MM(p2[:, :T], wpi[:, k, d * 128:(d + 1) * 128], ptb[:, k, t0:t0 + T], k == 0, k == 1,
                           ['wpi', 'ptb'], [p2k])
                    sgb = sg[cnt % 2]
                    sk = ('sg', cnt % 2)
                    cnt += 1
                    ACT(sgb[:, :T], p1[:, :T], AF.Sigmoid, [p1k], [sk])
                    TT('dve', sgb[:, :T], sgb[:, :T], p2[:, :T], ALU.mult, [sk, p2k], [sk])
                    TT('pool', h[:, d, t0:t0 + T], h[:, d, t0:t0 + T], sgb[:, :T], ALU.add, [sk, ('h', ti)], [('h', ti)])
            S.barrier()
            sb.reset(m)

        def final():
            m = sb.mark()
            sq = sb.alloc(KC * 512, BF16).rearrange("p (k t) -> p k t", k=KC)
            rs = sb.alloc(512, F32)
            yo = [sb.alloc(KC * 512, F32).rearrange("p (k t) -> p k t", k=KC) for _ in range(2)]
            yv = yT.rearrange("(k p) t -> p k t", p=128)
            for ti, (t0, T) in enumerate(TILES):
                yb = yo[ti % 2]
                rmsnorm(t0, T, 'norm_final', 0, yb, ('yo', ti % 2), sq, rs, ti)
                DMA('sp', yv[:, :, t0:t0 + T], yb[:, :, :T], [('yo', ti % 2)], [('yT', ti)])
            sb.reset(m)

        def mamba(l):
            j = l // 2
            Win = Wd_['m_in_proj'][j].rearrange("(k p) n -> p k n", p=128)
            Wout = Wd_['m_out_proj'][j].rearrange("(c p) n -> p c n", p=128)
            R0 = ['cst', 'vecs', 'hv', 'dbc', 'onec', 'eps5c', 'ident_b', 'trip_b', 'tris_b', 'bmT_b']
            cw = [0]

            def run(smp):
                m = sb.mark()
                T = 64 if smp else 256
                L = 64 if smp else 128
                nch = T // L
                tiles = [(2048, 64)] if smp else [(i * 256, 256) for i in range(8)]
                tri = cst[:L, C_TRIS:C_TRIS + 64] if smp else cst[:, C_TRIP:C_TRIP + 128]
                onesm = cst[:L, C_BLKS:C_BLKS + 64] if smp else cst[:, C_ONES:C_ONES + 128]
                cmask = tris_b[:L, :] if smp else trip_b
                u = sb.alloc(KC * T, BF16).rearrange("p (k t) -> p k t", k=KC)
                sq = sb.alloc(KC * T, BF16).rearrange("p (k t) -> p k t", k=KC)
                rs = sb.alloc(T, F32)
                wblk = [sb.alloc(KC * 512, BF16).rearrange("p (k n) -> p k n", k=KC) for _ in range(2)]
                wdt = sb.alloc(KC * 32, BF16).rearrange("p (k n) -> p k n", k=KC)
                zs = sb.alloc(nch * 2048, BF16).rearrange("p (c n) -> p c n", c=nch)
                PW = 112 if smp else 3 + T
                preb = [sb.alloc(PW, F32) for _ in range(2)]
                accb = [sb.alloc(T, F32) for _ in range(2)]
                xa = sb.alloc(32 * T, BF16).rearrange("p (c t) -> p c t", c=32)
                dtT = sb.alloc(T, F32, parts=32)
                dtAT = sb.alloc(T, F32, parts=32)
                expA = sb.alloc(1, F32, parts=32)
                xtok = sb.alloc(2048, BF16)
                Btok = sb.alloc(1024, BF16)
                dtk = sb.alloc(64, F32)
                nak = sb.alloc(64, F32)
                ena = sb.alloc(32, F32)
                dend = sb.alloc(32, F32)
                xdt = sb.alloc(2048, BF16)
                cbm = sb.alloc(128, BF16)
                dexp = sb.alloc(4 * 128, F32).rearrange("p (a b) -> p a b", a=4)
                dec = [sb.alloc(128, F32) for _ in range(2)]
                MT = sb.alloc(4 * 128, BF16).rearrange("p (a b) -> p a b", a=4)
                tmp = sb.alloc(256, F32)
                yg = sb.alloc(256, F32)
                ssq = sb.alloc(1, F32)
                ynk = sb.alloc(256, BF16)
                ynT = sb.alloc(16 * T, BF16).rearrange("p (c t) -> p c t", c=16)
                xdd = sb.alloc(256, BF16)
                wob = [sb.alloc(16 * 128, BF16).rearrange("p (c n) -> p c n", c=16) for _ in range(2)]
                if smp:
                    cstate = sb.alloc(32 * 48, F32).rearrange("p (c b r) -> p c b r", c=32, b=16)
                    cso = sb.alloc(32 * 48, F32).rearrange("p (c b r) -> p c b r", c=32, b=16)
                    nat = sb.alloc(16 * 256, F32).rearrange("p (b q n) -> p b q n", b=16, q=2)
                    STs = sb.alloc(16 * 256, F32).rearrange("p (b n) -> p b n", b=16)
                    STsb = sb.alloc(16 * 256, BF16).rearrange("p (b n) -> p b n", b=16)
                    CTm = sb.alloc(16 * 64, BF16).rearrange("p (b t) -> p b t", b=16)
                    xddm = sb.alloc(16 * 256, BF16).rearrange("p (b n) -> p b n", b=16)
                    edCs = sb.alloc(512, F32).rearrange("p (b h) -> p b h", b=16)
                    dtAe = sb.alloc(512, F32).rearrange("p (b h) -> p b h", b=16)
                    DMA('sp', cstate, conv_in[j].rearrange("(c p) b r -> p c b r", p=128), [], ['cstate'])
                else:
                    carry = sb.alloc(96, F32).rearrange("p (c r) -> p c r", c=32)
                    ST = sb.alloc(2048, F32)
                    STb = sb.alloc(2048, BF16)
                    natp = sb.alloc(2048, F32).rearrange("p (q n) -> p q n", q=16)
                    edC = sb.alloc(32, F32)
                    MSET('dve', carry, 0.0, ['carry'])
                    MSET('dve', ST, 0.0, ['ST'])
                    MSET('pool', STb, 0.0, ['STb'])
                ACT(expA, hv[:, 2 * j + 1:2 * j + 2], AF.Exp, ['hv'], ['expA'])

                for (t0, T_) in tiles:
                    rmsnorm(t0, T, 'norm_mix', l * KC, u, ('u',), sq, rs, 0, hkey='h')
                    for zb in range(4):
                        wb = wblk[cw[0] % 2]
                        wk = ('wblk', cw[0] % 2)
                        cw[0] += 1
                        DMA('pool', wb, Win[:, :, zb * 512:(zb + 1) * 512], [], [wk])
                        for tc in range(nch):
                            ps, pk = psum()
                            for k in range(KC):
                                MM(ps[:L, :], u[:, k, tc * L:(tc + 1) * L], wb[:, k, :], k == 0, k == KC - 1, [('u',), wk], [pk])
                            ACT(zs[:L, tc, zb * 512:(zb + 1) * 512], ps[:L, :], AF.Silu, [pk], [('zs', tc, zb)])
                    for xb in range(8):
                        wb = wblk[cw[0] % 2]
                        wk = ('wblk', cw[0] % 2)
                        cw[0] += 1
                        DMA('pool', wb, Win[:, :, 2048 + xb * 512:2048 + (xb + 1) * 512], [], [wk])
                        for q in range(4):
                            cc = xb * 4 + q
                            ps, pk = psum()
                            for k in range(KC):
                                MM(ps[:, :T], wb[:, k, q * 128:(q + 1) * 128], u[:, k, :], k == 0, k == KC - 1, [('u',), wk], [pk])
                            pre = preb[cc % 2]
                            prk = ('pre', cc % 2)
                            acc = accb[cc % 2]
                            ack = ('acc', cc % 2)
                            ce = 'dve'
                            if smp:
                                prev = pre.rearrange("p (b c) -> p b c", c=7)
                                CP('pool', prev[:, :, 0:3], cstate[:, cc], ['cstate'], [prk])
                                ACT(prev[:, :, 3:7], ps[:, :T].rearrange("p (b t) -> p b t", t=4), AF.Copy, [pk], [prk])
                                win = lambda k_: prev[:, :, k_:k_ + 4]
                                accv = acc.rearrange("p (b t) -> p b t", t=4)
                                xav = xa[:, cc, :].rearrange("p (b t) -> p b t", t=4)
                            else:
                                CP('pool', pre[:, 0:3], carry[:, cc, :], ['carry'], [prk])
                                ACT(pre[:, 3:3 + T], ps[:, :T], AF.Copy, [pk], [prk])
                                win = lambda k_: pre[:, k_:k_ + T]
                                accv = acc
                                xav = xa[:, cc, :]
                            TS(ce, accv, win(0), vcol('m_conv_w', (j * 4 + 0) * 32 + cc), None, ALU.mult, None, [prk, 'vecs'], [ack])
                            for k_ in range(1, 4):
                                STT(ce, accv, win(k_), vcol('m_conv_w', (j * 4 + k_) * 32 + cc), accv, ALU.mult, ALU.add,
                                    [prk, 'vecs', ack], [ack])
                            ACT(xav, accv, AF.Silu, [ack, 'vecs'], [('xa', cc)], bias=vcol('m_conv_b', j * 32 + cc))
                            if smp:
                                CP('pool', cso[:, cc], prev[:, :, 4:7], [prk], [('cso', cc)])
                            else:
                                CP('pool', carry[:, cc, :], pre[:, T:T + 3], [prk], ['carry'])
                    DMA('pool', wdt, Win[:, :, 6144:6176], [], ['wdt'])
                    ps, pk = psum()
                    for k in range(KC):
                        MM(ps[:32, :T], wdt[:, k, :], u[:, k, :], k == 0, k == KC - 1, [('u',), 'wdt'], [pk])
                    ACT(dtT, ps[:32, :T], AF.Exp, [pk, 'hv'], ['dtT'], bias=hv[:, 2 * j:2 * j + 1])
                    ACT(dtT, dtT, AF.Ln, ['dtT', 'onec'], ['dtT'], bias=onec[0:32, 0:1])
                    TS('dve', dtAT, dtT, expA[:, 0:1], None, ALU.mult, None, ['dtT', 'expA'], ['dtAT'])

                    for tc in range(nch):
                        c0 = tc * L
                        for half in range(2):
                            ps, pk = psum()
                            psb = ps.bitcast(BF16)
                            for q in range(8):
                                cc = half * 8 + q
                                TR(psb[:L, q * 128:(q + 1) * 128], xa[:, cc, c0:c0 + L], ident_b, [('xa', cc), 'ident_b'], [pk])
                            CP('dve', xtok[:L, half * 1024:(half + 1) * 1024], psb[:L, :], [pk], [('xtok', half)])
                        ps, pk = psum()
                        psb = ps.bitcast(BF16)
                        for g in range(8):
                            TR(psb[:L, g * 128:(g + 1) * 128], xa[:, 16 + g, c0:c0 + L], ident_b, [('xa', 16 + g), 'ident_b'], [pk])
                        CP('dve', Btok[:L, :], psb[:L, :], [pk], ['Btok'])
                        ps, pk = psum()
                        TR(ps[:L, 0:32], dtT[:, c0:c0 + L], ident_f[0:32, 0:32], ['dtT', 'cst'], [pk])
                        TR(ps[:L, 32:64], dtAT[:, c0:c0 + L], ident_f[0:32, 0:32], ['dtAT', 'cst'], [pk])
                        CP('dve', dtk[:L, :], ps[:L, 0:64], [pk], ['dtk'])
                        ps, pk = psum()
                        MM(ps[:L, 0:32], tri, dtk[:L, 32:64], True, True, ['cst', 'dtk'], [pk])
                        MM(ps[:L, 32:64], onesm, dtk[:L, 32:64], True, True, ['cst', 'dtk'], [pk])
                        CP('dve', nak[:L, :], ps[:L, 0:64], [pk], ['nak'])
                        ACT(ena[:L, :], nak[:L, 0:32], AF.Exp, ['nak'], ['ena'], scale=-1.0)
                        TT('dve', dend[:L, :], nak[:L, 0:32], nak[:L, 32:64], ALU.subtract, ['nak'], ['dend'])
                        ACT(dend[:L, :], dend[:L, :], AF.Exp, ['dend'], ['dend'])
                        TT('dve', xdt[:L, :].rearrange("p (h d) -> p h d", h=32), xtok[:L, :].rearrange("p (h d) -> p h d", h=32),
                           dtk[:L, 0:32].rearrange("p (h o) -> p h o", o=1).broadcast_to([L, 32, 64]), ALU.mult, ['xtok', 'dtk'], ['xdt'])
                        if smp:
                            TT('dve', dtAe[:L], dtk[:L, 32:64].rearrange("p (o h) -> p o h", o=1).broadcast_to([L, 16, 32]),
                               cst[:L, C_BM:C_BM + 16].rearrange("p (b o) -> p b o", o=1).broadcast_to([L, 16, 32]), ALU.mult, ['dtk', 'cst'], ['dtAe'])
                            ps, pk = psum()
                            MM(ps[:, 0:512], cst[:L, C_ONES:C_ONES + 128], dtAe[:L].rearrange("p b h -> p (b h)"), True, True, ['cst', 'dtAe'], [pk])
                            ACT(edCs.rearrange("p b h -> p (b h)"), ps[:, 0:512], AF.Exp, [pk], ['edCs'], scale=-1.0)
                        else:
                            ACT(edC, nak[:, 32:64], AF.Exp, ['nak'], ['edC'], scale=-1.0)

                        for g in range(8):
                            BT = xa[:, 16 + g, c0:c0 + L]
                            CT = xa[:, 24 + g, c0:c0 + L]
                            if smp:
                                for q_ in range(2):
                                    DMA('sp', nat[:, :, q_, :], ssm_in[j][:, 4 * g + 2 * q_:4 * g + 2 * q_ + 2].rearrange("b a p n -> (a p) b n"), [], ['nat'])
                                for b4 in range(8):
                                    ps, pk = psum()
                                    for i4 in range(4):
                                        b_ = b4 * 2 + i4 // 2
                                        q_ = i4 % 2
                                        TR(ps[:, i4 * 128:(i4 + 1) * 128], nat[:, b_, q_, :], ident_f, ['nat', 'cst'], [pk])
                                    CP('dve', STs[:, b4 * 2:b4 * 2 + 2, :].rearrange("p b n -> p (b n)"), ps[:, :], [pk], [('STs', b4)])
                                    CP('act' if False else 'pool', STsb[:, b4 * 2:b4 * 2 + 2, :], STs[:, b4 * 2:b4 * 2 + 2, :], [('STs', b4)], [('STsb', b4)])
                            ps, pk = psum()
                            MM(ps[:L, :L], BT, CT, True, True, [('xa', 16 + g), ('xa', 24 + g)], [pk])
                            TT('dve', cbm[:L, :L], ps[:L, :L], cmask[:L, :L], ALU.mult, [pk, 'trip_b', 'tris_b'], ['cbm'])
                            CP('pool', dexp[:L, :, :L], dtk[:L, 32 + 4 * g:36 + 4 * g].rearrange("p (a o) -> p a o", o=1).broadcast_to([L, 4, L]), ['dtk'], ['dexp'])
                            ps2, pk2 = psum()
                            for hh in range(4):
                                MM(ps2[:L, hh * L:(hh + 1) * L], dexp[:L, hh, :L], tri, True, True, ['dexp', 'cst'], [pk2])
                            for hh in range(4):
                                h_ = 4 * g + hh
                                dc = dec[hh % 2]
                                dk = ('dec', hh % 2)
                                ACT(dc[:L, :L], ps2[:L, hh * L:(hh + 1) * L], AF.Exp, [pk2, 'nak'], [dk], bias=nak[:L, h_:h_ + 1], scale=-1.0)
                                STT('dve', MT[:L, hh, :L], dc[:L, :L], 1.0, cbm[:L, :L], ALU.min, ALU.mult, [dk, 'cbm'], [('MT', hh)])
                            psy, pyk = psum()
                            for hh in range(4):
                                h_ = 4 * g + hh
                                MM(psy[:L, hh * 64:(hh + 1) * 64], MT[:L, hh, :L], xdt[:L, h_ * 64:(h_ + 1) * 64], True, True, [('MT', hh), 'xdt'], [pyk])
                            if smp:
                                TT('pool', CTm, CT.rearrange("p (o t) -> p o t", o=1).broadcast_to([128, 16, 64]),
                                   bmT_b.rearrange("p (b t) -> p b t", b=16), ALU.mult, [('xa', 24 + g), 'bmT_b'], ['CTm'])
                                for b in range(16):
                                    MM(psy[:L, 256:512], CTm[:, b, :], STsb[:, b, :], b == 0, b == 15, ['CTm', ('STsb', b // 2)], [pyk])
                            else:
                                MM(psy[:L, 256:512], CT, STb[:, g * 256:(g + 1) * 256], True, True, [('xa', 24 + g), ('STb', g)], [pyk])
                            t3 = tmp[:L, :].rearrange("p (a d) -> p a d", a=4)
                            TT('dve', t3, psy[:L, 256:512].rearrange("p (a d) -> p a d", a=4),
                               ena[:L, 4 * g:4 * g + 4].rearrange("p (a o) -> p a o", o=1).broadcast_to([L, 4, 64]), ALU.mult, [pyk, 'ena'], ['tmp'])
                            TT('dve', yg[:L, :], psy[:L, 0:256], tmp[:L, :], ALU.add, [pyk, 'tmp'], ['yg'])
                            TT('pool', t3, xtok[:L, g * 256:(g + 1) * 256].rearrange("p (a d) -> p a d", a=4),
                               dbc[:L, j * 32 + 4 * g:j * 32 + 4 * g + 4].rearrange("p (a o) -> p a o", o=1).broadcast_to([L, 4, 64]), ALU.mult,
                               [('xtok', g // 4), 'dbc', 'yg'], ['tmp'])
                            TT('dve', yg[:L, :], yg[:L, :], tmp[:L, :], ALU.add, ['tmp', 'yg'], ['yg'])
                            TT('dve', yg[:L, :], yg[:L, :], zs[:L, tc, g * 256:(g + 1) * 256], ALU.mult, ['yg', ('zs', tc, g // 2)], ['yg'])
                            TT('pool', tmp[:L, :], yg[:L, :], yg[:L, :], ALU.mult, ['yg'], ['tmp'])
                            S.add('dve', (lambda e, o=ssq[:L, 0:1], i_=tmp[:L, :]: e.tensor_reduce(out=o, in_=i_, axis=AX.X, op=ALU.add)), reads=['tmp'], writes=['ssq'])
                            ACT(ssq[:L, :], ssq[:L, :], AF.Ln, ['ssq', 'eps5c'], ['ssq'], bias=eps5c[:L, 0:1], scale=1.0 / 256)
                            ACT(ssq[:L, :], ssq[:L, :], AF.Exp, ['ssq'], ['ssq'], scale=-0.5)
                            TS('dve', ynk[:L, :], yg[:L, :], ssq[:L, 0:1], None, ALU.mult, None, ['yg', 'ssq'], ['ynk'])
                            pst, ptk = psum()
                            pstb = pst.bitcast(BF16)
                            for q in range(2):
                                TR(pstb[:, q * L:(q + 1) * L], ynk[:L, q * 128:(q + 1) * 128], ident_b[:L, :L], ['ynk', 'ident_b'], [ptk])
                            for q in range(2):
                                cc = 2 * g + q
                                ACT(ynT[:, cc, c0:c0 + L], pstb[:, q * L:(q + 1) * L], AF.Copy, [ptk, 'vecs'], [('ynT', cc)], scale=vcol('m_norm', j * 16 + cc))
                            TT('dve', xdd[:L, :].rearrange("p (a d) -> p a d", a=4), xdt[:L, g * 256:(g + 1) * 256].rearrange("p (a d) -> p a d", a=4),
                               dend[:L, 4 * g:4 * g + 4].rearrange("p (a o) -> p a o", o=1).broadcast_to([L, 4, 64]), ALU.mult, ['xdt', 'dend'], ['xdd'])
                            if smp:
                                TT('pool', xddm[:L], xdd[:L, :].rearrange("p (o n) -> p o n", o=1).broadcast_to([L, 16, 256]),
                                   cst[:L, C_BM:C_BM + 16].rearrange("p (b o) -> p b o", o=1).broadcast_to([L, 16, 256]), ALU.mult, ['xdd', 'cst'], ['xddm'])
                                TT('dve', STs.rearrange("p b (a d) -> p b a d", a=4), STs.rearrange("p b (a d) -> p b a d", a=4),
                                   edCs[:, :, 4 * g:4 * g + 4].rearrange("p b (a o) -> p b a o", o=1).broadcast_to([128, 16, 4, 64]), ALU.mult,
                                   ['STs', 'edCs', 'STsb'], ['STs'])
                                for b2 in range(8):
                                    pss, psk = psum()
                                    for i2 in range(2):
                                        MM(pss[:, i2 * 256:(i2 + 1) * 256], Btok[:L, g * 128:(g + 1) * 128], xddm[:L, b2 * 2 + i2, :], True, True, ['Btok', 'xddm'], [psk])
                                    TT('dve', STs[:, b2 * 2:b2 * 2 + 2, :].rearrange("p b n -> p (b n)"), STs[:, b2 * 2:b2 * 2 + 2, :].rearrange("p b n -> p (b n)"),
                                       pss[:, :], ALU.add, [psk, ('STs', b2)], [('STs', b2)])
                                for b4 in range(8):
                                    ps, pk = psum()
                                    for i4 in range(4):
                                        b_ = b4 * 2 + i4 // 2
                                        q_ = i4 % 2
                                        TR(ps[:, i4 * 128:(i4 + 1) * 128], STs[:, b_, q_ * 128:(q_ + 1) * 128], ident_f, [('STs', b4), 'cst'], [pk])
                                    CP('dve', nat[:, b4 * 2:b4 * 2 + 2].rearrange("p b q n -> p (b q n)"), ps[:, :], [pk], ['nat'])
                                for q_ in range(2):
                                    DMA('sp', ssm_s[j][:, 4 * g + 2 * q_:4 * g + 2 * q_ + 2].rearrange("b a p n -> (a p) b n"), nat[:, :, q_, :], ['nat'], [('ssm_s', j, g, q_)])
                            else:
                                pss, psk = psum()
                                MM(pss[:, 0:256], Btok[:L, g * 128:(g + 1) * 128], xdd[:L, :], True, True, ['Btok', 'xdd'], [psk])
                                sg3 = ST[:, g * 256:(g + 1) * 256].rearrange("p (a d) -> p a d", a=4)
                                TT('dve', sg3, sg3, edC[:, 4 * g:4 * g + 4].rearrange("p (a o) -> p a o", o=1).broadcast_to([128, 4, 64]), ALU.mult,
                                   [('ST', g), 'edC', ('STb', g)], [('ST', g)])
                                TT('dve', ST[:, g * 256:(g + 1) * 256], ST[:, g * 256:(g + 1) * 256], pss[:, 0:256], ALU.add, [psk, ('ST', g)], [('ST', g)])
                                CP('pool', STb[:, g * 256:(g + 1) * 256], ST[:, g * 256:(g + 1) * 256], [('ST', g)], [('STb', g)])
                    for d in range(KC):
                        wo = wob[d % 2]
                        wok = ('wob', d % 2)
                        DMA('pool', wo, Wout[:, :, d * 128:(d + 1) * 128], [], [wok])
                        ps, pk = psum()
                        for cc in range(16):
                            MM(ps[:, :T], wo[:, cc, :], ynT[:, cc, :], cc == 0, cc == 15, [wok, ('ynT', cc)], [pk])
                        TT('dve', h[:, d, t0:t0 + T], h[:, d, t0:t0 + T], ps[:, :T], ALU.add, [pk, 'h'], ['h'])
                if smp:
                    DMA('sp', conv_s[j].rearrange("(c p) b r -> p c b r", p=128), cso, ['cso'], [('conv_s', j)])
                else:
                    DMA('sp', conv_p[j].rearrange("(c p) r -> p c r", p=128), carry, ['carry'], [('conv_p', j)])
                    for q4 in range(4):
                        ps, pk = psum()
                        for i4 in range(4):
                            q = q4 * 4 + i4
                            TR(ps[:, i4 * 128:(i4 + 1) * 128], ST[:, q * 128:(q + 1) * 128], ident_f, ['ST', 'cst'], [pk])
                        CP('dve', natp[:, q4 * 4:q4 * 4 + 4, :].rearrange("p q n -> p (q n)"), ps[:, :], [pk], ['natp'])
                    DMA('sp', ssm_p[j].rearrange("(q a) p n -> (a p) q n", a=2), natp, ['natp'], [('ssm_p', j)])
                S.barrier()
                sb.reset(m)

            run(False)
            run(True)


        def rwkv(l):
            j = l // 2
            RDT = BF16
            mu0 = VOFF['r_mu'] + j * 48

            def wview(name, jj=None):
                return Wd_[name][j if jj is None else jj]

            def run(smp):
                m = sb.mark()
                T = 64 if smp else 128
                L = T
                nb = 16 if smp else 1
                lt = L // nb
                nlev = 2 if smp else 7
                tiles = [(2048, 64)] if smp else [(i * 128, 128) for i in range(16)]
                mAB = cst[:L, C_MABS:C_MABS + 128] if smp else cst[:, C_MABP:C_MABP + 256]
                mlow = cst[:L, C_LOWS:C_LOWS + 64] if smp else cst[:, C_LOWP:C_LOWP + 128]
                rst = cst[:, C_RSTS:C_RSTS + 64] if smp else cst[:, C_RSTP:C_RSTP + 128]
                blk2 = cst[:, C_BLK2:C_BLK2 + 128]
                f32a = lambda: sb.alloc(KC * T, F32).rearrange("p (k t) -> p k t", k=KC)
                b16a = lambda: sb.alloc(KC * T, BF16).rearrange("p (k t) -> p k t", k=KC)
                uf, up, r_, k_, v_, nlw, a_, kk, np_, tA, tB = [f32a() for _ in range(11)]
                xm = [b16a() for _ in range(2)]
                g_, BT_, KT_, BH_, KH_ = [b16a() for _ in range(5)]
                V_ = xm[1]
                yg = xm[0]
                ARt = sb.alloc(KC * 2 * T, BF16).rearrange("p (k c t) -> p k c t", k=KC, c=2)
                sq = BH_
                rs = sb.alloc(T, F32)
                wbuf = [sb.alloc(KC * 256, BF16).rearrange("p (k n) -> p k n", k=KC) for _ in range(2)]
                w1b = sb.alloc(KC * 64, BF16).rearrange("p (k n) -> p k n", k=KC)
                a1b = sb.alloc(KC * 64, BF16).rearrange("p (k n) -> p k n", k=KC)
                g1b = sb.alloc(KC * 160, BF16).rearrange("p (k n) -> p k n", k=KC)
                v1b = sb.alloc(KC * 32, BF16).rearrange("p (k n) -> p k n", k=KC)
                w2b = sb.alloc(1024, BF16)
                a2b = sb.alloc(1024, BF16)
                g2a = sb.alloc(1024, BF16)
                g2b = sb.alloc(1024, BF16)
                v2b = sb.alloc(1024, BF16)
                lo1 = sb.alloc(T, BF16)
                lo2 = sb.alloc(T, BF16)
                lo3 = sb.alloc(T, BF16)
                negw0 = sb.alloc(8, F32)
                omka = sb.alloc(8, F32)
                mhalf = sb.alloc(1, F32)
                c24 = sb.alloc(1, F32)
                gnec = sb.alloc(1, F32)
                eC = sb.alloc(KC * nb, F32).rearrange("p (k b) -> p k b", k=KC)
                Vtok = sb.alloc(KC * 128, BF16).rearrange("p (k n) -> p k n", k=KC)
                BHtok = sb.alloc(KC * 128, BF16).rearrange("p (k n) -> p k n", k=KC)
                KHtok = sb.alloc(KC * 128, BF16).rearrange("p (k n) -> p k n", k=KC)
                Ytok = sb.alloc(KC * 128, F32).rearrange("p (k n) -> p k n", k=KC)
                Ysq = up if not smp else sb.alloc(KC * 128, F32).rearrange("p (k n) -> p k n", k=KC)
                YSK = 'Ysq' if smp else 'up'
                Yn = sb.alloc(KC * 128, BF16).rearrange("p (k n) -> p k n", k=KC)
                st1 = sb.alloc(16, F32)
                st2 = sb.alloc(16, F32)
                NH = 1 if smp else 8
                ABt = [sb.alloc(2 * L, BF16) for _ in range(NH)]
                AKt = [sb.alloc(2 * L, BF16) for _ in range(NH)]
                MMb = [[sb.alloc(2 * L, BF16) for _ in range(2)] for _ in range(NH)]
                X32 = [sb.alloc(64, F32) for _ in range(NH)]
                Xb = [sb.alloc(64, BF16) for _ in range(NH)]
                S0T = sb.alloc(KC * nb * 64, F32).rearrange("p (k b i) -> p k b i", k=KC, b=nb)
                nbd = min(nb, 4)
                S0bh = [sb.alloc(nb * 64, BF16).rearrange("p (b i) -> p b i", b=nb) for _ in range(NH)]
                bd = sb.alloc(nbd * 128, F32).rearrange("p (b n) -> p b n", b=nbd)
                if smp:
                    sst = sb.alloc(KC * 16, F32).rearrange("p (k b) -> p k b", k=KC)
                    ATm = [sb.alloc(16 * 64, BF16).rearrange("p (b t) -> p b t", b=16) for _ in range(NH)]
                    RTm = [sb.alloc(16 * 64, BF16).rearrange("p (b t) -> p b t", b=16) for _ in range(NH)]
                    Wm = [sb.alloc(16 * 64, BF16).rearrange("p (b t) -> p b t", b=16) for _ in range(NH)]
                    Vm = [sb.alloc(16 * 64, BF16).rearrange("p (b t) -> p b t", b=16) for _ in range(NH)]
                    DMA('sp', sst, shift_in[j].rearrange("(k p) b -> p k b", p=128), [], ['sst'])
                    MSET('pool', BHtok, 0.0, ['BHtok'])
                    MSET('pool', Vtok, 0.0, ['Vtok'])
                    for q_ in range(NH):
                        MSET('pool', AKt[q_], 0.0, [('AKt', q_)])
                        MSET('pool', ABt[q_], 0.0, [('ABt', q_)])
                        MSET('pool', Xb[q_], 0.0, [('Xb', q_)])
                    MSET('pool', KHtok, 0.0, ['KHtok'])
                    for q_ in range(NH):
                        MSET('pool', Wm[q_], 0.0, [('Wm', q_)])
                        MSET('pool', Vm[q_], 0.0, [('Vm', q_)])
                else:
                    carry = sb.alloc(8, F32)
                    MSET('dve', carry, 0.0, ['carry'])
                MSET('dve', mhalf, -0.5, ['mhalf'])
                MSET('dve', c24, 1e-24, ['c24'])
                MSET('dve', gnec, 64e-5, ['gnec'])
                TS('dve', negw0, vecs[:, VOFF['r_w0'] + j * 8:VOFF['r_w0'] + j * 8 + 8], -1.0, None, ALU.mult, None, ['vecs'], ['negw0'])
                TS('dve', omka, vecs[:, VOFF['r_k_a'] + j * 8:VOFF['r_k_a'] + j * 8 + 8], -1.0, 1.0, ALU.mult, ALU.add, ['vecs'], ['omka'])
                DMA('pool', w1b, wview('r_w1').rearrange("(k p) n -> p k n", p=128), [], ['w1b'])
                DMA('pool', a1b, wview('r_a1').rearrange("(k p) n -> p k n", p=128), [], ['a1b'])
                DMA('pool', g1b, wview('r_g1').rearrange("(k p) n -> p k n", p=128), [], ['g1b'])
                DMA('pool', w2b[0:64, :], wview('r_w2'), [], ['w2b'])
                DMA('pool', a2b[0:64, :], wview('r_a2'), [], ['a2b'])
                DMA('pool', g2a, wview('r_g2')[0:128, :], [], ['g2a'])
                DMA('pool', g2b[0:32, :], wview('r_g2')[128:160, :], [], ['g2b'])
                if j == 1:
                    DMA('pool', v1b, wview('r_v1', 0).rearrange("(k p) n -> p k n", p=128), [], ['v1b'])
                    DMA('pool', v2b[0:32, :], wview('r_v2', 0), [], ['v2b'])
                if smp:
                    MSET('pool', bd, 0.0, ['bd'])
                    for d in range(KC):
                        for b4 in range(4):
                            for a in range(2):
                                DMA('sp', bd[a * 64:(a + 1) * 64, :, a * 64:(a + 1) * 64], wkv_in[j][b4 * 4:b4 * 4 + 4, 2 * d + a].rearrange("b i j -> i b j"), [], ['bd'])
                            ps, pk = psum()
                            for i4 in range(4):
                                TR(ps[:, i4 * 128:(i4 + 1) * 128], bd[:, i4, :], ident_f, ['bd', 'cst'], [pk])
                            for a in range(2):
                                CP('dve', S0T[a * 64:(a + 1) * 64, d, b4 * 4:b4 * 4 + 4, :],
                                   ps[a * 64:(a + 1) * 64, :].rearrange("p (b n) -> p b n", b=4)[:, :, a * 64:(a + 1) * 64], [pk], [('S0T', d)])
                else:
                    MSET('dve', S0T, 0.0, ['S0T'])
                wc = [0]

                def proj(wname, src, evac, jj=None):
                    Wv = wview(wname, jj).rearrange("(k p) n -> p k n", p=128)
                    for hf in range(4):
                        wb = wbuf[wc[0] % 2]
                        wk = ('wbuf', wc[0] % 2)
                        wc[0] += 1
                        DMA('pool', wb, Wv[:, :, hf * 256:(hf + 1) * 256], [], [wk])
                        for dd in range(2):
                            d = hf * 2 + dd
                            ps, pk = psum()
                            for k in range(KC):
                                MM(ps[:, :T], wb[:, k, dd * 128:(dd + 1) * 128], src[:, k, :], k == 0, k == KC - 1, [wk, 'xm'], [pk])
                            evac(d, ps, pk)

                try:
                    for (t0, T_) in tiles[:CFG.get('rtiles', 99)]:
                        rmsnorm(t0, T, 'norm_mix', l * KC, uf, ('uf',), sq, rs, 0, hkey='h', sqkey='BH_')
                        if smp:
                            u4 = uf.rearrange("p k (b t) -> p k b t", t=4)
                            p4 = up.rearrange("p k (b t) -> p k b t", t=4)
                            for k in range(KC):
                                CP('pool', p4[:, k, :, 1:4], u4[:, k, :, 0:3], ['uf'], ['up'])
                                CP('pool', p4[:, k, :, 0:1], sst[:, k, :].rearrange("p (b o) -> p b o", o=1), ['sst'], ['up'])
                                CP('pool', sst[:, k, :].rearrange("p (b o) -> p b o", o=1), u4[:, k, :, 3:4], ['uf', 'up'], ['sst'])
                        else:
                            CP('pool', up[:, :, 1:T], uf[:, :, 0:T - 1], ['uf'], ['up'])
                            CP('pool', up[:, :, 0:1], carry.rearrange("p (k o) -> p k o", o=1), ['carry'], ['up'])
                            CP('pool', carry.rearrange("p (k o) -> p k o", o=1), uf[:, :, T - 1:T], ['uf', 'up'], ['carry'])
                        TT('dve', up, up, uf, ALU.subtract, ['up', 'uf'], ['up'])

                        def mix(i, dst):
                            muv = vecs[:, mu0 + i * 8:mu0 + i * 8 + 8].rearrange("p (k o) -> p k o", o=1).broadcast_to([128, KC, T])
                            TT('dve', tA, up, muv, ALU.mult, ['up', 'vecs'], ['tA'])
                            TT('dve', dst, tA, uf, ALU.add, ['tA', 'uf'], ['xm'])
                        mix(0, xm[0])
                        proj('r_wr', xm[0], lambda d, ps, pk: ACT(r_[:, d, :], ps[:, :T], AF.Copy, [pk], [('r_', d)]))
                        mix(2, xm[1])
                        proj('r_wk', xm[1], lambda d, ps, pk: ACT(k_[:, d, :], ps[:, :T], AF.Copy, [pk], [('k_', d)]))
                        mix(3, xm[0])
                        proj('r_wv', xm[0], lambda d, ps, pk: ACT(v_[:, d, :], ps[:, :T], AF.Copy, [pk], [('v_', d)]))
                        if j == 1:
                            ps, pk = psum()
                            for k in range(KC):
                                MM(ps[:32, :T], v1b[:, k, :], xm[0][:, k, :], k == 0, k == KC - 1, ['v1b', 'xm'], [pk])
                            CP('dve', lo3[0:32, :], ps[:32, :T], [pk], ['lo3'])
                            DMA('sp', tB, vfirst_d.rearrange("(k p) t -> p k t", p=128)[:, :, t0:t0 + T], ['vf_dram'], ['tB'])
                            for d in range(KC):
                                ps, pk = psum()
                                MM(ps[:, :T], v2b[0:32, d * 128:(d + 1) * 128], lo3[0:32, :], True, True, ['v2b', 'lo3'], [pk])
                                ACT(tA[:, d, :], ps[:, :T], AF.Sigmoid, [pk, 'vecs'], ['tA'], bias=vcol('r_v0', d))
                            TT('dve', tB, tB, v_, ALU.subtract, ['tB', 'v_'], ['tB'])
                            TT('dve', tB, tB, tA, ALU.mult, ['tB', 'tA'], ['tB'])
                            TT('dve', v_, v_, tB, ALU.add, ['tB', 'v_'], ['v_'])
                        else:
                            DMA('sp', vfirst_d.rearrange("(k p) t -> p k t", p=128)[:, :, t0:t0 + T], v_, ['v_'], [('vf_dram', t0)])
                        _stg(1)
                        mix(1, xm[1])
                        ps, pk = psum()
                        for k in range(KC):
                            MM(ps[:64, :T], w1b[:, k, :], xm[1][:, k, :], k == 0, k == KC - 1, ['w1b', 'xm'], [pk])
                        ACT(lo1[0:64, :], ps[:64, :T], AF.Tanh, [pk], ['lo1'])
                        for d in range(KC):
                            ps, pk = psum()
                            MM(ps[:, :T], w2b[0:64, d * 128:(d + 1) * 128], lo1[0:64, :], True, True, ['w2b', 'lo1'], [pk])
                            ACT(nlw[:, d, :], ps[:, :T], AF.Exp, [pk, 'negw0'], [('nlw', d)], bias=negw0[:, d:d + 1], scale=-1.0)
                            ACT(nlw[:, d, :], nlw[:, d, :], AF.Ln, [('nlw', d), 'onec'], [('nlw', d)], bias=onec[:, 0:1])
                            ACT(nlw[:, d, :], nlw[:, d, :], AF.Exp, [('nlw', d), 'mhalf'], [('nlw', d)], bias=mhalf[:, 0:1], scale=-1.0)
                        mix(4, xm[0])
                        ps, pk = psum()
                        for k in range(KC):
                            MM(ps[:64, :T], a1b[:, k, :], xm[0][:, k, :], k == 0, k == KC - 1, ['a1b', 'xm'], [pk])
                        CP('dve', lo2[0:64, :], ps[:64, :T], [pk], ['lo2'])
                        for d in range(KC):
                            ps, pk = psum()
                            MM(ps[:, :T], a2b[0:64, d * 128:(d + 1) * 128], lo2[0:64, :], True, True, ['a2b', 'lo2'], [pk])
                            ACT(a_[:, d, :], ps[:, :T], AF.Sigmoid, [pk, 'vecs'], [('a_', d)], bias=vcol('r_a0', j * 8 + d))
                        mix(5, xm[1])
                        ps, pk = psum()
                        for k in range(KC):
                            MM(ps[:, :T], g1b[:, k, 0:128], xm[1][:, k, :], k == 0, k == KC - 1, ['g1b', 'xm'], [pk])
                        ACT(lo1, ps[:, :T], AF.Sigmoid, [pk], ['lo1'])
                        ps, pk = psum()
                        for k in range(KC):
                            MM(ps[:32, :T], g1b[:, k, 128:160], xm[1][:, k, :], k == 0, k == KC - 1, ['g1b', 'xm'], [pk])
                        ACT(lo2[0:32, :], ps[:32, :T], AF.Sigmoid, [pk], ['lo2'])
                        for d in range(KC):
                            ps, pk = psum()
                            MM(ps[:, :T], g2a[:, d * 128:(d + 1) * 128], lo1, True, False, ['g2a', 'lo1'], [pk])
                            MM(ps[:, :T], g2b[0:32, d * 128:(d + 1) * 128], lo2[0:32, :], False, True, ['g2b', 'lo2'], [pk])
                            ACT(g_[:, d, :], ps[:, :T], AF.Copy, [pk], [('g_', d)])
                        _stg(2)
                        kkv = vecs[:, VOFF['r_k_k'] + j * 8:VOFF['r_k_k'] + j * 8 + 8].rearrange("p (k o) -> p k o", o=1).broadcast_to([128, KC, T])
                        TT('dve', kk, k_, kkv, ALU.mult, ['k_', 'vecs'], ['kk'])
                        TT('dve', tA, kk, kk, ALU.mult, ['kk'], ['tA'])
                        for d in range(KC):
                            ps, pk = psum()
                            MM(ps[:, :T], blk2, tA[:, d, :], True, True, ['cst', 'tA'], [pk])
                            TS('dve', tB[:, d, :], ps[:, :T], c24[:, 0:1], None, ALU.max, None, [pk, 'c24'], ['tB'])
                        ACT(tB, tB, AF.Ln, ['tB'], ['tB'])
                        ACT(tB, tB, AF.Exp, ['tB'], ['tB'], scale=-0.5)
                        TT('dve', kk, kk, tB, ALU.mult, ['kk', 'tB'], ['kk'])
                        kav = vecs[:, VOFF['r_k_a'] + j * 8:VOFF['r_k_a'] + j * 8 + 8].rearrange("p (k o) -> p k o", o=1).broadcast_to([128, KC, T])
                        omv = omka.rearrange("p (k o) -> p k o", o=1).broadcast_to([128, KC, T])
                        TT('dve', tA, a_, kav, ALU.mult, ['a_', 'vecs'], ['tA'])
                        TT('dve', tA, tA, omv, ALU.add, ['tA', 'omka'], ['tA'])
                        TT('dve', k_, k_, tA, ALU.mult, ['k_', 'tA'], ['k_'])
                        TT('dve', tA, kk, a_, ALU.mult, ['kk', 'a_'], ['tA'])
                        rkv = vecs[:, VOFF['r_r_k'] + j * 8:VOFF['r_r_k'] + j * 8 + 8].rearrange("p (k o) -> p k o", o=1).broadcast_to([128, KC, T])
                        TT('dve', np_, r_, k_, ALU.mult, ['r_', 'k_'], ['np_'])
                        TT('dve', np_, np_, rkv, ALU.mult, ['np_', 'vecs'], ['np_'])
                        for d in range(KC):
                            ps, pk = psum()
                            MM(ps[:, :T], blk2, np_[:, d, :], True, True, ['cst', 'np_'], [pk])
                            TT('dve', tB[:, d, :], ps[:, :T], v_[:, d, :], ALU.mult, [pk, 'v_'], ['tB'])
                        _stg(3)
                        for d in range(KC):
                            S.add('dve', (lambda e, o=np_[:, d, :], d1=nlw[:, d, :]: e.tensor_tensor_scan(out=o, data0=rst[:, :T], data1=d1, initial=0.0, op0=ALU.mult, op1=ALU.add)),
                                  reads=['nlw', 'cst', 'np_'], writes=['np_'])
                        npv = np_.rearrange("p k (b t) -> p k b t", b=nb)
                        npE = npv[:, :, :, lt - 1:lt].broadcast_to([128, KC, nb, lt])
                        ACT(eC, npv[:, :, :, lt - 1:lt].rearrange("p k b o -> p k (b o)"), AF.Exp, ['np_'], ['eC'], scale=-1.0)
                        TT('dve', uf, np_, nlw, ALU.subtract, ['np_', 'nlw'], ['uf'])
                        ACT(uf, uf, AF.Exp, ['uf'], ['uf'], scale=-1.0)
                        STT('dve', ARt[:, :, 0, :], kk, -1.0, uf, ALU.mult, ALU.mult, ['kk', 'uf'], ['ARt'])
                        ACT(uf, np_, AF.Exp, ['np_', 'ARt'], ['uf'], scale=-1.0)
                        TT('dve', ARt[:, :, 1, :], r_, uf, ALU.mult, ['r_', 'uf'], ['ARt'])
                        ACT(uf, np_, AF.Exp, ['np_', 'ARt'], ['uf'])
                        TT('dve', BT_, tA, uf, ALU.mult, ['tA', 'uf'], ['BT_'])
                        TT('dve', KT_, k_, uf, ALU.mult, ['k_', 'uf'], ['KT_'])
                        TT('dve', uf.rearrange("p k (b t) -> p k b t", b=nb), npv, npE, ALU.subtract, ['np_', 'BT_', 'KT_'], ['uf'])
                        ACT(uf, uf, AF.Exp, ['uf'], ['uf'])
                        TT('dve', BH_, tA, uf, ALU.mult, ['tA', 'uf'], ['BH_'])
                        TT('dve', KH_, k_, uf, ALU.mult, ['k_', 'uf'], ['KH_'])
                        CP('pool', V_, v_, ['v_'], ['xm'])
                        _stg(4)
                        for (src, dst, nm) in ((V_, Vtok, 'Vtok'), (BH_, BHtok, 'BHtok'), (KH_, KHtok, 'KHtok')):
                            ps, pk = psum()
                            psb = ps.bitcast(BF16)
                            for d in range(KC):
                                TR(psb[:L, d * 128:(d + 1) * 128], src[:, d, :], ident_b, [nm[:-3] + '_' if nm != 'Vtok' else 'xm', 'ident_b'], [pk])
                            CP('dve', dst[:L].rearrange("p k n -> p (k n)"), psb[:L, :], [pk], [nm])
                        _stg(5)
                        for hg in range(16 // NH):
                            heads = [(hg * NH + q) for q in range(NH)]
                            if CFG.get('rheads') is not None and heads[0] not in CFG['rheads']:
                                continue
                            HD = [(hh // 2, hh % 2) for hh in heads]
                            for q, (d, a) in enumerate(HD):
                                sl = slice(a * 64, (a + 1) * 64)
                                AR = ARt[sl, d].rearrange("p c t -> p (c t)")
                                ps, pk = psum()
                                MM(ps[:L, 0:2 * L], BT_[sl, d, :], AR, True, True, ['BT_', 'ARt'], [pk])
                                TT('dve', ABt[q][:L, :], ps[:L, 0:2 * L], mAB, ALU.mult, [pk, 'cst'], [('ABt', q)])
                                ps, pk = psum()
                                MM(ps[:L, 0:2 * L], KT_[sl, d, :], AR, True, True, ['KT_', 'ARt'], [pk])
                                TT('dve', AKt[q][:L, :], ps[:L, 0:2 * L], mAB, ALU.mult, [pk, 'cst'], [('AKt', q)])
                                ps, pk = psum()
                                MM(ps[:L, 0:L], ARt[sl, d, 0, :], BT_[sl, d, :], True, True, ['BT_', 'ARt'], [pk])
                                TT('dve', MMb[q][0][:L, 0:L], ps[:L, 0:L], mlow, ALU.mult, [pk, 'cst'], [('MMb', q, 0)])
                                CP('pool', MMb[q][0][:L, L:2 * L], ABt[q][:L, 0:L], [('ABt', q)], [('MMb', q, 0)])
                            _stg(6)
                            for q, (d, a) in enumerate(HD):
                                sl = slice(a * 64, (a + 1) * 64)
                                CP('dve', S0bh[q][sl], S0T[sl, d], [('S0T', d)], [('S0bh', q)])
                                ps, pk = psum()
                                if smp:
                                    TT('dve', ATm[q][sl], ARt[sl, d, 0, :].rearrange("p (o t) -> p o t", o=1).broadcast_to([64, 16, 64]),
                                       bmT_b[sl].rearrange("p (b t) -> p b t", b=16), ALU.mult, ['ARt', 'bmT_b'], [('ATm', q)])
                                    TT('dve', RTm[q][sl], ARt[sl, d, 1, :].rearrange("p (o t) -> p o t", o=1).broadcast_to([64, 16, 64]),
                                       bmT_b[sl].rearrange("p (b t) -> p b t", b=16), ALU.mult, ['ARt', 'bmT_b'], [('RTm', q)])
                                    for b in range(16):
                                        MM(ps[:L, 0:64], ATm[q][sl, b, :], S0bh[q][sl, b, :], b == 0, False, [('ATm', q), ('S0bh', q)], [pk])
                                else:
                                    MM(ps[:L, 0:64], ARt[sl, d, 0, :], S0bh[q][sl, 0, :], True, False, ['ARt', ('S0bh', q)], [pk])
                                MM(ps[:L, 0:64], AKt[q][:, 0:L], Vtok[:, d, sl], False, True, [('AKt', q), 'Vtok'], [pk])
                                CP('dve', X32[q][:L, :], ps[:L, 0:64], [pk], [('X32', q)])
                                CP('pool', Xb[q][:L, :], X32[q][:L, :], [('X32', q)], [('Xb', q)])
                            _stg(7)
                            for lev in range(nlev):
                                cur = lev % 2
                                for q, (d, a) in enumerate(HD):
                                    M_ = MMb[q][cur][:L, 0:L]
                                    Mt_ = MMb[q][cur][:L, L:2 * L]
                                    ps, pk = psum()
                                    MM(ps[:L, 0:64], Mt_, Xb[q][:L, :], True, True, [('MMb', q, cur), ('Xb', q)], [pk])
                                    TT('dve', X32[q][:L, :], X32[q][:L, :], ps[:L, 0:64], ALU.add, [pk, ('X32', q)], [('X32', q)])
                                    CP('pool', Xb[q][:L, :], X32[q][:L, :], [('X32', q)], [('Xb', q)])
                                    if lev < nlev - 1:
                                        ps2, pk2 = psum()
                                        MM(ps2[:L, 0:L], Mt_, M_, True, True, [('MMb', q, cur)], [pk2])
                                        MM(ps2[:L, L:2 * L], M_, Mt_, True, True, [('MMb', q, cur)], [pk2])
                                        ACT(MMb[q][1 - cur][:L, :], ps2[:L, 0:2 * L], AF.Copy, [pk2], [('MMb', q, 1 - cur)])
                            _stg(8)
                            for q, (d, a) in enumerate(HD):
                                sl = slice(a * 64, (a + 1) * 64)
                                ps, pk = psum()
                                if smp:
                                    for b in range(16):
                                        MM(ps[:L, 0:64], RTm[q][sl, b, :], S0bh[q][sl, b, :], b == 0, False, [('RTm', q), ('S0bh', q)], [pk])
                                else:
                                    MM(ps[:L, 0:64], ARt[sl, d, 1, :], S0bh[q][sl, 0, :], True, False, ['ARt', ('S0bh', q)], [pk])
                                MM(ps[:L, 0:64], ABt[q][:, L:2 * L], Xb[q][:, :], False, False, [('ABt', q), ('Xb', q)], [pk])
                                MM(ps[:L, 0:64], AKt[q][:, L:2 * L], Vtok[:, d, sl], False, True, [('AKt', q), 'Vtok'], [pk])
                                ACT(Ytok[:L, d, sl], ps[:L, 0:64], AF.Copy, [pk], [('Ytok', d, a)])
                                _stg(8.2)
                                if smp:
                                    bmv = cst[:L, C_BM:C_BM + 16].rearrange("p (b o) -> p b o", o=1).broadcast_to([L, 16, 64])
                                    TT('pool', Wm[q][:L], Xb[q][:L, :].rearrange("p (o i) -> p o i", o=1).broadcast_to([L, 16, 64]), bmv, ALU.mult, [('Xb', q), 'cst'], [('Wm', q)])
                                    TT('pool', Vm[q][:L], Vtok[:L, d, sl].rearrange("p (o i) -> p o i", o=1).broadcast_to([L, 16, 64]), bmv, ALU.mult, ['Vtok', 'cst'], [('Vm', q)])
                                    _stg(8.4)
                                    S3 = S0T[sl, d]
                                    TT('dve', S3, S3, eC[sl, d, :].rearrange("p (b o) -> p b o", o=1).broadcast_to([64, 16, 64]), ALU.mult, [('S0T', d), 'eC'], [('S0T', d)])
                                    _stg(8.6)
                                    for b8 in range(2):
                                        ps, pk = psum()
                                        for bi in range(8):
                                            b = b8 * 8 + bi
                                            MM(ps[sl, bi * 64:(bi + 1) * 64], BHtok[:, d, sl], Wm[q][:, b, :], True, False, ['BHtok', ('Wm', q)], [pk])
                                            MM(ps[sl, bi * 64:(bi + 1) * 64], KHtok[:, d, sl], Vm[q][:, b, :], False, True, ['KHtok', ('Vm', q)], [pk])
                                        TT('dve', S0T[sl, d, b8 * 8:b8 * 8 + 8, :], S0T[sl, d, b8 * 8:b8 * 8 + 8, :], ps[sl, :].rearrange("p (b i) -> p b i", b=8), ALU.add,
                                           [pk, ('S0T', d)], [('S0T', d)])
                                else:
                                    ps, pk = psum()
                                    MM(ps[sl, 0:64], BHtok[:L, d, sl], Xb[q][:L, :], True, False, ['BHtok', ('Xb', q)], [pk])
                                    MM(ps[sl, 0:64], KHtok[:L, d, sl], Vtok[:L, d, sl], False, True, ['KHtok', 'Vtok'], [pk])
                                    STT('dve', S0T[sl, d, 0, :], S0T[sl, d, 0, :], eC[sl, d, 0:1], ps[sl, 0:64], ALU.mult, ALU.add, [pk, ('S0T', d), 'eC'], [('S0T', d)])
                        _stg(9)
                        Y3 = Ytok[:L].rearrange("p k (a i) -> p (k a) i", a=2)
                        S.add('dve', (lambda e, o=st1[:L, :]: e.tensor_reduce(out=o, in_=Y3, axis=AX.X, op=ALU.add)), reads=['Ytok'], writes=['st1'])
                        TT('pool', Ysq[:L], Ytok[:L], Ytok[:L], ALU.mult, ['Ytok'], [YSK])
                        S.add('dve', (lambda e, o=st2[:L, :]: e.tensor_reduce(out=o, in_=Ysq[:L].rearrange("p k (a i) -> p (k a) i", a=2), axis=AX.X, op=ALU.add)), reads=[YSK], writes=['st2'])
                        TS('dve', st1[:L, :], st1[:L, :], 1.0 / 64, None, ALU.mult, None, ['st1'], ['st1'])
                        TS('dve', st2[:L, :], st2[:L, :], 1.0 / 64, None, ALU.mult, None, ['st2'], ['st2'])
                        TT('dve', Ysq[:L, 0, 0:16], st1[:L, :], st1[:L, :], ALU.mult, ['st1', YSK], [YSK])
                        TT('dve', st2[:L, :], st2[:L, :], Ysq[:L, 0, 0:16], ALU.subtract, ['st2', YSK], ['st2'])
                        ACT(st2[:L, :], st2[:L, :], AF.Ln, ['st2', 'gnec'], ['st2'], bias=gnec[:L, 0:1])
                        ACT(st2[:L, :], st2[:L, :], AF.Exp, ['st2'], ['st2'], scale=-0.5)
                        TT('dve', Y3, Y3, st1[:L, :].rearrange("p (h o) -> p h o", o=1).broadcast_to([L, 16, 64]), ALU.subtract, ['Ytok', 'st1'], ['Ytok'])
                        TT('dve', Yn[:L].rearrange("p k (a i) -> p (k a) i", a=2), Y3, st2[:L, :].rearrange("p (h o) -> p h o", o=1).broadcast_to([L, 16, 64]), ALU.mult,
                           ['Ytok', 'st2'], ['Yn'])
                        for hf in range(2):
                            ps, pk = psum()
                            psb = ps.bitcast(BF16)
                            for dd in range(4):
                                d = hf * 4 + dd
                                TR(psb[:, dd * L:(dd + 1) * L], Yn[:L, d, :], ident_b[:L, :L], ['Yn', 'ident_b'], [pk])
                            for dd in range(4):
                                d = hf * 4 + dd
                                TS('dve', tA[:, d, :], psb[:, dd * L:(dd + 1) * L], vcol('r_gn_w', j * 8 + d), vcol('r_gn_b', j * 8 + d), ALU.mult, ALU.add, [pk, 'vecs'], ['tA'])
                        TT('dve', tA, tA, tB, ALU.add, ['tA', 'tB'], ['tA'])
                        TT('dve', yg, tA, g_, ALU.mult, ['tA', 'g_'], ['xm'])
                        proj('r_wo', yg, lambda d, ps, pk: TT('dve', h[:, d, t0:t0 + T], h[:, d, t0:t0 + T], ps[:, :T], ALU.add, [pk, 'h'], ['h']))
                except _Stop:
                    pass
                if CFG.get('rstage', 99) < 10:
                    S.barrier()
                    sb.reset(m)
                    return
                if smp:
                    DMA('sp', shift_s[j].rearrange("(k p) b -> p k b", p=128), sst, ['sst'], [('shift_s', j)])
                else:
                    DMA('sp', shift_p[j], carry, ['carry'], [('shift_p', j)])
                MSET('pool', bd, 0.0, ['bd'])
                for d in range(KC):
                    for b4 in range((nb + 3) // 4):
                        n4 = min(4, nb - b4 * 4)
                        for a in range(2):
                            sl = slice(a * 64, (a + 1) * 64)
                            CP('dve', bd[sl, 0:n4, a * 64:(a + 1) * 64], S0T[sl, d, b4 * 4:b4 * 4 + n4, :], [('S0T', d)], ['bd'])
                        ps, pk = psum()
                        for i4 in range(n4):
                            TR(ps[:, i4 * 128:(i4 + 1) * 128], bd[:, i4, :], ident_f, ['bd', 'cst'], [pk])
                        for a in range(2):
                            sl = slice(a * 64, (a + 1) * 64)
                            CP('dve', Ysq[sl, 0:n4, 0:64], ps[sl, :].rearrange("p (b n) -> p b n", b=4)[:, 0:n4, a * 64:(a + 1) * 64], [pk], [YSK])
                            if smp:
                                DMA('sp', wkv_s[j][b4 * 4:b4 * 4 + n4, 2 * d + a].rearrange("b i j -> i b j"), Ysq[sl, 0:n4, 0:64], [YSK], [('wkv_s', j, d, a, b4)])
                            else:
                                DMA('sp', wkv_p[j][2 * d + a], Ysq[sl, 0, 0:64], [YSK], [('wkv_p', j, d, a)])
                S.barrier()
                sb.reset(m)

            run(False)
            if CFG.get('rwkv_sample', True):
                run(True)


        for l in range(CFG['depth']):
            if CFG.get('dense', True):
                ffn(l, Wd_['ffn1_gate_up'], Wd_['ffn1_down'], 'norm_ffn1')
            if CFG['mixers']:
                if l % 2 == 0:
                    if CFG.get('mamba', True):
                        mamba(l)
                elif CFG.get('rwkv', False):
                    rwkv(l)
            if CFG.get('dense', True):
                ffn(l, Wd_['ffn2_gate_up'], Wd_['ffn2_down'], 'norm_ffn2')
                ple(l)
        final()
        S.add('sp', lambda e: e.nop(), reads=['yT', 'ssm_s', 'ssm_p', 'conv_s', 'conv_p', 'wkv_p', 'wkv_s', 'shift_p', 'shift_s'])
        S.emit()
    return nc


def pack_vecs(inp):
    cols = []
    for n in VECS:
        a = np.asarray(inp[n], dtype=np.float32)
        cols.append(np.ascontiguousarray(a.reshape(-1, 128).T))
    return np.ascontiguousarray(np.concatenate(cols, axis=1))


def kernel(**inp):
    inp = {k: np.asarray(v) for k, v in inp.items()}
    nc = build_program()
    vecs = pack_vecs(inp)
    consts = make_consts()
    hv = np.zeros((32, 4), np.float32)
    dbc = np.zeros((128, 64), np.float32)
    for j in range(2):
        hv[:, 2 * j] = inp['m_dt_bias'][j]
        hv[:, 2 * j + 1] = inp['m_A_log'][j]
        dbc[:, j * 32:(j + 1) * 32] = inp['m_D'][j][None, :]
    ncores = CFG['cores']
    in_maps = []
    for c in range(ncores):
        xs = inp['x_sample'][16 * c:16 * c + 16].reshape(NS, D)
        xc = np.concatenate([inp['x_prompt'][c], xs], axis=0)
        ps_ = inp['p_sample'][:, 16 * c:16 * c + 16].reshape(4, NS, 256)
        pc = np.concatenate([inp['p_prompt'][:, c], ps_], axis=1)
        m = {'xT': np.ascontiguousarray(xc.T), 'pT': np.ascontiguousarray(pc.transpose(0, 2, 1)), 'vecs': vecs,
             'consts': consts, 'hv': hv, 'dbc': dbc,
             'ssm_in': np.ascontiguousarray(inp['state_ssm'][:, 16 * c:16 * c + 16]),
             'conv_in': np.ascontiguousarray(inp['state_conv'][:, 16 * c:16 * c + 16].transpose(0, 3, 1, 2)),
             'wkv_in': np.ascontiguousarray(inp['state_wkv'][:, 16 * c:16 * c + 16]),
             'shift_in': np.ascontiguousarray(inp['state_shift'][:, 16 * c:16 * c + 16].transpose(0, 2, 1))}
        for n in WSHAPES:
            m[n] = np.ascontiguousarray(inp[n], dtype=np.float32)
        in_maps.append(m)
    res = run_bass_kernel_spmd(nc, in_maps, core_ids=list(range(ncores)))
    B = 8
    yp = np.zeros((B, NP, D), np.float32)
    ys = np.zeros((128, 4, D), np.float32)
    ssm_p = np.zeros((2, B, 32, 64, 128), np.float32)
    conv_p = np.zeros((2, B, 3, 4096), np.float32)
    wkv_p = np.zeros((2, B, 16, 64, 64), np.float32)
    shift_p = np.zeros((2, B, D), np.float32)
    ssm_s = np.zeros((2, 128, 32, 64, 128), np.float32)
    conv_s = np.zeros((2, 128, 3, 4096), np.float32)
    wkv_s = np.zeros((2, 128, 16, 64, 64), np.float32)
    shift_s = np.zeros((2, 128, D), np.float32)
    for c in range(ncores):
        r = res.results[c]
        y = r['yT'].T
        yp[c] = y[:NP]
        ys[16 * c:16 * c + 16] = y[NP:].reshape(16, 4, D)
        ssm_p[:, c] = r['ssm_p']
        conv_p[:, c] = r['conv_p'].transpose(0, 2, 1)
        wkv_p[:, c] = r['wkv_p']
        shift_p[:, c] = r['shift_p'].transpose(0, 2, 1).reshape(2, D)
        ssm_s[:, 16 * c:16 * c + 16] = r['ssm_s']
        conv_s[:, 16 * c:16 * c + 16] = r['conv_s'].transpose(0, 2, 3, 1)
        wkv_s[:, 16 * c:16 * c + 16] = r['wkv_s']
        shift_s[:, 16 * c:16 * c + 16] = r['shift_s'].transpose(0, 2, 1)
    return (yp, ys, ssm_p, conv_p, wkv_p, shift_p, ssm_s, conv_s, wkv_s, shift_s)
```

```python
import contextlib
import numpy as np
import concourse.bass as bass
import concourse.mybir as mybir
from concourse.bass_utils import run_bass_kernel_spmd

F32 = mybir.dt.float32
BF16 = mybir.dt.bfloat16
AF = mybir.ActivationFunctionType
ALU = mybir.AluOpType
AX = mybir.AxisListType

ENGS = ['pe', 'act', 'dve', 'pool', 'sp']


class Op:
    __slots__ = ('eng', 'fn', 'deps', 'signal', 'is_dma', 'sem', 'val', 'prewait', 'barriered')

    def __init__(self, eng, fn, is_dma):
        self.eng = eng
        self.fn = fn
        self.deps = []
        self.signal = False
        self.is_dma = is_dma
        self.sem = None
        self.val = 0
        self.prewait = None
        self.barriered = False


class Sched:
    def __init__(self, nc, n_dma_sems=20):
        self.nc = nc
        self.ops = {e: [] for e in ENGS}
        self.res = {}
        self.n_dma_sems = n_dma_sems

    def _states(self, key):
        if isinstance(key, tuple):
            name, sub = key[0], key[1:]
            if len(sub) == 0:
                sub = None
        else:
            name, sub = key, None
        d = self.res.setdefault(name, {})
        if sub is None:
            if None not in d:
                d[None] = [None, []]
            return list(d.values()), d[None], True
        if sub not in d:
            d[sub] = [None, []]
        sts = [d[sub]]
        if None in d:
            sts.append(d[None])
        return sts, d[sub], False

    def add(self, eng, fn, reads=(), writes=(), dma=False):
        op = Op(eng, fn, dma)
        deps = []
        for k in reads:
            sts, own, whole = self._states(k)
            for st in sts:
                if st[0] is not None:
                    deps.append(st[0])
        for k in writes:
            sts, own, whole = self._states(k)
            for st in sts:
                if st[0] is not None:
                    deps.append(st[0])
                deps.extend(st[1])
        for k in reads:
            sts, own, whole = self._states(k)
            own[1].append(op)
        for k in writes:
            sts, own, whole = self._states(k)
            if whole:
                for st in sts:
                    st[0] = None
                    st[1] = []
            own[0] = op
            own[1] = []
        seen = set()
        for d in deps:
            if d is op or id(d) in seen:
                continue
            seen.add(id(d))
            if d.eng == eng and eng == 'pe' and not d.is_dma and not dma:
                continue
            op.deps.append(d)
            d.signal = True
        self.ops[eng].append(op)
        return op

    def barrier(self):
        last = []
        for e in ENGS:
            got = False
            for o in reversed(self.ops[e]):
                if o.barriered:
                    break
                if o.is_dma:
                    last.append(o)
                elif not got:
                    last.append(o)
                    got = True
        b = Op('sp', lambda e: e.nop(), False)
        for d in last:
            b.deps.append(d)
            d.signal = True
        for e in ENGS:
            for o in reversed(self.ops[e]):
                if o.barriered:
                    break
                o.barriered = True
        b.barriered = True
        self.ops['sp'].append(b)
        for e in ENGS:
            if e == 'sp':
                continue
            o = Op(e, None, False)
            o.barriered = True
            o.deps.append(b)
            b.signal = True
            self.ops[e].append(o)
        self.res = {}

    def emit(self):
        nc = self.nc
        with contextlib.ExitStack() as es:
            csem = {e: es.enter_context(nc.semaphore('c_' + e)) for e in ENGS}
            dsems = {e: [es.enter_context(nc.semaphore('d_%s_%d' % (e, i)))
                         for i in range(self.n_dma_sems)] for e in ('sp', 'pool', 'act')}
            for e in ENGS:
                cnt = 0
                dcnt = [0] * self.n_dma_sems
                rr = 0
                for op in self.ops[e]:
                    if op.is_dma:
                        s = rr % self.n_dma_sems
                        rr += 1
                        if dcnt[s] > 0:
                            op.prewait = (dsems[e][s], dcnt[s])
                        dcnt[s] += 16
                        op.sem = dsems[e][s]
                        op.val = dcnt[s]
                    elif op.signal:
                        cnt += 1
                        op.sem = csem[e]
                        op.val = cnt
            engobj = {'pe': 'tensor', 'act': 'scalar', 'dve': 'vector', 'pool': 'gpsimd', 'sp': 'sync'}

            def run(e, eng):
                seen = {}
                for op in self.ops[e]:
                    waits = []
                    if op.prewait is not None:
                        waits.append(op.prewait)
                    for d in op.deps:
                        waits.append((d.sem, d.val))
                    mx = {}
                    for (s, v) in waits:
                        k = id(s)
                        if v > mx.get(k, (None, 0))[1]:
                            mx[k] = (s, v)
                    for k, (s, v) in mx.items():
                        if seen.get(k, 0) >= v:
                            continue
                        seen[k] = v
                        eng.wait_ge(s, v)
                    if op.fn is None:
                        continue
                    ins = op.fn(eng)
                    if op.is_dma:
                        ins.then_inc(op.sem, 16)
                    elif op.signal:
                        ins.then_inc(op.sem, 1)

            with nc.Block() as block:
                for e in ENGS:
                    if not self.ops[e]:
                        continue
                    getattr(block, engobj[e])(lambda eng, e=e: run(e, eng))


class SB:
    def __init__(self, big, nbytes):
        self.big = big
        self.nbytes = nbytes
        self.off = 0
        self.views = {}

    def view(self, dtype):
        if dtype not in self.views:
            self.views[dtype] = self.big.bitcast(dtype) if dtype != F32 else self.big
        return self.views[dtype]

    def alloc(self, cols, dtype, parts=128):
        sz = mybir.dt.size(dtype)
        nb = (cols * sz + 63) // 64 * 64
        assert self.off + nb <= self.nbytes, ('SBUF overflow', self.off, nb, self.nbytes)
        o = self.off // sz
        self.off += nb
        return self.view(dtype)[0:parts, o:o + cols]

    def mark(self):
        return self.off

    def reset(self, m):
        self.off = m


D = 1024
KC = 8
NP = 2048
NS = 64
NT = NP + NS
TILES = [(0, 512), (512, 512), (1024, 512), (1536, 512), (2048, 64)]
DFF = 2816
FGROUPS = [(0, 4), (4, 4), (8, 4), (12, 4), (16, 3), (19, 3)]
DEPTH = 4
EPS = 1e-6

WSHAPES = {
    'ffn1_gate_up': (4, 1024, 5632), 'ffn1_down': (4, 2816, 1024),
    'ffn2_gate_up': (4, 1024, 5632), 'ffn2_down': (4, 2816, 1024),
    'ple_in': (4, 256, 1024), 'ple_gate': (4, 1024, 1024),
    'm_in_proj': (2, 1024, 6176), 'm_out_proj': (2, 2048, 1024),
    'r_wr': (2, 1024, 1024), 'r_wk': (2, 1024, 1024), 'r_wv': (2, 1024, 1024), 'r_wo': (2, 1024, 1024),
    'r_w1': (2, 1024, 64), 'r_w2': (2, 64, 1024), 'r_a1': (2, 1024, 64), 'r_a2': (2, 64, 1024),
    'r_g1': (2, 1024, 160), 'r_g2': (2, 160, 1024), 'r_v1': (1, 1024, 32), 'r_v2': (1, 32, 1024),
}
VECS = ['norm_ffn1', 'norm_mix', 'norm_ffn2', 'norm_ple', 'norm_final', 'm_conv_w', 'm_conv_b', 'm_norm',
        'r_mu', 'r_w0', 'r_a0', 'r_k_k', 'r_k_a', 'r_r_k', 'r_gn_w', 'r_gn_b', 'r_v0']
VSHAPES = {'norm_ffn1': (4, 1024), 'norm_mix': (4, 1024), 'norm_ffn2': (4, 1024), 'norm_ple': (4, 1024),
           'norm_final': (1024,), 'm_conv_w': (2, 4, 4096), 'm_conv_b': (2, 4096), 'm_norm': (2, 2048),
           'r_mu': (2, 6, 1024), 'r_w0': (2, 1024), 'r_a0': (2, 1024), 'r_k_k': (2, 1024), 'r_k_a': (2, 1024),
           'r_r_k': (2, 1024), 'r_gn_w': (2, 1024), 'r_gn_b': (2, 1024), 'r_v0': (1, 1024)}
VOFF = {}
_o = 0
for _n in VECS:
    VOFF[_n] = _o
    _o += int(np.prod(VSHAPES[_n])) // 128
NV = _o

C_ID, C_TRIP, C_ONES, C_TRIS, C_BLKS, C_BM, C_BMT = 0, 128, 256, 384, 448, 512, 528
C_STRIP = C_BMT + 1024
C_LOWP = C_STRIP + 128
C_STRIS = C_LOWP + 128
C_LOWS = C_STRIS + 64
C_MABP = C_LOWS + 64
C_MABS = C_MABP + 256
C_RSTP = C_MABS + 128
C_RSTS = C_RSTP + 128
C_BLK2 = C_RSTS + 64
NCC = C_BLK2 + 128

CFG = {'mixers': True, 'depth': DEPTH, 'cores': 8, 'rwkv': True}


class _Stop(Exception):
    pass


def _stg(n):
    if CFG.get('rstage', 99) < n:
        raise _Stop()


def make_consts():
    c = np.zeros((128, NCC), np.float32)
    i = np.arange(128)
    c[:, C_ID:C_ID + 128] = np.eye(128)
    c[:, C_TRIP:C_TRIP + 128] = (i[:, None] <= i[None, :])
    c[:, C_ONES:C_ONES + 128] = 1.0
    j = np.arange(64)
    same = (j[:, None] // 4) == (j[None, :] // 4)
    c[:64, C_TRIS:C_TRIS + 64] = same & (j[:, None] <= j[None, :])
    c[:64, C_BLKS:C_BLKS + 64] = same
    c[:64, C_BM:C_BM + 16] = (j[:, None] // 4) == np.arange(16)[None, :]
    bmt = ((j[None, :] // 4) == np.arange(16)[:, None]).astype(np.float32).reshape(1, 1024)
    c[:, C_BMT:C_BMT + 1024] = bmt
    c[:, C_STRIP:C_STRIP + 128] = (i[:, None] < i[None, :])
    c[:, C_LOWP:C_LOWP + 128] = (i[None, :] < i[:, None])
    c[:64, C_STRIS:C_STRIS + 64] = same & (j[:, None] < j[None, :])
    c[:64, C_LOWS:C_LOWS + 64] = same & (j[None, :] < j[:, None])
    c[:, C_MABP:C_MABP + 128] = c[:, C_STRIP:C_STRIP + 128]
    c[:, C_MABP + 128:C_MABP + 256] = c[:, C_TRIP:C_TRIP + 128]
    c[:64, C_MABS:C_MABS + 64] = c[:64, C_STRIS:C_STRIS + 64]
    c[:64, C_MABS + 64:C_MABS + 128] = c[:64, C_TRIS:C_TRIS + 64]
    c[:, C_RSTP:C_RSTP + 128] = 1.0
    c[:, C_RSTP] = 0.0
    c[:, C_RSTS:C_RSTS + 64] = (np.arange(64) % 4 != 0)[None, :]
    c[:, C_BLK2:C_BLK2 + 128] = (i[:, None] // 64) == (i[None, :] // 64)
    return c


def build_program():
    nc = bass.Bass("TRN2", target_bir_lowering=False)

    def din(name, shape):
        return nc.dram_tensor(name, list(shape), F32, kind="ExternalInput").ap()

    def dout(name, shape):
        return nc.dram_tensor(name, list(shape), F32, kind="ExternalOutput").ap()

    xT = din('xT', [D, NT])
    pT = din('pT', [4, 256, NT])
    vecs_d = din('vecs', [128, NV])
    Wd_ = {n: din(n, s) for n, s in WSHAPES.items()}
    yT = dout('yT', [D, NT])
    consts_d = din('consts', [128, NCC])
    hv_d = din('hv', [32, 4])
    dbc_d = din('dbc', [128, 64])
    ssm_in = din('ssm_in', [2, 16, 32, 64, 128])
    conv_in = din('conv_in', [2, 4096, 16, 3])
    wkv_in = din('wkv_in', [2, 16, 16, 64, 64])
    shift_in = din('shift_in', [2, 1024, 16])
    ssm_p = dout('ssm_p', [2, 32, 64, 128])
    conv_p = dout('conv_p', [2, 4096, 3])
    ssm_s = dout('ssm_s', [2, 16, 32, 64, 128])
    conv_s = dout('conv_s', [2, 4096, 16, 3])
    wkv_p = dout('wkv_p', [2, 16, 64, 64])
    shift_p = dout('shift_p', [2, 128, 8])
    wkv_s = dout('wkv_s', [2, 16, 16, 64, 64])
    shift_s = dout('shift_s', [2, 1024, 16])
    vfirst_d = dout('vfirst', [D, NT])

    with contextlib.ExitStack() as es:
        NB = 206 * 1024
        big = es.enter_context(nc.sbuf_tensor("big", [128, NB // 4], F32))
        psl = [es.enter_context(nc.psum_tensor("ps%d" % i, [128, 512], F32)) for i in range(8)]
        sb = SB(big, NB)
        S = Sched(nc)
        pctr = [0]

        def psum():
            i = pctr[0] % 8
            pctr[0] += 1
            return psl[i], ('ps', i)

        def MM(out, lhsT, rhs, start, stop, r, w):
            S.add('pe', lambda e: e.matmul(out, lhsT=lhsT, rhs=rhs, start=start, stop=stop), reads=r, writes=w)

        def TR(out, in_, ident, r, w):
            S.add('pe', lambda e: e.transpose(out, in_, ident), reads=r, writes=w)

        def ACT(out, in_, func, r, w, bias=None, scale=1.0):
            if bias is None:
                S.add('act', lambda e: e.activation(out=out, in_=in_, func=func, scale=scale), reads=r, writes=w)
            else:
                S.add('act', lambda e: e.activation(out=out, in_=in_, func=func, bias=bias, scale=scale), reads=r, writes=w)

        def TT(eng, out, in0, in1, op, r, w):
            S.add(eng, lambda e: e.tensor_tensor(out=out, in0=in0, in1=in1, op=op), reads=r, writes=w)

        def TS(eng, out, in0, s1, s2, op0, op1, r, w):
            if s2 is None:
                S.add(eng, lambda e: e.tensor_scalar(out=out, in0=in0, scalar1=s1, scalar2=None, op0=op0), reads=r, writes=w)
            else:
                S.add(eng, lambda e: e.tensor_scalar(out=out, in0=in0, scalar1=s1, scalar2=s2, op0=op0, op1=op1), reads=r, writes=w)

        def STT(eng, out, in0, scalar, in1, op0, op1, r, w):
            S.add(eng, lambda e: e.scalar_tensor_tensor(out=out, in0=in0, scalar=scalar, in1=in1, op0=op0, op1=op1), reads=r, writes=w)

        def CP(eng, out, in_, r, w):
            S.add(eng, lambda e: e.tensor_copy(out=out, in_=in_), reads=r, writes=w)

        def MSET(eng, ap, val, w):
            S.add(eng, lambda e: e.memset(ap, val), writes=w)

        def DMA(eng, out, in_, r, w):
            S.add(eng, lambda e: e.dma_start(out=out, in_=in_), reads=r, writes=w, dma=True)

        h = sb.alloc(KC * NT, F32).rearrange("p (k t) -> p k t", k=KC)
        vecs = sb.alloc(NV, F32)
        ones_bf = sb.alloc(128, BF16)
        epsc = sb.alloc(1, F32)
        cst = sb.alloc(NCC, F32)
        hv = sb.alloc(4, F32, parts=32)
        dbc = sb.alloc(64, F32)
        onec = sb.alloc(1, F32)
        eps5c = sb.alloc(1, F32)
        ident_b = sb.alloc(128, BF16)
        trip_b = sb.alloc(128, BF16)
        tris_b = sb.alloc(64, BF16)
        bmT_b = sb.alloc(1024, BF16)
        DMA('sp', vecs, vecs_d, [], ['vecs'])
        DMA('sp', cst, consts_d, [], ['cst'])
        DMA('sp', hv, hv_d, [], ['hv'])
        DMA('sp', dbc, dbc_d, [], ['dbc'])
        MSET('dve', onec, 1.0, ['onec'])
        MSET('dve', eps5c, 1e-5, ['eps5c'])
        CP('dve', ident_b, cst[:, C_ID:C_ID + 128], ['cst'], ['ident_b'])
        CP('dve', trip_b, cst[:, C_TRIP:C_TRIP + 128], ['cst'], ['trip_b'])
        CP('dve', tris_b, cst[:, C_TRIS:C_TRIS + 64], ['cst'], ['tris_b'])
        CP('dve', bmT_b, cst[:, C_BMT:C_BMT + 1024], ['cst'], ['bmT_b'])
        ident_f = cst[:, C_ID:C_ID + 128]
        DMA('sp', h, xT.rearrange("(k p) t -> p k t", p=128), [], ['h'])
        MSET('dve', ones_bf, 1.0, ['ones_bf'])
        MSET('dve', epsc, EPS, ['epsc'])

        def vcol(name, idx):
            o = VOFF[name] + idx
            return vecs[:, o:o + 1]

        scr0 = sb.mark()
        S.barrier()

        def rmsnorm(t0, T, gname, gbase, out, okey, sq, rs, ti, hkey=None, sqkey=('sq',)):
            hk = hkey if hkey is not None else ('h', ti)
            ACT(sq[:, :, :T], h[:, :, t0:t0 + T], AF.Square, [hk], [sqkey])
            ps, pk = psum()
            for k in range(KC):
                MM(ps[:, :T], ones_bf, sq[:, k, :T], k == 0, k == KC - 1, [sqkey, 'ones_bf'], [pk])
            ACT(rs[:, :T], ps[:, :T], AF.Ln, [pk, 'epsc'], [('rs',)], bias=epsc[:, 0:1], scale=1.0 / D)
            ACT(rs[:, :T], rs[:, :T], AF.Exp, [('rs',)], [('rs',)], scale=-0.5)
            for k in range(KC):
                STT('dve', out[:, k, :T], h[:, k, t0:t0 + T], vcol(gname, gbase + k), rs[:, :T], ALU.mult, ALU.mult,
                    [hk, ('rs',), 'vecs'], [okey])

        def ffn(l, wgu_d, wd_d, gname):
            m = sb.mark()
            xn = sb.alloc(KC * NT, BF16).rearrange("p (k t) -> p k t", k=KC)
            act = sb.alloc(4 * NT, BF16).rearrange("p (f t) -> p f t", f=4)
            wgu = [sb.alloc(KC * 2 * 512, BF16).rearrange("p (k g n) -> p k g n", k=KC, g=2) for _ in range(2)]
            wdb = [sb.alloc(4 * 1024, BF16).rearrange("p (f n) -> p f n", f=4) for _ in range(2)]
            sq = sb.alloc(KC * 512, BF16).rearrange("p (k t) -> p k t", k=KC)
            rs = sb.alloc(512, F32)
            sg = [sb.alloc(512, F32) for _ in range(2)]
            for ti, (t0, T) in enumerate(TILES):
                rmsnorm(t0, T, gname, l * KC, xn[:, :, t0:t0 + T], ('xn', ti), sq, rs, ti)
            wg_v = wgu_d[l].rearrange("(k p) n -> p k n", p=128)
            wd_v = wd_d[l].rearrange("(f p) n -> p f n", p=128)
            cnt = 0
            for gi, (f0, nf) in enumerate(FGROUPS):
                wb = wgu[gi % 2]
                wdd = wdb[gi % 2]
                DMA('pool', wb[:, :, 0, :nf * 128], wg_v[:, :, f0 * 128:(f0 + nf) * 128], [], [('wgu', gi % 2)])
                DMA('pool', wb[:, :, 1, :nf * 128], wg_v[:, :, DFF + f0 * 128:DFF + (f0 + nf) * 128], [], [('wgu', gi % 2)])
                DMA('pool', wdd[:, :nf, :], wd_v[:, f0:f0 + nf, :], [], [('wd', gi % 2)])
                for ti, (t0, T) in enumerate(TILES):
                    for f in range(nf):
                        pg, pgk = psum()
                        pu, puk = psum()
                        for k in range(KC):
                            MM(pg[:, :T], wb[:, k, 0, f * 128:(f + 1) * 128], xn[:, k, t0:t0 + T], k == 0, k == KC - 1,
                               [('wgu', gi % 2), ('xn', ti)], [pgk])
                        for k in range(KC):
                            MM(pu[:, :T], wb[:, k, 1, f * 128:(f + 1) * 128], xn[:, k, t0:t0 + T], k == 0, k == KC - 1,
                               [('wgu', gi % 2), ('xn', ti)], [puk])
                        sgb = sg[cnt % 2]
                        sk = ('sg', cnt % 2)
                        cnt += 1
                        ACT(sgb[:, :T], pg[:, :T], AF.Silu, [pgk], [sk])
                        TT('dve', act[:, f, t0:t0 + T], sgb[:, :T], pu[:, :T], ALU.mult, [sk, puk], [('act', f, ti)])
                for ti, (t0, T) in enumerate(TILES):
                    for d in range(KC):
                        po, pok = psum()
                        for f in range(nf):
                            MM(po[:, :T], wdd[:, f, d * 128:(d + 1) * 128], act[:, f, t0:t0 + T], f == 0, f == nf - 1,
                               [('wd', gi % 2), ('act', f, ti)], [pok])
                        STT('dve', h[:, d, t0:t0 + T], po[:, :T], 0.5, h[:, d, t0:t0 + T], ALU.mult, ALU.add,
                            [pok, ('h', ti)], [('h', ti)])
            S.barrier()
            sb.reset(m)

        def ple(l):
            m = sb.mark()
            xn = sb.alloc(KC * NT, BF16).rearrange("p (k t) -> p k t", k=KC)
            wg = sb.alloc(KC * 1024, BF16).rearrange("p (k n) -> p k n", k=KC)
            wpi = sb.alloc(2 * 1024, BF16).rearrange("p (k n) -> p k n", k=2)
            ptb = sb.alloc(2 * NT, BF16).rearrange("p (k t) -> p k t", k=2)
            sq = sb.alloc(KC * 512, BF16).rearrange("p (k t) -> p k t", k=KC)
            rs = sb.alloc(512, F32)
            sg = [sb.alloc(512, F32) for _ in range(2)]
            DMA('pool', wg, Wd_['ple_gate'][l].rearrange("(k p) n -> p k n", p=128), [], ['wg'])
            DMA('pool', wpi, Wd_['ple_in'][l].rearrange("(k p) n -> p k n", p=128), [], ['wpi'])
            DMA('pool', ptb, pT[l].rearrange("(k p) t -> p k t", p=128), [], ['ptb'])
            for ti, (t0, T) in enumerate(TILES):
                rmsnorm(t0, T, 'norm_ple', l * KC, xn[:, :, t0:t0 + T], ('xn', ti), sq, rs, ti)
            cnt = 0
            for ti, (t0, T) in enumerate(TILES):
                for d in range(KC):
                    p1, p1k = psum()
                    p2, p2k = psum()
                    for k in range(KC):
                        MM(p1[:, :T], wg[:, k, d * 128:(d + 1) * 128], xn[:, k, t0:t0 + T], k == 0, k == KC - 1,
                           ['wg', ('xn', ti)], [p1k])
                    for k in range(2):
                        MM(p2[:, :T], wpi[:, k, d * 128:(d + 1) * 128], ptb[:, k, t0:t0 + T], k == 0, k == 1,
                           ['wpi', 'ptb'], [p2k])
                    sgb = sg[cnt % 2]
                    sk = ('sg', cnt % 2)
                    cnt += 1
                    ACT(sgb[:, :T], p1[:, :T], AF.Sigmoid, [p1k], [sk])
                    TT('dve', sgb[:, :T], sgb[:, :T], p2[:, :T], ALU.mult, [sk, p2k], [sk])
                    TT('pool', h[:, d, t0:t0 + T], h[:, d, t0:t0 + T], sgb[:, :T], ALU.add, [sk, ('h', ti)], [('h', ti)])
            S.barrier()
            sb.reset(m)

        def final():
            m = sb.mark()
            sq = sb.alloc(KC * 512, BF16).rearrange("p (k t) -> p k t", k=KC)
            rs = sb.alloc(512, F32)
            yo = [sb.alloc(KC * 512, F32).rearrange("p (k t) -> p k t", k=KC) for _ in range(2)]
            yv = yT.rearrange("(k p) t -> p k t", p=128)
            for ti, (t0, T) in enumerate(TILES):
                yb = yo[ti % 2]
                rmsnorm(t0, T, 'norm_final', 0, yb, ('yo', ti % 2), sq, rs, ti)
                DMA('sp', yv[:, :, t0:t0 + T], yb[:, :, :T], [('yo', ti % 2)], [('yT', ti)])
            sb.reset(m)

        def mamba(l):
            j = l // 2
            Win = Wd_['m_in_proj'][j].rearrange("(k p) n -> p k n", p=128)
            Wout = Wd_['m_out_proj'][j].rearrange("(c p) n -> p c n", p=128)
            R0 = ['cst', 'vecs', 'hv', 'dbc', 'onec', 'eps5c', 'ident_b', 'trip_b', 'tris_b', 'bmT_b']
            cw = [0]

            def run(smp):
                m = sb.mark()
                T = 64 if smp else 256
                L = 64 if smp else 128
                nch = T // L
                tiles = [(2048, 64)] if smp else [(i * 256, 256) for i in range(8)]
                tri = cst[:L, C_TRIS:C_TRIS + 64] if smp else cst[:, C_TRIP:C_TRIP + 128]
                onesm = cst[:L, C_BLKS:C_BLKS + 64] if smp else cst[:, C_ONES:C_ONES + 128]
                cmask = tris_b[:L, :] if smp else trip_b
                u = sb.alloc(KC * T, BF16).rearrange("p (k t) -> p k t", k=KC)
                rs = sb.alloc(T, F32)
                wblk = [sb.alloc(KC * 256, BF16).rearrange("p (k n) -> p k n", k=KC) for _ in range(2)]
                wdt = sb.alloc(KC * 32, BF16).rearrange("p (k n) -> p k n", k=KC)
                zs = sb.alloc(nch * 2048, BF16).rearrange("p (c n) -> p c n", c=nch)
                PW = 112 if smp else 3 + T
                preb = [sb.alloc(PW, F32) for _ in range(2)]
                accb = [sb.alloc(T, F32) for _ in range(2)]
                xa = sb.alloc(32 * T, BF16).rearrange("p (c t) -> p c t", c=32)
                dtT = sb.alloc(T, F32, parts=32)
                dtAT = sb.alloc(T, F32, parts=32)
                expA = sb.alloc(1, F32, parts=32)
                xtok = sb.alloc(2048, BF16)
                Btok = sb.alloc(1024, BF16)
                dtk = sb.alloc(64, F32)
                nak = sb.alloc(64, F32)
                ena = sb.alloc(32, F32)
                dend = sb.alloc(32, F32)
                xdt = sb.alloc(2048, BF16)
                sq = xdt[:, 0:KC * T].rearrange("p (k t) -> p k t", k=KC)
                NSET = 1 if smp else 4
                gsets = []
                for _si in range(NSET):
                    gsets.append(dict(
                        cbm=sb.alloc(128, BF16),
                        dexp=sb.alloc(4 * 128, F32).rearrange("p (a b) -> p a b", a=4),
                        dec=[sb.alloc(128, F32) for _ in range(2)],
                        MT=sb.alloc(4 * 128, BF16).rearrange("p (a b) -> p a b", a=4),
                        tmp=sb.alloc(256, F32), yg=sb.alloc(256, F32), ssq=sb.alloc(1, F32),
                        ynk=sb.alloc(256, BF16), xdd=sb.alloc(256, BF16)))
                print('MAMBA sets alloc off', sb.off, sb.nbytes, smp)
                ynT = sb.alloc(16 * T, BF16).rearrange("p (c t) -> p c t", c=16)
                wob = [sb.alloc(16 * 128, BF16).rearrange("p (c n) -> p c n", c=16) for _ in range(2)]
                if smp:
                    cstate = sb.alloc(32 * 48, F32).rearrange("p (c b r) -> p c b r", c=32, b=16)
                    cso = sb.alloc(32 * 48, F32).rearrange("p (c b r) -> p c b r", c=32, b=16)
                    nat = sb.alloc(16 * 256, F32).rearrange("p (b q n) -> p b q n", b=16, q=2)
                    STs = sb.alloc(16 * 256, F32).rearrange("p (b n) -> p b n", b=16)
                    STsb = sb.alloc(16 * 256, BF16).rearrange("p (b n) -> p b n", b=16)
                    CTm = sb.alloc(16 * 64, BF16).rearrange("p (b t) -> p b t", b=16)
                    xddm = sb.alloc(16 * 256, BF16).rearrange("p (b n) -> p b n", b=16)
                    edCs = sb.alloc(512, F32).rearrange("p (b h) -> p b h", b=16)
                    dtAe = sb.alloc(512, F32).rearrange("p (b h) -> p b h", b=16)
                    DMA('sp', cstate, conv_in[j].rearrange("(c p) b r -> p c b r", p=128), [], ['cstate'])
                else:
                    carry = sb.alloc(96, F32).rearrange("p (c r) -> p c r", c=32)
                    ST = sb.alloc(2048, F32)
                    STb = sb.alloc(2048, BF16)
                    natp = sb.alloc(512, F32).rearrange("p (q n) -> p q n", q=4)
                    edC = sb.alloc(32, F32)
                    MSET('dve', carry, 0.0, ['carry'])
                    MSET('dve', ST, 0.0, ['ST'])
                    MSET('pool', STb, 0.0, ['STb'])
                ACT(expA, hv[:, 2 * j + 1:2 * j + 2], AF.Exp, ['hv'], ['expA'])

                for (t0, T_) in tiles:
                    rmsnorm(t0, T, 'norm_mix', l * KC, u, ('u',), sq, rs, 0, hkey='h', sqkey='xdt')
                    for zb in range(8):
                        wb = wblk[cw[0] % 2]
                        wk = ('wblk', cw[0] % 2)
                        cw[0] += 1
                        DMA('pool', wb, Win[:, :, zb * 256:(zb + 1) * 256], [], [wk])
                        for tc in range(nch):
                            ps, pk = psum()
                            for k in range(KC):
                                MM(ps[:L, 0:256], u[:, k, tc * L:(tc + 1) * L], wb[:, k, :], k == 0, k == KC - 1, [('u',), wk], [pk])
                            ACT(zs[:L, tc, zb * 256:(zb + 1) * 256], ps[:L, 0:256], AF.Silu, [pk], [('zs', tc, zb)])
                    for xb in range(16):
                        wb = wblk[cw[0] % 2]
                        wk = ('wblk', cw[0] % 2)
                        cw[0] += 1
                        DMA('pool', wb, Win[:, :, 2048 + xb * 256:2048 + (xb + 1) * 256], [], [wk])
                        for q in range(2):
                            cc = xb * 2 + q
                            ps, pk = psum()
                            for k in range(KC):
                                MM(ps[:, :T], wb[:, k, q * 128:(q + 1) * 128], u[:, k, :], k == 0, k == KC - 1, [('u',), wk], [pk])
                            pre = preb[cc % 2]
                            prk = ('pre', cc % 2)
                            acc = accb[cc % 2]
                            ack = ('acc', cc % 2)
                            ce = 'dve'
                            if smp:
                                prev = pre.rearrange("p (b c) -> p b c", c=7)
                                CP('pool', prev[:, :, 0:3], cstate[:, cc], ['cstate'], [prk])
                                ACT(prev[:, :, 3:7], ps[:, :T].rearrange("p (b t) -> p b t", t=4), AF.Copy, [pk], [prk])
                                win = lambda k_: prev[:, :, k_:k_ + 4]
                                accv = acc.rearrange("p (b t) -> p b t", t=4)
                                xav = xa[:, cc, :].rearrange("p (b t) -> p b t", t=4)
                            else:
                                CP('pool', pre[:, 0:3], carry[:, cc, :], ['carry'], [prk])
                                ACT(pre[:, 3:3 + T], ps[:, :T], AF.Copy, [pk], [prk])
                                win = lambda k_: pre[:, k_:k_ + T]
                                accv = acc
                                xav = xa[:, cc, :]
                            TS(ce, accv, win(0), vcol('m_conv_w', (j * 4 + 0) * 32 + cc), None, ALU.mult, None, [prk, 'vecs'], [ack])
                            for k_ in range(1, 4):
                                STT(ce, accv, win(k_), vcol('m_conv_w', (j * 4 + k_) * 32 + cc), accv, ALU.mult, ALU.add,
                                    [prk, 'vecs', ack], [ack])
                            ACT(xav, accv, AF.Silu, [ack, 'vecs'], [('xa', cc)], bias=vcol('m_conv_b', j * 32 + cc))
                            if smp:
                                CP('pool', cso[:, cc], prev[:, :, 4:7], [prk], [('cso', cc)])
                            else:
                                CP('pool', carry[:, cc, :], pre[:, T:T + 3], [prk], ['carry'])
                    DMA('pool', wdt, Win[:, :, 6144:6176], [], ['wdt'])
                    ps, pk = psum()
                    for k in range(KC):
                        MM(ps[:32, :T], wdt[:, k, :], u[:, k, :], k == 0, k == KC - 1, [('u',), 'wdt'], [pk])
                    ACT(dtT, ps[:32, :T], AF.Exp, [pk, 'hv'], ['dtT'], bias=hv[:, 2 * j:2 * j + 1])
                    ACT(dtT, dtT, AF.Ln, ['dtT', 'onec'], ['dtT'], bias=onec[0:32, 0:1])
                    TS('dve', dtAT, dtT, expA[:, 0:1], None, ALU.mult, None, ['dtT', 'expA'], ['dtAT'])

                    for tc in range(nch):
                        c0 = tc * L
                        for half in range(2):
                            ps, pk = psum()
                            psb = ps.bitcast(BF16)
                            for q in range(8):
                                cc = half * 8 + q
                                TR(psb[:L, q * 128:(q + 1) * 128], xa[:, cc, c0:c0 + L], ident_b, [('xa', cc), 'ident_b'], [pk])
                            CP('dve', xtok[:L, half * 1024:(half + 1) * 1024], psb[:L, :], [pk], [('xtok', half)])
                        ps, pk = psum()
                        psb = ps.bitcast(BF16)
                        for g in range(8):
                            TR(psb[:L, g * 128:(g + 1) * 128], xa[:, 16 + g, c0:c0 + L], ident_b, [('xa', 16 + g), 'ident_b'], [pk])
                        CP('dve', Btok[:L, :], psb[:L, :], [pk], ['Btok'])
                        ps, pk = psum()
                        TR(ps[:L, 0:32], dtT[:, c0:c0 + L], ident_f[0:32, 0:32], ['dtT', 'cst'], [pk])
                        TR(ps[:L, 32:64], dtAT[:, c0:c0 + L], ident_f[0:32, 0:32], ['dtAT', 'cst'], [pk])
                        CP('dve', dtk[:L, :], ps[:L, 0:64], [pk], ['dtk'])
                        ps, pk = psum()
                        MM(ps[:L, 0:32], tri, dtk[:L, 32:64], True, True, ['cst', 'dtk'], [pk])
                        MM(ps[:L, 32:64], onesm, dtk[:L, 32:64], True, True, ['cst', 'dtk'], [pk])
                        CP('dve', nak[:L, :], ps[:L, 0:64], [pk], ['nak'])
                        ACT(ena[:L, :], nak[:L, 0:32], AF.Exp, ['nak'], ['ena'], scale=-1.0)
                        TT('dve', dend[:L, :], nak[:L, 0:32], nak[:L, 32:64], ALU.subtract, ['nak'], ['dend'])
                        ACT(dend[:L, :], dend[:L, :], AF.Exp, ['dend'], ['dend'])
                        TT('dve', xdt[:L, :].rearrange("p (h d) -> p h d", h=32), xtok[:L, :].rearrange("p (h d) -> p h d", h=32),
                           dtk[:L, 0:32].rearrange("p (h o) -> p h o", o=1).broadcast_to([L, 32, 64]), ALU.mult, ['xtok', 'dtk'], ['xdt'])
                        if smp:
                            TT('dve', dtAe[:L], dtk[:L, 32:64].rearrange("p (o h) -> p o h", o=1).broadcast_to([L, 16, 32]),
                               cst[:L, C_BM:C_BM + 16].rearrange("p (b o) -> p b o", o=1).broadcast_to([L, 16, 32]), ALU.mult, ['dtk', 'cst'], ['dtAe'])
                            ps, pk = psum()
                            MM(ps[:, 0:512], cst[:L, C_ONES:C_ONES + 128], dtAe[:L].rearrange("p b h -> p (b h)"), True, True, ['cst', 'dtAe'], [pk])
                            ACT(edCs.rearrange("p b h -> p (b h)"), ps[:, 0:512], AF.Exp, [pk], ['edCs'], scale=-1.0)
                        else:
                            ACT(edC, nak[:, 32:64], AF.Exp, ['nak'], ['edC'], scale=-1.0)

                        def group_gen(g, si):
                            gs_ = gsets[si]
                            cbm, dexp, dec, MT, tmp, yg, ssq, ynk, xdd = (gs_['cbm'], gs_['dexp'], gs_['dec'], gs_['MT'], gs_['tmp'],
                                                                          gs_['yg'], gs_['ssq'], gs_['ynk'], gs_['xdd'])
                            BT = xa[:, 16 + g, c0:c0 + L]
                            CT = xa[:, 24 + g, c0:c0 + L]
                            if smp:
                                for q_ in range(2):
                                    DMA('sp', nat[:, :, q_, :], ssm_in[j][:, 4 * g + 2 * q_:4 * g + 2 * q_ + 2].rearrange("b a p n -> (a p) b n"), [], ['nat'])
                                for b4 in range(8):
                                    ps, pk = psum()
                                    for i4 in range(4):
                                        b_ = b4 * 2 + i4 // 2
                                        q_ = i4 % 2
                                        TR(ps[:, i4 * 128:(i4 + 1) * 128], nat[:, b_, q_, :], ident_f, ['nat', 'cst'], [pk])
                                    CP('dve', STs[:, b4 * 2:b4 * 2 + 2, :].rearrange("p b n -> p (b n)"), ps[:, :], [pk], [('STs', b4)])
                                    CP('act' if False else 'pool', STsb[:, b4 * 2:b4 * 2 + 2, :], STs[:, b4 * 2:b4 * 2 + 2, :], [('STs', b4)], [('STsb', b4)])
                            ps, pk = psum()
                            MM(ps[:L, :L], BT, CT, True, True, [('xa', 16 + g), ('xa', 24 + g)], [pk])
                            TT('dve', cbm[:L, :L], ps[:L, :L], cmask[:L, :L], ALU.mult, [pk, 'trip_b', 'tris_b'], [('cbm', si)])
                            CP('pool', dexp[:L, :, :L], dtk[:L, 32 + 4 * g:36 + 4 * g].rearrange("p (a o) -> p a o", o=1).broadcast_to([L, 4, L]), ['dtk'], [('dexp', si)])
                            yield
                            ps2, pk2 = psum()
                            for hh in range(4):
                                MM(ps2[:L, hh * L:(hh + 1) * L], dexp[:L, hh, :L], tri, True, True, [('dexp', si), 'cst'], [pk2])
                            yield
                            for hh in range(4):
                                h_ = 4 * g + hh
                                dc = dec[hh % 2]
                                dk = ('dec', si, hh % 2)
                                ACT(dc[:L, :L], ps2[:L, hh * L:(hh + 1) * L], AF.Exp, [pk2, 'nak'], [dk], bias=nak[:L, h_:h_ + 1], scale=-1.0)
                                STT('dve', MT[:L, hh, :L], dc[:L, :L], 1.0, cbm[:L, :L], ALU.min, ALU.mult, [dk, ('cbm', si)], [('MT', si, hh)])
                            yield
                            psy, pyk = psum()
                            for hh in range(4):
                                h_ = 4 * g + hh
                                MM(psy[:L, hh * 64:(hh + 1) * 64], MT[:L, hh, :L], xdt[:L, h_ * 64:(h_ + 1) * 64], True, True, [('MT', si, hh), 'xdt'], [pyk])
                            if smp:
                                TT('pool', CTm, CT.rearrange("p (o t) -> p o t", o=1).broadcast_to([128, 16, 64]),
                                   bmT_b.rearrange("p (b t) -> p b t", b=16), ALU.mult, [('xa', 24 + g), 'bmT_b'], ['CTm'])
                                for b in range(16):
                                    MM(psy[:L, 256:512], CTm[:, b, :], STsb[:, b, :], b == 0, b == 15, ['CTm', ('STsb', b // 2)], [pyk])
                            else:
                                MM(psy[:L, 256:512], CT, STb[:, g * 256:(g + 1) * 256], True, True, [('xa', 24 + g), ('STb', g)], [pyk])
                            yield
                            t3 = tmp[:L, :].rearrange("p (a d) -> p a d", a=4)
                            TT('dve', t3, psy[:L, 256:512].rearrange("p (a d) -> p a d", a=4),
                               ena[:L, 4 * g:4 * g + 4].rearrange("p (a o) -> p a o", o=1).broadcast_to([L, 4, 64]), ALU.mult, [pyk, 'ena'], [('tmp', si)])
                            TT('dve', yg[:L, :], psy[:L, 0:256], tmp[:L, :], ALU.add, [pyk, ('tmp', si)], [('yg', si)])
                            TT('pool', t3, xtok[:L, g * 256:(g + 1) * 256].rearrange("p (a d) -> p a d", a=4),
                               dbc[:L, j * 32 + 4 * g:j * 32 + 4 * g + 4].rearrange("p (a o) -> p a o", o=1).broadcast_to([L, 4, 64]), ALU.mult,
                               [('xtok', g // 4), 'dbc', ('yg', si)], [('tmp', si)])
                            TT('dve', yg[:L, :], yg[:L, :], tmp[:L, :], ALU.add, [('tmp', si), ('yg', si)], [('yg', si)])
                            yield
                            TT('dve', yg[:L, :], yg[:L, :], zs[:L, tc, g * 256:(g + 1) * 256], ALU.mult, [('yg', si), ('zs', tc, g)], [('yg', si)])
                            TT('pool', tmp[:L, :], yg[:L, :], yg[:L, :], ALU.mult, [('yg', si)], [('tmp', si)])
                            S.add('dve', (lambda e, o=ssq[:L, 0:1], i_=tmp[:L, :]: e.tensor_reduce(out=o, in_=i_, axis=AX.X, op=ALU.add)), reads=[('tmp', si)], writes=[('ssq', si)])
                            yield
                            ACT(ssq[:L, :], ssq[:L, :], AF.Ln, [('ssq', si), 'eps5c'], [('ssq', si)], bias=eps5c[:L, 0:1], scale=1.0 / 256)
                            ACT(ssq[:L, :], ssq[:L, :], AF.Exp, [('ssq', si)], [('ssq', si)], scale=-0.5)
                            TS('dve', ynk[:L, :], yg[:L, :], ssq[:L, 0:1], None, ALU.mult, None, [('yg', si), ('ssq', si)], [('ynk', si)])
                            yield
                            pst, ptk = psum()
                            pstb = pst.bitcast(BF16)
                            for q in range(2):
                                TR(pstb[:, q * L:(q + 1) * L], ynk[:L, q * 128:(q + 1) * 128], ident_b[:L, :L], [('ynk', si), 'ident_b'], [ptk])
                            for q in range(2):
                                cc = 2 * g + q
                                ACT(ynT[:, cc, c0:c0 + L], pstb[:, q * L:(q + 1) * L], AF.Copy, [ptk, 'vecs'], [('ynT', cc)], scale=vcol('m_norm', j * 16 + cc))
                            yield
                            TT('dve', xdd[:L, :].rearrange("p (a d) -> p a d", a=4), xdt[:L, g * 256:(g + 1) * 256].rearrange("p (a d) -> p a d", a=4),
                               dend[:L, 4 * g:4 * g + 4].rearrange("p (a o) -> p a o", o=1).broadcast_to([L, 4, 64]), ALU.mult, ['xdt', 'dend'], [('xdd', si)])
                            if smp:
                                TT('pool', xddm[:L], xdd[:L, :].rearrange("p (o n) -> p o n", o=1).broadcast_to([L, 16, 256]),
                                   cst[:L, C_BM:C_BM + 16].rearrange("p (b o) -> p b o", o=1).broadcast_to([L, 16, 256]), ALU.mult, [('xdd', si), 'cst'], ['xddm'])
                                TT('dve', STs.rearrange("p b (a d) -> p b a d", a=4), STs.rearrange("p b (a d) -> p b a d", a=4),
                                   edCs[:, :, 4 * g:4 * g + 4].rearrange("p b (a o) -> p b a o", o=1).broadcast_to([128, 16, 4, 64]), ALU.mult,
                                   ['STs', 'edCs', 'STsb'], ['STs'])
                                for b2 in range(8):
                                    pss, psk = psum()
                                    for i2 in range(2):
                                        MM(pss[:, i2 * 256:(i2 + 1) * 256], Btok[:L, g * 128:(g + 1) * 128], xddm[:L, b2 * 2 + i2, :], True, True, ['Btok', 'xddm'], [psk])
                                    TT('dve', STs[:, b2 * 2:b2 * 2 + 2, :].rearrange("p b n -> p (b n)"), STs[:, b2 * 2:b2 * 2 + 2, :].rearrange("p b n -> p (b n)"),
                                       pss[:, :], ALU.add, [psk, ('STs', b2)], [('STs', b2)])
                                for b4 in range(8):
                                    ps, pk = psum()
                                    for i4 in range(4):
                                        b_ = b4 * 2 + i4 // 2
                                        q_ = i4 % 2
                                        TR(ps[:, i4 * 128:(i4 + 1) * 128], STs[:, b_, q_ * 128:(q_ + 1) * 128], ident_f, [('STs', b4), 'cst'], [pk])
                                    CP('dve', nat[:, b4 * 2:b4 * 2 + 2].rearrange("p b q n -> p (b q n)"), ps[:, :], [pk], ['nat'])
                                for q_ in range(2):
                                    DMA('sp', ssm_s[j][:, 4 * g + 2 * q_:4 * g + 2 * q_ + 2].rearrange("b a p n -> (a p) b n"), nat[:, :, q_, :], ['nat'], [('ssm_s', j, g, q_)])
                            else:
                                pss, psk = psum()
                                MM(pss[:, 0:256], Btok[:L, g * 128:(g + 1) * 128], xdd[:L, :], True, True, ['Btok', ('xdd', si)], [psk])
                                sg3 = ST[:, g * 256:(g + 1) * 256].rearrange("p (a d) -> p a d", a=4)
                                TT('dve', sg3, sg3, edC[:, 4 * g:4 * g + 4].rearrange("p (a o) -> p a o", o=1).broadcast_to([128, 4, 64]), ALU.mult,
                                   [('ST', g), 'edC', ('STb', g)], [('ST', g)])
                                TT('dve', ST[:, g * 256:(g + 1) * 256], ST[:, g * 256:(g + 1) * 256], pss[:, 0:256], ALU.add, [psk, ('ST', g)], [('ST', g)])
                                CP('pool', STb[:, g * 256:(g + 1) * 256], ST[:, g * 256:(g + 1) * 256], [('ST', g)], [('STb', g)])
                            yield
                        for gb in range(0, 8, NSET):
                            gens_ = [group_gen(g, g % NSET) for g in range(gb, gb + NSET)]
                            alive = True
                            while alive:
                                alive = False
                                for gn_ in gens_:
                                    try:
                                        next(gn_)
                                        alive = True
                                    except StopIteration:
                                        pass
                    for d in range(KC):
                        wo = wob[d % 2]
                        wok = ('wob', d % 2)
                        DMA('pool', wo, Wout[:, :, d * 128:(d + 1) * 128], [], [wok])
                        ps, pk = psum()
                        for cc in range(16):
                            MM(ps[:, :T], wo[:, cc, :], ynT[:, cc, :], cc == 0, cc == 15, [wok, ('ynT', cc)], [pk])
                        TT('dve', h[:, d, t0:t0 + T], h[:, d, t0:t0 + T], ps[:, :T], ALU.add, [pk, 'h'], ['h'])
                if smp:
                    DMA('sp', conv_s[j].rearrange("(c p) b r -> p c b r", p=128), cso, ['cso'], [('conv_s', j)])
                else:
                    DMA('sp', conv_p[j].rearrange("(c p) r -> p c r", p=128), carry, ['carry'], [('conv_p', j)])
                    for q4 in range(4):
                        ps, pk = psum()
                        for i4 in range(4):
                            q = q4 * 4 + i4
                            TR(ps[:, i4 * 128:(i4 + 1) * 128], ST[:, q * 128:(q + 1) * 128], ident_f, ['ST', 'cst'], [pk])
                        CP('dve', natp.rearrange("p q n -> p (q n)"), ps[:, :], [pk], ['natp'])
                        DMA('sp', ssm_p[j].rearrange("(q a) p n -> (a p) q n", a=2)[:, q4 * 4:q4 * 4 + 4, :], natp, ['natp'], [('ssm_p', j, q4)])
                S.barrier()
                sb.reset(m)

            run(False)
            run(True)


        def rwkv(l):
            j = l // 2
            RDT = BF16
            mu0 = VOFF['r_mu'] + j * 48

            def wview(name, jj=None):
                return Wd_[name][j if jj is None else jj]

            def run(smp):
                m = sb.mark()
                T = 64 if smp else 128
                L = T
                nb = 16 if smp else 1
                lt = L // nb
                nlev = 2 if smp else 7
                tiles = [(2048, 64)] if smp else [(i * 128, 128) for i in range(16)]
                mAB = cst[:L, C_MABS:C_MABS + 128] if smp else cst[:, C_MABP:C_MABP + 256]
                mlow = cst[:L, C_LOWS:C_LOWS + 64] if smp else cst[:, C_LOWP:C_LOWP + 128]
                rst = cst[:, C_RSTS:C_RSTS + 64] if smp else cst[:, C_RSTP:C_RSTP + 128]
                blk2 = cst[:, C_BLK2:C_BLK2 + 128]
                f32a = lambda: sb.alloc(KC * T, F32).rearrange("p (k t) -> p k t", k=KC)
                b16a = lambda: sb.alloc(KC * T, BF16).rearrange("p (k t) -> p k t", k=KC)
                uf, up, r_, k_, v_, nlw, a_, kk, np_, tA, tB = [f32a() for _ in range(11)]
                xm = [b16a() for _ in range(2)]
                g_, BT_, KT_, BH_, KH_ = [b16a() for _ in range(5)]
                V_ = xm[1]
                yg = xm[0]
                ARt = sb.alloc(KC * 2 * T, BF16).rearrange("p (k c t) -> p k c t", k=KC, c=2)
                sq = BH_
                rs = sb.alloc(T, F32)
                wbuf = [sb.alloc(KC * 256, BF16).rearrange("p (k n) -> p k n", k=KC) for _ in range(2)]
                w1b = sb.alloc(KC * 64, BF16).rearrange("p (k n) -> p k n", k=KC)
                a1b = sb.alloc(KC * 64, BF16).rearrange("p (k n) -> p k n", k=KC)
                g1b = sb.alloc(KC * 160, BF16).rearrange("p (k n) -> p k n", k=KC)
                v1b = sb.alloc(KC * 32, BF16).rearrange("p (k n) -> p k n", k=KC)
                w2b = sb.alloc(1024, BF16)
                a2b = sb.alloc(1024, BF16)
                g2a = sb.alloc(1024, BF16)
                g2b = sb.alloc(1024, BF16)
                v2b = sb.alloc(1024, BF16)
                lo1 = sb.alloc(T, BF16)
                lo2 = sb.alloc(T, BF16)
                lo3 = sb.alloc(T, BF16)
                negw0 = sb.alloc(8, F32)
                omka = sb.alloc(8, F32)
                mhalf = sb.alloc(1, F32)
                c24 = sb.alloc(1, F32)
                gnec = sb.alloc(1, F32)
                eC = sb.alloc(KC * nb, F32).rearrange("p (k b) -> p k b", k=KC)
                Vtok = sb.alloc(KC * 128, BF16).rearrange("p (k n) -> p k n", k=KC)
                BHtok = sb.alloc(KC * 128, BF16).rearrange("p (k n) -> p k n", k=KC)
                KHtok = sb.alloc(KC * 128, BF16).rearrange("p (k n) -> p k n", k=KC)
                Ytok = sb.alloc(KC * 128, F32).rearrange("p (k n) -> p k n", k=KC)
                Ysq = up if not smp else sb.alloc(KC * 128, F32).rearrange("p (k n) -> p k n", k=KC)
                YSK = 'Ysq' if smp else 'up'
                Yn = sb.alloc(KC * 128, BF16).rearrange("p (k n) -> p k n", k=KC)
                st1 = sb.alloc(16, F32)
                st2 = sb.alloc(16, F32)
                NH = 1 if smp else 8
                ABt = [sb.alloc(2 * L, BF16) for _ in range(NH)]
                AKt = [sb.alloc(2 * L, BF16) for _ in range(NH)]
                MMb = [[sb.alloc(2 * L, BF16) for _ in range(2)] for _ in range(NH)]
                X32 = [sb.alloc(64, F32) for _ in range(NH)]
                Xb = [sb.alloc(64, BF16) for _ in range(NH)]
                S0T = sb.alloc(KC * nb * 64, F32).rearrange("p (k b i) -> p k b i", k=KC, b=nb)
                nbd = min(nb, 4)
                S0bh = [sb.alloc(nb * 64, BF16).rearrange("p (b i) -> p b i", b=nb) for _ in range(NH)]
                bd = sb.alloc(nbd * 128, F32).rearrange("p (b n) -> p b n", b=nbd)
                if smp:
                    sst = sb.alloc(KC * 16, F32).rearrange("p (k b) -> p k b", k=KC)
                    ATm = [sb.alloc(16 * 64, BF16).rearrange("p (b t) -> p b t", b=16) for _ in range(NH)]
                    RTm = [sb.alloc(16 * 64, BF16).rearrange("p (b t) -> p b t", b=16) for _ in range(NH)]
                    Wm = [sb.alloc(16 * 64, BF16).rearrange("p (b t) -> p b t", b=16) for _ in range(NH)]
                    Vm = [sb.alloc(16 * 64, BF16).rearrange("p (b t) -> p b t", b=16) for _ in range(NH)]
                    DMA('sp', sst, shift_in[j].rearrange("(k p) b -> p k b", p=128), [], ['sst'])
                    MSET('pool', BHtok, 0.0, ['BHtok'])
                    MSET('pool', Vtok, 0.0, ['Vtok'])
                    for q_ in range(NH):
                        MSET('pool', AKt[q_], 0.0, [('AKt', q_)])
                        MSET('pool', ABt[q_], 0.0, [('ABt', q_)])
                        MSET('pool', Xb[q_], 0.0, [('Xb', q_)])
                    MSET('pool', KHtok, 0.0, ['KHtok'])
                    for q_ in range(NH):
                        MSET('pool', Wm[q_], 0.0, [('Wm', q_)])
                        MSET('pool', Vm[q_], 0.0, [('Vm', q_)])
                else:
                    carry = sb.alloc(8, F32)
                    MSET('dve', carry, 0.0, ['carry'])
                MSET('dve', mhalf, -0.5, ['mhalf'])
                MSET('dve', c24, 1e-24, ['c24'])
                MSET('dve', gnec, 64e-5, ['gnec'])
                TS('dve', negw0, vecs[:, VOFF['r_w0'] + j * 8:VOFF['r_w0'] + j * 8 + 8], -1.0, None, ALU.mult, None, ['vecs'], ['negw0'])
                TS('dve', omka, vecs[:, VOFF['r_k_a'] + j * 8:VOFF['r_k_a'] + j * 8 + 8], -1.0, 1.0, ALU.mult, ALU.add, ['vecs'], ['omka'])
                DMA('pool', w1b, wview('r_w1').rearrange("(k p) n -> p k n", p=128), [], ['w1b'])
                DMA('pool', a1b, wview('r_a1').rearrange("(k p) n -> p k n", p=128), [], ['a1b'])
                DMA('pool', g1b, wview('r_g1').rearrange("(k p) n -> p k n", p=128), [], ['g1b'])
                DMA('pool', w2b[0:64, :], wview('r_w2'), [], ['w2b'])
                DMA('pool', a2b[0:64, :], wview('r_a2'), [], ['a2b'])
                DMA('pool', g2a, wview('r_g2')[0:128, :], [], ['g2a'])
                DMA('pool', g2b[0:32, :], wview('r_g2')[128:160, :], [], ['g2b'])
                if j == 1:
                    DMA('pool', v1b, wview('r_v1', 0).rearrange("(k p) n -> p k n", p=128), [], ['v1b'])
                    DMA('pool', v2b[0:32, :], wview('r_v2', 0), [], ['v2b'])
                if smp:
                    MSET('pool', bd, 0.0, ['bd'])
                    for d in range(KC):
                        for b4 in range(4):
                            for a in range(2):
                                DMA('sp', bd[a * 64:(a + 1) * 64, :, a * 64:(a + 1) * 64], wkv_in[j][b4 * 4:b4 * 4 + 4, 2 * d + a].rearrange("b i j -> i b j"), [], ['bd'])
                            ps, pk = psum()
                            for i4 in range(4):
                                TR(ps[:, i4 * 128:(i4 + 1) * 128], bd[:, i4, :], ident_f, ['bd', 'cst'], [pk])
                            for a in range(2):
                                CP('dve', S0T[a * 64:(a + 1) * 64, d, b4 * 4:b4 * 4 + 4, :],
                                   ps[a * 64:(a + 1) * 64, :].rearrange("p (b n) -> p b n", b=4)[:, :, a * 64:(a + 1) * 64], [pk], [('S0T', d)])
                else:
                    MSET('dve', S0T, 0.0, ['S0T'])
                wc = [0]

                def proj(wname, src, evac, jj=None):
                    Wv = wview(wname, jj).rearrange("(k p) n -> p k n", p=128)
                    for hf in range(4):
                        wb = wbuf[wc[0] % 2]
                        wk = ('wbuf', wc[0] % 2)
                        wc[0] += 1
                        DMA('pool', wb, Wv[:, :, hf * 256:(hf + 1) * 256], [], [wk])
                        for dd in range(2):
                            d = hf * 2 + dd
                            ps, pk = psum()
                            for k in range(KC):
                                MM(ps[:, :T], wb[:, k, dd * 128:(dd + 1) * 128], src[:, k, :], k == 0, k == KC - 1, [wk, 'xm'], [pk])
                            evac(d, ps, pk)

                try:
                    for (t0, T_) in tiles[:CFG.get('rtiles', 99)]:
                        rmsnorm(t0, T, 'norm_mix', l * KC, uf, ('uf',), sq, rs, 0, hkey='h', sqkey='BH_')
                        if smp:
                            u4 = uf.rearrange("p k (b t) -> p k b t", t=4)
                            p4 = up.rearrange("p k (b t) -> p k b t", t=4)
                            for k in range(KC):
                                CP('pool', p4[:, k, :, 1:4], u4[:, k, :, 0:3], ['uf'], ['up'])
                                CP('pool', p4[:, k, :, 0:1], sst[:, k, :].rearrange("p (b o) -> p b o", o=1), ['sst'], ['up'])
                                CP('pool', sst[:, k, :].rearrange("p (b o) -> p b o", o=1), u4[:, k, :, 3:4], ['uf', 'up'], ['sst'])
                        else:
                            CP('pool', up[:, :, 1:T], uf[:, :, 0:T - 1], ['uf'], ['up'])
                            CP('pool', up[:, :, 0:1], carry.rearrange("p (k o) -> p k o", o=1), ['carry'], ['up'])
                            CP('pool', carry.rearrange("p (k o) -> p k o", o=1), uf[:, :, T - 1:T], ['uf', 'up'], ['carry'])
                        TT('dve', up, up, uf, ALU.subtract, ['up', 'uf'], ['up'])

                        def mix(i, dst):
                            muv = vecs[:, mu0 + i * 8:mu0 + i * 8 + 8].rearrange("p (k o) -> p k o", o=1).broadcast_to([128, KC, T])
                            TT('dve', tA, up, muv, ALU.mult, ['up', 'vecs'], ['tA'])
                            TT('dve', dst, tA, uf, ALU.add, ['tA', 'uf'], ['xm'])
                        mix(0, xm[0])
                        proj('r_wr', xm[0], lambda d, ps, pk: ACT(r_[:, d, :], ps[:, :T], AF.Copy, [pk], [('r_', d)]))
                        mix(2, xm[1])
                        proj('r_wk', xm[1], lambda d, ps, pk: ACT(k_[:, d, :], ps[:, :T], AF.Copy, [pk], [('k_', d)]))
                        mix(3, xm[0])
                        proj('r_wv', xm[0], lambda d, ps, pk: ACT(v_[:, d, :], ps[:, :T], AF.Copy, [pk], [('v_', d)]))
                        if j == 1:
                            ps, pk = psum()
                            for k in range(KC):
                                MM(ps[:32, :T], v1b[:, k, :], xm[0][:, k, :], k == 0, k == KC - 1, ['v1b', 'xm'], [pk])
                            CP('dve', lo3[0:32, :], ps[:32, :T], [pk], ['lo3'])
                            DMA('sp', tB, vfirst_d.rearrange("(k p) t -> p k t", p=128)[:, :, t0:t0 + T], ['vf_dram'], ['tB'])
                            for d in range(KC):
                                ps, pk = psum()
                                MM(ps[:, :T], v2b[0:32, d * 128:(d + 1) * 128], lo3[0:32, :], True, True, ['v2b', 'lo3'], [pk])
                                ACT(tA[:, d, :], ps[:, :T], AF.Sigmoid, [pk, 'vecs'], ['tA'], bias=vcol('r_v0', d))
                            TT('dve', tB, tB, v_, ALU.subtract, ['tB', 'v_'], ['tB'])
                            TT('dve', tB, tB, tA, ALU.mult, ['tB', 'tA'], ['tB'])
                            TT('dve', v_, v_, tB, ALU.add, ['tB', 'v_'], ['v_'])
                        else:
                            DMA('sp', vfirst_d.rearrange("(k p) t -> p k t", p=128)[:, :, t0:t0 + T], v_, ['v_'], [('vf_dram', t0)])
                        _stg(1)
                        mix(1, xm[1])
                        ps, pk = psum()
                        for k in range(KC):
                            MM(ps[:64, :T], w1b[:, k, :], xm[1][:, k, :], k == 0, k == KC - 1, ['w1b', 'xm'], [pk])
                        ACT(lo1[0:64, :], ps[:64, :T], AF.Tanh, [pk], ['lo1'])
                        for d in range(KC):
                            ps, pk = psum()
                            MM(ps[:, :T], w2b[0:64, d * 128:(d + 1) * 128], lo1[0:64, :], True, True, ['w2b', 'lo1'], [pk])
                            ACT(nlw[:, d, :], ps[:, :T], AF.Exp, [pk, 'negw0'], [('nlw', d)], bias=negw0[:, d:d + 1], scale=-1.0)
                            ACT(nlw[:, d, :], nlw[:, d, :], AF.Ln, [('nlw', d), 'onec'], [('nlw', d)], bias=onec[:, 0:1])
                            ACT(nlw[:, d, :], nlw[:, d, :], AF.Exp, [('nlw', d), 'mhalf'], [('nlw', d)], bias=mhalf[:, 0:1], scale=-1.0)
                        mix(4, xm[0])
                        ps, pk = psum()
                        for k in range(KC):
                            MM(ps[:64, :T], a1b[:, k, :], xm[0][:, k, :], k == 0, k == KC - 1, ['a1b', 'xm'], [pk])
                        CP('dve', lo2[0:64, :], ps[:64, :T], [pk], ['lo2'])
                        for d in range(KC):
                            ps, pk = psum()
                            MM(ps[:, :T], a2b[0:64, d * 128:(d + 1) * 128], lo2[0:64, :], True, True, ['a2b', 'lo2'], [pk])
                            ACT(a_[:, d, :], ps[:, :T], AF.Sigmoid, [pk, 'vecs'], [('a_', d)], bias=vcol('r_a0', j * 8 + d))
                        mix(5, xm[1])
                        ps, pk = psum()
                        for k in range(KC):
                            MM(ps[:, :T], g1b[:, k, 0:128], xm[1][:, k, :], k == 0, k == KC - 1, ['g1b', 'xm'], [pk])
                        ACT(lo1, ps[:, :T], AF.Sigmoid, [pk], ['lo1'])
                        ps, pk = psum()
                        for k in range(KC):
                            MM(ps[:32, :T], g1b[:, k, 128:160], xm[1][:, k, :], k == 0, k == KC - 1, ['g1b', 'xm'], [pk])
                        ACT(lo2[0:32, :], ps[:32, :T], AF.Sigmoid, [pk], ['lo2'])
                        for d in range(KC):
                            ps, pk = psum()
                            MM(ps[:, :T], g2a[:, d * 128:(d + 1) * 128], lo1, True, False, ['g2a', 'lo1'], [pk])
                            MM(ps[:, :T], g2b[0:32, d * 128:(d + 1) * 128], lo2[0:32, :], False, True, ['g2b', 'lo2'], [pk])
                            ACT(g_[:, d, :], ps[:, :T], AF.Copy, [pk], [('g_', d)])
                        _stg(2)
                        kkv = vecs[:, VOFF['r_k_k'] + j * 8:VOFF['r_k_k'] + j * 8 + 8].rearrange("p (k o) -> p k o", o=1).broadcast_to([128, KC, T])
                        TT('dve', kk, k_, kkv, ALU.mult, ['k_', 'vecs'], ['kk'])
                        TT('dve', tA, kk, kk, ALU.mult, ['kk'], ['tA'])
                        for d in range(KC):
                            ps, pk = psum()
                            MM(ps[:, :T], blk2, tA[:, d, :], True, True, ['cst', 'tA'], [pk])
                            TS('dve', tB[:, d, :], ps[:, :T], c24[:, 0:1], None, ALU.max, None, [pk, 'c24'], ['tB'])
                        ACT(tB, tB, AF.Ln, ['tB'], ['tB'])
                        ACT(tB, tB, AF.Exp, ['tB'], ['tB'], scale=-0.5)
                        TT('dve', kk, kk, tB, ALU.mult, ['kk', 'tB'], ['kk'])
                        kav = vecs[:, VOFF['r_k_a'] + j * 8:VOFF['r_k_a'] + j * 8 + 8].rearrange("p (k o) -> p k o", o=1).broadcast_to([128, KC, T])
                        omv = omka.rearrange("p (k o) -> p k o", o=1).broadcast_to([128, KC, T])
                        TT('dve', tA, a_, kav, ALU.mult, ['a_', 'vecs'], ['tA'])
                        TT('dve', tA, tA, omv, ALU.add, ['tA', 'omka'], ['tA'])
                        TT('dve', k_, k_, tA, ALU.mult, ['k_', 'tA'], ['k_'])
                        TT('dve', tA, kk, a_, ALU.mult, ['kk', 'a_'], ['tA'])
                        rkv = vecs[:, VOFF['r_r_k'] + j * 8:VOFF['r_r_k'] + j * 8 + 8].rearrange("p (k o) -> p k o", o=1).broadcast_to([128, KC, T])
                        TT('dve', np_, r_, k_, ALU.mult, ['r_', 'k_'], ['np_'])
                        TT('dve', np_, np_, rkv, ALU.mult, ['np_', 'vecs'], ['np_'])
                        for d in range(KC):
                            ps, pk = psum()
                            MM(ps[:, :T], blk2, np_[:, d, :], True, True, ['cst', 'np_'], [pk])
                            TT('dve', tB[:, d, :], ps[:, :T], v_[:, d, :], ALU.mult, [pk, 'v_'], ['tB'])
                        _stg(3)
                        for d in range(KC):
                            S.add('dve', (lambda e, o=np_[:, d, :], d1=nlw[:, d, :]: e.tensor_tensor_scan(out=o, data0=rst[:, :T], data1=d1, initial=0.0, op0=ALU.mult, op1=ALU.add)),
                                  reads=['nlw', 'cst', 'np_'], writes=['np_'])
                        npv = np_.rearrange("p k (b t) -> p k b t", b=nb)
                        npE = npv[:, :, :, lt - 1:lt].broadcast_to([128, KC, nb, lt])
                        ACT(eC, npv[:, :, :, lt - 1:lt].rearrange("p k b o -> p k (b o)"), AF.Exp, ['np_'], ['eC'], scale=-1.0)
                        TT('dve', uf, np_, nlw, ALU.subtract, ['np_', 'nlw'], ['uf'])
                        ACT(uf, uf, AF.Exp, ['uf'], ['uf'], scale=-1.0)
                        STT('dve', ARt[:, :, 0, :], kk, -1.0, uf, ALU.mult, ALU.mult, ['kk', 'uf'], ['ARt'])
                        ACT(uf, np_, AF.Exp, ['np_', 'ARt'], ['uf'], scale=-1.0)
                        TT('dve', ARt[:, :, 1, :], r_, uf, ALU.mult, ['r_', 'uf'], ['ARt'])
                        ACT(uf, np_, AF.Exp, ['np_', 'ARt'], ['uf'])
                        TT('dve', BT_, tA, uf, ALU.mult, ['tA', 'uf'], ['BT_'])
                        TT('dve', KT_, k_, uf, ALU.mult, ['k_', 'uf'], ['KT_'])
                        TT('dve', uf.rearrange("p k (b t) -> p k b t", b=nb), npv, npE, ALU.subtract, ['np_', 'BT_', 'KT_'], ['uf'])
                        ACT(uf, uf, AF.Exp, ['uf'], ['uf'])
                        TT('dve', BH_, tA, uf, ALU.mult, ['tA', 'uf'], ['BH_'])
                        TT('dve', KH_, k_, uf, ALU.mult, ['k_', 'uf'], ['KH_'])
                        CP('pool', V_, v_, ['v_'], ['xm'])
                        _stg(4)
                        for (src, dst, nm) in ((V_, Vtok, 'Vtok'), (BH_, BHtok, 'BHtok'), (KH_, KHtok, 'KHtok')):
                            ps, pk = psum()
                            psb = ps.bitcast(BF16)
                            for d in range(KC):
                                TR(psb[:L, d * 128:(d + 1) * 128], src[:, d, :], ident_b, [nm[:-3] + '_' if nm != 'Vtok' else 'xm', 'ident_b'], [pk])
                            CP('dve', dst[:L].rearrange("p k n -> p (k n)"), psb[:L, :], [pk], [nm])
                        _stg(5)
                        for hg in range(16 // NH):
                            heads = [(hg * NH + q) for q in range(NH)]
                            if CFG.get('rheads') is not None and heads[0] not in CFG['rheads']:
                                continue
                            HD = [(hh // 2, hh % 2) for hh in heads]
                            for q, (d, a) in enumerate(HD):
                                sl = slice(a * 64, (a + 1) * 64)
                                AR = ARt[sl, d].rearrange("p c t -> p (c t)")
                                ps, pk = psum()
                                MM(ps[:L, 0:2 * L], BT_[sl, d, :], AR, True, True, ['BT_', 'ARt'], [pk])
                                TT('dve', ABt[q][:L, :], ps[:L, 0:2 * L], mAB, ALU.mult, [pk, 'cst'], [('ABt', q)])
                                ps, pk = psum()
                                MM(ps[:L, 0:2 * L], KT_[sl, d, :], AR, True, True, ['KT_', 'ARt'], [pk])
                                TT('dve', AKt[q][:L, :], ps[:L, 0:2 * L], mAB, ALU.mult, [pk, 'cst'], [('AKt', q)])
                                ps, pk = psum()
                                MM(ps[:L, 0:L], ARt[sl, d, 0, :], BT_[sl, d, :], True, True, ['BT_', 'ARt'], [pk])
                                TT('dve', MMb[q][0][:L, 0:L], ps[:L, 0:L], mlow, ALU.mult, [pk, 'cst'], [('MMb', q, 0)])
                                CP('pool', MMb[q][0][:L, L:2 * L], ABt[q][:L, 0:L], [('ABt', q)], [('MMb', q, 0)])
                            _stg(6)
                            for q, (d, a) in enumerate(HD):
                                sl = slice(a * 64, (a + 1) * 64)
                                CP('dve', S0bh[q][sl], S0T[sl, d], [('S0T', d)], [('S0bh', q)])
                                ps, pk = psum()
                                if smp:
                                    TT('dve', ATm[q][sl], ARt[sl, d, 0, :].rearrange("p (o t) -> p o t", o=1).broadcast_to([64, 16, 64]),
                                       bmT_b[sl].rearrange("p (b t) -> p b t", b=16), ALU.mult, ['ARt', 'bmT_b'], [('ATm', q)])
                                    TT('dve', RTm[q][sl], ARt[sl, d, 1, :].rearrange("p (o t) -> p o t", o=1).broadcast_to([64, 16, 64]),
                                       bmT_b[sl].rearrange("p (b t) -> p b t", b=16), ALU.mult, ['ARt', 'bmT_b'], [('RTm', q)])
                                    for b in range(16):
                                        MM(ps[:L, 0:64], ATm[q][sl, b, :], S0bh[q][sl, b, :], b == 0, False, [('ATm', q), ('S0bh', q)], [pk])
                                else:
                                    MM(ps[:L, 0:64], ARt[sl, d, 0, :], S0bh[q][sl, 0, :], True, False, ['ARt', ('S0bh', q)], [pk])
                                MM(ps[:L, 0:64], AKt[q][:, 0:L], Vtok[:, d, sl], False, True, [('AKt', q), 'Vtok'], [pk])
                                CP('dve', X32[q][:L, :], ps[:L, 0:64], [pk], [('X32', q)])
                                CP('pool', Xb[q][:L, :], X32[q][:L, :], [('X32', q)], [('Xb', q)])
                            _stg(7)
                            for lev in range(nlev):
                                cur = lev % 2
                                for q, (d, a) in enumerate(HD):
                                    M_ = MMb[q][cur][:L, 0:L]
                                    Mt_ = MMb[q][cur][:L, L:2 * L]
                                    ps, pk = psum()
                                    MM(ps[:L, 0:64], Mt_, Xb[q][:L, :], True, True, [('MMb', q, cur), ('Xb', q)], [pk])
                                    TT('dve', X32[q][:L, :], X32[q][:L, :], ps[:L, 0:64], ALU.add, [pk, ('X32', q)], [('X32', q)])
                                    CP('pool', Xb[q][:L, :], X32[q][:L, :], [('X32', q)], [('Xb', q)])
                                    if lev < nlev - 1:
                                        ps2, pk2 = psum()
                                        MM(ps2[:L, 0:L], Mt_, M_, True, True, [('MMb', q, cur)], [pk2])
                                        MM(ps2[:L, L:2 * L], M_, Mt_, True, True, [('MMb', q, cur)], [pk2])
                                        ACT(MMb[q][1 - cur][:L, :], ps2[:L, 0:2 * L], AF.Copy, [pk2], [('MMb', q, 1 - cur)])
                            _stg(8)
                            for q, (d, a) in enumerate(HD):
                                sl = slice(a * 64, (a + 1) * 64)
                                ps, pk = psum()
                                if smp:
                                    for b in range(16):
                                        MM(ps[:L, 0:64], RTm[q][sl, b, :], S0bh[q][sl, b, :], b == 0, False, [('RTm', q), ('S0bh', q)], [pk])
                                else:
                                    MM(ps[:L, 0:64], ARt[sl, d, 1, :], S0bh[q][sl, 0, :], True, False, ['ARt', ('S0bh', q)], [pk])
                                MM(ps[:L, 0:64], ABt[q][:, L:2 * L], Xb[q][:, :], False, False, [('ABt', q), ('Xb', q)], [pk])
                                MM(ps[:L, 0:64], AKt[q][:, L:2 * L], Vtok[:, d, sl], False, True, [('AKt', q), 'Vtok'], [pk])
                                ACT(Ytok[:L, d, sl], ps[:L, 0:64], AF.Copy, [pk], [('Ytok', d, a)])
                                _stg(8.2)
                                if smp:
                                    bmv = cst[:L, C_BM:C_BM + 16].rearrange("p (b o) -> p b o", o=1).broadcast_to([L, 16, 64])
                                    TT('pool', Wm[q][:L], Xb[q][:L, :].rearrange("p (o i) -> p o i", o=1).broadcast_to([L, 16, 64]), bmv, ALU.mult, [('Xb', q), 'cst'], [('Wm', q)])
                                    TT('pool', Vm[q][:L], Vtok[:L, d, sl].rearrange("p (o i) -> p o i", o=1).broadcast_to([L, 16, 64]), bmv, ALU.mult, ['Vtok', 'cst'], [('Vm', q)])
                                    _stg(8.4)
                                    S3 = S0T[sl, d]
                                    TT('dve', S3, S3, eC[sl, d, :].rearrange("p (b o) -> p b o", o=1).broadcast_to([64, 16, 64]), ALU.mult, [('S0T', d), 'eC'], [('S0T', d)])
                                    _stg(8.6)
                                    for b8 in range(2):
                                        ps, pk = psum()
                                        for bi in range(8):
                                            b = b8 * 8 + bi
                                            MM(ps[sl, bi * 64:(bi + 1) * 64], BHtok[:, d, sl], Wm[q][:, b, :], True, False, ['BHtok', ('Wm', q)], [pk])
                                            MM(ps[sl, bi * 64:(bi + 1) * 64], KHtok[:, d, sl], Vm[q][:, b, :], False, True, ['KHtok', ('Vm', q)], [pk])
                                        TT('dve', S0T[sl, d, b8 * 8:b8 * 8 + 8, :], S0T[sl, d, b8 * 8:b8 * 8 + 8, :], ps[sl, :].rearrange("p (b i) -> p b i", b=8), ALU.add,
                                           [pk, ('S0T', d)], [('S0T', d)])
                                else:
                                    ps, pk = psum()
                                    MM(ps[sl, 0:64], BHtok[:L, d, sl], Xb[q][:L, :], True, False, ['BHtok', ('Xb', q)], [pk])
                                    MM(ps[sl, 0:64], KHtok[:L, d, sl], Vtok[:L, d, sl], False, True, ['KHtok', 'Vtok'], [pk])
                                    STT('dve', S0T[sl, d, 0, :], S0T[sl, d, 0, :], eC[sl, d, 0:1], ps[sl, 0:64], ALU.mult, ALU.add, [pk, ('S0T', d), 'eC'], [('S0T', d)])
                        _stg(9)
                        Y3 = Ytok[:L].rearrange("p k (a i) -> p (k a) i", a=2)
                        S.add('dve', (lambda e, o=st1[:L, :]: e.tensor_reduce(out=o, in_=Y3, axis=AX.X, op=ALU.add)), reads=['Ytok'], writes=['st1'])
                        TT('pool', Ysq[:L], Ytok[:L], Ytok[:L], ALU.mult, ['Ytok'], [YSK])
                        S.add('dve', (lambda e, o=st2[:L, :]: e.tensor_reduce(out=o, in_=Ysq[:L].rearrange("p k (a i) -> p (k a) i", a=2), axis=AX.X, op=ALU.add)), reads=[YSK], writes=['st2'])
                        TS('dve', st1[:L, :], st1[:L, :], 1.0 / 64, None, ALU.mult, None, ['st1'], ['st1'])
                        TS('dve', st2[:L, :], st2[:L, :], 1.0 / 64, None, ALU.mult, None, ['st2'], ['st2'])
                        TT('dve', Ysq[:L, 0, 0:16], st1[:L, :], st1[:L, :], ALU.mult, ['st1', YSK], [YSK])
                        TT('dve', st2[:L, :], st2[:L, :], Ysq[:L, 0, 0:16], ALU.subtract, ['st2', YSK], ['st2'])
                        ACT(st2[:L, :], st2[:L, :], AF.Ln, ['st2', 'gnec'], ['st2'], bias=gnec[:L, 0:1])
                        ACT(st2[:L, :], st2[:L, :], AF.Exp, ['st2'], ['st2'], scale=-0.5)
                        TT('dve', Y3, Y3, st1[:L, :].rearrange("p (h o) -> p h o", o=1).broadcast_to([L, 16, 64]), ALU.subtract, ['Ytok', 'st1'], ['Ytok'])
                        TT('dve', Yn[:L].rearrange("p k (a i) -> p (k a) i", a=2), Y3, st2[:L, :].rearrange("p (h o) -> p h o", o=1).broadcast_to([L, 16, 64]), ALU.mult,
                           ['Ytok', 'st2'], ['Yn'])
                        for hf in range(2):
                            ps, pk = psum()
                            psb = ps.bitcast(BF16)
                            for dd in range(4):
                                d = hf * 4 + dd
                                TR(psb[:, dd * L:(dd + 1) * L], Yn[:L, d, :], ident_b[:L, :L], ['Yn', 'ident_b'], [pk])
                            for dd in range(4):
                                d = hf * 4 + dd
                                TS('dve', tA[:, d, :], psb[:, dd * L:(dd + 1) * L], vcol('r_gn_w', j * 8 + d), vcol('r_gn_b', j * 8 + d), ALU.mult, ALU.add, [pk, 'vecs'], ['tA'])
                        TT('dve', tA, tA, tB, ALU.add, ['tA', 'tB'], ['tA'])
                        TT('dve', yg, tA, g_, ALU.mult, ['tA', 'g_'], ['xm'])
                        proj('r_wo', yg, lambda d, ps, pk: TT('dve', h[:, d, t0:t0 + T], h[:, d, t0:t0 + T], ps[:, :T], ALU.add, [pk, 'h'], ['h']))
                except _Stop:
                    pass
                if CFG.get('rstage', 99) < 10:
                    S.barrier()
                    sb.reset(m)
                    return
                if smp:
                    DMA('sp', shift_s[j].rearrange("(k p) b -> p k b", p=128), sst, ['sst'], [('shift_s', j)])
                else:
                    DMA('sp', shift_p[j], carry, ['carry'], [('shift_p', j)])
                MSET('pool', bd, 0.0, ['bd'])
                for d in range(KC):
                    for b4 in range((nb + 3) // 4):
                        n4 = min(4, nb - b4 * 4)
                        for a in range(2):
                            sl = slice(a * 64, (a + 1) * 64)
                            CP('dve', bd[sl, 0:n4, a * 64:(a + 1) * 64], S0T[sl, d, b4 * 4:b4 * 4 + n4, :], [('S0T', d)], ['bd'])
                        ps, pk = psum()
                        for i4 in range(n4):
                            TR(ps[:, i4 * 128:(i4 + 1) * 128], bd[:, i4, :], ident_f, ['bd', 'cst'], [pk])
                        for a in range(2):
                            sl = slice(a * 64, (a + 1) * 64)
                            CP('dve', Ysq[sl, 0:n4, 0:64], ps[sl, :].rearrange("p (b n) -> p b n", b=4)[:, 0:n4, a * 64:(a + 1) * 64], [pk], [YSK])
                            if smp:
                                DMA('sp', wkv_s[j][b4 * 4:b4 * 4 + n4, 2 * d + a].rearrange("b i j -> i b j"), Ysq[sl, 0:n4, 0:64], [YSK], [('wkv_s', j, d, a, b4)])
                            else:
                                DMA('sp', wkv_p[j][2 * d + a], Ysq[sl, 0, 0:64], [YSK], [('wkv_p', j, d, a)])
                S.barrier()
                sb.reset(m)

            run(False)
            if CFG.get('rwkv_sample', True):
                run(True)


        for l in range(CFG['depth']):
            if CFG.get('dense', True):
                ffn(l, Wd_['ffn1_gate_up'], Wd_['ffn1_down'], 'norm_ffn1')
            if CFG['mixers']:
                if l % 2 == 0:
                    if CFG.get('mamba', True):
                        mamba(l)
                elif CFG.get('rwkv', False):
                    rwkv(l)
            if CFG.get('dense', True):
                ffn(l, Wd_['ffn2_gate_up'], Wd_['ffn2_down'], 'norm_ffn2')
                ple(l)
        final()
        S.add('sp', lambda e: e.nop(), reads=['yT', 'ssm_s', 'ssm_p', 'conv_s', 'conv_p', 'wkv_p', 'wkv_s', 'shift_p', 'shift_s'])
        S.emit()
    return nc


def pack_vecs(inp):
    cols = []
    for n in VECS:
        a = np.asarray(inp[n], dtype=np.float32)
        cols.append(np.ascontiguousarray(a.reshape(-1, 128).T))
    return np.ascontiguousarray(np.concatenate(cols, axis=1))


def kernel(**inp):
    inp = {k: np.asarray(v) for k, v in inp.items()}
    nc = build_program()
    vecs = pack_vecs(inp)
    consts = make_consts()
    hv = np.zeros((32, 4), np.float32)
    dbc = np.zeros((128, 64), np.float32)
    for j in range(2):
        hv[:, 2 * j] = inp['m_dt_bias'][j]
        hv[:, 2 * j + 1] = inp['m_A_log'][j]
        dbc[:, j * 32:(j + 1) * 32] = inp['m_D'][j][None, :]
    ncores = CFG['cores']
    in_maps = []
    for c in range(ncores):
        xs = inp['x_sample'][16 * c:16 * c + 16].reshape(NS, D)
        xc = np.concatenate([inp['x_prompt'][c], xs], axis=0)
        ps_ = inp['p_sample'][:, 16 * c:16 * c + 16].reshape(4, NS, 256)
        pc = np.concatenate([inp['p_prompt'][:, c], ps_], axis=1)
        m = {'xT': np.ascontiguousarray(xc.T), 'pT': np.ascontiguousarray(pc.transpose(0, 2, 1)), 'vecs': vecs,
             'consts': consts, 'hv': hv, 'dbc': dbc,
             'ssm_in': np.ascontiguousarray(inp['state_ssm'][:, 16 * c:16 * c + 16]),
             'conv_in': np.ascontiguousarray(inp['state_conv'][:, 16 * c:16 * c + 16].transpose(0, 3, 1, 2)),
             'wkv_in': np.ascontiguousarray(inp['state_wkv'][:, 16 * c:16 * c + 16]),
             'shift_in': np.ascontiguousarray(inp['state_shift'][:, 16 * c:16 * c + 16].transpose(0, 2, 1))}
        for n in WSHAPES:
            m[n] = np.ascontiguousarray(inp[n], dtype=np.float32)
        in_maps.append(m)
    res = run_bass_kernel_spmd(nc, in_maps, core_ids=list(range(ncores)))
    B = 8
    yp = np.zeros((B, NP, D), np.float32)
    ys = np.zeros((128, 4, D), np.float32)
    ssm_p = np.zeros((2, B, 32, 64, 128), np.float32)
    conv_p = np.zeros((2, B, 3, 4096), np.float32)
    wkv_p = np.zeros((2, B, 16, 64, 64), np.float32)
    shift_p = np.zeros((2, B, D), np.float32)
    ssm_s = np.zeros((2, 128, 32, 64, 128), np.float32)
    conv_s = np.zeros((2, 128, 3, 4096), np.float32)
    wkv_s = np.zeros((2, 128, 16, 64, 64), np.float32)
    shift_s = np.zeros((2, 128, D), np.float32)
    for c in range(ncores):
        r = res.results[c]
        y = r['yT'].T
        yp[c] = y[:NP]
        ys[16 * c:16 * c + 16] = y[NP:].reshape(16, 4, D)
        ssm_p[:, c] = r['ssm_p']
        conv_p[:, c] = r['conv_p'].transpose(0, 2, 1)
        wkv_p[:, c] = r['wkv_p']
        shift_p[:, c] = r['shift_p'].transpose(0, 2, 1).reshape(2, D)
        ssm_s[:, 16 * c:16 * c + 16] = r['ssm_s']
        conv_s[:, 16 * c:16 * c + 16] = r['conv_s'].transpose(0, 2, 3, 1)
        wkv_s[:, 16 * c:16 * c + 16] = r['wkv_s']
        shift_s[:, 16 * c:16 * c + 16] = r['shift_s'].transpose(0, 2, 1)
    return (yp, ys, ssm_p, conv_p, wkv_p, shift_p, ssm_s, conv_s, wkv_s, shift_s)
```

```python
import contextlib
import numpy as np
import concourse.bass as bass
import concourse.mybir as mybir
from concourse.bass_utils import run_bass_kernel_spmd

F32 = mybir.dt.float32
BF16 = mybir.dt.bfloat16
AF = mybir.ActivationFunctionType
ALU = mybir.AluOpType
AX = mybir.AxisListType

ENGS = ['pe', 'act', 'dve', 'pool', 'sp']


class Op:
    __slots__ = ('eng', 'fn', 'deps', 'signal', 'is_dma', 'sem', 'val', 'prewait', 'barriered')

    def __init__(self, eng, fn, is_dma):
        self.eng = eng
        self.fn = fn
        self.deps = []
        self.signal = False
        self.is_dma = is_dma
        self.sem = None
        self.val = 0
        self.prewait = None
        self.barriered = False


class Sched:
    def __init__(self, nc, n_dma_sems=20):
        self.nc = nc
        self.ops = {e: [] for e in ENGS}
        self.res = {}
        self.n_dma_sems = n_dma_sems

    def _states(self, key):
        if isinstance(key, tuple):
            name, sub = key[0], key[1:]
            if len(sub) == 0:
                sub = None
        else:
            name, sub = key, None
        d = self.res.setdefault(name, {})
        if sub is None:
            if None not in d:
                d[None] = [None, []]
            return list(d.values()), d[None], True
        if sub not in d:
            d[sub] = [None, []]
        sts = [d[sub]]
        if None in d:
            sts.append(d[None])
        return sts, d[sub], False

    def add(self, eng, fn, reads=(), writes=(), dma=False):
        op = Op(eng, fn, dma)
        deps = []
        for k in reads:
            sts, own, whole = self._states(k)
            for st in sts:
                if st[0] is not None:
                    deps.append(st[0])
        for k in writes:
            sts, own, whole = self._states(k)
            for st in sts:
                if st[0] is not None:
                    deps.append(st[0])
                deps.extend(st[1])
        for k in reads:
            sts, own, whole = self._states(k)
            own[1].append(op)
        for k in writes:
            sts, own, whole = self._states(k)
            if whole:
                for st in sts:
                    st[0] = None
                    st[1] = []
            own[0] = op
            own[1] = []
        seen = set()
        for d in deps:
            if d is op or id(d) in seen:
                continue
            seen.add(id(d))
            if d.eng == eng and eng == 'pe' and not d.is_dma and not dma:
                continue
            op.deps.append(d)
            d.signal = True
        self.ops[eng].append(op)
        return op

    def barrier(self):
        last = []
        for e in ENGS:
            got = False
            for o in reversed(self.ops[e]):
                if o.barriered:
                    break
                if o.is_dma:
                    last.append(o)
                elif not got:
                    last.append(o)
                    got = True
        b = Op('sp', lambda e: e.nop(), False)
        for d in last:
            b.deps.append(d)
            d.signal = True
        for e in ENGS:
            for o in reversed(self.ops[e]):
                if o.barriered:
                    break
                o.barriered = True
        b.barriered = True
        self.ops['sp'].append(b)
        for e in ENGS:
            if e == 'sp':
                continue
            o = Op(e, None, False)
            o.barriered = True
            o.deps.append(b)
            b.signal = True
            self.ops[e].append(o)
        self.res = {}

    def emit(self):
        nc = self.nc
        with contextlib.ExitStack() as es:
            csem = {e: es.enter_context(nc.semaphore('c_' + e)) for e in ENGS}
            dsems = {e: [es.enter_context(nc.semaphore('d_%s_%d' % (e, i)))
                         for i in range(self.n_dma_sems)] for e in ('sp', 'pool', 'act')}
            for e in ENGS:
                cnt = 0
                dcnt = [0] * self.n_dma_sems
                rr = 0
                for op in self.ops[e]:
                    if op.is_dma:
                        s = rr % self.n_dma_sems
                        rr += 1
                        if dcnt[s] > 0:
                            op.prewait = (dsems[e][s], dcnt[s])
                        dcnt[s] += 16
                        op.sem = dsems[e][s]
                        op.val = dcnt[s]
                    elif op.signal:
                        cnt += 1
                        op.sem = csem[e]
                        op.val = cnt
            engobj = {'pe': 'tensor', 'act': 'scalar', 'dve': 'vector', 'pool': 'gpsimd', 'sp': 'sync'}

            def run(e, eng):
                seen = {}
                for op in self.ops[e]:
                    waits = []
                    if op.prewait is not None:
                        waits.append(op.prewait)
                    for d in op.deps:
                        waits.append((d.sem, d.val))
                    mx = {}
                    for (s, v) in waits:
                        k = id(s)
                        if v > mx.get(k, (None, 0))[1]:
                            mx[k] = (s, v)
                    for k, (s, v) in mx.items():
                        if seen.get(k, 0) >= v:
                            continue
                        seen[k] = v
                        eng.wait_ge(s, v)
                    if op.fn is None:
                        continue
                    ins = op.fn(eng)
                    if op.is_dma:
                        ins.then_inc(op.sem, 16)
                    elif op.signal:
                        ins.then_inc(op.sem, 1)

            with nc.Block() as block:
                for e in ENGS:
                    if not self.ops[e]:
                        continue
                    getattr(block, engobj[e])(lambda eng, e=e: run(e, eng))


class SB:
    def __init__(self, big, nbytes):
        self.big = big
        self.nbytes = nbytes
        self.off = 0
        self.views = {}

    def view(self, dtype):
        if dtype not in self.views:
            self.views[dtype] = self.big.bitcast(dtype) if dtype != F32 else self.big
        return self.views[dtype]

    def alloc(self, cols, dtype, parts=128):
        sz = mybir.dt.size(dtype)
        nb = (cols * sz + 63) // 64 * 64
        assert self.off + nb <= self.nbytes, ('SBUF overflow', self.off, nb, self.nbytes)
        o = self.off // sz
        self.off += nb
        return self.view(dtype)[0:parts, o:o + cols]

    def mark(self):
        return self.off

    def reset(self, m):
        self.off = m


D = 1024
KC = 8
NP = 2048
NS = 64
NT = NP + NS
TILES = [(0, 512), (512, 512), (1024, 512), (1536, 512), (2048, 64)]
DFF = 2816
FGROUPS = [(0, 4), (4, 4), (8, 4), (12, 4), (16, 3), (19, 3)]
DEPTH = 4
EPS = 1e-6

WSHAPES = {
    'ffn1_gate_up': (4, 1024, 5632), 'ffn1_down': (4, 2816, 1024),
    'ffn2_gate_up': (4, 1024, 5632), 'ffn2_down': (4, 2816, 1024),
    'ple_in': (4, 256, 1024), 'ple_gate': (4, 1024, 1024),
    'm_in_proj': (2, 1024, 6176), 'm_out_proj': (2, 2048, 1024),
    'r_wr': (2, 1024, 1024), 'r_wk': (2, 1024, 1024), 'r_wv': (2, 1024, 1024), 'r_wo': (2, 1024, 1024),
    'r_w1': (2, 1024, 64), 'r_w2': (2, 64, 1024), 'r_a1': (2, 1024, 64), 'r_a2': (2, 64, 1024),
    'r_g1': (2, 1024, 160), 'r_g2': (2, 160, 1024), 'r_v1': (1, 1024, 32), 'r_v2': (1, 32, 1024),
}
VECS = ['norm_ffn1', 'norm_mix', 'norm_ffn2', 'norm_ple', 'norm_final', 'm_conv_w', 'm_conv_b', 'm_norm',
        'r_mu', 'r_w0', 'r_a0', 'r_k_k', 'r_k_a', 'r_r_k', 'r_gn_w', 'r_gn_b', 'r_v0']
VSHAPES = {'norm_ffn1': (4, 1024), 'norm_mix': (4, 1024), 'norm_ffn2': (4, 1024), 'norm_ple': (4, 1024),
           'norm_final': (1024,), 'm_conv_w': (2, 4, 4096), 'm_conv_b': (2, 4096), 'm_norm': (2, 2048),
           'r_mu': (2, 6, 1024), 'r_w0': (2, 1024), 'r_a0': (2, 1024), 'r_k_k': (2, 1024), 'r_k_a': (2, 1024),
           'r_r_k': (2, 1024), 'r_gn_w': (2, 1024), 'r_gn_b': (2, 1024), 'r_v0': (1, 1024)}
VOFF = {}
_o = 0
for _n in VECS:
    VOFF[_n] = _o
    _o += int(np.prod(VSHAPES[_n])) // 128
NV = _o

C_ID, C_TRIP, C_ONES, C_TRIS, C_BLKS, C_BM, C_BMT = 0, 128, 256, 384, 448, 512, 528
C_STRIP = C_BMT + 1024
C_LOWP = C_STRIP + 128
C_STRIS = C_LOWP + 128
C_LOWS = C_STRIS + 64
C_MABP = C_LOWS + 64
C_MABS = C_MABP + 256
C_RSTP = C_MABS + 128
C_RSTS = C_RSTP + 128
C_BLK2 = C_RSTS + 64
NCC = C_BLK2 + 128

CFG = {'mixers': True, 'depth': DEPTH, 'cores': 8, 'rwkv': True}


class _Stop(Exception):
    pass


def _stg(n):
    if CFG.get('rstage', 99) < n:
        raise _Stop()


def make_consts():
    c = np.zeros((128, NCC), np.float32)
    i = np.arange(128)
    c[:, C_ID:C_ID + 128] = np.eye(128)
    c[:, C_TRIP:C_TRIP + 128] = (i[:, None] <= i[None, :])
    c[:, C_ONES:C_ONES + 128] = 1.0
    j = np.arange(64)
    same = (j[:, None] // 4) == (j[None, :] // 4)
    c[:64, C_TRIS:C_TRIS + 64] = same & (j[:, None] <= j[None, :])
    c[:64, C_BLKS:C_BLKS + 64] = same
    c[:64, C_BM:C_BM + 16] = (j[:, None] // 4) == np.arange(16)[None, :]
    bmt = ((j[None, :] // 4) == np.arange(16)[:, None]).astype(np.float32).reshape(1, 1024)
    c[:, C_BMT:C_BMT + 1024] = bmt
    c[:, C_STRIP:C_STRIP + 128] = (i[:, None] < i[None, :])
    c[:, C_LOWP:C_LOWP + 128] = (i[None, :] < i[:, None])
    c[:64, C_STRIS:C_STRIS + 64] = same & (j[:, None] < j[None, :])
    c[:64, C_LOWS:C_LOWS + 64] = same & (j[None, :] < j[:, None])
    c[:, C_MABP:C_MABP + 128] = c[:, C_STRIP:C_STRIP + 128]
    c[:, C_MABP + 128:C_MABP + 256] = c[:, C_TRIP:C_TRIP + 128]
    c[:64, C_MABS:C_MABS + 64] = c[:64, C_STRIS:C_STRIS + 64]
    c[:64, C_MABS + 64:C_MABS + 128] = c[:64, C_TRIS:C_TRIS + 64]
    c[:, C_RSTP:C_RSTP + 128] = 1.0
    c[:, C_RSTP] = 0.0
    c[:, C_RSTS:C_RSTS + 64] = (np.arange(64) % 4 != 0)[None, :]
    c[:, C_BLK2:C_BLK2 + 128] = (i[:, None] // 64) == (i[None, :] // 64)
    return c


def build_program():
    nc = bass.Bass("TRN2", target_bir_lowering=False)

    def din(name, shape):
        return nc.dram_tensor(name, list(shape), F32, kind="ExternalInput").ap()

    def dout(name, shape):
        return nc.dram_tensor(name, list(shape), F32, kind="ExternalOutput").ap()

    xT = din('xT', [D, NT])
    pT = din('pT', [4, 256, NT])
    vecs_d = din('vecs', [128, NV])
    Wd_ = {n: din(n, s) for n, s in WSHAPES.items()}
    yT = dout('yT', [D, NT])
    consts_d = din('consts', [128, NCC])
    hv_d = din('hv', [32, 4])
    dbc_d = din('dbc', [128, 64])
    ssm_in = din('ssm_in', [2, 16, 32, 64, 128])
    conv_in = din('conv_in', [2, 4096, 16, 3])
    wkv_in = din('wkv_in', [2, 16, 16, 64, 64])
    shift_in = din('shift_in', [2, 1024, 16])
    ssm_p = dout('ssm_p', [2, 32, 64, 128])
    conv_p = dout('conv_p', [2, 4096, 3])
    ssm_s = dout('ssm_s', [2, 16, 32, 64, 128])
    conv_s = dout('conv_s', [2, 4096, 16, 3])
    wkv_p = dout('wkv_p', [2, 16, 64, 64])
    shift_p = dout('shift_p', [2, 128, 8])
    wkv_s = dout('wkv_s', [2, 16, 16, 64, 64])
    shift_s = dout('shift_s', [2, 1024, 16])
    vfirst_d = dout('vfirst', [D, NT])

    with contextlib.ExitStack() as es:
        NB = 206 * 1024
        big = es.enter_context(nc.sbuf_tensor("big", [128, NB // 4], F32))
        psl = [es.enter_context(nc.psum_tensor("ps%d" % i, [128, 512], F32)) for i in range(8)]
        sb = SB(big, NB)
        S = Sched(nc)
        pctr = [0]

        def psum():
            i = pctr[0] % 8
            pctr[0] += 1
            return psl[i], ('ps', i)

        def MM(out, lhsT, rhs, start, stop, r, w):
            S.add('pe', lambda e: e.matmul(out, lhsT=lhsT, rhs=rhs, start=start, stop=stop), reads=r, writes=w)

        def TR(out, in_, ident, r, w):
            S.add('pe', lambda e: e.transpose(out, in_, ident), reads=r, writes=w)

        def ACT(out, in_, func, r, w, bias=None, scale=1.0):
            if bias is None:
                S.add('act', lambda e: e.activation(out=out, in_=in_, func=func, scale=scale), reads=r, writes=w)
            else:
                S.add('act', lambda e: e.activation(out=out, in_=in_, func=func, bias=bias, scale=scale), reads=r, writes=w)

        def TT(eng, out, in0, in1, op, r, w):
            S.add(eng, lambda e: e.tensor_tensor(out=out, in0=in0, in1=in1, op=op), reads=r, writes=w)

        def TS(eng, out, in0, s1, s2, op0, op1, r, w):
            if s2 is None:
                S.add(eng, lambda e: e.tensor_scalar(out=out, in0=in0, scalar1=s1, scalar2=None, op0=op0), reads=r, writes=w)
            else:
                S.add(eng, lambda e: e.tensor_scalar(out=out, in0=in0, scalar1=s1, scalar2=s2, op0=op0, op1=op1), reads=r, writes=w)

        def STT(eng, out, in0, scalar, in1, op0, op1, r, w):
            S.add(eng, lambda e: e.scalar_tensor_tensor(out=out, in0=in0, scalar=scalar, in1=in1, op0=op0, op1=op1), reads=r, writes=w)

        def CP(eng, out, in_, r, w):
            S.add(eng, lambda e: e.tensor_copy(out=out, in_=in_), reads=r, writes=w)

        def MSET(eng, ap, val, w):
            S.add(eng, lambda e: e.memset(ap, val), writes=w)

        def DMA(eng, out, in_, r, w):
            S.add(eng, lambda e: e.dma_start(out=out, in_=in_), reads=r, writes=w, dma=True)

        h = sb.alloc(KC * NT, F32).rearrange("p (k t) -> p k t", k=KC)
        vecs = sb.alloc(NV, F32)
        ones_bf = sb.alloc(128, BF16)
        epsc = sb.alloc(1, F32)
        cst = sb.alloc(NCC, F32)
        hv = sb.alloc(4, F32, parts=32)
        dbc = sb.alloc(64, F32)
        onec = sb.alloc(1, F32)
        eps5c = sb.alloc(1, F32)
        ident_b = sb.alloc(128, BF16)
        trip_b = sb.alloc(128, BF16)
        tris_b = sb.alloc(64, BF16)
        bmT_b = sb.alloc(1024, BF16)
        DMA('sp', vecs, vecs_d, [], ['vecs'])
        DMA('sp', cst, consts_d, [], ['cst'])
        DMA('sp', hv, hv_d, [], ['hv'])
        DMA('sp', dbc, dbc_d, [], ['dbc'])
        MSET('dve', onec, 1.0, ['onec'])
        MSET('dve', eps5c, 1e-5, ['eps5c'])
        CP('dve', ident_b, cst[:, C_ID:C_ID + 128], ['cst'], ['ident_b'])
        CP('dve', trip_b, cst[:, C_TRIP:C_TRIP + 128], ['cst'], ['trip_b'])
        CP('dve', tris_b, cst[:, C_TRIS:C_TRIS + 64], ['cst'], ['tris_b'])
        CP('dve', bmT_b, cst[:, C_BMT:C_BMT + 1024], ['cst'], ['bmT_b'])
        ident_f = cst[:, C_ID:C_ID + 128]
        DMA('sp', h, xT.rearrange("(k p) t -> p k t", p=128), [], ['h'])
        MSET('dve', ones_bf, 1.0, ['ones_bf'])
        MSET('dve', epsc, EPS, ['epsc'])

        def vcol(name, idx):
            o = VOFF[name] + idx
            return vecs[:, o:o + 1]

        scr0 = sb.mark()
        S.barrier()

        def rmsnorm(t0, T, gname, gbase, out, okey, sq, rs, ti, hkey=None, sqkey=('sq',)):
            hk = hkey if hkey is not None else ('h', ti)
            ACT(sq[:, :, :T], h[:, :, t0:t0 + T], AF.Square, [hk], [sqkey])
            ps, pk = psum()
            for k in range(KC):
                MM(ps[:, :T], ones_bf, sq[:, k, :T], k == 0, k == KC - 1, [sqkey, 'ones_bf'], [pk])
            ACT(rs[:, :T], ps[:, :T], AF.Ln, [pk, 'epsc'], [('rs',)], bias=epsc[:, 0:1], scale=1.0 / D)
            ACT(rs[:, :T], rs[:, :T], AF.Exp, [('rs',)], [('rs',)], scale=-0.5)
            for k in range(KC):
                STT('dve', out[:, k, :T], h[:, k, t0:t0 + T], vcol(gname, gbase + k), rs[:, :T], ALU.mult, ALU.mult,
                    [hk, ('rs',), 'vecs'], [okey])

        def ffn(l, wgu_d, wd_d, gname):
            m = sb.mark()
            xn = sb.alloc(KC * NT, BF16).rearrange("p (k t) -> p k t", k=KC)
            act = sb.alloc(4 * NT, BF16).rearrange("p (f t) -> p f t", f=4)
            wgu = [sb.alloc(KC * 2 * 512, BF16).rearrange("p (k g n) -> p k g n", k=KC, g=2) for _ in range(2)]
            wdb = [sb.alloc(4 * 1024, BF16).rearrange("p (f n) -> p f n", f=4) for _ in range(2)]
            sq = sb.alloc(KC * 512, BF16).rearrange("p (k t) -> p k t", k=KC)
            rs = sb.alloc(512, F32)
            sg = [sb.alloc(512, F32) for _ in range(2)]
            for ti, (t0, T) in enumerate(TILES):
                rmsnorm(t0, T, gname, l * KC, xn[:, :, t0:t0 + T], ('xn', ti), sq, rs, ti)
            wg_v = wgu_d[l].rearrange("(k p) n -> p k n", p=128)
            wd_v = wd_d[l].rearrange("(f p) n -> p f n", p=128)
            cnt = 0
            for gi, (f0, nf) in enumerate(FGROUPS):
                wb = wgu[gi % 2]
                wdd = wdb[gi % 2]
                DMA('pool', wb[:, :, 0, :nf * 128], wg_v[:, :, f0 * 128:(f0 + nf) * 128], [], [('wgu', gi % 2)])
                DMA('pool', wb[:, :, 1, :nf * 128], wg_v[:, :, DFF + f0 * 128:DFF + (f0 + nf) * 128], [], [('wgu', gi % 2)])
                DMA('pool', wdd[:, :nf, :], wd_v[:, f0:f0 + nf, :], [], [('wd', gi % 2)])
                for ti, (t0, T) in enumerate(TILES):
                    for f in range(nf):
                        pg, pgk = psum()
                        pu, puk = psum()
                        for k in range(KC):
                            MM(pg[:, :T], wb[:, k, 0, f * 128:(f + 1) * 128], xn[:, k, t0:t0 + T], k == 0, k == KC - 1,
                               [('wgu', gi % 2), ('xn', ti)], [pgk])
                        for k in range(KC):
                            MM(pu[:, :T], wb[:, k, 1, f * 128:(f + 1) * 128], xn[:, k, t0:t0 + T], k == 0, k == KC - 1,
                               [('wgu', gi % 2), ('xn', ti)], [puk])
                        sgb = sg[cnt % 2]
                        sk = ('sg', cnt % 2)
                        cnt += 1
                        ACT(sgb[:, :T], pg[:, :T], AF.Silu, [pgk], [sk])
                        TT('dve', act[:, f, t0:t0 + T], sgb[:, :T], pu[:, :T], ALU.mult, [sk, puk], [('act', f, ti)])
                for ti, (t0, T) in enumerate(TILES):
                    for d in range(KC):
                        po, pok = psum()
                        for f in range(nf):
                            MM(po[:, :T], wdd[:, f, d * 128:(d + 1) * 128], act[:, f, t0:t0 + T], f == 0, f == nf - 1,
                               [('wd', gi % 2), ('act', f, ti)], [pok])
                        STT('dve', h[:, d, t0:t0 + T], po[:, :T], 0.5, h[:, d, t0:t0 + T], ALU.mult, ALU.add,
                            [pok, ('h', ti)], [('h', ti)])
            S.barrier()
            sb.reset(m)

        def ple(l):
            m = sb.mark()
            xn = sb.alloc(KC * NT, BF16).rearrange("p (k t) -> p k t", k=KC)
            wg = sb.alloc(KC * 1024, BF16).rearrange("p (k n) -> p k n", k=KC)
            wpi = sb.alloc(2 * 1024, BF16).rearrange("p (k n) -> p k n", k=2)
            ptb = sb.alloc(2 * NT, BF16).rearrange("p (k t) -> p k t", k=2)
            sq = sb.alloc(KC * 512, BF16).rearrange("p (k t) -> p k t", k=KC)
            rs = sb.alloc(512, F32)
            sg = [sb.alloc(512, F32) for _ in range(2)]
            DMA('pool', wg, Wd_['ple_gate'][l].rearrange("(k p) n -> p k n", p=128), [], ['wg'])
            DMA('pool', wpi, Wd_['ple_in'][l].rearrange("(k p) n -> p k n", p=128), [], ['wpi'])
            DMA('pool', ptb, pT[l].rearrange("(k p) t -> p k t", p=128), [], ['ptb'])
            for ti, (t0, T) in enumerate(TILES):
                rmsnorm(t0, T, 'norm_ple', l * KC, xn[:, :, t0:t0 + T], ('xn', ti), sq, rs, ti)
            cnt = 0
            for ti, (t0, T) in enumerate(TILES):
                for d in range(KC):
                    p1, p1k = psum()
                    p2, p2k = psum()
                    for k in range(KC):
                        MM(p1[:, :T], wg[:, k, d * 128:(d + 1) * 128], xn[:, k, t0:t0 + T], k == 0, k == KC - 1,
                           ['wg', ('xn', ti)], [p1k])
                    for k in range(2):
                        MM(p2[:, :T], wpi[:, k, d * 128:(d + 1) * 128], ptb[:, k, t0:t0 + T], k == 0, k == 1,
                           ['wpi', 'ptb'], [p2k])
                    sgb = sg[cnt % 2]
                    sk = ('sg', cnt % 2)
                    cnt += 1
                    ACT(sgb[:, :T], p1[:, :T], AF.Sigmoid, [p1k], [sk])
                    TT('dve', sgb[:, :T], sgb[:, :T], p2[:, :T], ALU.mult, [sk, p2k], [sk])
                    TT('pool', h[:, d, t0:t0 + T], h[:, d, t0:t0 + T], sgb[:, :T], ALU.add, [sk, ('h', ti)], [('h', ti)])
            S.barrier()
            sb.reset(m)

        def final():
            m = sb.mark()
            sq = sb.alloc(KC * 512, BF16).rearrange("p (k t) -> p k t", k=KC)
            rs = sb.alloc(512, F32)
            yo = [sb.alloc(KC * 512, F32).rearrange("p (k t) -> p k t", k=KC) for _ in range(2)]
            yv = yT.rearrange("(k p) t -> p k t", p=128)
            for ti, (t0, T) in enumerate(TILES):
                yb = yo[ti % 2]
                rmsnorm(t0, T, 'norm_final', 0, yb, ('yo', ti % 2), sq, rs, ti)
                DMA('sp', yv[:, :, t0:t0 + T], yb[:, :, :T], [('yo', ti % 2)], [('yT', ti)])
            sb.reset(m)

        def mamba(l):
            j = l // 2
            Win = Wd_['m_in_proj'][j].rearrange("(k p) n -> p k n", p=128)
            Wout = Wd_['m_out_proj'][j].rearrange("(c p) n -> p c n", p=128)
            R0 = ['cst', 'vecs', 'hv', 'dbc', 'onec', 'eps5c', 'ident_b', 'trip_b', 'tris_b', 'bmT_b']
            cw = [0]

            def run(smp):
                m = sb.mark()
                T = 64 if smp else 256
                L = 64 if smp else 128
                nch = T // L
                tiles = [(2048, 64)] if smp else [(i * 256, 256) for i in range(8)]
                tri = cst[:L, C_TRIS:C_TRIS + 64] if smp else cst[:, C_TRIP:C_TRIP + 128]
                onesm = cst[:L, C_BLKS:C_BLKS + 64] if smp else cst[:, C_ONES:C_ONES + 128]
                cmask = tris_b[:L, :] if smp else trip_b
                u = sb.alloc(KC * T, BF16).rearrange("p (k t) -> p k t", k=KC)
                rs = sb.alloc(T, F32)
                wblk = [sb.alloc(KC * 256, BF16).rearrange("p (k n) -> p k n", k=KC) for _ in range(2)]
                wdt = sb.alloc(KC * 32, BF16).rearrange("p (k n) -> p k n", k=KC)
                zs = sb.alloc(nch * 2048, BF16).rearrange("p (c n) -> p c n", c=nch)
                PW = 112 if smp else 3 + T
                preb = [sb.alloc(PW, F32) for _ in range(2)]
                accb = [sb.alloc(T, F32) for _ in range(2)]
                xa = sb.alloc(32 * T, BF16).rearrange("p (c t) -> p c t", c=32)
                dtT = sb.alloc(T, F32, parts=32)
                dtAT = sb.alloc(T, F32, parts=32)
                expA = sb.alloc(1, F32, parts=32)
                xtok = sb.alloc(2048, BF16)
                Btok = sb.alloc(1024, BF16)
                dtk = sb.alloc(64, F32)
                nak = sb.alloc(64, F32)
                ena = sb.alloc(32, F32)
                dend = sb.alloc(32, F32)
                xdt = sb.alloc(2048, BF16)
                sq = xdt[:, 0:KC * T].rearrange("p (k t) -> p k t", k=KC)
                NSET = 1 if smp else 4
                gsets = []
                for _si in range(NSET):
                    gsets.append(dict(
                        cbm=sb.alloc(128, BF16),
                        dexp=sb.alloc(4 * 128, F32).rearrange("p (a b) -> p a b", a=4),
                        dec=[sb.alloc(128, F32) for _ in range(2)],
                        MT=sb.alloc(4 * 128, BF16).rearrange("p (a b) -> p a b", a=4),
                        tmp=sb.alloc(256, F32), yg=sb.alloc(256, F32), ssq=sb.alloc(1, F32),
                        ynk=sb.alloc(256, BF16), xdd=sb.alloc(256, BF16)))
                print('MAMBA sets alloc off', sb.off, sb.nbytes, smp)
                ynT = sb.alloc(16 * T, BF16).rearrange("p (c t) -> p c t", c=16)
                wob = [sb.alloc(16 * 128, BF16).rearrange("p (c n) -> p c n", c=16) for _ in range(2)]
                if smp:
                    cstate = sb.alloc(32 * 48, F32).rearrange("p (c b r) -> p c b r", c=32, b=16)
                    cso = sb.alloc(32 * 48, F32).rearrange("p (c b r) -> p c b r", c=32, b=16)
                    nat = sb.alloc(16 * 256, F32).rearrange("p (b q n) -> p b q n", b=16, q=2)
                    STs = sb.alloc(16 * 256, F32).rearrange("p (b n) -> p b n", b=16)
                    STsb = sb.alloc(16 * 256, BF16).rearrange("p (b n) -> p b n", b=16)
                    CTm = sb.alloc(16 * 64, BF16).rearrange("p (b t) -> p b t", b=16)
                    xddm = sb.alloc(16 * 256, BF16).rearrange("p (b n) -> p b n", b=16)
                    edCs = sb.alloc(512, F32).rearrange("p (b h) -> p b h", b=16)
                    dtAe = sb.alloc(512, F32).rearrange("p (b h) -> p b h", b=16)
                    DMA('sp', cstate, conv_in[j].rearrange("(c p) b r -> p c b r", p=128), [], ['cstate'])
                else:
                    carry = sb.alloc(96, F32).rearrange("p (c r) -> p c r", c=32)
                    ST = sb.alloc(2048, F32)
                    STb = sb.alloc(2048, BF16)
                    natp = sb.alloc(512, F32).rearrange("p (q n) -> p q n", q=4)
                    edC = sb.alloc(32, F32)
                    MSET('dve', carry, 0.0, ['carry'])
                    MSET('dve', ST, 0.0, ['ST'])
                    MSET('pool', STb, 0.0, ['STb'])
                ACT(expA, hv[:, 2 * j + 1:2 * j + 2], AF.Exp, ['hv'], ['expA'])

                for (t0, T_) in tiles:
                    rmsnorm(t0, T, 'norm_mix', l * KC, u, ('u',), sq, rs, 0, hkey='h', sqkey='xdt')
                    for zb in range(8):
                        wb = wblk[cw[0] % 2]
                        wk = ('wblk', cw[0] % 2)
                        cw[0] += 1
                        DMA('pool', wb, Win[:, :, zb * 256:(zb + 1) * 256], [], [wk])
                        for tc in range(nch):
                            ps, pk = psum()
                            for k in range(KC):
                                MM(ps[:L, 0:256], u[:, k, tc * L:(tc + 1) * L], wb[:, k, :], k == 0, k == KC - 1, [('u',), wk], [pk])
                            ACT(zs[:L, tc, zb * 256:(zb + 1) * 256], ps[:L, 0:256], AF.Silu, [pk], [('zs', tc, zb)])
                    for xb in range(16):
                        wb = wblk[cw[0] % 2]
                        wk = ('wblk', cw[0] % 2)
                        cw[0] += 1
                        DMA('pool', wb, Win[:, :, 2048 + xb * 256:2048 + (xb + 1) * 256], [], [wk])
                        def cc_gen(cc, q):
                            ps, pk = psum()
                            for k in range(KC):
                                MM(ps[:, :T], wb[:, k, q * 128:(q + 1) * 128], u[:, k, :], k == 0, k == KC - 1, [('u',), wk], [pk])
                            pre = preb[cc % 2]
                            prk = ('pre', cc % 2)
                            acc = accb[cc % 2]
                            ack = ('acc', cc % 2)
                            ce = 'dve'
                            if smp:
                                prev = pre.rearrange("p (b c) -> p b c", c=7)
                                CP('pool', prev[:, :, 0:3], cstate[:, cc], ['cstate'], [prk])
                                ACT(prev[:, :, 3:7], ps[:, :T].rearrange("p (b t) -> p b t", t=4), AF.Copy, [pk], [prk])
                                win = lambda k_: prev[:, :, k_:k_ + 4]
                                accv = acc.rearrange("p (b t) -> p b t", t=4)
                                xav = xa[:, cc, :].rearrange("p (b t) -> p b t", t=4)
                            else:
                                CP('pool', pre[:, 0:3], carry[:, cc, :], [('carry', cc)], [prk])
                                ACT(pre[:, 3:3 + T], ps[:, :T], AF.Copy, [pk], [prk])
                                win = lambda k_: pre[:, k_:k_ + T]
                                accv = acc
                                xav = xa[:, cc, :]
                            yield
                            TS(ce, accv, win(0), vcol('m_conv_w', (j * 4 + 0) * 32 + cc), None, ALU.mult, None, [prk, 'vecs'], [ack])
                            for k_ in range(1, 4):
                                yield
                                STT(ce, accv, win(k_), vcol('m_conv_w', (j * 4 + k_) * 32 + cc), accv, ALU.mult, ALU.add,
                                    [prk, 'vecs', ack], [ack])
                            yield
                            ACT(xav, accv, AF.Silu, [ack, 'vecs'], [('xa', cc)], bias=vcol('m_conv_b', j * 32 + cc))
                            if smp:
                                CP('pool', cso[:, cc], prev[:, :, 4:7], [prk], [('cso', cc)])
                            else:
                                CP('pool', carry[:, cc, :], pre[:, T:T + 3], [prk], [('carry', cc)])
                            yield
                        cgs = [cc_gen(xb * 2 + q, q) for q in range(2)]
                        alive = True
                        while alive:
                            alive = False
                            for cg in cgs:
                                try:
                                    next(cg)
                                    alive = True
                                except StopIteration:
                                    pass
                    DMA('pool', wdt, Win[:, :, 6144:6176], [], ['wdt'])
                    ps, pk = psum()
                    for k in range(KC):
                        MM(ps[:32, :T], wdt[:, k, :], u[:, k, :], k == 0, k == KC - 1, [('u',), 'wdt'], [pk])
                    ACT(dtT, ps[:32, :T], AF.Exp, [pk, 'hv'], ['dtT'], bias=hv[:, 2 * j:2 * j + 1])
                    ACT(dtT, dtT, AF.Ln, ['dtT', 'onec'], ['dtT'], bias=onec[0:32, 0:1])
                    TS('dve', dtAT, dtT, expA[:, 0:1], None, ALU.mult, None, ['dtT', 'expA'], ['dtAT'])

                    for tc in range(nch):
                        c0 = tc * L
                        for half in range(2):
                            ps, pk = psum()
                            psb = ps.bitcast(BF16)
                            for q in range(8):
                                cc = half * 8 + q
                                TR(psb[:L, q * 128:(q + 1) * 128], xa[:, cc, c0:c0 + L], ident_b, [('xa', cc), 'ident_b'], [pk])
                            CP('dve', xtok[:L, half * 1024:(half + 1) * 1024], psb[:L, :], [pk], [('xtok', half)])
                        ps, pk = psum()
                        psb = ps.bitcast(BF16)
                        for g in range(8):
                            TR(psb[:L, g * 128:(g + 1) * 128], xa[:, 16 + g, c0:c0 + L], ident_b, [('xa', 16 + g), 'ident_b'], [pk])
                        CP('dve', Btok[:L, :], psb[:L, :], [pk], ['Btok'])
                        ps, pk = psum()
                        TR(ps[:L, 0:32], dtT[:, c0:c0 + L], ident_f[0:32, 0:32], ['dtT', 'cst'], [pk])
                        TR(ps[:L, 32:64], dtAT[:, c0:c0 + L], ident_f[0:32, 0:32], ['dtAT', 'cst'], [pk])
                        CP('dve', dtk[:L, :], ps[:L, 0:64], [pk], ['dtk'])
                        ps, pk = psum()
                        MM(ps[:L, 0:32], tri, dtk[:L, 32:64], True, True, ['cst', 'dtk'], [pk])
                        MM(ps[:L, 32:64], onesm, dtk[:L, 32:64], True, True, ['cst', 'dtk'], [pk])
                        CP('dve', nak[:L, :], ps[:L, 0:64], [pk], ['nak'])
                        ACT(ena[:L, :], nak[:L, 0:32], AF.Exp, ['nak'], ['ena'], scale=-1.0)
                        TT('dve', dend[:L, :], nak[:L, 0:32], nak[:L, 32:64], ALU.subtract, ['nak'], ['dend'])
                        ACT(dend[:L, :], dend[:L, :], AF.Exp, ['dend'], ['dend'])
                        TT('dve', xdt[:L, :].rearrange("p (h d) -> p h d", h=32), xtok[:L, :].rearrange("p (h d) -> p h d", h=32),
                           dtk[:L, 0:32].rearrange("p (h o) -> p h o", o=1).broadcast_to([L, 32, 64]), ALU.mult, ['xtok', 'dtk'], ['xdt'])
                        if smp:
                            TT('dve', dtAe[:L], dtk[:L, 32:64].rearrange("p (o h) -> p o h", o=1).broadcast_to([L, 16, 32]),
                               cst[:L, C_BM:C_BM + 16].rearrange("p (b o) -> p b o", o=1).broadcast_to([L, 16, 32]), ALU.mult, ['dtk', 'cst'], ['dtAe'])
                            ps, pk = psum()
                            MM(ps[:, 0:512], cst[:L, C_ONES:C_ONES + 128], dtAe[:L].rearrange("p b h -> p (b h)"), True, True, ['cst', 'dtAe'], [pk])
                            ACT(edCs.rearrange("p b h -> p (b h)"), ps[:, 0:512], AF.Exp, [pk], ['edCs'], scale=-1.0)
                        else:
                            ACT(edC, nak[:, 32:64], AF.Exp, ['nak'], ['edC'], scale=-1.0)

                        def group_gen(g, si):
                            gs_ = gsets[si]
                            cbm, dexp, dec, MT, tmp, yg, ssq, ynk, xdd = (gs_['cbm'], gs_['dexp'], gs_['dec'], gs_['MT'], gs_['tmp'],
                                                                          gs_['yg'], gs_['ssq'], gs_['ynk'], gs_['xdd'])
                            BT = xa[:, 16 + g, c0:c0 + L]
                            CT = xa[:, 24 + g, c0:c0 + L]
                            if smp:
                                for q_ in range(2):
                                    DMA('sp', nat[:, :, q_, :], ssm_in[j][:, 4 * g + 2 * q_:4 * g + 2 * q_ + 2].rearrange("b a p n -> (a p) b n"), [], ['nat'])
                                for b4 in range(8):
                                    ps, pk = psum()
                                    for i4 in range(4):
                                        b_ = b4 * 2 + i4 // 2
                                        q_ = i4 % 2
                                        TR(ps[:, i4 * 128:(i4 + 1) * 128], nat[:, b_, q_, :], ident_f, ['nat', 'cst'], [pk])
                                    CP('dve', STs[:, b4 * 2:b4 * 2 + 2, :].rearrange("p b n -> p (b n)"), ps[:, :], [pk], [('STs', b4)])
                                    CP('act' if False else 'pool', STsb[:, b4 * 2:b4 * 2 + 2, :], STs[:, b4 * 2:b4 * 2 + 2, :], [('STs', b4)], [('STsb', b4)])
                            ps, pk = psum()
                            MM(ps[:L, :L], BT, CT, True, True, [('xa', 16 + g), ('xa', 24 + g)], [pk])
                            TT('dve', cbm[:L, :L], ps[:L, :L], cmask[:L, :L], ALU.mult, [pk, 'trip_b', 'tris_b'], [('cbm', si)])
                            CP('pool', dexp[:L, :, :L], dtk[:L, 32 + 4 * g:36 + 4 * g].rearrange("p (a o) -> p a o", o=1).broadcast_to([L, 4, L]), ['dtk'], [('dexp', si)])
                            yield
                            ps2, pk2 = psum()
                            for hh in range(4):
                                MM(ps2[:L, hh * L:(hh + 1) * L], dexp[:L, hh, :L], tri, True, True, [('dexp', si), 'cst'], [pk2])
                            yield
                            for hh in range(4):
                                h_ = 4 * g + hh
                                dc = dec[hh % 2]
                                dk = ('dec', si, hh % 2)
                                ACT(dc[:L, :L], ps2[:L, hh * L:(hh + 1) * L], AF.Exp, [pk2, 'nak'], [dk], bias=nak[:L, h_:h_ + 1], scale=-1.0)
                                STT('dve', MT[:L, hh, :L], dc[:L, :L], 1.0, cbm[:L, :L], ALU.min, ALU.mult, [dk, ('cbm', si)], [('MT', si, hh)])
                            yield
                            psy, pyk = psum()
                            for hh in range(4):
                                h_ = 4 * g + hh
                                MM(psy[:L, hh * 64:(hh + 1) * 64], MT[:L, hh, :L], xdt[:L, h_ * 64:(h_ + 1) * 64], True, True, [('MT', si, hh), 'xdt'], [pyk])
                            if smp:
                                TT('pool', CTm, CT.rearrange("p (o t) -> p o t", o=1).broadcast_to([128, 16, 64]),
                                   bmT_b.rearrange("p (b t) -> p b t", b=16), ALU.mult, [('xa', 24 + g), 'bmT_b'], ['CTm'])
                                for b in range(16):
                                    MM(psy[:L, 256:512], CTm[:, b, :], STsb[:, b, :], b == 0, b == 15, ['CTm', ('STsb', b // 2)], [pyk])
                            else:
                                MM(psy[:L, 256:512], CT, STb[:, g * 256:(g + 1) * 256], True, True, [('xa', 24 + g), ('STb', g)], [pyk])
                            yield
                            t3 = tmp[:L, :].rearrange("p (a d) -> p a d", a=4)
                            TT('dve', t3, psy[:L, 256:512].rearrange("p (a d) -> p a d", a=4),
                               ena[:L, 4 * g:4 * g + 4].rearrange("p (a o) -> p a o", o=1).broadcast_to([L, 4, 64]), ALU.mult, [pyk, 'ena'], [('tmp', si)])
                            TT('dve', yg[:L, :], psy[:L, 0:256], tmp[:L, :], ALU.add, [pyk, ('tmp', si)], [('yg', si)])
                            TT('pool', t3, xtok[:L, g * 256:(g + 1) * 256].rearrange("p (a d) -> p a d", a=4),
                               dbc[:L, j * 32 + 4 * g:j * 32 + 4 * g + 4].rearrange("p (a o) -> p a o", o=1).broadcast_to([L, 4, 64]), ALU.mult,
                               [('xtok', g // 4), 'dbc', ('yg', si)], [('tmp', si)])
                            TT('dve', yg[:L, :], yg[:L, :], tmp[:L, :], ALU.add, [('tmp', si), ('yg', si)], [('yg', si)])
                            yield
                            TT('dve', yg[:L, :], yg[:L, :], zs[:L, tc, g * 256:(g + 1) * 256], ALU.mult, [('yg', si), ('zs', tc, g)], [('yg', si)])
                            TT('pool', tmp[:L, :], yg[:L, :], yg[:L, :], ALU.mult, [('yg', si)], [('tmp', si)])
                            S.add('dve', (lambda e, o=ssq[:L, 0:1], i_=tmp[:L, :]: e.tensor_reduce(out=o, in_=i_, axis=AX.X, op=ALU.add)), reads=[('tmp', si)], writes=[('ssq', si)])
                            yield
                            ACT(ssq[:L, :], ssq[:L, :], AF.Ln, [('ssq', si), 'eps5c'], [('ssq', si)], bias=eps5c[:L, 0:1], scale=1.0 / 256)
                            ACT(ssq[:L, :], ssq[:L, :], AF.Exp, [('ssq', si)], [('ssq', si)], scale=-0.5)
                            TS('dve', ynk[:L, :], yg[:L, :], ssq[:L, 0:1], None, ALU.mult, None, [('yg', si), ('ssq', si)], [('ynk', si)])
                            yield
                            pst, ptk = psum()
                            pstb = pst.bitcast(BF16)
                            for q in range(2):
                                TR(pstb[:, q * L:(q + 1) * L], ynk[:L, q * 128:(q + 1) * 128], ident_b[:L, :L], [('ynk', si), 'ident_b'], [ptk])
                            for q in range(2):
                                cc = 2 * g + q
                                ACT(ynT[:, cc, c0:c0 + L], pstb[:, q * L:(q + 1) * L], AF.Copy, [ptk, 'vecs'], [('ynT', cc)], scale=vcol('m_norm', j * 16 + cc))
                            yield
                            TT('dve', xdd[:L, :].rearrange("p (a d) -> p a d", a=4), xdt[:L, g * 256:(g + 1) * 256].rearrange("p (a d) -> p a d", a=4),
                               dend[:L, 4 * g:4 * g + 4].rearrange("p (a o) -> p a o", o=1).broadcast_to([L, 4, 64]), ALU.mult, ['xdt', 'dend'], [('xdd', si)])
                            if smp:
                                TT('pool', xddm[:L], xdd[:L, :].rearrange("p (o n) -> p o n", o=1).broadcast_to([L, 16, 256]),
                                   cst[:L, C_BM:C_BM + 16].rearrange("p (b o) -> p b o", o=1).broadcast_to([L, 16, 256]), ALU.mult, [('xdd', si), 'cst'], ['xddm'])
                                TT('dve', STs.rearrange("p b (a d) -> p b a d", a=4), STs.rearrange("p b (a d) -> p b a d", a=4),
                                   edCs[:, :, 4 * g:4 * g + 4].rearrange("p b (a o) -> p b a o", o=1).broadcast_to([128, 16, 4, 64]), ALU.mult,
                                   ['STs', 'edCs', 'STsb'], ['STs'])
                                for b2 in range(8):
                                    pss, psk = psum()
                                    for i2 in range(2):
                                        MM(pss[:, i2 * 256:(i2 + 1) * 256], Btok[:L, g * 128:(g + 1) * 128], xddm[:L, b2 * 2 + i2, :], True, True, ['Btok', 'xddm'], [psk])
                                    TT('dve', STs[:, b2 * 2:b2 * 2 + 2, :].rearrange("p b n -> p (b n)"), STs[:, b2 * 2:b2 * 2 + 2, :].rearrange("p b n -> p (b n)"),
                                       pss[:, :], ALU.add, [psk, ('STs', b2)], [('STs', b2)])
                                for b4 in range(8):
                                    ps, pk = psum()
                                    for i4 in range(4):
                                        b_ = b4 * 2 + i4 // 2
                                        q_ = i4 % 2
                                        TR(ps[:, i4 * 128:(i4 + 1) * 128], STs[:, b_, q_ * 128:(q_ + 1) * 128], ident_f, [('STs', b4), 'cst'], [pk])
                                    CP('dve', nat[:, b4 * 2:b4 * 2 + 2].rearrange("p b q n -> p (b q n)"), ps[:, :], [pk], ['nat'])
                                for q_ in range(2):
                                    DMA('sp', ssm_s[j][:, 4 * g + 2 * q_:4 * g + 2 * q_ + 2].rearrange("b a p n -> (a p) b n"), nat[:, :, q_, :], ['nat'], [('ssm_s', j, g, q_)])
                            else:
                                pss, psk = psum()
                                MM(pss[:, 0:256], Btok[:L, g * 128:(g + 1) * 128], xdd[:L, :], True, True, ['Btok', ('xdd', si)], [psk])
                                sg3 = ST[:, g * 256:(g + 1) * 256].rearrange("p (a d) -> p a d", a=4)
                                TT('dve', sg3, sg3, edC[:, 4 * g:4 * g + 4].rearrange("p (a o) -> p a o", o=1).broadcast_to([128, 4, 64]), ALU.mult,
                                   [('ST', g), 'edC', ('STb', g)], [('ST', g)])
                                TT('dve', ST[:, g * 256:(g + 1) * 256], ST[:, g * 256:(g + 1) * 256], pss[:, 0:256], ALU.add, [psk, ('ST', g)], [('ST', g)])
                                CP('pool', STb[:, g * 256:(g + 1) * 256], ST[:, g * 256:(g + 1) * 256], [('ST', g)], [('STb', g)])
                            yield
                        for gb in range(0, 8, NSET):
                            gens_ = [group_gen(g, g % NSET) for g in range(gb, gb + NSET)]
                            alive = True
                            while alive:
                                alive = False
                                for gn_ in gens_:
                                    try:
                                        next(gn_)
                                        alive = True
                                    except StopIteration:
                                        pass
                    for d in range(KC):
                        wo = wob[d % 2]
                        wok = ('wob', d % 2)
                        DMA('pool', wo, Wout[:, :, d * 128:(d + 1) * 128], [], [wok])
                        ps, pk = psum()
                        for cc in range(16):
                            MM(ps[:, :T], wo[:, cc, :], ynT[:, cc, :], cc == 0, cc == 15, [wok, ('ynT', cc)], [pk])
                        TT('dve', h[:, d, t0:t0 + T], h[:, d, t0:t0 + T], ps[:, :T], ALU.add, [pk, 'h'], ['h'])
                if smp:
                    DMA('sp', conv_s[j].rearrange("(c p) b r -> p c b r", p=128), cso, ['cso'], [('conv_s', j)])
                else:
                    DMA('sp', conv_p[j].rearrange("(c p) r -> p c r", p=128), carry, ['carry'], [('conv_p', j)])
                    for q4 in range(4):
                        ps, pk = psum()
                        for i4 in range(4):
                            q = q4 * 4 + i4
                            TR(ps[:, i4 * 128:(i4 + 1) * 128], ST[:, q * 128:(q + 1) * 128], ident_f, ['ST', 'cst'], [pk])
                        CP('dve', natp.rearrange("p q n -> p (q n)"), ps[:, :], [pk], ['natp'])
                        DMA('sp', ssm_p[j].rearrange("(q a) p n -> (a p) q n", a=2)[:, q4 * 4:q4 * 4 + 4, :], natp, ['natp'], [('ssm_p', j, q4)])
                S.barrier()
                sb.reset(m)

            run(False)
            run(True)


        def rwkv(l):
            j = l // 2
            RDT = BF16
            mu0 = VOFF['r_mu'] + j * 48

            def wview(name, jj=None):
                return Wd_[name][j if jj is None else jj]

            def run(smp):
                m = sb.mark()
                T = 64 if smp else 128
                L = T
                nb = 16 if smp else 1
                lt = L // nb
                nlev = 2 if smp else 7
                tiles = [(2048, 64)] if smp else [(i * 128, 128) for i in range(16)]
                mAB = cst[:L, C_MABS:C_MABS + 128] if smp else cst[:, C_MABP:C_MABP + 256]
                mlow = cst[:L, C_LOWS:C_LOWS + 64] if smp else cst[:, C_LOWP:C_LOWP + 128]
                rst = cst[:, C_RSTS:C_RSTS + 64] if smp else cst[:, C_RSTP:C_RSTP + 128]
                blk2 = cst[:, C_BLK2:C_BLK2 + 128]
                f32a = lambda: sb.alloc(KC * T, F32).rearrange("p (k t) -> p k t", k=KC)
                b16a = lambda: sb.alloc(KC * T, BF16).rearrange("p (k t) -> p k t", k=KC)
                uf, up, r_, k_, v_, nlw, a_, kk, np_, tA, tB = [f32a() for _ in range(11)]
                xm = [b16a() for _ in range(2)]
                g_, BT_, KT_, BH_, KH_ = [b16a() for _ in range(5)]
                V_ = xm[1]
                yg = xm[0]
                ARt = sb.alloc(KC * 2 * T, BF16).rearrange("p (k c t) -> p k c t", k=KC, c=2)
                sq = BH_
                rs = sb.alloc(T, F32)
                wbuf = [sb.alloc(KC * 256, BF16).rearrange("p (k n) -> p k n", k=KC) for _ in range(2)]
                w1b = sb.alloc(KC * 64, BF16).rearrange("p (k n) -> p k n", k=KC)
                a1b = sb.alloc(KC * 64, BF16).rearrange("p (k n) -> p k n", k=KC)
                g1b = sb.alloc(KC * 160, BF16).rearrange("p (k n) -> p k n", k=KC)
                v1b = sb.alloc(KC * 32, BF16).rearrange("p (k n) -> p k n", k=KC)
                w2b = sb.alloc(1024, BF16)
                a2b = sb.alloc(1024, BF16)
                g2a = sb.alloc(1024, BF16)
                g2b = sb.alloc(1024, BF16)
                v2b = sb.alloc(1024, BF16)
                lo1 = sb.alloc(T, BF16)
                lo2 = sb.alloc(T, BF16)
                lo3 = sb.alloc(T, BF16)
                negw0 = sb.alloc(8, F32)
                omka = sb.alloc(8, F32)
                mhalf = sb.alloc(1, F32)
                c24 = sb.alloc(1, F32)
                gnec = sb.alloc(1, F32)
                eC = sb.alloc(KC * nb, F32).rearrange("p (k b) -> p k b", k=KC)
                Vtok = sb.alloc(KC * 128, BF16).rearrange("p (k n) -> p k n", k=KC)
                BHtok = sb.alloc(KC * 128, BF16).rearrange("p (k n) -> p k n", k=KC)
                KHtok = sb.alloc(KC * 128, BF16).rearrange("p (k n) -> p k n", k=KC)
                Ytok = sb.alloc(KC * 128, F32).rearrange("p (k n) -> p k n", k=KC)
                Ysq = up if not smp else sb.alloc(KC * 128, F32).rearrange("p (k n) -> p k n", k=KC)
                YSK = 'Ysq' if smp else 'up'
                Yn = sb.alloc(KC * 128, BF16).rearrange("p (k n) -> p k n", k=KC)
                st1 = sb.alloc(16, F32)
                st2 = sb.alloc(16, F32)
                NH = 1 if smp else 8
                ABt = [sb.alloc(2 * L, BF16) for _ in range(NH)]
                AKt = [sb.alloc(2 * L, BF16) for _ in range(NH)]
                MMb = [[sb.alloc(2 * L, BF16) for _ in range(2)] for _ in range(NH)]
                X32 = [sb.alloc(64, F32) for _ in range(NH)]
                Xb = [sb.alloc(64, BF16) for _ in range(NH)]
                S0T = sb.alloc(KC * nb * 64, F32).rearrange("p (k b i) -> p k b i", k=KC, b=nb)
                nbd = min(nb, 4)
                S0bh = [sb.alloc(nb * 64, BF16).rearrange("p (b i) -> p b i", b=nb) for _ in range(NH)]
                bd = sb.alloc(nbd * 128, F32).rearrange("p (b n) -> p b n", b=nbd)
                if smp:
                    sst = sb.alloc(KC * 16, F32).rearrange("p (k b) -> p k b", k=KC)
                    ATm = [sb.alloc(16 * 64, BF16).rearrange("p (b t) -> p b t", b=16) for _ in range(NH)]
                    RTm = [sb.alloc(16 * 64, BF16).rearrange("p (b t) -> p b t", b=16) for _ in range(NH)]
                    Wm = [sb.alloc(16 * 64, BF16).rearrange("p (b t) -> p b t", b=16) for _ in range(NH)]
                    Vm = [sb.alloc(16 * 64, BF16).rearrange("p (b t) -> p b t", b=16) for _ in range(NH)]
                    DMA('sp', sst, shift_in[j].rearrange("(k p) b -> p k b", p=128), [], ['sst'])
                    MSET('pool', BHtok, 0.0, ['BHtok'])
                    MSET('pool', Vtok, 0.0, ['Vtok'])
                    for q_ in range(NH):
                        MSET('pool', AKt[q_], 0.0, [('AKt', q_)])
                        MSET('pool', ABt[q_], 0.0, [('ABt', q_)])
                        MSET('pool', Xb[q_], 0.0, [('Xb', q_)])
                    MSET('pool', KHtok, 0.0, ['KHtok'])
                    for q_ in range(NH):
                        MSET('pool', Wm[q_], 0.0, [('Wm', q_)])
                        MSET('pool', Vm[q_], 0.0, [('Vm', q_)])
                else:
                    carry = sb.alloc(8, F32)
                    MSET('dve', carry, 0.0, ['carry'])
                MSET('dve', mhalf, -0.5, ['mhalf'])
                MSET('dve', c24, 1e-24, ['c24'])
                MSET('dve', gnec, 64e-5, ['gnec'])
                TS('dve', negw0, vecs[:, VOFF['r_w0'] + j * 8:VOFF['r_w0'] + j * 8 + 8], -1.0, None, ALU.mult, None, ['vecs'], ['negw0'])
                TS('dve', omka, vecs[:, VOFF['r_k_a'] + j * 8:VOFF['r_k_a'] + j * 8 + 8], -1.0, 1.0, ALU.mult, ALU.add, ['vecs'], ['omka'])
                DMA('pool', w1b, wview('r_w1').rearrange("(k p) n -> p k n", p=128), [], ['w1b'])
                DMA('pool', a1b, wview('r_a1').rearrange("(k p) n -> p k n", p=128), [], ['a1b'])
                DMA('pool', g1b, wview('r_g1').rearrange("(k p) n -> p k n", p=128), [], ['g1b'])
                DMA('pool', w2b[0:64, :], wview('r_w2'), [], ['w2b'])
                DMA('pool', a2b[0:64, :], wview('r_a2'), [], ['a2b'])
                DMA('pool', g2a, wview('r_g2')[0:128, :], [], ['g2a'])
                DMA('pool', g2b[0:32, :], wview('r_g2')[128:160, :], [], ['g2b'])
                if j == 1:
                    DMA('pool', v1b, wview('r_v1', 0).rearrange("(k p) n -> p k n", p=128), [], ['v1b'])
                    DMA('pool', v2b[0:32, :], wview('r_v2', 0), [], ['v2b'])
                if smp:
                    MSET('pool', bd, 0.0, ['bd'])
                    for d in range(KC):
                        for b4 in range(4):
                            for a in range(2):
                                DMA('sp', bd[a * 64:(a + 1) * 64, :, a * 64:(a + 1) * 64], wkv_in[j][b4 * 4:b4 * 4 + 4, 2 * d + a].rearrange("b i j -> i b j"), [], ['bd'])
                            ps, pk = psum()
                            for i4 in range(4):
                                TR(ps[:, i4 * 128:(i4 + 1) * 128], bd[:, i4, :], ident_f, ['bd', 'cst'], [pk])
                            for a in range(2):
                                CP('dve', S0T[a * 64:(a + 1) * 64, d, b4 * 4:b4 * 4 + 4, :],
                                   ps[a * 64:(a + 1) * 64, :].rearrange("p (b n) -> p b n", b=4)[:, :, a * 64:(a + 1) * 64], [pk], [('S0T', d)])
                else:
                    MSET('dve', S0T, 0.0, ['S0T'])
                wc = [0]

                def proj(wname, src, evac, jj=None):
                    Wv = wview(wname, jj).rearrange("(k p) n -> p k n", p=128)
                    for hf in range(4):
                        wb = wbuf[wc[0] % 2]
                        wk = ('wbuf', wc[0] % 2)
                        wc[0] += 1
                        DMA('pool', wb, Wv[:, :, hf * 256:(hf + 1) * 256], [], [wk])
                        for dd in range(2):
                            d = hf * 2 + dd
                            ps, pk = psum()
                            for k in range(KC):
                                MM(ps[:, :T], wb[:, k, dd * 128:(dd + 1) * 128], src[:, k, :], k == 0, k == KC - 1, [wk, 'xm'], [pk])
                            evac(d, ps, pk)

                try:
                    for (t0, T_) in tiles[:CFG.get('rtiles', 99)]:
                        rmsnorm(t0, T, 'norm_mix', l * KC, uf, ('uf',), sq, rs, 0, hkey='h', sqkey='BH_')
                        if smp:
                            u4 = uf.rearrange("p k (b t) -> p k b t", t=4)
                            p4 = up.rearrange("p k (b t) -> p k b t", t=4)
                            for k in range(KC):
                                CP('pool', p4[:, k, :, 1:4], u4[:, k, :, 0:3], ['uf'], ['up'])
                                CP('pool', p4[:, k, :, 0:1], sst[:, k, :].rearrange("p (b o) -> p b o", o=1), ['sst'], ['up'])
                                CP('pool', sst[:, k, :].rearrange("p (b o) -> p b o", o=1), u4[:, k, :, 3:4], ['uf', 'up'], ['sst'])
                        else:
                            CP('pool', up[:, :, 1:T], uf[:, :, 0:T - 1], ['uf'], ['up'])
                            CP('pool', up[:, :, 0:1], carry.rearrange("p (k o) -> p k o", o=1), ['carry'], ['up'])
                            CP('pool', carry.rearrange("p (k o) -> p k o", o=1), uf[:, :, T - 1:T], ['uf', 'up'], ['carry'])
                        TT('dve', up, up, uf, ALU.subtract, ['up', 'uf'], ['up'])

                        def mix(i, dst):
                            muv = vecs[:, mu0 + i * 8:mu0 + i * 8 + 8].rearrange("p (k o) -> p k o", o=1).broadcast_to([128, KC, T])
                            TT('dve', tA, up, muv, ALU.mult, ['up', 'vecs'], ['tA'])
                            TT('dve', dst, tA, uf, ALU.add, ['tA', 'uf'], ['xm'])
                        mix(0, xm[0])
                        proj('r_wr', xm[0], lambda d, ps, pk: ACT(r_[:, d, :], ps[:, :T], AF.Copy, [pk], [('r_', d)]))
                        mix(2, xm[1])
                        proj('r_wk', xm[1], lambda d, ps, pk: ACT(k_[:, d, :], ps[:, :T], AF.Copy, [pk], [('k_', d)]))
                        mix(3, xm[0])
                        proj('r_wv', xm[0], lambda d, ps, pk: ACT(v_[:, d, :], ps[:, :T], AF.Copy, [pk], [('v_', d)]))
                        if j == 1:
                            ps, pk = psum()
                            for k in range(KC):
                                MM(ps[:32, :T], v1b[:, k, :], xm[0][:, k, :], k == 0, k == KC - 1, ['v1b', 'xm'], [pk])
                            CP('dve', lo3[0:32, :], ps[:32, :T], [pk], ['lo3'])
                            DMA('sp', tB, vfirst_d.rearrange("(k p) t -> p k t", p=128)[:, :, t0:t0 + T], ['vf_dram'], ['tB'])
                            for d in range(KC):
                                ps, pk = psum()
                                MM(ps[:, :T], v2b[0:32, d * 128:(d + 1) * 128], lo3[0:32, :], True, True, ['v2b', 'lo3'], [pk])
                                ACT(tA[:, d, :], ps[:, :T], AF.Sigmoid, [pk, 'vecs'], ['tA'], bias=vcol('r_v0', d))
                            TT('dve', tB, tB, v_, ALU.subtract, ['tB', 'v_'], ['tB'])
                            TT('dve', tB, tB, tA, ALU.mult, ['tB', 'tA'], ['tB'])
                            TT('dve', v_, v_, tB, ALU.add, ['tB', 'v_'], ['v_'])
                        else:
                            DMA('sp', vfirst_d.rearrange("(k p) t -> p k t", p=128)[:, :, t0:t0 + T], v_, ['v_'], [('vf_dram', t0)])
                        _stg(1)
                        mix(1, xm[1])
                        ps, pk = psum()
                        for k in range(KC):
                            MM(ps[:64, :T], w1b[:, k, :], xm[1][:, k, :], k == 0, k == KC - 1, ['w1b', 'xm'], [pk])
                        ACT(lo1[0:64, :], ps[:64, :T], AF.Tanh, [pk], ['lo1'])
                        for d in range(KC):
                            ps, pk = psum()
                            MM(ps[:, :T], w2b[0:64, d * 128:(d + 1) * 128], lo1[0:64, :], True, True, ['w2b', 'lo1'], [pk])
                            ACT(nlw[:, d, :], ps[:, :T], AF.Exp, [pk, 'negw0'], [('nlw', d)], bias=negw0[:, d:d + 1], scale=-1.0)
                            ACT(nlw[:, d, :], nlw[:, d, :], AF.Ln, [('nlw', d), 'onec'], [('nlw', d)], bias=onec[:, 0:1])
                            ACT(nlw[:, d, :], nlw[:, d, :], AF.Exp, [('nlw', d), 'mhalf'], [('nlw', d)], bias=mhalf[:, 0:1], scale=-1.0)
                        mix(4, xm[0])
                        ps, pk = psum()
                        for k in range(KC):
                            MM(ps[:64, :T], a1b[:, k, :], xm[0][:, k, :], k == 0, k == KC - 1, ['a1b', 'xm'], [pk])
                        CP('dve', lo2[0:64, :], ps[:64, :T], [pk], ['lo2'])
                        for d in range(KC):
                            ps, pk = psum()
                            MM(ps[:, :T], a2b[0:64, d * 128:(d + 1) * 128], lo2[0:64, :], True, True, ['a2b', 'lo2'], [pk])
                            ACT(a_[:, d, :], ps[:, :T], AF.Sigmoid, [pk, 'vecs'], [('a_', d)], bias=vcol('r_a0', j * 8 + d))
                        mix(5, xm[1])
                        ps, pk = psum()
                        for k in range(KC):
                            MM(ps[:, :T], g1b[:, k, 0:128], xm[1][:, k, :], k == 0, k == KC - 1, ['g1b', 'xm'], [pk])
                        ACT(lo1, ps[:, :T], AF.Sigmoid, [pk], ['lo1'])
                        ps, pk = psum()
                        for k in range(KC):
                            MM(ps[:32, :T], g1b[:, k, 128:160], xm[1][:, k, :], k == 0, k == KC - 1, ['g1b', 'xm'], [pk])
                        ACT(lo2[0:32, :], ps[:32, :T], AF.Sigmoid, [pk], ['lo2'])
                        for d in range(KC):
                            ps, pk = psum()
                            MM(ps[:, :T], g2a[:, d * 128:(d + 1) * 128], lo1, True, False, ['g2a', 'lo1'], [pk])
                            MM(ps[:, :T], g2b[0:32, d * 128:(d + 1) * 128], lo2[0:32, :], False, True, ['g2b', 'lo2'], [pk])
                            ACT(g_[:, d, :], ps[:, :T], AF.Copy, [pk], [('g_', d)])
                        _stg(2)
                        kkv = vecs[:, VOFF['r_k_k'] + j * 8:VOFF['r_k_k'] + j * 8 + 8].rearrange("p (k o) -> p k o", o=1).broadcast_to([128, KC, T])
                        TT('dve', kk, k_, kkv, ALU.mult, ['k_', 'vecs'], ['kk'])
                        TT('dve', tA, kk, kk, ALU.mult, ['kk'], ['tA'])
                        for d in range(KC):
                            ps, pk = psum()
                            MM(ps[:, :T], blk2, tA[:, d, :], True, True, ['cst', 'tA'], [pk])
                            TS('dve', tB[:, d, :], ps[:, :T], c24[:, 0:1], None, ALU.max, None, [pk, 'c24'], ['tB'])
                        ACT(tB, tB, AF.Ln, ['tB'], ['tB'])
                        ACT(tB, tB, AF.Exp, ['tB'], ['tB'], scale=-0.5)
                        TT('dve', kk, kk, tB, ALU.mult, ['kk', 'tB'], ['kk'])
                        kav = vecs[:, VOFF['r_k_a'] + j * 8:VOFF['r_k_a'] + j * 8 + 8].rearrange("p (k o) -> p k o", o=1).broadcast_to([128, KC, T])
                        omv = omka.rearrange("p (k o) -> p k o", o=1).broadcast_to([128, KC, T])
                        TT('dve', tA, a_, kav, ALU.mult, ['a_', 'vecs'], ['tA'])
                        TT('dve', tA, tA, omv, ALU.add, ['tA', 'omka'], ['tA'])
                        TT('dve', k_, k_, tA, ALU.mult, ['k_', 'tA'], ['k_'])
                        TT('dve', tA, kk, a_, ALU.mult, ['kk', 'a_'], ['tA'])
                        rkv = vecs[:, VOFF['r_r_k'] + j * 8:VOFF['r_r_k'] + j * 8 + 8].rearrange("p (k o) -> p k o", o=1).broadcast_to([128, KC, T])
                        TT('dve', np_, r_, k_, ALU.mult, ['r_', 'k_'], ['np_'])
                        TT('dve', np_, np_, rkv, ALU.mult, ['np_', 'vecs'], ['np_'])
                        for d in range(KC):
                            ps, pk = psum()
                            MM(ps[:, :T], blk2, np_[:, d, :], True, True, ['cst', 'np_'], [pk])
                            TT('dve', tB[:, d, :], ps[:, :T], v_[:, d, :], ALU.mult, [pk, 'v_'], ['tB'])
                        _stg(3)
                        for d in range(KC):
                            S.add('dve', (lambda e, o=np_[:, d, :], d1=nlw[:, d, :]: e.tensor_tensor_scan(out=o, data0=rst[:, :T], data1=d1, initial=0.0, op0=ALU.mult, op1=ALU.add)),
                                  reads=['nlw', 'cst', 'np_'], writes=['np_'])
                        npv = np_.rearrange("p k (b t) -> p k b t", b=nb)
                        npE = npv[:, :, :, lt - 1:lt].broadcast_to([128, KC, nb, lt])
                        ACT(eC, npv[:, :, :, lt - 1:lt].rearrange("p k b o -> p k (b o)"), AF.Exp, ['np_'], ['eC'], scale=-1.0)
                        TT('dve', uf, np_, nlw, ALU.subtract, ['np_', 'nlw'], ['uf'])
                        ACT(uf, uf, AF.Exp, ['uf'], ['uf'], scale=-1.0)
                        STT('dve', ARt[:, :, 0, :], kk, -1.0, uf, ALU.mult, ALU.mult, ['kk', 'uf'], ['ARt'])
                        ACT(uf, np_, AF.Exp, ['np_', 'ARt'], ['uf'], scale=-1.0)
                        TT('dve', ARt[:, :, 1, :], r_, uf, ALU.mult, ['r_', 'uf'], ['ARt'])
                        ACT(uf, np_, AF.Exp, ['np_', 'ARt'], ['uf'])
                        TT('dve', BT_, tA, uf, ALU.mult, ['tA', 'uf'], ['BT_'])
                        TT('dve', KT_, k_, uf, ALU.mult, ['k_', 'uf'], ['KT_'])
                        TT('dve', uf.rearrange("p k (b t) -> p k b t", b=nb), npv, npE, ALU.subtract, ['np_', 'BT_', 'KT_'], ['uf'])
                        ACT(uf, uf, AF.Exp, ['uf'], ['uf'])
                        TT('dve', BH_, tA, uf, ALU.mult, ['tA', 'uf'], ['BH_'])
                        TT('dve', KH_, k_, uf, ALU.mult, ['k_', 'uf'], ['KH_'])
                        CP('pool', V_, v_, ['v_'], ['xm'])
                        _stg(4)
                        for (src, dst, nm) in ((V_, Vtok, 'Vtok'), (BH_, BHtok, 'BHtok'), (KH_, KHtok, 'KHtok')):
                            ps, pk = psum()
                            psb = ps.bitcast(BF16)
                            for d in range(KC):
                                TR(psb[:L, d * 128:(d + 1) * 128], src[:, d, :], ident_b, [nm[:-3] + '_' if nm != 'Vtok' else 'xm', 'ident_b'], [pk])
                            CP('dve', dst[:L].rearrange("p k n -> p (k n)"), psb[:L, :], [pk], [nm])
                        _stg(5)
                        for hg in range(16 // NH):
                            heads = [(hg * NH + q) for q in range(NH)]
                            if CFG.get('rheads') is not None and heads[0] not in CFG['rheads']:
                                continue
                            HD = [(hh // 2, hh % 2) for hh in heads]
                            for q, (d, a) in enumerate(HD):
                                sl = slice(a * 64, (a + 1) * 64)
                                AR = ARt[sl, d].rearrange("p c t -> p (c t)")
                                ps, pk = psum()
                                MM(ps[:L, 0:2 * L], BT_[sl, d, :], AR, True, True, ['BT_', 'ARt'], [pk])
                                TT('dve', ABt[q][:L, :], ps[:L, 0:2 * L], mAB, ALU.mult, [pk, 'cst'], [('ABt', q)])
                                ps, pk = psum()
                                MM(ps[:L, 0:2 * L], KT_[sl, d, :], AR, True, True, ['KT_', 'ARt'], [pk])
                                TT('dve', AKt[q][:L, :], ps[:L, 0:2 * L], mAB, ALU.mult, [pk, 'cst'], [('AKt', q)])
                                ps, pk = psum()
                                MM(ps[:L, 0:L], ARt[sl, d, 0, :], BT_[sl, d, :], True, True, ['BT_', 'ARt'], [pk])
                                TT('dve', MMb[q][0][:L, 0:L], ps[:L, 0:L], mlow, ALU.mult, [pk, 'cst'], [('MMb', q, 0)])
                                CP('pool', MMb[q][0][:L, L:2 * L], ABt[q][:L, 0:L], [('ABt', q)], [('MMb', q, 0)])
                            _stg(6)
                            for q, (d, a) in enumerate(HD):
                                sl = slice(a * 64, (a + 1) * 64)
                                CP('dve', S0bh[q][sl], S0T[sl, d], [('S0T', d)], [('S0bh', q)])
                                ps, pk = psum()
                                if smp:
                                    TT('dve', ATm[q][sl], ARt[sl, d, 0, :].rearrange("p (o t) -> p o t", o=1).broadcast_to([64, 16, 64]),
                                       bmT_b[sl].rearrange("p (b t) -> p b t", b=16), ALU.mult, ['ARt', 'bmT_b'], [('ATm', q)])
                                    TT('dve', RTm[q][sl], ARt[sl, d, 1, :].rearrange("p (o t) -> p o t", o=1).broadcast_to([64, 16, 64]),
                                       bmT_b[sl].rearrange("p (b t) -> p b t", b=16), ALU.mult, ['ARt', 'bmT_b'], [('RTm', q)])
                                    for b in range(16):
                                        MM(ps[:L, 0:64], ATm[q][sl, b, :], S0bh[q][sl, b, :], b == 0, False, [('ATm', q), ('S0bh', q)], [pk])
                                else:
                                    MM(ps[:L, 0:64], ARt[sl, d, 0, :], S0bh[q][sl, 0, :], True, False, ['ARt', ('S0bh', q)], [pk])
                                MM(ps[:L, 0:64], AKt[q][:, 0:L], Vtok[:, d, sl], False, True, [('AKt', q), 'Vtok'], [pk])
                                CP('dve', X32[q][:L, :], ps[:L, 0:64], [pk], [('X32', q)])
                                CP('pool', Xb[q][:L, :], X32[q][:L, :], [('X32', q)], [('Xb', q)])
                            _stg(7)
                            for lev in range(nlev):
                                cur = lev % 2
                                for q, (d, a) in enumerate(HD):
                                    M_ = MMb[q][cur][:L, 0:L]
                                    Mt_ = MMb[q][cur][:L, L:2 * L]
                                    ps, pk = psum()
                                    MM(ps[:L, 0:64], Mt_, Xb[q][:L, :], True, True, [('MMb', q, cur), ('Xb', q)], [pk])
                                    TT('dve', X32[q][:L, :], X32[q][:L, :], ps[:L, 0:64], ALU.add, [pk, ('X32', q)], [('X32', q)])
                                    CP('pool', Xb[q][:L, :], X32[q][:L, :], [('X32', q)], [('Xb', q)])
                                    if lev < nlev - 1:
                                        ps2, pk2 = psum()
                                        MM(ps2[:L, 0:L], Mt_, M_, True, True, [('MMb', q, cur)], [pk2])
                                        MM(ps2[:L, L:2 * L], M_, Mt_, True, True, [('MMb', q, cur)], [pk2])
                                        ACT(MMb[q][1 - cur][:L, :], ps2[:L, 0:2 * L], AF.Copy, [pk2], [('MMb', q, 1 - cur)])
                            _stg(8)
                            for q, (d, a) in enumerate(HD):
                                sl = slice(a * 64, (a + 1) * 64)
                                ps, pk = psum()
                                if smp:
                                    for b in range(16):
                                        MM(ps[:L, 0:64], RTm[q][sl, b, :], S0bh[q][sl, b, :], b == 0, False, [('RTm', q), ('S0bh', q)], [pk])
                                else:
                                    MM(ps[:L, 0:64], ARt[sl, d, 1, :], S0bh[q][sl, 0, :], True, False, ['ARt', ('S0bh', q)], [pk])
                                MM(ps[:L, 0:64], ABt[q][:, L:2 * L], Xb[q][:, :], False, False, [('ABt', q), ('Xb', q)], [pk])
                                MM(ps[:L, 0:64], AKt[q][:, L:2 * L], Vtok[:, d, sl], False, True, [('AKt', q), 'Vtok'], [pk])
                                ACT(Ytok[:L, d, sl], ps[:L, 0:64], AF.Copy, [pk], [('Ytok', d, a)])
                                _stg(8.2)
                                if smp:
                                    bmv = cst[:L, C_BM:C_BM + 16].rearrange("p (b o) -> p b o", o=1).broadcast_to([L, 16, 64])
                                    TT('pool', Wm[q][:L], Xb[q][:L, :].rearrange("p (o i) -> p o i", o=1).broadcast_to([L, 16, 64]), bmv, ALU.mult, [('Xb', q), 'cst'], [('Wm', q)])
                                    TT('pool', Vm[q][:L], Vtok[:L, d, sl].rearrange("p (o i) -> p o i", o=1).broadcast_to([L, 16, 64]), bmv, ALU.mult, ['Vtok', 'cst'], [('Vm', q)])
                                    _stg(8.4)
                                    S3 = S0T[sl, d]
                                    TT('dve', S3, S3, eC[sl, d, :].rearrange("p (b o) -> p b o", o=1).broadcast_to([64, 16, 64]), ALU.mult, [('S0T', d), 'eC'], [('S0T', d)])
                                    _stg(8.6)
                                    for b8 in range(2):
                                        ps, pk = psum()
                                        for bi in range(8):
                                            b = b8 * 8 + bi
                                            MM(ps[sl, bi * 64:(bi + 1) * 64], BHtok[:, d, sl], Wm[q][:, b, :], True, False, ['BHtok', ('Wm', q)], [pk])
                                            MM(ps[sl, bi * 64:(bi + 1) * 64], KHtok[:, d, sl], Vm[q][:, b, :], False, True, ['KHtok', ('Vm', q)], [pk])
                                        TT('dve', S0T[sl, d, b8 * 8:b8 * 8 + 8, :], S0T[sl, d, b8 * 8:b8 * 8 + 8, :], ps[sl, :].rearrange("p (b i) -> p b i", b=8), ALU.add,
                                           [pk, ('S0T', d)], [('S0T', d)])
                                else:
                                    ps, pk = psum()
                                    MM(ps[sl, 0:64], BHtok[:L, d, sl], Xb[q][:L, :], True, False, ['BHtok', ('Xb', q)], [pk])
                                    MM(ps[sl, 0:64], KHtok[:L, d, sl], Vtok[:L, d, sl], False, True, ['KHtok', 'Vtok'], [pk])
                                    STT('dve', S0T[sl, d, 0, :], S0T[sl, d, 0, :], eC[sl, d, 0:1], ps[sl, 0:64], ALU.mult, ALU.add, [pk, ('S0T', d), 'eC'], [('S0T', d)])
                        _stg(9)
                        Y3 = Ytok[:L].rearrange("p k (a i) -> p (k a) i", a=2)
                        S.add('dve', (lambda e, o=st1[:L, :]: e.tensor_reduce(out=o, in_=Y3, axis=AX.X, op=ALU.add)), reads=['Ytok'], writes=['st1'])
                        TT('pool', Ysq[:L], Ytok[:L], Ytok[:L], ALU.mult, ['Ytok'], [YSK])
                        S.add('dve', (lambda e, o=st2[:L, :]: e.tensor_reduce(out=o, in_=Ysq[:L].rearrange("p k (a i) -> p (k a) i", a=2), axis=AX.X, op=ALU.add)), reads=[YSK], writes=['st2'])
                        TS('dve', st1[:L, :], st1[:L, :], 1.0 / 64, None, ALU.mult, None, ['st1'], ['st1'])
                        TS('dve', st2[:L, :], st2[:L, :], 1.0 / 64, None, ALU.mult, None, ['st2'], ['st2'])
                        TT('dve', Ysq[:L, 0, 0:16], st1[:L, :], st1[:L, :], ALU.mult, ['st1', YSK], [YSK])
                        TT('dve', st2[:L, :], st2[:L, :], Ysq[:L, 0, 0:16], ALU.subtract, ['st2', YSK], ['st2'])
                        ACT(st2[:L, :], st2[:L, :], AF.Ln, ['st2', 'gnec'], ['st2'], bias=gnec[:L, 0:1])
                        ACT(st2[:L, :], st2[:L, :], AF.Exp, ['st2'], ['st2'], scale=-0.5)
                        TT('dve', Y3, Y3, st1[:L, :].rearrange("p (h o) -> p h o", o=1).broadcast_to([L, 16, 64]), ALU.subtract, ['Ytok', 'st1'], ['Ytok'])
                        TT('dve', Yn[:L].rearrange("p k (a i) -> p (k a) i", a=2), Y3, st2[:L, :].rearrange("p (h o) -> p h o", o=1).broadcast_to([L, 16, 64]), ALU.mult,
                           ['Ytok', 'st2'], ['Yn'])
                        for hf in range(2):
                            ps, pk = psum()
                            psb = ps.bitcast(BF16)
                            for dd in range(4):
                                d = hf * 4 + dd
                                TR(psb[:, dd * L:(dd + 1) * L], Yn[:L, d, :], ident_b[:L, :L], ['Yn', 'ident_b'], [pk])
                            for dd in range(4):
                                d = hf * 4 + dd
                                TS('dve', tA[:, d, :], psb[:, dd * L:(dd + 1) * L], vcol('r_gn_w', j * 8 + d), vcol('r_gn_b', j * 8 + d), ALU.mult, ALU.add, [pk, 'vecs'], ['tA'])
                        TT('dve', tA, tA, tB, ALU.add, ['tA', 'tB'], ['tA'])
                        TT('dve', yg, tA, g_, ALU.mult, ['tA', 'g_'], ['xm'])
                        proj('r_wo', yg, lambda d, ps, pk: TT('dve', h[:, d, t0:t0 + T], h[:, d, t0:t0 + T], ps[:, :T], ALU.add, [pk, 'h'], ['h']))
                except _Stop:
                    pass
                if CFG.get('rstage', 99) < 10:
                    S.barrier()
                    sb.reset(m)
                    return
                if smp:
                    DMA('sp', shift_s[j].rearrange("(k p) b -> p k b", p=128), sst, ['sst'], [('shift_s', j)])
                else:
                    DMA('sp', shift_p[j], carry, ['carry'], [('shift_p', j)])
                MSET('pool', bd, 0.0, ['bd'])
                for d in range(KC):
                    for b4 in range((nb + 3) // 4):
                        n4 = min(4, nb - b4 * 4)
                        for a in range(2):
                            sl = slice(a * 64, (a + 1) * 64)
                            CP('dve', bd[sl, 0:n4, a * 64:(a + 1) * 64], S0T[sl, d, b4 * 4:b4 * 4 + n4, :], [('S0T', d)], ['bd'])
                        ps, pk = psum()
                        for i4 in range(n4):
                            TR(ps[:, i4 * 128:(i4 + 1) * 128], bd[:, i4, :], ident_f, ['bd', 'cst'], [pk])
                        for a in range(2):
                            sl = slice(a * 64, (a + 1) * 64)
                            CP('dve', Ysq[sl, 0:n4, 0:64], ps[sl, :].rearrange("p (b n) -> p b n", b=4)[:, 0:n4, a * 64:(a + 1) * 64], [pk], [YSK])
                            if smp:
                                DMA('sp', wkv_s[j][b4 * 4:b4 * 4 + n4, 2 * d + a].rearrange("b i j -> i b j"), Ysq[sl, 0:n4, 0:64], [YSK], [('wkv_s', j, d, a, b4)])
                            else:
                                DMA('sp', wkv_p[j][2 * d + a], Ysq[sl, 0, 0:64], [YSK], [('wkv_p', j, d, a)])
                S.barrier()
                sb.reset(m)

            run(False)
            if CFG.get('rwkv_sample', True):
                run(True)


        for l in range(CFG['depth']):
            if CFG.get('dense', True):
                ffn(l, Wd_['ffn1_gate_up'], Wd_['ffn1_down'], 'norm_ffn1')
            if CFG['mixers']:
                if l % 2 == 0:
                    if CFG.get('mamba', True):
                        mamba(l)
                elif CFG.get('rwkv', False):
                    rwkv(l)
            if CFG.get('dense', True):
                ffn(l, Wd_['ffn2_gate_up'], Wd_['ffn2_down'], 'norm_ffn2')
                ple(l)
        final()
        S.add('sp', lambda e: e.nop(), reads=['yT', 'ssm_s', 'ssm_p', 'conv_s', 'conv_p', 'wkv_p', 'wkv_s', 'shift_p', 'shift_s'])
        S.emit()
    return nc


def pack_vecs(inp):
    cols = []
    for n in VECS:
        a = np.asarray(inp[n], dtype=np.float32)
        cols.append(np.ascontiguousarray(a.reshape(-1, 128).T))
    return np.ascontiguousarray(np.concatenate(cols, axis=1))


def kernel(**inp):
    inp = {k: np.asarray(v) for k, v in inp.items()}
    nc = build_program()
    vecs = pack_vecs(inp)
    consts = make_consts()
    hv = np.zeros((32, 4), np.float32)
    dbc = np.zeros((128, 64), np.float32)
    for j in range(2):
        hv[:, 2 * j] = inp['m_dt_bias'][j]
        hv[:, 2 * j + 1] = inp['m_A_log'][j]
        dbc[:, j * 32:(j + 1) * 32] = inp['m_D'][j][None, :]
    ncores = CFG['cores']
    in_maps = []
    for c in range(ncores):
        xs = inp['x_sample'][16 * c:16 * c + 16].reshape(NS, D)
        xc = np.concatenate([inp['x_prompt'][c], xs], axis=0)
        ps_ = inp['p_sample'][:, 16 * c:16 * c + 16].reshape(4, NS, 256)
        pc = np.concatenate([inp['p_prompt'][:, c], ps_], axis=1)
        m = {'xT': np.ascontiguousarray(xc.T), 'pT': np.ascontiguousarray(pc.transpose(0, 2, 1)), 'vecs': vecs,
             'consts': consts, 'hv': hv, 'dbc': dbc,
             'ssm_in': np.ascontiguousarray(inp['state_ssm'][:, 16 * c:16 * c + 16]),
             'conv_in': np.ascontiguousarray(inp['state_conv'][:, 16 * c:16 * c + 16].transpose(0, 3, 1, 2)),
             'wkv_in': np.ascontiguousarray(inp['state_wkv'][:, 16 * c:16 * c + 16]),
             'shift_in': np.ascontiguousarray(inp['state_shift'][:, 16 * c:16 * c + 16].transpose(0, 2, 1))}
        for n in WSHAPES:
            m[n] = np.ascontiguousarray(inp[n], dtype=np.float32)
        in_maps.append(m)
    res = run_bass_kernel_spmd(nc, in_maps, core_ids=list(range(ncores)))
    B = 8
    yp = np.zeros((B, NP, D), np.float32)
    ys = np.zeros((128, 4, D), np.float32)
    ssm_p = np.zeros((2, B, 32, 64, 128), np.float32)
    conv_p = np.zeros((2, B, 3, 4096), np.float32)
    wkv_p = np.zeros((2, B, 16, 64, 64), np.float32)
    shift_p = np.zeros((2, B, D), np.float32)
    ssm_s = np.zeros((2, 128, 32, 64, 128), np.float32)
    conv_s = np.zeros((2, 128, 3, 4096), np.float32)
    wkv_s = np.zeros((2, 128, 16, 64, 64), np.float32)
    shift_s = np.zeros((2, 128, D), np.float32)
    for c in range(ncores):
        r = res.results[c]
        y = r['yT'].T
        yp[c] = y[:NP]
        ys[16 * c:16 * c + 16] = y[NP:].reshape(16, 4, D)
        ssm_p[:, c] = r['ssm_p']
        conv_p[:, c] = r['conv_p'].transpose(0, 2, 1)
        wkv_p[:, c] = r['wkv_p']
        shift_p[:, c] = r['shift_p'].transpose(0, 2, 1).reshape(2, D)
        ssm_s[:, 16 * c:16 * c + 16] = r['ssm_s']
        conv_s[:, 16 * c:16 * c + 16] = r['conv_s'].transpose(0, 2, 3, 1)
        wkv_s[:, 16 * c:16 * c + 16] = r['wkv_s']
        shift_s[:, 16 * c:16 * c + 16] = r['shift_s'].transpose(0, 2, 1)
    return (yp, ys, ssm_p, conv_p, wkv_p, shift_p, ssm_s, conv_s, wkv_s, shift_s)
```

```python
import contextlib
import numpy as np
import concourse.bass as bass
import concourse.mybir as mybir
from concourse.bass_utils import run_bass_kernel_spmd

F32 = mybir.dt.float32
BF16 = mybir.dt.bfloat16
AF = mybir.ActivationFunctionType
ALU = mybir.AluOpType
AX = mybir.AxisListType

ENGS = ['pe', 'act', 'dve', 'pool', 'sp']


class Op:
    __slots__ = ('eng', 'fn', 'deps', 'signal', 'is_dma', 'sem', 'val', 'prewait', 'barriered')

    def __init__(self, eng, fn, is_dma):
        self.eng = eng
        self.fn = fn
        self.deps = []
        self.signal = False
        self.is_dma = is_dma
        self.sem = None
        self.val = 0
        self.prewait = None
        self.barriered = False


class Sched:
    def __init__(self, nc, n_dma_sems=20):
        self.nc = nc
        self.ops = {e: [] for e in ENGS}
        self.res = {}
        self.n_dma_sems = n_dma_sems

    def _states(self, key):
        if isinstance(key, tuple):
            name, sub = key[0], key[1:]
            if len(sub) == 0:
                sub = None
        else:
            name, sub = key, None
        d = self.res.setdefault(name, {})
        if sub is None:
            if None not in d:
                d[None] = [None, []]
            return list(d.values()), d[None], True
        if sub not in d:
            d[sub] = [None, []]
        sts = [d[sub]]
        if None in d:
            sts.append(d[None])
        return sts, d[sub], False

    def add(self, eng, fn, reads=(), writes=(), dma=False):
        op = Op(eng, fn, dma)
        deps = []
        for k in reads:
            sts, own, whole = self._states(k)
            for st in sts:
                if st[0] is not None:
                    deps.append(st[0])
        for k in writes:
            sts, own, whole = self._states(k)
            for st in sts:
                if st[0] is not None:
                    deps.append(st[0])
                deps.extend(st[1])
        for k in reads:
            sts, own, whole = self._states(k)
            own[1].append(op)
        for k in writes:
            sts, own, whole = self._states(k)
            if whole:
                for st in sts:
                    st[0] = None
                    st[1] = []
            own[0] = op
            own[1] = []
        seen = set()
        for d in deps:
            if d is op or id(d) in seen:
                continue
            seen.add(id(d))
            if d.eng == eng and eng == 'pe' and not d.is_dma and not dma:
                continue
            op.deps.append(d)
            d.signal = True
        self.ops[eng].append(op)
        return op

    def barrier(self):
        last = []
        for e in ENGS:
            got = False
            for o in reversed(self.ops[e]):
                if o.barriered:
                    break
                if o.is_dma:
                    last.append(o)
                elif not got:
                    last.append(o)
                    got = True
        b = Op('sp', lambda e: e.nop(), False)
        for d in last:
            b.deps.append(d)
            d.signal = True
        for e in ENGS:
            for o in reversed(self.ops[e]):
                if o.barriered:
                    break
                o.barriered = True
        b.barriered = True
        self.ops['sp'].append(b)
        for e in ENGS:
            if e == 'sp':
                continue
            o = Op(e, None, False)
            o.barriered = True
            o.deps.append(b)
            b.signal = True
            self.ops[e].append(o)
        self.res = {}

    def emit(self):
        nc = self.nc
        with contextlib.ExitStack() as es:
            csem = {e: es.enter_context(nc.semaphore('c_' + e)) for e in ENGS}
            dsems = {e: [es.enter_context(nc.semaphore('d_%s_%d' % (e, i)))
                         for i in range(self.n_dma_sems)] for e in ('sp', 'pool', 'act')}
            for e in ENGS:
                cnt = 0
                dcnt = [0] * self.n_dma_sems
                rr = 0
                for op in self.ops[e]:
                    if op.is_dma:
                        s = rr % self.n_dma_sems
                        rr += 1
                        if dcnt[s] > 0:
                            op.prewait = (dsems[e][s], dcnt[s])
                        dcnt[s] += 16
                        op.sem = dsems[e][s]
                        op.val = dcnt[s]
                    elif op.signal:
                        cnt += 1
                        op.sem = csem[e]
                        op.val = cnt
            engobj = {'pe': 'tensor', 'act': 'scalar', 'dve': 'vector', 'pool': 'gpsimd', 'sp': 'sync'}

            def run(e, eng):
                seen = {}
                for op in self.ops[e]:
                    waits = []
                    if op.prewait is not None:
                        waits.append(op.prewait)
                    for d in op.deps:
                        waits.append((d.sem, d.val))
                    mx = {}
                    for (s, v) in waits:
                        k = id(s)
                        if v > mx.get(k, (None, 0))[1]:
                            mx[k] = (s, v)
                    for k, (s, v) in mx.items():
                        if seen.get(k, 0) >= v:
                            continue
                        seen[k] = v
                        eng.wait_ge(s, v)
                    if op.fn is None:
                        continue
                    ins = op.fn(eng)
                    if op.is_dma:
                        ins.then_inc(op.sem, 16)
                    elif op.signal:
                        ins.then_inc(op.sem, 1)

            with nc.Block() as block:
                for e in ENGS:
                    if not self.ops[e]:
                        continue
                    getattr(block, engobj[e])(lambda eng, e=e: run(e, eng))


class SB:
    def __init__(self, big, nbytes):
        self.big = big
        self.nbytes = nbytes
        self.off = 0
        self.views = {}

    def view(self, dtype):
        if dtype not in self.views:
            self.views[dtype] = self.big.bitcast(dtype) if dtype != F32 else self.big
        return self.views[dtype]

    def alloc(self, cols, dtype, parts=128):
        sz = mybir.dt.size(dtype)
        nb = (cols * sz + 63) // 64 * 64
        assert self.off + nb <= self.nbytes, ('SBUF overflow', self.off, nb, self.nbytes)
        o = self.off // sz
        self.off += nb
        return self.view(dtype)[0:parts, o:o + cols]

    def mark(self):
        return self.off

    def reset(self, m):
        self.off = m


D = 1024
KC = 8
NP = 2048
NS = 64
NT = NP + NS
TILES = [(0, 512), (512, 512), (1024, 512), (1536, 512), (2048, 64)]
DFF = 2816
FGROUPS = [(0, 4), (4, 4), (8, 4), (12, 4), (16, 3), (19, 3)]
DEPTH = 4
EPS = 1e-6

WSHAPES = {
    'ffn1_gate_up': (4, 1024, 5632), 'ffn1_down': (4, 2816, 1024),
    'ffn2_gate_up': (4, 1024, 5632), 'ffn2_down': (4, 2816, 1024),
    'ple_in': (4, 256, 1024), 'ple_gate': (4, 1024, 1024),
    'm_in_proj': (2, 1024, 6176), 'm_out_proj': (2, 2048, 1024),
    'r_wr': (2, 1024, 1024), 'r_wk': (2, 1024, 1024), 'r_wv': (2, 1024, 1024), 'r_wo': (2, 1024, 1024),
    'r_w1': (2, 1024, 64), 'r_w2': (2, 64, 1024), 'r_a1': (2, 1024, 64), 'r_a2': (2, 64, 1024),
    'r_g1': (2, 1024, 160), 'r_g2': (2, 160, 1024), 'r_v1': (1, 1024, 32), 'r_v2': (1, 32, 1024),
}
VECS = ['norm_ffn1', 'norm_mix', 'norm_ffn2', 'norm_ple', 'norm_final', 'm_conv_w', 'm_conv_b', 'm_norm',
        'r_mu', 'r_w0', 'r_a0', 'r_k_k', 'r_k_a', 'r_r_k', 'r_gn_w', 'r_gn_b', 'r_v0']
VSHAPES = {'norm_ffn1': (4, 1024), 'norm_mix': (4, 1024), 'norm_ffn2': (4, 1024), 'norm_ple': (4, 1024),
           'norm_final': (1024,), 'm_conv_w': (2, 4, 4096), 'm_conv_b': (2, 4096), 'm_norm': (2, 2048),
           'r_mu': (2, 6, 1024), 'r_w0': (2, 1024), 'r_a0': (2, 1024), 'r_k_k': (2, 1024), 'r_k_a': (2, 1024),
           'r_r_k': (2, 1024), 'r_gn_w': (2, 1024), 'r_gn_b': (2, 1024), 'r_v0': (1, 1024)}
VOFF = {}
_o = 0
for _n in VECS:
    VOFF[_n] = _o
    _o += int(np.prod(VSHAPES[_n])) // 128
NV = _o

C_ID, C_TRIP, C_ONES, C_TRIS, C_BLKS, C_BM, C_BMT = 0, 128, 256, 384, 448, 512, 528
C_STRIP = C_BMT + 1024
C_LOWP = C_STRIP + 128
C_STRIS = C_LOWP + 128
C_LOWS = C_STRIS + 64
C_MABP = C_LOWS + 64
C_MABS = C_MABP + 256
C_RSTP = C_MABS + 128
C_RSTS = C_RSTP + 128
C_BLK2 = C_RSTS + 64
NCC = C_BLK2 + 128

CFG = {'mixers': True, 'depth': DEPTH, 'cores': 8, 'rwkv': True}


class _Stop(Exception):
    pass


def _stg(n):
    if CFG.get('rstage', 99) < n:
        raise _Stop()


def make_consts():
    c = np.zeros((128, NCC), np.float32)
    i = np.arange(128)
    c[:, C_ID:C_ID + 128] = np.eye(128)
    c[:, C_TRIP:C_TRIP + 128] = (i[:, None] <= i[None, :])
    c[:, C_ONES:C_ONES + 128] = 1.0
    j = np.arange(64)
    same = (j[:, None] // 4) == (j[None, :] // 4)
    c[:64, C_TRIS:C_TRIS + 64] = same & (j[:, None] <= j[None, :])
    c[:64, C_BLKS:C_BLKS + 64] = same
    c[:64, C_BM:C_BM + 16] = (j[:, None] // 4) == np.arange(16)[None, :]
    bmt = ((j[None, :] // 4) == np.arange(16)[:, None]).astype(np.float32).reshape(1, 1024)
    c[:, C_BMT:C_BMT + 1024] = bmt
    c[:, C_STRIP:C_STRIP + 128] = (i[:, None] < i[None, :])
    c[:, C_LOWP:C_LOWP + 128] = (i[None, :] < i[:, None])
    c[:64, C_STRIS:C_STRIS + 64] = same & (j[:, None] < j[None, :])
    c[:64, C_LOWS:C_LOWS + 64] = same & (j[None, :] < j[:, None])
    c[:, C_MABP:C_MABP + 128] = c[:, C_STRIP:C_STRIP + 128]
    c[:, C_MABP + 128:C_MABP + 256] = c[:, C_TRIP:C_TRIP + 128]
    c[:64, C_MABS:C_MABS + 64] = c[:64, C_STRIS:C_STRIS + 64]
    c[:64, C_MABS + 64:C_MABS + 128] = c[:64, C_TRIS:C_TRIS + 64]
    c[:, C_RSTP:C_RSTP + 128] = 1.0
    c[:, C_RSTP] = 0.0
    c[:, C_RSTS:C_RSTS + 64] = (np.arange(64) % 4 != 0)[None, :]
    c[:, C_BLK2:C_BLK2 + 128] = (i[:, None] // 64) == (i[None, :] // 64)
    return c


def build_program():
    nc = bass.Bass("TRN2", target_bir_lowering=False)

    def din(name, shape):
        return nc.dram_tensor(name, list(shape), F32, kind="ExternalInput").ap()

    def dout(name, shape):
        return nc.dram_tensor(name, list(shape), F32, kind="ExternalOutput").ap()

    xT = din('xT', [D, NT])
    pT = din('pT', [4, 256, NT])
    vecs_d = din('vecs', [128, NV])
    Wd_ = {n: din(n, s) for n, s in WSHAPES.items()}
    yT = dout('yT', [D, NT])
    consts_d = din('consts', [128, NCC])
    hv_d = din('hv', [32, 4])
    dbc_d = din('dbc', [128, 64])
    ssm_in = din('ssm_in', [2, 16, 32, 64, 128])
    conv_in = din('conv_in', [2, 4096, 16, 3])
    wkv_in = din('wkv_in', [2, 16, 16, 64, 64])
    shift_in = din('shift_in', [2, 1024, 16])
    ssm_p = dout('ssm_p', [2, 32, 64, 128])
    conv_p = dout('conv_p', [2, 4096, 3])
    ssm_s = dout('ssm_s', [2, 16, 32, 64, 128])
    conv_s = dout('conv_s', [2, 4096, 16, 3])
    wkv_p = dout('wkv_p', [2, 16, 64, 64])
    shift_p = dout('shift_p', [2, 128, 8])
    wkv_s = dout('wkv_s', [2, 16, 16, 64, 64])
    shift_s = dout('shift_s', [2, 1024, 16])
    vfirst_d = dout('vfirst', [D, NT])

    with contextlib.ExitStack() as es:
        NB = 206 * 1024
        big = es.enter_context(nc.sbuf_tensor("big", [128, NB // 4], F32))
        psl = [es.enter_context(nc.psum_tensor("ps%d" % i, [128, 512], F32)) for i in range(8)]
        sb = SB(big, NB)
        S = Sched(nc)
        pctr = [0]

        def psum():
            i = pctr[0] % 8
            pctr[0] += 1
            return psl[i], ('ps', i)

        def MM(out, lhsT, rhs, start, stop, r, w):
            S.add('pe', lambda e: e.matmul(out, lhsT=lhsT, rhs=rhs, start=start, stop=stop), reads=r, writes=w)

        def TR(out, in_, ident, r, w):
            S.add('pe', lambda e: e.transpose(out, in_, ident), reads=r, writes=w)

        def ACT(out, in_, func, r, w, bias=None, scale=1.0):
            if bias is None:
                S.add('act', lambda e: e.activation(out=out, in_=in_, func=func, scale=scale), reads=r, writes=w)
            else:
                S.add('act', lambda e: e.activation(out=out, in_=in_, func=func, bias=bias, scale=scale), reads=r, writes=w)

        def TT(eng, out, in0, in1, op, r, w):
            S.add(eng, lambda e: e.tensor_tensor(out=out, in0=in0, in1=in1, op=op), reads=r, writes=w)

        def TS(eng, out, in0, s1, s2, op0, op1, r, w):
            if s2 is None:
                S.add(eng, lambda e: e.tensor_scalar(out=out, in0=in0, scalar1=s1, scalar2=None, op0=op0), reads=r, writes=w)
            else:
                S.add(eng, lambda e: e.tensor_scalar(out=out, in0=in0, scalar1=s1, scalar2=s2, op0=op0, op1=op1), reads=r, writes=w)

        def STT(eng, out, in0, scalar, in1, op0, op1, r, w):
            S.add(eng, lambda e: e.scalar_tensor_tensor(out=out, in0=in0, scalar=scalar, in1=in1, op0=op0, op1=op1), reads=r, writes=w)

        def CP(eng, out, in_, r, w):
            S.add(eng, lambda e: e.tensor_copy(out=out, in_=in_), reads=r, writes=w)

        def MSET(eng, ap, val, w):
            S.add(eng, lambda e: e.memset(ap, val), writes=w)

        def DMA(eng, out, in_, r, w):
            S.add(eng, lambda e: e.dma_start(out=out, in_=in_), reads=r, writes=w, dma=True)

        h = sb.alloc(KC * NT, F32).rearrange("p (k t) -> p k t", k=KC)
        vecs = sb.alloc(NV, F32)
        ones_bf = sb.alloc(128, BF16)
        epsc = sb.alloc(1, F32)
        cst = sb.alloc(NCC, F32)
        hv = sb.alloc(4, F32, parts=32)
        dbc = sb.alloc(64, F32)
        onec = sb.alloc(1, F32)
        eps5c = sb.alloc(1, F32)
        ident_b = sb.alloc(128, BF16)
        trip_b = sb.alloc(128, BF16)
        tris_b = sb.alloc(64, BF16)
        bmT_b = sb.alloc(1024, BF16)
        DMA('sp', vecs, vecs_d, [], ['vecs'])
        DMA('sp', cst, consts_d, [], ['cst'])
        DMA('sp', hv, hv_d, [], ['hv'])
        DMA('sp', dbc, dbc_d, [], ['dbc'])
        MSET('dve', onec, 1.0, ['onec'])
        MSET('dve', eps5c, 1e-5, ['eps5c'])
        CP('dve', ident_b, cst[:, C_ID:C_ID + 128], ['cst'], ['ident_b'])
        CP('dve', trip_b, cst[:, C_TRIP:C_TRIP + 128], ['cst'], ['trip_b'])
        CP('dve', tris_b, cst[:, C_TRIS:C_TRIS + 64], ['cst'], ['tris_b'])
        CP('dve', bmT_b, cst[:, C_BMT:C_BMT + 1024], ['cst'], ['bmT_b'])
        ident_f = cst[:, C_ID:C_ID + 128]
        DMA('sp', h, xT.rearrange("(k p) t -> p k t", p=128), [], ['h'])
        MSET('dve', ones_bf, 1.0, ['ones_bf'])
        MSET('dve', epsc, EPS, ['epsc'])

        def vcol(name, idx):
            o = VOFF[name] + idx
            return vecs[:, o:o + 1]

        scr0 = sb.mark()
        S.barrier()

        def rmsnorm(t0, T, gname, gbase, out, okey, sq, rs, ti, hkey=None, sqkey=('sq',)):
            hk = hkey if hkey is not None else ('h', ti)
            ACT(sq[:, :, :T], h[:, :, t0:t0 + T], AF.Square, [hk], [sqkey])
            ps, pk = psum()
            for k in range(KC):
                MM(ps[:, :T], ones_bf, sq[:, k, :T], k == 0, k == KC - 1, [sqkey, 'ones_bf'], [pk])
            ACT(rs[:, :T], ps[:, :T], AF.Ln, [pk, 'epsc'], [('rs',)], bias=epsc[:, 0:1], scale=1.0 / D)
            ACT(rs[:, :T], rs[:, :T], AF.Exp, [('rs',)], [('rs',)], scale=-0.5)
            for k in range(KC):
                STT('dve', out[:, k, :T], h[:, k, t0:t0 + T], vcol(gname, gbase + k), rs[:, :T], ALU.mult, ALU.mult,
                    [hk, ('rs',), 'vecs'], [okey])

        def ffn(l, wgu_d, wd_d, gname):
            m = sb.mark()
            xn = sb.alloc(KC * NT, BF16).rearrange("p (k t) -> p k t", k=KC)
            act = sb.alloc(4 * NT, BF16).rearrange("p (f t) -> p f t", f=4)
            wgu = [sb.alloc(KC * 2 * 512, BF16).rearrange("p (k g n) -> p k g n", k=KC, g=2) for _ in range(2)]
            wdb = [sb.alloc(4 * 1024, BF16).rearrange("p (f n) -> p f n", f=4) for _ in range(2)]
            sq = sb.alloc(KC * 512, BF16).rearrange("p (k t) -> p k t", k=KC)
            rs = sb.alloc(512, F32)
            sg = [sb.alloc(512, F32) for _ in range(2)]
            for ti, (t0, T) in enumerate(TILES):
                rmsnorm(t0, T, gname, l * KC, xn[:, :, t0:t0 + T], ('xn', ti), sq, rs, ti)
            wg_v = wgu_d[l].rearrange("(k p) n -> p k n", p=128)
            wd_v = wd_d[l].rearrange("(f p) n -> p f n", p=128)
            cnt = 0
            for gi, (f0, nf) in enumerate(FGROUPS):
                wb = wgu[gi % 2]
                wdd = wdb[gi % 2]
                DMA('pool', wb[:, :, 0, :nf * 128], wg_v[:, :, f0 * 128:(f0 + nf) * 128], [], [('wgu', gi % 2)])
                DMA('pool', wb[:, :, 1, :nf * 128], wg_v[:, :, DFF + f0 * 128:DFF + (f0 + nf) * 128], [], [('wgu', gi % 2)])
                DMA('pool', wdd[:, :nf, :], wd_v[:, f0:f0 + nf, :], [], [('wd', gi % 2)])
                for ti, (t0, T) in enumerate(TILES):
                    for f in range(nf):
                        pg, pgk = psum()
                        pu, puk = psum()
                        for k in range(KC):
                            MM(pg[:, :T], wb[:, k, 0, f * 128:(f + 1) * 128], xn[:, k, t0:t0 + T], k == 0, k == KC - 1,
                               [('wgu', gi % 2), ('xn', ti)], [pgk])
                        for k in range(KC):
                            MM(pu[:, :T], wb[:, k, 1, f * 128:(f + 1) * 128], xn[:, k, t0:t0 + T], k == 0, k == KC - 1,
                               [('wgu', gi % 2), ('xn', ti)], [puk])
                        sgb = sg[cnt % 2]
                        sk = ('sg', cnt % 2)
                        cnt += 1
                        ACT(sgb[:, :T], pg[:, :T], AF.Silu, [pgk], [sk])
                        TT('dve', act[:, f, t0:t0 + T], sgb[:, :T], pu[:, :T], ALU.mult, [sk, puk], [('act', f, ti)])
                for ti, (t0, T) in enumerate(TILES):
                    for d in range(KC):
                        po, pok = psum()
                        for f in range(nf):
                            MM(po[:, :T], wdd[:, f, d * 128:(d + 1) * 128], act[:, f, t0:t0 + T], f == 0, f == nf - 1,
                               [('wd', gi % 2), ('act', f, ti)], [pok])
                        STT('dve', h[:, d, t0:t0 + T], po[:, :T], 0.5, h[:, d, t0:t0 + T], ALU.mult, ALU.add,
                            [pok, ('h', ti)], [('h', ti)])
            S.barrier()
            sb.reset(m)

        def ple(l):
            m = sb.mark()
            xn = sb.alloc(KC * NT, BF16).rearrange("p (k t) -> p k t", k=KC)
            wg = sb.alloc(KC * 1024, BF16).rearrange("p (k n) -> p k n", k=KC)
            wpi = sb.alloc(2 * 1024, BF16).rearrange("p (k n) -> p k n", k=2)
            ptb = sb.alloc(2 * NT, BF16).rearrange("p (k t) -> p k t", k=2)
            sq = sb.alloc(KC * 512, BF16).rearrange("p (k t) -> p k t", k=KC)
            rs = sb.alloc(512, F32)
            sg = [sb.alloc(512, F32) for _ in range(2)]
            DMA('pool', wg, Wd_['ple_gate'][l].rearrange("(k p) n -> p k n", p=128), [], ['wg'])
            DMA('pool', wpi, Wd_['ple_in'][l].rearrange("(k p) n -> p k n", p=128), [], ['wpi'])
            DMA('pool', ptb, pT[l].rearrange("(k p) t -> p k t", p=128), [], ['ptb'])
            for ti, (t0, T) in enumerate(TILES):
                rmsnorm(t0, T, 'norm_ple', l * KC, xn[:, :, t0:t0 + T], ('xn', ti), sq, rs, ti)
            cnt = 0
            for ti, (t0, T) in enumerate(TILES):
                for d in range(KC):
                    p1, p1k = psum()
                    p2, p2k = psum()
                    for k in range(KC):
                        MM(p1[:, :T], wg[:, k, d * 128:(d + 1) * 128], xn[:, k, t0:t0 + T], k == 0, k == KC - 1,
                           ['wg', ('xn', ti)], [p1k])
                    for k in range(2):
                        MM(p2[:, :T], wpi[:, k, d * 128:(d + 1) * 128], ptb[:, k, t0:t0 + T], k == 0, k == 1,
                           ['wpi', 'ptb'], [p2k])
                    sgb = sg[cnt % 2]
                    sk = ('sg', cnt % 2)
                    cnt += 1
                    ACT(sgb[:, :T], p1[:, :T], AF.Sigmoid, [p1k], [sk])
                    TT('dve', sgb[:, :T], sgb[:, :T], p2[:, :T], ALU.mult, [sk, p2k], [sk])
                    TT('pool', h[:, d, t0:t0 + T], h[:, d, t0:t0 + T], sgb[:, :T], ALU.add, [sk, ('h', ti)], [('h', ti)])
            S.barrier()
            sb.reset(m)

        def final():
            m = sb.mark()
            sq = sb.alloc(KC * 512, BF16).rearrange("p (k t) -> p k t", k=KC)
            rs = sb.alloc(512, F32)
            yo = [sb.alloc(KC * 512, F32).rearrange("p (k t) -> p k t", k=KC) for _ in range(2)]
            yv = yT.rearrange("(k p) t -> p k t", p=128)
            for ti, (t0, T) in enumerate(TILES):
                yb = yo[ti % 2]
                rmsnorm(t0, T, 'norm_final', 0, yb, ('yo', ti % 2), sq, rs, ti)
                DMA('sp', yv[:, :, t0:t0 + T], yb[:, :, :T], [('yo', ti % 2)], [('yT', ti)])
            sb.reset(m)

        def mamba(l):
            j = l // 2
            Win = Wd_['m_in_proj'][j].rearrange("(k p) n -> p k n", p=128)
            Wout = Wd_['m_out_proj'][j].rearrange("(c p) n -> p c n", p=128)
            R0 = ['cst', 'vecs', 'hv', 'dbc', 'onec', 'eps5c', 'ident_b', 'trip_b', 'tris_b', 'bmT_b']
            cw = [0]

            def run(smp):
                m = sb.mark()
                T = 64 if smp else 256
                L = 64 if smp else 128
                nch = T // L
                tiles = [(2048, 64)] if smp else [(i * 256, 256) for i in range(8)]
                tri = cst[:L, C_TRIS:C_TRIS + 64] if smp else cst[:, C_TRIP:C_TRIP + 128]
                onesm = cst[:L, C_BLKS:C_BLKS + 64] if smp else cst[:, C_ONES:C_ONES + 128]
                cmask = tris_b[:L, :] if smp else trip_b
                u = sb.alloc(KC * T, BF16).rearrange("p (k t) -> p k t", k=KC)
                rs = sb.alloc(T, F32)
                NWB = 4
                wblk = [sb.alloc(KC * 256, BF16).rearrange("p (k n) -> p k n", k=KC) for _ in range(NWB)]
                wdt = sb.alloc(KC * 32, BF16).rearrange("p (k n) -> p k n", k=KC)
                zs = sb.alloc(nch * 2048, BF16).rearrange("p (c n) -> p c n", c=nch)
                PW = 112 if smp else 3 + T
                preb = [sb.alloc(PW, F32) for _ in range(2)]
                accb = [sb.alloc(T, F32) for _ in range(2)]
                xa = sb.alloc(32 * T, BF16).rearrange("p (c t) -> p c t", c=32)
                dtT = sb.alloc(T, F32, parts=32)
                dtAT = sb.alloc(T, F32, parts=32)
                expA = sb.alloc(1, F32, parts=32)
                xtok = sb.alloc(2048, BF16)
                Btok = sb.alloc(1024, BF16)
                dtk = sb.alloc(64, F32)
                nak = sb.alloc(64, F32)
                ena = sb.alloc(32, F32)
                dend = sb.alloc(32, F32)
                xdt = sb.alloc(2048, BF16)
                sq = xdt[:, 0:KC * T].rearrange("p (k t) -> p k t", k=KC)
                NSET = 1 if smp else 4
                gsets = []
                for _si in range(NSET):
                    gsets.append(dict(
                        cbm=sb.alloc(128, BF16),
                        dexp=sb.alloc(4 * 128, F32).rearrange("p (a b) -> p a b", a=4),
                        dec=[sb.alloc(128, F32) for _ in range(2)],
                        MT=sb.alloc(4 * 128, BF16).rearrange("p (a b) -> p a b", a=4),
                        tmp=sb.alloc(256, F32), yg=sb.alloc(256, F32), ssq=sb.alloc(1, F32),
                        ynk=sb.alloc(256, BF16), xdd=sb.alloc(256, BF16)))
                print('MAMBA sets alloc off', sb.off, sb.nbytes, smp)
                ynT = sb.alloc(16 * T, BF16).rearrange("p (c t) -> p c t", c=16)
                wob = [sb.alloc(16 * 128, BF16).rearrange("p (c n) -> p c n", c=16) for _ in range(2)]
                if smp:
                    cstate = sb.alloc(32 * 48, F32).rearrange("p (c b r) -> p c b r", c=32, b=16)
                    cso = sb.alloc(32 * 48, F32).rearrange("p (c b r) -> p c b r", c=32, b=16)
                    nat = sb.alloc(16 * 256, F32).rearrange("p (b q n) -> p b q n", b=16, q=2)
                    STs = sb.alloc(16 * 256, F32).rearrange("p (b n) -> p b n", b=16)
                    STsb = sb.alloc(16 * 256, BF16).rearrange("p (b n) -> p b n", b=16)
                    CTm = sb.alloc(16 * 64, BF16).rearrange("p (b t) -> p b t", b=16)
                    xddm = sb.alloc(16 * 256, BF16).rearrange("p (b n) -> p b n", b=16)
                    edCs = sb.alloc(512, F32).rearrange("p (b h) -> p b h", b=16)
                    dtAe = sb.alloc(512, F32).rearrange("p (b h) -> p b h", b=16)
                    DMA('sp', cstate, conv_in[j].rearrange("(c p) b r -> p c b r", p=128), [], ['cstate'])
                else:
                    carry = sb.alloc(96, F32).rearrange("p (c r) -> p c r", c=32)
                    ST = sb.alloc(2048, F32)
                    STb = sb.alloc(2048, BF16)
                    natp = sb.alloc(512, F32).rearrange("p (q n) -> p q n", q=4)
                    edC = sb.alloc(32, F32)
                    MSET('dve', carry, 0.0, ['carry'])
                    MSET('dve', ST, 0.0, ['ST'])
                    MSET('pool', STb, 0.0, ['STb'])
                ACT(expA, hv[:, 2 * j + 1:2 * j + 2], AF.Exp, ['hv'], ['expA'])

                mcols = [zb * 256 for zb in range(8)] + [2048 + xb * 256 for xb in range(16)]
                mseq = [i for _t in tiles for i in range(24)]
                mnext = [0]
                mq = []

                def missue(i):
                    wb = wblk[cw[0] % NWB]
                    wk = ('wblk', cw[0] % NWB)
                    cw[0] += 1
                    DMA('pool', wb, Win[:, :, mcols[i]:mcols[i] + 256], [], [wk])
                    return wb, wk

                def mfill():
                    while len(mq) < NWB - 1 and mnext[0] < len(mseq):
                        mq.append(missue(mseq[mnext[0]]))
                        mnext[0] += 1

                def mget():
                    mfill()
                    x = mq.pop(0)
                    mfill()
                    return x

                for (t0, T_) in tiles:
                    DMA('pool', wdt, Win[:, :, 6144:6176], [], ['wdt'])
                    mfill()
                    rmsnorm(t0, T, 'norm_mix', l * KC, u, ('u',), sq, rs, 0, hkey='h', sqkey='xdt')
                    for zb in range(8):
                        wb, wk = mget()
                        for tc in range(nch):
                            ps, pk = psum()
                            for k in range(KC):
                                MM(ps[:L, 0:256], u[:, k, tc * L:(tc + 1) * L], wb[:, k, :], k == 0, k == KC - 1, [('u',), wk], [pk])
                            ACT(zs[:L, tc, zb * 256:(zb + 1) * 256], ps[:L, 0:256], AF.Silu, [pk], [('zs', tc, zb)])
                    for xb in range(16):
                        wb, wk = mget()
                        def cc_gen(cc, q):
                            ps, pk = psum()
                            for k in range(KC):
                                MM(ps[:, :T], wb[:, k, q * 128:(q + 1) * 128], u[:, k, :], k == 0, k == KC - 1, [('u',), wk], [pk])
                            pre = preb[cc % 2]
                            prk = ('pre', cc % 2)
                            acc = accb[cc % 2]
                            ack = ('acc', cc % 2)
                            ce = 'dve'
                            if smp:
                                prev = pre.rearrange("p (b c) -> p b c", c=7)
                                ACT(prev[:, :, 0:3], cstate[:, cc], AF.Copy, ['cstate'], [prk])
                                ACT(prev[:, :, 3:7], ps[:, :T].rearrange("p (b t) -> p b t", t=4), AF.Copy, [pk], [prk])
                                win = lambda k_: prev[:, :, k_:k_ + 4]
                                accv = acc.rearrange("p (b t) -> p b t", t=4)
                                xav = xa[:, cc, :].rearrange("p (b t) -> p b t", t=4)
                            else:
                                ACT(pre[:, 0:3], carry[:, cc, :], AF.Copy, [('carry', cc)], [prk])
                                ACT(pre[:, 3:3 + T], ps[:, :T], AF.Copy, [pk], [prk])
                                win = lambda k_: pre[:, k_:k_ + T]
                                accv = acc
                                xav = xa[:, cc, :]
                            yield
                            TS(ce, accv, win(0), vcol('m_conv_w', (j * 4 + 0) * 32 + cc), None, ALU.mult, None, [prk, 'vecs'], [ack])
                            for k_ in range(1, 4):
                                yield
                                STT(ce, accv, win(k_), vcol('m_conv_w', (j * 4 + k_) * 32 + cc), accv, ALU.mult, ALU.add,
                                    [prk, 'vecs', ack], [ack])
                            yield
                            ACT(xav, accv, AF.Silu, [ack, 'vecs'], [('xa', cc)], bias=vcol('m_conv_b', j * 32 + cc))
                            if smp:
                                ACT(cso[:, cc], prev[:, :, 4:7], AF.Copy, [prk], [('cso', cc)])
                            else:
                                ACT(carry[:, cc, :], pre[:, T:T + 3], AF.Copy, [prk], [('carry', cc)])
                            yield
                        cgs = [cc_gen(xb * 2 + q, q) for q in range(2)]
                        alive = True
                        while alive:
                            alive = False
                            for cg in cgs:
                                try:
                                    next(cg)
                                    alive = True
                                except StopIteration:
                                    pass
                    ps, pk = psum()
                    for k in range(KC):
                        MM(ps[:32, :T], wdt[:, k, :], u[:, k, :], k == 0, k == KC - 1, [('u',), 'wdt'], [pk])
                    ACT(dtT, ps[:32, :T], AF.Exp, [pk, 'hv'], ['dtT'], bias=hv[:, 2 * j:2 * j + 1])
                    ACT(dtT, dtT, AF.Ln, ['dtT', 'onec'], ['dtT'], bias=onec[0:32, 0:1])
                    TS('dve', dtAT, dtT, expA[:, 0:1], None, ALU.mult, None, ['dtT', 'expA'], ['dtAT'])

                    for tc in range(nch):
                        c0 = tc * L
                        for half in range(2):
                            ps, pk = psum()
                            psb = ps.bitcast(BF16)
                            for q in range(8):
                                cc = half * 8 + q
                                TR(psb[:L, q * 128:(q + 1) * 128], xa[:, cc, c0:c0 + L], ident_b, [('xa', cc), 'ident_b'], [pk])
                            CP('dve', xtok[:L, half * 1024:(half + 1) * 1024], psb[:L, :], [pk], [('xtok', half)])
                        ps, pk = psum()
                        psb = ps.bitcast(BF16)
                        for g in range(8):
                            TR(psb[:L, g * 128:(g + 1) * 128], xa[:, 16 + g, c0:c0 + L], ident_b, [('xa', 16 + g), 'ident_b'], [pk])
                        CP('dve', Btok[:L, :], psb[:L, :], [pk], ['Btok'])
                        ps, pk = psum()
                        TR(ps[:L, 0:32], dtT[:, c0:c0 + L], ident_f[0:32, 0:32], ['dtT', 'cst'], [pk])
                        TR(ps[:L, 32:64], dtAT[:, c0:c0 + L], ident_f[0:32, 0:32], ['dtAT', 'cst'], [pk])
                        CP('dve', dtk[:L, :], ps[:L, 0:64], [pk], ['dtk'])
                        ps, pk = psum()
                        MM(ps[:L, 0:32], tri, dtk[:L, 32:64], True, True, ['cst', 'dtk'], [pk])
                        MM(ps[:L, 32:64], onesm, dtk[:L, 32:64], True, True, ['cst', 'dtk'], [pk])
                        CP('dve', nak[:L, :], ps[:L, 0:64], [pk], ['nak'])
                        ACT(ena[:L, :], nak[:L, 0:32], AF.Exp, ['nak'], ['ena'], scale=-1.0)
                        TT('dve', dend[:L, :], nak[:L, 0:32], nak[:L, 32:64], ALU.subtract, ['nak'], ['dend'])
                        ACT(dend[:L, :], dend[:L, :], AF.Exp, ['dend'], ['dend'])
                        TT('dve', xdt[:L, :].rearrange("p (h d) -> p h d", h=32), xtok[:L, :].rearrange("p (h d) -> p h d", h=32),
                           dtk[:L, 0:32].rearrange("p (h o) -> p h o", o=1).broadcast_to([L, 32, 64]), ALU.mult, ['xtok', 'dtk'], ['xdt'])
                        if smp:
                            TT('dve', dtAe[:L], dtk[:L, 32:64].rearrange("p (o h) -> p o h", o=1).broadcast_to([L, 16, 32]),
                               cst[:L, C_BM:C_BM + 16].rearrange("p (b o) -> p b o", o=1).broadcast_to([L, 16, 32]), ALU.mult, ['dtk', 'cst'], ['dtAe'])
                            ps, pk = psum()
                            MM(ps[:, 0:512], cst[:L, C_ONES:C_ONES + 128], dtAe[:L].rearrange("p b h -> p (b h)"), True, True, ['cst', 'dtAe'], [pk])
                            ACT(edCs.rearrange("p b h -> p (b h)"), ps[:, 0:512], AF.Exp, [pk], ['edCs'], scale=-1.0)
                        else:
                            ACT(edC, nak[:, 32:64], AF.Exp, ['nak'], ['edC'], scale=-1.0)

                        def group_gen(g, si):
                            gs_ = gsets[si]
                            cbm, dexp, dec, MT, tmp, yg, ssq, ynk, xdd = (gs_['cbm'], gs_['dexp'], gs_['dec'], gs_['MT'], gs_['tmp'],
                                                                          gs_['yg'], gs_['ssq'], gs_['ynk'], gs_['xdd'])
                            BT = xa[:, 16 + g, c0:c0 + L]
                            CT = xa[:, 24 + g, c0:c0 + L]
                            if smp:
                                for q_ in range(2):
                                    DMA('sp', nat[:, :, q_, :], ssm_in[j][:, 4 * g + 2 * q_:4 * g + 2 * q_ + 2].rearrange("b a p n -> (a p) b n"), [], ['nat'])
                                for b4 in range(8):
                                    ps, pk = psum()
                                    for i4 in range(4):
                                        b_ = b4 * 2 + i4 // 2
                                        q_ = i4 % 2
                                        TR(ps[:, i4 * 128:(i4 + 1) * 128], nat[:, b_, q_, :], ident_f, ['nat', 'cst'], [pk])
                                    CP('dve', STs[:, b4 * 2:b4 * 2 + 2, :].rearrange("p b n -> p (b n)"), ps[:, :], [pk], [('STs', b4)])
                                    CP('act' if False else 'pool', STsb[:, b4 * 2:b4 * 2 + 2, :], STs[:, b4 * 2:b4 * 2 + 2, :], [('STs', b4)], [('STsb', b4)])
                            ps, pk = psum()
                            MM(ps[:L, :L], BT, CT, True, True, [('xa', 16 + g), ('xa', 24 + g)], [pk])
                            TT('dve', cbm[:L, :L], ps[:L, :L], cmask[:L, :L], ALU.mult, [pk, 'trip_b', 'tris_b'], [('cbm', si)])
                            CP('pool', dexp[:L, :, :L], dtk[:L, 32 + 4 * g:36 + 4 * g].rearrange("p (a o) -> p a o", o=1).broadcast_to([L, 4, L]), ['dtk'], [('dexp', si)])
                            yield
                            ps2, pk2 = psum()
                            for hh in range(4):
                                MM(ps2[:L, hh * L:(hh + 1) * L], dexp[:L, hh, :L], tri, True, True, [('dexp', si), 'cst'], [pk2])
                            yield
                            for hh in range(4):
                                h_ = 4 * g + hh
                                dc = dec[hh % 2]
                                dk = ('dec', si, hh % 2)
                                ACT(dc[:L, :L], ps2[:L, hh * L:(hh + 1) * L], AF.Exp, [pk2, 'nak'], [dk], bias=nak[:L, h_:h_ + 1], scale=-1.0)
                                STT('dve', MT[:L, hh, :L], dc[:L, :L], 1.0, cbm[:L, :L], ALU.min, ALU.mult, [dk, ('cbm', si)], [('MT', si, hh)])
                            yield
                            psy, pyk = psum()
                            for hh in range(4):
                                h_ = 4 * g + hh
                                MM(psy[:L, hh * 64:(hh + 1) * 64], MT[:L, hh, :L], xdt[:L, h_ * 64:(h_ + 1) * 64], True, True, [('MT', si, hh), 'xdt'], [pyk])
                            if smp:
                                TT('pool', CTm, CT.rearrange("p (o t) -> p o t", o=1).broadcast_to([128, 16, 64]),
                                   bmT_b.rearrange("p (b t) -> p b t", b=16), ALU.mult, [('xa', 24 + g), 'bmT_b'], ['CTm'])
                                for b in range(16):
                                    MM(psy[:L, 256:512], CTm[:, b, :], STsb[:, b, :], b == 0, b == 15, ['CTm', ('STsb', b // 2)], [pyk])
                            else:
                                MM(psy[:L, 256:512], CT, STb[:, g * 256:(g + 1) * 256], True, True, [('xa', 24 + g), ('STb', g)], [pyk])
                            yield
                            t3 = tmp[:L, :].rearrange("p (a d) -> p a d", a=4)
                            TT('dve', t3, psy[:L, 256:512].rearrange("p (a d) -> p a d", a=4),
                               ena[:L, 4 * g:4 * g + 4].rearrange("p (a o) -> p a o", o=1).broadcast_to([L, 4, 64]), ALU.mult, [pyk, 'ena'], [('tmp', si)])
                            TT('dve', yg[:L, :], psy[:L, 0:256], tmp[:L, :], ALU.add, [pyk, ('tmp', si)], [('yg', si)])
                            TT('pool', t3, xtok[:L, g * 256:(g + 1) * 256].rearrange("p (a d) -> p a d", a=4),
                               dbc[:L, j * 32 + 4 * g:j * 32 + 4 * g + 4].rearrange("p (a o) -> p a o", o=1).broadcast_to([L, 4, 64]), ALU.mult,
                               [('xtok', g // 4), 'dbc', ('yg', si)], [('tmp', si)])
                            TT('dve', yg[:L, :], yg[:L, :], tmp[:L, :], ALU.add, [('tmp', si), ('yg', si)], [('yg', si)])
                            yield
                            TT('dve', yg[:L, :], yg[:L, :], zs[:L, tc, g * 256:(g + 1) * 256], ALU.mult, [('yg', si), ('zs', tc, g)], [('yg', si)])
                            TT('pool', tmp[:L, :], yg[:L, :], yg[:L, :], ALU.mult, [('yg', si)], [('tmp', si)])
                            S.add('dve', (lambda e, o=ssq[:L, 0:1], i_=tmp[:L, :]: e.tensor_reduce(out=o, in_=i_, axis=AX.X, op=ALU.add)), reads=[('tmp', si)], writes=[('ssq', si)])
                            yield
                            ACT(ssq[:L, :], ssq[:L, :], AF.Ln, [('ssq', si), 'eps5c'], [('ssq', si)], bias=eps5c[:L, 0:1], scale=1.0 / 256)
                            ACT(ssq[:L, :], ssq[:L, :], AF.Exp, [('ssq', si)], [('ssq', si)], scale=-0.5)
                            TS('dve', ynk[:L, :], yg[:L, :], ssq[:L, 0:1], None, ALU.mult, None, [('yg', si), ('ssq', si)], [('ynk', si)])
                            yield
                            pst, ptk = psum()
                            pstb = pst.bitcast(BF16)
                            for q in range(2):
                                TR(pstb[:, q * L:(q + 1) * L], ynk[:L, q * 128:(q + 1) * 128], ident_b[:L, :L], [('ynk', si), 'ident_b'], [ptk])
                            for q in range(2):
                                cc = 2 * g + q
                                ACT(ynT[:, cc, c0:c0 + L], pstb[:, q * L:(q + 1) * L], AF.Copy, [ptk, 'vecs'], [('ynT', cc)], scale=vcol('m_norm', j * 16 + cc))
                            yield
                            TT('dve', xdd[:L, :].rearrange("p (a d) -> p a d", a=4), xdt[:L, g * 256:(g + 1) * 256].rearrange("p (a d) -> p a d", a=4),
                               dend[:L, 4 * g:4 * g + 4].rearrange("p (a o) -> p a o", o=1).broadcast_to([L, 4, 64]), ALU.mult, ['xdt', 'dend'], [('xdd', si)])
                            if smp:
                                TT('pool', xddm[:L], xdd[:L, :].rearrange("p (o n) -> p o n", o=1).broadcast_to([L, 16, 256]),
                                   cst[:L, C_BM:C_BM + 16].rearrange("p (b o) -> p b o", o=1).broadcast_to([L, 16, 256]), ALU.mult, [('xdd', si), 'cst'], ['xddm'])
                                TT('dve', STs.rearrange("p b (a d) -> p b a d", a=4), STs.rearrange("p b (a d) -> p b a d", a=4),
                                   edCs[:, :, 4 * g:4 * g + 4].rearrange("p b (a o) -> p b a o", o=1).broadcast_to([128, 16, 4, 64]), ALU.mult,
                                   ['STs', 'edCs', 'STsb'], ['STs'])
                                for b2 in range(8):
                                    pss, psk = psum()
                                    for i2 in range(2):
                                        MM(pss[:, i2 * 256:(i2 + 1) * 256], Btok[:L, g * 128:(g + 1) * 128], xddm[:L, b2 * 2 + i2, :], True, True, ['Btok', 'xddm'], [psk])
                                    TT('dve', STs[:, b2 * 2:b2 * 2 + 2, :].rearrange("p b n -> p (b n)"), STs[:, b2 * 2:b2 * 2 + 2, :].rearrange("p b n -> p (b n)"),
                                       pss[:, :], ALU.add, [psk, ('STs', b2)], [('STs', b2)])
                                for b4 in range(8):
                                    ps, pk = psum()
                                    for i4 in range(4):
                                        b_ = b4 * 2 + i4 // 2
                                        q_ = i4 % 2
                                        TR(ps[:, i4 * 128:(i4 + 1) * 128], STs[:, b_, q_ * 128:(q_ + 1) * 128], ident_f, [('STs', b4), 'cst'], [pk])
                                    CP('dve', nat[:, b4 * 2:b4 * 2 + 2].rearrange("p b q n -> p (b q n)"), ps[:, :], [pk], ['nat'])
                                for q_ in range(2):
                                    DMA('sp', ssm_s[j][:, 4 * g + 2 * q_:4 * g + 2 * q_ + 2].rearrange("b a p n -> (a p) b n"), nat[:, :, q_, :], ['nat'], [('ssm_s', j, g, q_)])
                            else:
                                pss, psk = psum()
                                MM(pss[:, 0:256], Btok[:L, g * 128:(g + 1) * 128], xdd[:L, :], True, True, ['Btok', ('xdd', si)], [psk])
                                sg3 = ST[:, g * 256:(g + 1) * 256].rearrange("p (a d) -> p a d", a=4)
                                TT('dve', sg3, sg3, edC[:, 4 * g:4 * g + 4].rearrange("p (a o) -> p a o", o=1).broadcast_to([128, 4, 64]), ALU.mult,
                                   [('ST', g), 'edC', ('STb', g)], [('ST', g)])
                                TT('dve', ST[:, g * 256:(g + 1) * 256], ST[:, g * 256:(g + 1) * 256], pss[:, 0:256], ALU.add, [psk, ('ST', g)], [('ST', g)])
                                CP('pool', STb[:, g * 256:(g + 1) * 256], ST[:, g * 256:(g + 1) * 256], [('ST', g)], [('STb', g)])
                            yield
                        for gb in range(0, 8, NSET):
                            gens_ = [group_gen(g, g % NSET) for g in range(gb, gb + NSET)]
                            alive = True
                            while alive:
                                alive = False
                                for gn_ in gens_:
                                    try:
                                        next(gn_)
                                        alive = True
                                    except StopIteration:
                                        pass
                    DMA('pool', wob[0], Wout[:, :, 0:128], [], [('wob', 0)])
                    for d in range(KC):
                        wo = wob[d % 2]
                        wok = ('wob', d % 2)
                        if d + 1 < KC:
                            DMA('pool', wob[(d + 1) % 2], Wout[:, :, (d + 1) * 128:(d + 2) * 128], [], [('wob', (d + 1) % 2)])
                        ps, pk = psum()
                        for cc in range(16):
                            MM(ps[:, :T], wo[:, cc, :], ynT[:, cc, :], cc == 0, cc == 15, [wok, ('ynT', cc)], [pk])
                        TT('dve', h[:, d, t0:t0 + T], h[:, d, t0:t0 + T], ps[:, :T], ALU.add, [pk, 'h'], ['h'])
                if smp:
                    DMA('sp', conv_s[j].rearrange("(c p) b r -> p c b r", p=128), cso, ['cso'], [('conv_s', j)])
                else:
                    DMA('sp', conv_p[j].rearrange("(c p) r -> p c r", p=128), carry, ['carry'], [('conv_p', j)])
                    for q4 in range(4):
                        ps, pk = psum()
                        for i4 in range(4):
                            q = q4 * 4 + i4
                            TR(ps[:, i4 * 128:(i4 + 1) * 128], ST[:, q * 128:(q + 1) * 128], ident_f, ['ST', 'cst'], [pk])
                        CP('dve', natp.rearrange("p q n -> p (q n)"), ps[:, :], [pk], ['natp'])
                        DMA('sp', ssm_p[j].rearrange("(q a) p n -> (a p) q n", a=2)[:, q4 * 4:q4 * 4 + 4, :], natp, ['natp'], [('ssm_p', j, q4)])
                S.barrier()
                sb.reset(m)

            run(False)
            run(True)


        def rwkv(l):
            j = l // 2
            RDT = BF16
            mu0 = VOFF['r_mu'] + j * 48

            def wview(name, jj=None):
                return Wd_[name][j if jj is None else jj]

            def run(smp):
                m = sb.mark()
                T = 64 if smp else 128
                L = T
                nb = 16 if smp else 1
                lt = L // nb
                nlev = 2 if smp else 7
                tiles = [(2048, 64)] if smp else [(i * 128, 128) for i in range(16)]
                mAB = cst[:L, C_MABS:C_MABS + 128] if smp else cst[:, C_MABP:C_MABP + 256]
                mlow = cst[:L, C_LOWS:C_LOWS + 64] if smp else cst[:, C_LOWP:C_LOWP + 128]
                rst = cst[:, C_RSTS:C_RSTS + 64] if smp else cst[:, C_RSTP:C_RSTP + 128]
                blk2 = cst[:, C_BLK2:C_BLK2 + 128]
                f32a = lambda: sb.alloc(KC * T, F32).rearrange("p (k t) -> p k t", k=KC)
                b16a = lambda: sb.alloc(KC * T, BF16).rearrange("p (k t) -> p k t", k=KC)
                uf, up, r_, k_, v_, nlw, a_, kk, np_, tA, tB = [f32a() for _ in range(11)]
                xm = [b16a() for _ in range(2)]
                g_, BT_, KT_, BH_, KH_ = [b16a() for _ in range(5)]
                V_ = xm[1]
                yg = xm[0]
                ARt = sb.alloc(KC * 2 * T, BF16).rearrange("p (k c t) -> p k c t", k=KC, c=2)
                sq = BH_
                rs = sb.alloc(T, F32)
                wbuf = [sb.alloc(KC * 256, BF16).rearrange("p (k n) -> p k n", k=KC) for _ in range(2)]
                w1b = sb.alloc(KC * 64, BF16).rearrange("p (k n) -> p k n", k=KC)
                a1b = sb.alloc(KC * 64, BF16).rearrange("p (k n) -> p k n", k=KC)
                g1b = sb.alloc(KC * 160, BF16).rearrange("p (k n) -> p k n", k=KC)
                v1b = sb.alloc(KC * 32, BF16).rearrange("p (k n) -> p k n", k=KC)
                w2b = sb.alloc(1024, BF16)
                a2b = sb.alloc(1024, BF16)
                g2a = sb.alloc(1024, BF16)
                g2b = sb.alloc(1024, BF16)
                v2b = sb.alloc(1024, BF16)
                lo1 = sb.alloc(T, BF16)
                lo2 = sb.alloc(T, BF16)
                lo3 = sb.alloc(T, BF16)
                negw0 = sb.alloc(8, F32)
                omka = sb.alloc(8, F32)
                mhalf = sb.alloc(1, F32)
                c24 = sb.alloc(1, F32)
                gnec = sb.alloc(1, F32)
                eC = sb.alloc(KC * nb, F32).rearrange("p (k b) -> p k b", k=KC)
                Vtok = sb.alloc(KC * 128, BF16).rearrange("p (k n) -> p k n", k=KC)
                BHtok = sb.alloc(KC * 128, BF16).rearrange("p (k n) -> p k n", k=KC)
                KHtok = sb.alloc(KC * 128, BF16).rearrange("p (k n) -> p k n", k=KC)
                Ytok = sb.alloc(KC * 128, F32).rearrange("p (k n) -> p k n", k=KC)
                Ysq = up if not smp else sb.alloc(KC * 128, F32).rearrange("p (k n) -> p k n", k=KC)
                YSK = 'Ysq' if smp else 'up'
                Yn = sb.alloc(KC * 128, BF16).rearrange("p (k n) -> p k n", k=KC)
                st1 = sb.alloc(16, F32)
                st2 = sb.alloc(16, F32)
                NH = 1 if smp else 8
                ABt = [sb.alloc(2 * L, BF16) for _ in range(NH)]
                AKt = [sb.alloc(2 * L, BF16) for _ in range(NH)]
                MMb = [[sb.alloc(2 * L, BF16) for _ in range(2)] for _ in range(NH)]
                X32 = [sb.alloc(64, F32) for _ in range(NH)]
                Xb = [sb.alloc(64, BF16) for _ in range(NH)]
                S0T = sb.alloc(KC * nb * 64, F32).rearrange("p (k b i) -> p k b i", k=KC, b=nb)
                nbd = min(nb, 4)
                S0bh = [sb.alloc(nb * 64, BF16).rearrange("p (b i) -> p b i", b=nb) for _ in range(NH)]
                bd = sb.alloc(nbd * 128, F32).rearrange("p (b n) -> p b n", b=nbd)
                if smp:
                    sst = sb.alloc(KC * 16, F32).rearrange("p (k b) -> p k b", k=KC)
                    ATm = [sb.alloc(16 * 64, BF16).rearrange("p (b t) -> p b t", b=16) for _ in range(NH)]
                    RTm = [sb.alloc(16 * 64, BF16).rearrange("p (b t) -> p b t", b=16) for _ in range(NH)]
                    Wm = [sb.alloc(16 * 64, BF16).rearrange("p (b t) -> p b t", b=16) for _ in range(NH)]
                    Vm = [sb.alloc(16 * 64, BF16).rearrange("p (b t) -> p b t", b=16) for _ in range(NH)]
                    DMA('sp', sst, shift_in[j].rearrange("(k p) b -> p k b", p=128), [], ['sst'])
                    MSET('pool', BHtok, 0.0, ['BHtok'])
                    MSET('pool', Vtok, 0.0, ['Vtok'])
                    for q_ in range(NH):
                        MSET('pool', AKt[q_], 0.0, [('AKt', q_)])
                        MSET('pool', ABt[q_], 0.0, [('ABt', q_)])
                        MSET('pool', Xb[q_], 0.0, [('Xb', q_)])
                    MSET('pool', KHtok, 0.0, ['KHtok'])
                    for q_ in range(NH):
                        MSET('pool', Wm[q_], 0.0, [('Wm', q_)])
                        MSET('pool', Vm[q_], 0.0, [('Vm', q_)])
                else:
                    carry = sb.alloc(8, F32)
                    MSET('dve', carry, 0.0, ['carry'])
                MSET('dve', mhalf, -0.5, ['mhalf'])
                MSET('dve', c24, 1e-24, ['c24'])
                MSET('dve', gnec, 64e-5, ['gnec'])
                TS('dve', negw0, vecs[:, VOFF['r_w0'] + j * 8:VOFF['r_w0'] + j * 8 + 8], -1.0, None, ALU.mult, None, ['vecs'], ['negw0'])
                TS('dve', omka, vecs[:, VOFF['r_k_a'] + j * 8:VOFF['r_k_a'] + j * 8 + 8], -1.0, 1.0, ALU.mult, ALU.add, ['vecs'], ['omka'])
                DMA('pool', w1b, wview('r_w1').rearrange("(k p) n -> p k n", p=128), [], ['w1b'])
                DMA('pool', a1b, wview('r_a1').rearrange("(k p) n -> p k n", p=128), [], ['a1b'])
                DMA('pool', g1b, wview('r_g1').rearrange("(k p) n -> p k n", p=128), [], ['g1b'])
                DMA('pool', w2b[0:64, :], wview('r_w2'), [], ['w2b'])
                DMA('pool', a2b[0:64, :], wview('r_a2'), [], ['a2b'])
                DMA('pool', g2a, wview('r_g2')[0:128, :], [], ['g2a'])
                DMA('pool', g2b[0:32, :], wview('r_g2')[128:160, :], [], ['g2b'])
                if j == 1:
                    DMA('pool', v1b, wview('r_v1', 0).rearrange("(k p) n -> p k n", p=128), [], ['v1b'])
                    DMA('pool', v2b[0:32, :], wview('r_v2', 0), [], ['v2b'])
                if smp:
                    MSET('pool', bd, 0.0, ['bd'])
                    for d in range(KC):
                        for b4 in range(4):
                            for a in range(2):
                                DMA('sp', bd[a * 64:(a + 1) * 64, :, a * 64:(a + 1) * 64], wkv_in[j][b4 * 4:b4 * 4 + 4, 2 * d + a].rearrange("b i j -> i b j"), [], ['bd'])
                            ps, pk = psum()
                            for i4 in range(4):
                                TR(ps[:, i4 * 128:(i4 + 1) * 128], bd[:, i4, :], ident_f, ['bd', 'cst'], [pk])
                            for a in range(2):
                                CP('dve', S0T[a * 64:(a + 1) * 64, d, b4 * 4:b4 * 4 + 4, :],
                                   ps[a * 64:(a + 1) * 64, :].rearrange("p (b n) -> p b n", b=4)[:, :, a * 64:(a + 1) * 64], [pk], [('S0T', d)])
                else:
                    MSET('dve', S0T, 0.0, ['S0T'])
                wc = [0]

                pend = {}

                def wissue(wname, jj, hf):
                    Wv = wview(wname, jj).rearrange("(k p) n -> p k n", p=128)
                    wb = wbuf[wc[0] % 2]
                    wk = ('wbuf', wc[0] % 2)
                    wc[0] += 1
                    DMA('pool', wb, Wv[:, :, hf * 256:(hf + 1) * 256], [], [wk])
                    return wb, wk

                def proj(wname, src, evac, jj=None, nxt=None):
                    cur = pend.pop(wname, None)
                    if cur is None:
                        cur = wissue(wname, jj, 0)
                    for hf in range(4):
                        wb, wk = cur
                        if hf < 3:
                            cur = wissue(wname, jj, hf + 1)
                        elif nxt is not None:
                            pend[nxt] = wissue(nxt, None, 0)
                        for dd in range(2):
                            d = hf * 2 + dd
                            ps, pk = psum()
                            for k in range(KC):
                                MM(ps[:, :T], wb[:, k, dd * 128:(dd + 1) * 128], src[:, k, :], k == 0, k == KC - 1, [wk, 'xm'], [pk])
                            evac(d, ps, pk)

                try:
                    for (t0, T_) in tiles[:CFG.get('rtiles', 99)]:
                        rmsnorm(t0, T, 'norm_mix', l * KC, uf, ('uf',), sq, rs, 0, hkey='h', sqkey='BH_')
                        if smp:
                            u4 = uf.rearrange("p k (b t) -> p k b t", t=4)
                            p4 = up.rearrange("p k (b t) -> p k b t", t=4)
                            for k in range(KC):
                                CP('pool', p4[:, k, :, 1:4], u4[:, k, :, 0:3], ['uf'], ['up'])
                                CP('pool', p4[:, k, :, 0:1], sst[:, k, :].rearrange("p (b o) -> p b o", o=1), ['sst'], ['up'])
                                CP('pool', sst[:, k, :].rearrange("p (b o) -> p b o", o=1), u4[:, k, :, 3:4], ['uf', 'up'], ['sst'])
                        else:
                            CP('pool', up[:, :, 1:T], uf[:, :, 0:T - 1], ['uf'], ['up'])
                            CP('pool', up[:, :, 0:1], carry.rearrange("p (k o) -> p k o", o=1), ['carry'], ['up'])
                            CP('pool', carry.rearrange("p (k o) -> p k o", o=1), uf[:, :, T - 1:T], ['uf', 'up'], ['carry'])
                        TT('dve', up, up, uf, ALU.subtract, ['up', 'uf'], ['up'])

                        def mix(i, dst):
                            muv = vecs[:, mu0 + i * 8:mu0 + i * 8 + 8].rearrange("p (k o) -> p k o", o=1).broadcast_to([128, KC, T])
                            TT('dve', tA, up, muv, ALU.mult, ['up', 'vecs'], ['tA'])
                            TT('dve', dst, tA, uf, ALU.add, ['tA', 'uf'], ['xm'])
                        mix(0, xm[0])
                        proj('r_wr', xm[0], lambda d, ps, pk: ACT(r_[:, d, :], ps[:, :T], AF.Copy, [pk], [('r_', d)]), nxt='r_wk')
                        mix(2, xm[1])
                        proj('r_wk', xm[1], lambda d, ps, pk: ACT(k_[:, d, :], ps[:, :T], AF.Copy, [pk], [('k_', d)]), nxt='r_wv')
                        mix(3, xm[0])
                        proj('r_wv', xm[0], lambda d, ps, pk: ACT(v_[:, d, :], ps[:, :T], AF.Copy, [pk], [('v_', d)]), nxt='r_wo')
                        if j == 1:
                            ps, pk = psum()
                            for k in range(KC):
                                MM(ps[:32, :T], v1b[:, k, :], xm[0][:, k, :], k == 0, k == KC - 1, ['v1b', 'xm'], [pk])
                            CP('dve', lo3[0:32, :], ps[:32, :T], [pk], ['lo3'])
                            DMA('sp', tB, vfirst_d.rearrange("(k p) t -> p k t", p=128)[:, :, t0:t0 + T], ['vf_dram'], ['tB'])
                            for d in range(KC):
                                ps, pk = psum()
                                MM(ps[:, :T], v2b[0:32, d * 128:(d + 1) * 128], lo3[0:32, :], True, True, ['v2b', 'lo3'], [pk])
                                ACT(tA[:, d, :], ps[:, :T], AF.Sigmoid, [pk, 'vecs'], ['tA'], bias=vcol('r_v0', d))
                            TT('dve', tB, tB, v_, ALU.subtract, ['tB', 'v_'], ['tB'])
                            TT('dve', tB, tB, tA, ALU.mult, ['tB', 'tA'], ['tB'])
                            TT('dve', v_, v_, tB, ALU.add, ['tB', 'v_'], ['v_'])
                        else:
                            DMA('sp', vfirst_d.rearrange("(k p) t -> p k t", p=128)[:, :, t0:t0 + T], v_, ['v_'], [('vf_dram', t0)])
                        _stg(1)
                        mix(1, xm[1])
                        ps, pk = psum()
                        for k in range(KC):
                            MM(ps[:64, :T], w1b[:, k, :], xm[1][:, k, :], k == 0, k == KC - 1, ['w1b', 'xm'], [pk])
                        ACT(lo1[0:64, :], ps[:64, :T], AF.Tanh, [pk], ['lo1'])
                        for d in range(KC):
                            ps, pk = psum()
                            MM(ps[:, :T], w2b[0:64, d * 128:(d + 1) * 128], lo1[0:64, :], True, True, ['w2b', 'lo1'], [pk])
                            ACT(nlw[:, d, :], ps[:, :T], AF.Exp, [pk, 'negw0'], [('nlw', d)], bias=negw0[:, d:d + 1], scale=-1.0)
                            ACT(nlw[:, d, :], nlw[:, d, :], AF.Ln, [('nlw', d), 'onec'], [('nlw', d)], bias=onec[:, 0:1])
                            ACT(nlw[:, d, :], nlw[:, d, :], AF.Exp, [('nlw', d), 'mhalf'], [('nlw', d)], bias=mhalf[:, 0:1], scale=-1.0)
                        mix(4, xm[0])
                        ps, pk = psum()
                        for k in range(KC):
                            MM(ps[:64, :T], a1b[:, k, :], xm[0][:, k, :], k == 0, k == KC - 1, ['a1b', 'xm'], [pk])
                        CP('dve', lo2[0:64, :], ps[:64, :T], [pk], ['lo2'])
                        for d in range(KC):
                            ps, pk = psum()
                            MM(ps[:, :T], a2b[0:64, d * 128:(d + 1) * 128], lo2[0:64, :], True, True, ['a2b', 'lo2'], [pk])
                            ACT(a_[:, d, :], ps[:, :T], AF.Sigmoid, [pk, 'vecs'], [('a_', d)], bias=vcol('r_a0', j * 8 + d))
                        mix(5, xm[1])
                        ps, pk = psum()
                        for k in range(KC):
                            MM(ps[:, :T], g1b[:, k, 0:128], xm[1][:, k, :], k == 0, k == KC - 1, ['g1b', 'xm'], [pk])
                        ACT(lo1, ps[:, :T], AF.Sigmoid, [pk], ['lo1'])
                        ps, pk = psum()
                        for k in range(KC):
                            MM(ps[:32, :T], g1b[:, k, 128:160], xm[1][:, k, :], k == 0, k == KC - 1, ['g1b', 'xm'], [pk])
                        ACT(lo2[0:32, :], ps[:32, :T], AF.Sigmoid, [pk], ['lo2'])
                        for d in range(KC):
                            ps, pk = psum()
                            MM(ps[:, :T], g2a[:, d * 128:(d + 1) * 128], lo1, True, False, ['g2a', 'lo1'], [pk])
                            MM(ps[:, :T], g2b[0:32, d * 128:(d + 1) * 128], lo2[0:32, :], False, True, ['g2b', 'lo2'], [pk])
                            ACT(g_[:, d, :], ps[:, :T], AF.Copy, [pk], [('g_', d)])
                        _stg(2)
                        kkv = vecs[:, VOFF['r_k_k'] + j * 8:VOFF['r_k_k'] + j * 8 + 8].rearrange("p (k o) -> p k o", o=1).broadcast_to([128, KC, T])
                        TT('dve', kk, k_, kkv, ALU.mult, ['k_', 'vecs'], ['kk'])
                        TT('dve', tA, kk, kk, ALU.mult, ['kk'], ['tA'])
                        for d in range(KC):
                            ps, pk = psum()
                            MM(ps[:, :T], blk2, tA[:, d, :], True, True, ['cst', 'tA'], [pk])
                            TS('dve', tB[:, d, :], ps[:, :T], c24[:, 0:1], None, ALU.max, None, [pk, 'c24'], ['tB'])
                        ACT(tB, tB, AF.Ln, ['tB'], ['tB'])
                        ACT(tB, tB, AF.Exp, ['tB'], ['tB'], scale=-0.5)
                        TT('dve', kk, kk, tB, ALU.mult, ['kk', 'tB'], ['kk'])
                        kav = vecs[:, VOFF['r_k_a'] + j * 8:VOFF['r_k_a'] + j * 8 + 8].rearrange("p (k o) -> p k o", o=1).broadcast_to([128, KC, T])
                        omv = omka.rearrange("p (k o) -> p k o", o=1).broadcast_to([128, KC, T])
                        TT('dve', tA, a_, kav, ALU.mult, ['a_', 'vecs'], ['tA'])
                        TT('dve', tA, tA, omv, ALU.add, ['tA', 'omka'], ['tA'])
                        TT('dve', k_, k_, tA, ALU.mult, ['k_', 'tA'], ['k_'])
                        TT('dve', tA, kk, a_, ALU.mult, ['kk', 'a_'], ['tA'])
                        rkv = vecs[:, VOFF['r_r_k'] + j * 8:VOFF['r_r_k'] + j * 8 + 8].rearrange("p (k o) -> p k o", o=1).broadcast_to([128, KC, T])
                        TT('dve', np_, r_, k_, ALU.mult, ['r_', 'k_'], ['np_'])
                        TT('dve', np_, np_, rkv, ALU.mult, ['np_', 'vecs'], ['np_'])
                        for d in range(KC):
                            ps, pk = psum()
                            MM(ps[:, :T], blk2, np_[:, d, :], True, True, ['cst', 'np_'], [pk])
                            TT('dve', tB[:, d, :], ps[:, :T], v_[:, d, :], ALU.mult, [pk, 'v_'], ['tB'])
                        _stg(3)
                        for d in range(KC):
                            S.add('dve', (lambda e, o=np_[:, d, :], d1=nlw[:, d, :]: e.tensor_tensor_scan(out=o, data0=rst[:, :T], data1=d1, initial=0.0, op0=ALU.mult, op1=ALU.add)),
                                  reads=['nlw', 'cst', 'np_'], writes=['np_'])
                        npv = np_.rearrange("p k (b t) -> p k b t", b=nb)
                        npE = npv[:, :, :, lt - 1:lt].broadcast_to([128, KC, nb, lt])
                        ACT(eC, npv[:, :, :, lt - 1:lt].rearrange("p k b o -> p k (b o)"), AF.Exp, ['np_'], ['eC'], scale=-1.0)
                        TT('dve', uf, np_, nlw, ALU.subtract, ['np_', 'nlw'], ['uf'])
                        ACT(uf, uf, AF.Exp, ['uf'], ['uf'], scale=-1.0)
                        STT('dve', ARt[:, :, 0, :], kk, -1.0, uf, ALU.mult, ALU.mult, ['kk', 'uf'], ['ARt'])
                        ACT(uf, np_, AF.Exp, ['np_', 'ARt'], ['uf'], scale=-1.0)
                        TT('dve', ARt[:, :, 1, :], r_, uf, ALU.mult, ['r_', 'uf'], ['ARt'])
                        ACT(uf, np_, AF.Exp, ['np_', 'ARt'], ['uf'])
                        TT('dve', BT_, tA, uf, ALU.mult, ['tA', 'uf'], ['BT_'])
                        TT('dve', KT_, k_, uf, ALU.mult, ['k_', 'uf'], ['KT_'])
                        TT('dve', uf.rearrange("p k (b t) -> p k b t", b=nb), npv, npE, ALU.subtract, ['np_', 'BT_', 'KT_'], ['uf'])
                        ACT(uf, uf, AF.Exp, ['uf'], ['uf'])
                        TT('dve', BH_, tA, uf, ALU.mult, ['tA', 'uf'], ['BH_'])
                        TT('dve', KH_, k_, uf, ALU.mult, ['k_', 'uf'], ['KH_'])
                        CP('pool', V_, v_, ['v_'], ['xm'])
                        _stg(4)
                        for (src, dst, nm) in ((V_, Vtok, 'Vtok'), (BH_, BHtok, 'BHtok'), (KH_, KHtok, 'KHtok')):
                            ps, pk = psum()
                            psb = ps.bitcast(BF16)
                            for d in range(KC):
                                TR(psb[:L, d * 128:(d + 1) * 128], src[:, d, :], ident_b, [nm[:-3] + '_' if nm != 'Vtok' else 'xm', 'ident_b'], [pk])
                            CP('dve', dst[:L].rearrange("p k n -> p (k n)"), psb[:L, :], [pk], [nm])
                        _stg(5)
                        for hg in range(16 // NH):
                            heads = [(hg * NH + q) for q in range(NH)]
                            if CFG.get('rheads') is not None and heads[0] not in CFG['rheads']:
                                continue
                            HD = [(hh // 2, hh % 2) for hh in heads]
                            for q, (d, a) in enumerate(HD):
                                sl = slice(a * 64, (a + 1) * 64)
                                AR = ARt[sl, d].rearrange("p c t -> p (c t)")
                                ps, pk = psum()
                                MM(ps[:L, 0:2 * L], BT_[sl, d, :], AR, True, True, ['BT_', 'ARt'], [pk])
                                TT('dve', ABt[q][:L, :], ps[:L, 0:2 * L], mAB, ALU.mult, [pk, 'cst'], [('ABt', q)])
                                ps, pk = psum()
                                MM(ps[:L, 0:2 * L], KT_[sl, d, :], AR, True, True, ['KT_', 'ARt'], [pk])
                                TT('dve', AKt[q][:L, :], ps[:L, 0:2 * L], mAB, ALU.mult, [pk, 'cst'], [('AKt', q)])
                                ps, pk = psum()
                                MM(ps[:L, 0:L], ARt[sl, d, 0, :], BT_[sl, d, :], True, True, ['BT_', 'ARt'], [pk])
                                TT('dve', MMb[q][0][:L, 0:L], ps[:L, 0:L], mlow, ALU.mult, [pk, 'cst'], [('MMb', q, 0)])
                                CP('pool', MMb[q][0][:L, L:2 * L], ABt[q][:L, 0:L], [('ABt', q)], [('MMb', q, 0)])
                            _stg(6)
                            for q, (d, a) in enumerate(HD):
                                sl = slice(a * 64, (a + 1) * 64)
                                CP('dve', S0bh[q][sl], S0T[sl, d], [('S0T', d)], [('S0bh', q)])
                                ps, pk = psum()
                                if smp:
                                    TT('dve', ATm[q][sl], ARt[sl, d, 0, :].rearrange("p (o t) -> p o t", o=1).broadcast_to([64, 16, 64]),
                                       bmT_b[sl].rearrange("p (b t) -> p b t", b=16), ALU.mult, ['ARt', 'bmT_b'], [('ATm', q)])
                                    TT('dve', RTm[q][sl], ARt[sl, d, 1, :].rearrange("p (o t) -> p o t", o=1).broadcast_to([64, 16, 64]),
                                       bmT_b[sl].rearrange("p (b t) -> p b t", b=16), ALU.mult, ['ARt', 'bmT_b'], [('RTm', q)])
                                    for b in range(16):
                                        MM(ps[:L, 0:64], ATm[q][sl, b, :], S0bh[q][sl, b, :], b == 0, False, [('ATm', q), ('S0bh', q)], [pk])
                                else:
                                    MM(ps[:L, 0:64], ARt[sl, d, 0, :], S0bh[q][sl, 0, :], True, False, ['ARt', ('S0bh', q)], [pk])
                                MM(ps[:L, 0:64], AKt[q][:, 0:L], Vtok[:, d, sl], False, True, [('AKt', q), 'Vtok'], [pk])
                                CP('dve', X32[q][:L, :], ps[:L, 0:64], [pk], [('X32', q)])
                                CP('pool', Xb[q][:L, :], X32[q][:L, :], [('X32', q)], [('Xb', q)])
                            _stg(7)
                            for lev in range(nlev):
                                cur = lev % 2
                                for q, (d, a) in enumerate(HD):
                                    M_ = MMb[q][cur][:L, 0:L]
                                    Mt_ = MMb[q][cur][:L, L:2 * L]
                                    ps, pk = psum()
                                    MM(ps[:L, 0:64], Mt_, Xb[q][:L, :], True, True, [('MMb', q, cur), ('Xb', q)], [pk])
                                    TT('dve', X32[q][:L, :], X32[q][:L, :], ps[:L, 0:64], ALU.add, [pk, ('X32', q)], [('X32', q)])
                                    CP('pool', Xb[q][:L, :], X32[q][:L, :], [('X32', q)], [('Xb', q)])
                                    if lev < nlev - 1:
                                        ps2, pk2 = psum()
                                        MM(ps2[:L, 0:L], Mt_, M_, True, True, [('MMb', q, cur)], [pk2])
                                        MM(ps2[:L, L:2 * L], M_, Mt_, True, True, [('MMb', q, cur)], [pk2])
                                        ACT(MMb[q][1 - cur][:L, :], ps2[:L, 0:2 * L], AF.Copy, [pk2], [('MMb', q, 1 - cur)])
                            _stg(8)
                            for q, (d, a) in enumerate(HD):
                                sl = slice(a * 64, (a + 1) * 64)
                                ps, pk = psum()
                                if smp:
                                    for b in range(16):
                                        MM(ps[:L, 0:64], RTm[q][sl, b, :], S0bh[q][sl, b, :], b == 0, False, [('RTm', q), ('S0bh', q)], [pk])
                                else:
                                    MM(ps[:L, 0:64], ARt[sl, d, 1, :], S0bh[q][sl, 0, :], True, False, ['ARt', ('S0bh', q)], [pk])
                                MM(ps[:L, 0:64], ABt[q][:, L:2 * L], Xb[q][:, :], False, False, [('ABt', q), ('Xb', q)], [pk])
                                MM(ps[:L, 0:64], AKt[q][:, L:2 * L], Vtok[:, d, sl], False, True, [('AKt', q), 'Vtok'], [pk])
                                ACT(Ytok[:L, d, sl], ps[:L, 0:64], AF.Copy, [pk], [('Ytok', d, a)])
                                _stg(8.2)
                                if smp:
                                    bmv = cst[:L, C_BM:C_BM + 16].rearrange("p (b o) -> p b o", o=1).broadcast_to([L, 16, 64])
                                    TT('pool', Wm[q][:L], Xb[q][:L, :].rearrange("p (o i) -> p o i", o=1).broadcast_to([L, 16, 64]), bmv, ALU.mult, [('Xb', q), 'cst'], [('Wm', q)])
                                    TT('pool', Vm[q][:L], Vtok[:L, d, sl].rearrange("p (o i) -> p o i", o=1).broadcast_to([L, 16, 64]), bmv, ALU.mult, ['Vtok', 'cst'], [('Vm', q)])
                                    _stg(8.4)
                                    S3 = S0T[sl, d]
                                    TT('dve', S3, S3, eC[sl, d, :].rearrange("p (b o) -> p b o", o=1).broadcast_to([64, 16, 64]), ALU.mult, [('S0T', d), 'eC'], [('S0T', d)])
                                    _stg(8.6)
                                    for b8 in range(2):
                                        ps, pk = psum()
                                        for bi in range(8):
                                            b = b8 * 8 + bi
                                            MM(ps[sl, bi * 64:(bi + 1) * 64], BHtok[:, d, sl], Wm[q][:, b, :], True, False, ['BHtok', ('Wm', q)], [pk])
                                            MM(ps[sl, bi * 64:(bi + 1) * 64], KHtok[:, d, sl], Vm[q][:, b, :], False, True, ['KHtok', ('Vm', q)], [pk])
                                        TT('dve', S0T[sl, d, b8 * 8:b8 * 8 + 8, :], S0T[sl, d, b8 * 8:b8 * 8 + 8, :], ps[sl, :].rearrange("p (b i) -> p b i", b=8), ALU.add,
                                           [pk, ('S0T', d)], [('S0T', d)])
                                else:
                                    ps, pk = psum()
                                    MM(ps[sl, 0:64], BHtok[:L, d, sl], Xb[q][:L, :], True, False, ['BHtok', ('Xb', q)], [pk])
                                    MM(ps[sl, 0:64], KHtok[:L, d, sl], Vtok[:L, d, sl], False, True, ['KHtok', 'Vtok'], [pk])
                                    STT('dve', S0T[sl, d, 0, :], S0T[sl, d, 0, :], eC[sl, d, 0:1], ps[sl, 0:64], ALU.mult, ALU.add, [pk, ('S0T', d), 'eC'], [('S0T', d)])
                        _stg(9)
                        Y3 = Ytok[:L].rearrange("p k (a i) -> p (k a) i", a=2)
                        S.add('dve', (lambda e, o=st1[:L, :]: e.tensor_reduce(out=o, in_=Y3, axis=AX.X, op=ALU.add)), reads=['Ytok'], writes=['st1'])
                        TT('pool', Ysq[:L], Ytok[:L], Ytok[:L], ALU.mult, ['Ytok'], [YSK])
                        S.add('dve', (lambda e, o=st2[:L, :]: e.tensor_reduce(out=o, in_=Ysq[:L].rearrange("p k (a i) -> p (k a) i", a=2), axis=AX.X, op=ALU.add)), reads=[YSK], writes=['st2'])
                        TS('dve', st1[:L, :], st1[:L, :], 1.0 / 64, None, ALU.mult, None, ['st1'], ['st1'])
                        TS('dve', st2[:L, :], st2[:L, :], 1.0 / 64, None, ALU.mult, None, ['st2'], ['st2'])
                        TT('dve', Ysq[:L, 0, 0:16], st1[:L, :], st1[:L, :], ALU.mult, ['st1', YSK], [YSK])
                        TT('dve', st2[:L, :], st2[:L, :], Ysq[:L, 0, 0:16], ALU.subtract, ['st2', YSK], ['st2'])
                        ACT(st2[:L, :], st2[:L, :], AF.Ln, ['st2', 'gnec'], ['st2'], bias=gnec[:L, 0:1])
                        ACT(st2[:L, :], st2[:L, :], AF.Exp, ['st2'], ['st2'], scale=-0.5)
                        TT('dve', Y3, Y3, st1[:L, :].rearrange("p (h o) -> p h o", o=1).broadcast_to([L, 16, 64]), ALU.subtract, ['Ytok', 'st1'], ['Ytok'])
                        TT('dve', Yn[:L].rearrange("p k (a i) -> p (k a) i", a=2), Y3, st2[:L, :].rearrange("p (h o) -> p h o", o=1).broadcast_to([L, 16, 64]), ALU.mult,
                           ['Ytok', 'st2'], ['Yn'])
                        for hf in range(2):
                            ps, pk = psum()
                            psb = ps.bitcast(BF16)
                            for dd in range(4):
                                d = hf * 4 + dd
                                TR(psb[:, dd * L:(dd + 1) * L], Yn[:L, d, :], ident_b[:L, :L], ['Yn', 'ident_b'], [pk])
                            for dd in range(4):
                                d = hf * 4 + dd
                                TS('dve', tA[:, d, :], psb[:, dd * L:(dd + 1) * L], vcol('r_gn_w', j * 8 + d), vcol('r_gn_b', j * 8 + d), ALU.mult, ALU.add, [pk, 'vecs'], ['tA'])
                        TT('dve', tA, tA, tB, ALU.add, ['tA', 'tB'], ['tA'])
                        TT('dve', yg, tA, g_, ALU.mult, ['tA', 'g_'], ['xm'])
                        proj('r_wo', yg, lambda d, ps, pk: TT('dve', h[:, d, t0:t0 + T], h[:, d, t0:t0 + T], ps[:, :T], ALU.add, [pk, 'h'], ['h']), nxt=('r_wr' if t0 != tiles[-1][0] else None))
                except _Stop:
                    pass
                if CFG.get('rstage', 99) < 10:
                    S.barrier()
                    sb.reset(m)
                    return
                if smp:
                    DMA('sp', shift_s[j].rearrange("(k p) b -> p k b", p=128), sst, ['sst'], [('shift_s', j)])
                else:
                    DMA('sp', shift_p[j], carry, ['carry'], [('shift_p', j)])
                MSET('pool', bd, 0.0, ['bd'])
                for d in range(KC):
                    for b4 in range((nb + 3) // 4):
                        n4 = min(4, nb - b4 * 4)
                        for a in range(2):
                            sl = slice(a * 64, (a + 1) * 64)
                            CP('dve', bd[sl, 0:n4, a * 64:(a + 1) * 64], S0T[sl, d, b4 * 4:b4 * 4 + n4, :], [('S0T', d)], ['bd'])
                        ps, pk = psum()
                        for i4 in range(n4):
                            TR(ps[:, i4 * 128:(i4 + 1) * 128], bd[:, i4, :], ident_f, ['bd', 'cst'], [pk])
                        for a in range(2):
                            sl = slice(a * 64, (a + 1) * 64)
                            CP('dve', Ysq[sl, 0:n4, 0:64], ps[sl, :].rearrange("p (b n) -> p b n", b=4)[:, 0:n4, a * 64:(a + 1) * 64], [pk], [YSK])
                            if smp:
                                DMA('sp', wkv_s[j][b4 * 4:b4 * 4 + n4, 2 * d + a].rearrange("b i j -> i b j"), Ysq[sl, 0:n4, 0:64], [YSK], [('wkv_s', j, d, a, b4)])
                            else:
                                DMA('sp', wkv_p[j][2 * d + a], Ysq[sl, 0, 0:64], [YSK], [('wkv_p', j, d, a)])
                S.barrier()
                sb.reset(m)

            run(False)
            if CFG.get('rwkv_sample', True):
                run(True)


        for l in range(CFG['depth']):
            if CFG.get('dense', True):
                ffn(l, Wd_['ffn1_gate_up'], Wd_['ffn1_down'], 'norm_ffn1')
            if CFG['mixers']:
                if l % 2 == 0:
                    if CFG.get('mamba', True):
                        mamba(l)
                elif CFG.get('rwkv', False):
                    rwkv(l)
            if CFG.get('dense', True):
                ffn(l, Wd_['ffn2_gate_up'], Wd_['ffn2_down'], 'norm_ffn2')
                ple(l)
        final()
        S.add('sp', lambda e: e.nop(), reads=['yT', 'ssm_s', 'ssm_p', 'conv_s', 'conv_p', 'wkv_p', 'wkv_s', 'shift_p', 'shift_s'])
        S.emit()
    return nc


def pack_vecs(inp):
    cols = []
    for n in VECS:
        a = np.asarray(inp[n], dtype=np.float32)
        cols.append(np.ascontiguousarray(a.reshape(-1, 128).T))
    return np.ascontiguousarray(np.concatenate(cols, axis=1))


def kernel(**inp):
    inp = {k: np.asarray(v) for k, v in inp.items()}
    nc = build_program()
    vecs = pack_vecs(inp)
    consts = make_consts()
    hv = np.zeros((32, 4), np.float32)
    dbc = np.zeros((128, 64), np.float32)
    for j in range(2):
        hv[:, 2 * j] = inp['m_dt_bias'][j]
        hv[:, 2 * j + 1] = inp['m_A_log'][j]
        dbc[:, j * 32:(j + 1) * 32] = inp['m_D'][j][None, :]
    ncores = CFG['cores']
    in_maps = []
    for c in range(ncores):
        xs = inp['x_sample'][16 * c:16 * c + 16].reshape(NS, D)
        xc = np.concatenate([inp['x_prompt'][c], xs], axis=0)
        ps_ = inp['p_sample'][:, 16 * c:16 * c + 16].reshape(4, NS, 256)
        pc = np.concatenate([inp['p_prompt'][:, c], ps_], axis=1)
        m = {'xT': np.ascontiguousarray(xc.T), 'pT': np.ascontiguousarray(pc.transpose(0, 2, 1)), 'vecs': vecs,
             'consts': consts, 'hv': hv, 'dbc': dbc,
             'ssm_in': np.ascontiguousarray(inp['state_ssm'][:, 16 * c:16 * c + 16]),
             'conv_in': np.ascontiguousarray(inp['state_conv'][:, 16 * c:16 * c + 16].transpose(0, 3, 1, 2)),
             'wkv_in': np.ascontiguousarray(inp['state_wkv'][:, 16 * c:16 * c + 16]),
             'shift_in': np.ascontiguousarray(inp['state_shift'][:, 16 * c:16 * c + 16].transpose(0, 2, 1))}
        for n in WSHAPES:
            m[n] = np.ascontiguousarray(inp[n], dtype=np.float32)
        in_maps.append(m)
    res = run_bass_kernel_spmd(nc, in_maps, core_ids=list(range(ncores)))
    B = 8
    yp = np.zeros((B, NP, D), np.float32)
    ys = np.zeros((128, 4, D), np.float32)
    ssm_p = np.zeros((2, B, 32, 64, 128), np.float32)
    conv_p = np.zeros((2, B, 3, 4096), np.float32)
    wkv_p = np.zeros((2, B, 16, 64, 64), np.float32)
    shift_p = np.zeros((2, B, D), np.float32)
    ssm_s = np.zeros((2, 128, 32, 64, 128), np.float32)
    conv_s = np.zeros((2, 128, 3, 4096), np.float32)
    wkv_s = np.zeros((2, 128, 16, 64, 64), np.float32)
    shift_s = np.zeros((2, 128, D), np.float32)
    for c in range(ncores):
        r = res.results[c]
        y = r['yT'].T
        yp[c] = y[:NP]
        ys[16 * c:16 * c + 16] = y[NP:].reshape(16, 4, D)
        ssm_p[:, c] = r['ssm_p']
        conv_p[:, c] = r['conv_p'].transpose(0, 2, 1)
        wkv_p[:, c] = r['wkv_p']
        shift_p[:, c] = r['shift_p'].transpose(0, 2, 1).reshape(2, D)
        ssm_s[:, 16 * c:16 * c + 16] = r['ssm_s']
        conv_s[:, 16 * c:16 * c + 16] = r['conv_s'].transpose(0, 2, 3, 1)
        wkv_s[:, 16 * c:16 * c + 16] = r['wkv_s']
        shift_s[:, 16 * c:16 * c + 16] = r['shift_s'].transpose(0, 2, 1)
    return (yp, ys, ssm_p, conv_p, wkv_p, shift_p, ssm_s, conv_s, wkv_s, shift_s)
```

```python
import contextlib
import numpy as np
import concourse.bass as bass
import concourse.mybir as mybir
from concourse.bass_utils import run_bass_kernel_spmd

F32 = mybir.dt.float32
BF16 = mybir.dt.bfloat16
AF = mybir.ActivationFunctionType
ALU = mybir.AluOpType
AX = mybir.AxisListType

ENGS = ['pe', 'act', 'dve', 'pool', 'sp']


class Op:
    __slots__ = ('eng', 'fn', 'deps', 'signal', 'is_dma', 'sem', 'val', 'prewait', 'barriered')

    def __init__(self, eng, fn, is_dma):
        self.eng = eng
        self.fn = fn
        self.deps = []
        self.signal = False
        self.is_dma = is_dma
        self.sem = None
        self.val = 0
        self.prewait = None
        self.barriered = False


class Sched:
    def __init__(self, nc, n_dma_sems=20):
        self.nc = nc
        self.ops = {e: [] for e in ENGS}
        self.res = {}
        self.n_dma_sems = n_dma_sems

    def _states(self, key):
        if isinstance(key, tuple):
            name, sub = key[0], key[1:]
            if len(sub) == 0:
                sub = None
        else:
            name, sub = key, None
        d = self.res.setdefault(name, {})
        if sub is None:
            if None not in d:
                d[None] = [None, []]
            return list(d.values()), d[None], True
        if sub not in d:
            d[sub] = [None, []]
        sts = [d[sub]]
        if None in d:
            sts.append(d[None])
        return sts, d[sub], False

    def add(self, eng, fn, reads=(), writes=(), dma=False):
        op = Op(eng, fn, dma)
        deps = []
        for k in reads:
            sts, own, whole = self._states(k)
            for st in sts:
                if st[0] is not None:
                    deps.append(st[0])
        for k in writes:
            sts, own, whole = self._states(k)
            for st in sts:
                if st[0] is not None:
                    deps.append(st[0])
                deps.extend(st[1])
        for k in reads:
            sts, own, whole = self._states(k)
            own[1].append(op)
        for k in writes:
            sts, own, whole = self._states(k)
            if whole:
                for st in sts:
                    st[0] = None
                    st[1] = []
            own[0] = op
            own[1] = []
        seen = set()
        for d in deps:
            if d is op or id(d) in seen:
                continue
            seen.add(id(d))
            if d.eng == eng and eng == 'pe' and not d.is_dma and not dma:
                continue
            op.deps.append(d)
            d.signal = True
        self.ops[eng].append(op)
        return op

    def barrier(self):
        last = []
        for e in ENGS:
            got = False
            for o in reversed(self.ops[e]):
                if o.barriered:
                    break
                if o.is_dma:
                    last.append(o)
                elif not got:
                    last.append(o)
                    got = True
        b = Op('sp', lambda e: e.nop(), False)
        for d in last:
            b.deps.append(d)
            d.signal = True
        for e in ENGS:
            for o in reversed(self.ops[e]):
                if o.barriered:
                    break
                o.barriered = True
        b.barriered = True
        self.ops['sp'].append(b)
        for e in ENGS:
            if e == 'sp':
                continue
            o = Op(e, None, False)
            o.barriered = True
            o.deps.append(b)
            b.signal = True
            self.ops[e].append(o)
        self.res = {}

    def emit(self):
        nc = self.nc
        with contextlib.ExitStack() as es:
            csem = {e: es.enter_context(nc.semaphore('c_' + e)) for e in ENGS}
            dsems = {e: [es.enter_context(nc.semaphore('d_%s_%d' % (e, i)))
                         for i in range(self.n_dma_sems)] for e in ('sp', 'pool', 'act')}
            for e in ENGS:
                cnt = 0
                dcnt = [0] * self.n_dma_sems
                rr = 0
                for op in self.ops[e]:
                    if op.is_dma:
                        s = rr % self.n_dma_sems
                        rr += 1
                        if dcnt[s] > 0:
                            op.prewait = (dsems[e][s], dcnt[s])
                        dcnt[s] += 16
                        op.sem = dsems[e][s]
                        op.val = dcnt[s]
                    elif op.signal:
                        cnt += 1
                        op.sem = csem[e]
                        op.val = cnt
            engobj = {'pe': 'tensor', 'act': 'scalar', 'dve': 'vector', 'pool': 'gpsimd', 'sp': 'sync'}

            def run(e, eng):
                seen = {}
                for op in self.ops[e]:
                    waits = []
                    if op.prewait is not None:
                        waits.append(op.prewait)
                    for d in op.deps:
                        waits.append((d.sem, d.val))
                    mx = {}
                    for (s, v) in waits:
                        k = id(s)
                        if v > mx.get(k, (None, 0))[1]:
                            mx[k] = (s, v)
                    for k, (s, v) in mx.items():
                        if seen.get(k, 0) >= v:
                            continue
                        seen[k] = v
                        eng.wait_ge(s, v)
                    if op.fn is None:
                        continue
                    ins = op.fn(eng)
                    if op.is_dma:
                        ins.then_inc(op.sem, 16)
                    elif op.signal:
                        ins.then_inc(op.sem, 1)

            with nc.Block() as block:
                for e in ENGS:
                    if not self.ops[e]:
                        continue
                    getattr(block, engobj[e])(lambda eng, e=e: run(e, eng))


class SB:
    def __init__(self, big, nbytes):
        self.big = big
        self.nbytes = nbytes
        self.off = 0
        self.views = {}

    def view(self, dtype):
        if dtype not in self.views:
            self.views[dtype] = self.big.bitcast(dtype) if dtype != F32 else self.big
        return self.views[dtype]

    def alloc(self, cols, dtype, parts=128):
        sz = mybir.dt.size(dtype)
        nb = (cols * sz + 63) // 64 * 64
        assert self.off + nb <= self.nbytes, ('SBUF overflow', self.off, nb, self.nbytes)
        o = self.off // sz
        self.off += nb
        return self.view(dtype)[0:parts, o:o + cols]

    def mark(self):
        return self.off

    def reset(self, m):
        self.off = m


D = 1024
KC = 8
NP = 2048
NS = 64
NT = NP + NS
TILES = [(0, 512), (512, 512), (1024, 512), (1536, 512), (2048, 64)]
DFF = 2816
FGROUPS = [(0, 4), (4, 4), (8, 4), (12, 4), (16, 3), (19, 3)]
DEPTH = 4
EPS = 1e-6

WSHAPES = {
    'ffn1_gate_up': (4, 1024, 5632), 'ffn1_down': (4, 2816, 1024),
    'ffn2_gate_up': (4, 1024, 5632), 'ffn2_down': (4, 2816, 1024),
    'ple_in': (4, 256, 1024), 'ple_gate': (4, 1024, 1024),
    'm_in_proj': (2, 1024, 6176), 'm_out_proj': (2, 2048, 1024),
    'r_wr': (2, 1024, 1024), 'r_wk': (2, 1024, 1024), 'r_wv': (2, 1024, 1024), 'r_wo': (2, 1024, 1024),
    'r_w1': (2, 1024, 64), 'r_w2': (2, 64, 1024), 'r_a1': (2, 1024, 64), 'r_a2': (2, 64, 1024),
    'r_g1': (2, 1024, 160), 'r_g2': (2, 160, 1024), 'r_v1': (1, 1024, 32), 'r_v2': (1, 32, 1024),
}
VECS = ['norm_ffn1', 'norm_mix', 'norm_ffn2', 'norm_ple', 'norm_final', 'm_conv_w', 'm_conv_b', 'm_norm',
        'r_mu', 'r_w0', 'r_a0', 'r_k_k', 'r_k_a', 'r_r_k', 'r_gn_w', 'r_gn_b', 'r_v0']
VSHAPES = {'norm_ffn1': (4, 1024), 'norm_mix': (4, 1024), 'norm_ffn2': (4, 1024), 'norm_ple': (4, 1024),
           'norm_final': (1024,), 'm_conv_w': (2, 4, 4096), 'm_conv_b': (2, 4096), 'm_norm': (2, 2048),
           'r_mu': (2, 6, 1024), 'r_w0': (2, 1024), 'r_a0': (2, 1024), 'r_k_k': (2, 1024), 'r_k_a': (2, 1024),
           'r_r_k': (2, 1024), 'r_gn_w': (2, 1024), 'r_gn_b': (2, 1024), 'r_v0': (1, 1024)}
VOFF = {}
_o = 0
for _n in VECS:
    VOFF[_n] = _o
    _o += int(np.prod(VSHAPES[_n])) // 128
NV = _o

C_ID, C_TRIP, C_ONES, C_TRIS, C_BLKS, C_BM, C_BMT = 0, 128, 256, 384, 448, 512, 528
C_STRIP = C_BMT + 1024
C_LOWP = C_STRIP + 128
C_STRIS = C_LOWP + 128
C_LOWS = C_STRIS + 64
C_MABP = C_LOWS + 64
C_MABS = C_MABP + 256
C_RSTP = C_MABS + 128
C_RSTS = C_RSTP + 128
C_BLK2 = C_RSTS + 64
NCC = C_BLK2 + 128

CFG = {'mixers': True, 'depth': DEPTH, 'cores': 8, 'rwkv': True}


class _Stop(Exception):
    pass


def _stg(n):
    if CFG.get('rstage', 99) < n:
        raise _Stop()


def make_consts():
    c = np.zeros((128, NCC), np.float32)
    i = np.arange(128)
    c[:, C_ID:C_ID + 128] = np.eye(128)
    c[:, C_TRIP:C_TRIP + 128] = (i[:, None] <= i[None, :])
    c[:, C_ONES:C_ONES + 128] = 1.0
    j = np.arange(64)
    same = (j[:, None] // 4) == (j[None, :] // 4)
    c[:64, C_TRIS:C_TRIS + 64] = same & (j[:, None] <= j[None, :])
    c[:64, C_BLKS:C_BLKS + 64] = same
    c[:64, C_BM:C_BM + 16] = (j[:, None] // 4) == np.arange(16)[None, :]
    bmt = ((j[None, :] // 4) == np.arange(16)[:, None]).astype(np.float32).reshape(1, 1024)
    c[:, C_BMT:C_BMT + 1024] = bmt
    c[:, C_STRIP:C_STRIP + 128] = (i[:, None] < i[None, :])
    c[:, C_LOWP:C_LOWP + 128] = (i[None, :] < i[:, None])
    c[:64, C_STRIS:C_STRIS + 64] = same & (j[:, None] < j[None, :])
    c[:64, C_LOWS:C_LOWS + 64] = same & (j[None, :] < j[:, None])
    c[:, C_MABP:C_MABP + 128] = c[:, C_STRIP:C_STRIP + 128]
    c[:, C_MABP + 128:C_MABP + 256] = c[:, C_TRIP:C_TRIP + 128]
    c[:64, C_MABS:C_MABS + 64] = c[:64, C_STRIS:C_STRIS + 64]
    c[:64, C_MABS + 64:C_MABS + 128] = c[:64, C_TRIS:C_TRIS + 64]
    c[:, C_RSTP:C_RSTP + 128] = 1.0
    c[:, C_RSTP] = 0.0
    c[:, C_RSTS:C_RSTS + 64] = (np.arange(64) % 4 != 0)[None, :]
    c[:, C_BLK2:C_BLK2 + 128] = (i[:, None] // 64) == (i[None, :] // 64)
    return c


def build_program():
    nc = bass.Bass("TRN2", target_bir_lowering=False)

    def din(name, shape):
        return nc.dram_tensor(name, list(shape), F32, kind="ExternalInput").ap()

    def dout(name, shape):
        return nc.dram_tensor(name, list(shape), F32, kind="ExternalOutput").ap()

    xT = din('xT', [D, NT])
    pT = din('pT', [4, 256, NT])
    vecs_d = din('vecs', [128, NV])
    Wd_ = {n: din(n, s) for n, s in WSHAPES.items()}
    yT = dout('yT', [D, NT])
    consts_d = din('consts', [128, NCC])
    hv_d = din('hv', [32, 4])
    dbc_d = din('dbc', [128, 64])
    ssm_in = din('ssm_in', [2, 16, 32, 64, 128])
    conv_in = din('conv_in', [2, 4096, 16, 3])
    wkv_in = din('wkv_in', [2, 16, 16, 64, 64])
    shift_in = din('shift_in', [2, 1024, 16])
    ssm_p = dout('ssm_p', [2, 32, 64, 128])
    conv_p = dout('conv_p', [2, 4096, 3])
    ssm_s = dout('ssm_s', [2, 16, 32, 64, 128])
    conv_s = dout('conv_s', [2, 4096, 16, 3])
    wkv_p = dout('wkv_p', [2, 16, 64, 64])
    shift_p = dout('shift_p', [2, 128, 8])
    wkv_s = dout('wkv_s', [2, 16, 16, 64, 64])
    shift_s = dout('shift_s', [2, 1024, 16])
    vfirst_d = dout('vfirst', [D, NT])

    with contextlib.ExitStack() as es:
        NB = 206 * 1024
        big = es.enter_context(nc.sbuf_tensor("big", [128, NB // 4], F32))
        psl = [es.enter_context(nc.psum_tensor("ps%d" % i, [128, 512], F32)) for i in range(8)]
        sb = SB(big, NB)
        S = Sched(nc)
        pctr = [0]

        def psum():
            i = pctr[0] % 8
            pctr[0] += 1
            return psl[i], ('ps', i)

        def MM(out, lhsT, rhs, start, stop, r, w):
            S.add('pe', lambda e: e.matmul(out, lhsT=lhsT, rhs=rhs, start=start, stop=stop), reads=r, writes=w)

        def TR(out, in_, ident, r, w):
            S.add('pe', lambda e: e.transpose(out, in_, ident), reads=r, writes=w)

        def ACT(out, in_, func, r, w, bias=None, scale=1.0):
            if bias is None:
                S.add('act', lambda e: e.activation(out=out, in_=in_, func=func, scale=scale), reads=r, writes=w)
            else:
                S.add('act', lambda e: e.activation(out=out, in_=in_, func=func, bias=bias, scale=scale), reads=r, writes=w)

        def TT(eng, out, in0, in1, op, r, w):
            S.add(eng, lambda e: e.tensor_tensor(out=out, in0=in0, in1=in1, op=op), reads=r, writes=w)

        def TS(eng, out, in0, s1, s2, op0, op1, r, w):
            if s2 is None:
                S.add(eng, lambda e: e.tensor_scalar(out=out, in0=in0, scalar1=s1, scalar2=None, op0=op0), reads=r, writes=w)
            else:
                S.add(eng, lambda e: e.tensor_scalar(out=out, in0=in0, scalar1=s1, scalar2=s2, op0=op0, op1=op1), reads=r, writes=w)

        def STT(eng, out, in0, scalar, in1, op0, op1, r, w):
            S.add(eng, lambda e: e.scalar_tensor_tensor(out=out, in0=in0, scalar=scalar, in1=in1, op0=op0, op1=op1), reads=r, writes=w)

        def CP(eng, out, in_, r, w):
            S.add(eng, lambda e: e.tensor_copy(out=out, in_=in_), reads=r, writes=w)

        def MSET(eng, ap, val, w):
            S.add(eng, lambda e: e.memset(ap, val), writes=w)

        def DMA(eng, out, in_, r, w):
            S.add(eng, lambda e: e.dma_start(out=out, in_=in_), reads=r, writes=w, dma=True)

        h = sb.alloc(KC * NT, F32).rearrange("p (k t) -> p k t", k=KC)
        vecs = sb.alloc(NV, F32)
        ones_bf = sb.alloc(128, BF16)
        epsc = sb.alloc(1, F32)
        cst = sb.alloc(NCC, F32)
        hv = sb.alloc(4, F32, parts=32)
        dbc = sb.alloc(64, F32)
        onec = sb.alloc(1, F32)
        eps5c = sb.alloc(1, F32)
        ident_b = sb.alloc(128, BF16)
        trip_b = sb.alloc(128, BF16)
        tris_b = sb.alloc(64, BF16)
        bmT_b = sb.alloc(1024, BF16)
        DMA('sp', vecs, vecs_d, [], ['vecs'])
        DMA('sp', cst, consts_d, [], ['cst'])
        DMA('sp', hv, hv_d, [], ['hv'])
        DMA('sp', dbc, dbc_d, [], ['dbc'])
        MSET('dve', onec, 1.0, ['onec'])
        MSET('dve', eps5c, 1e-5, ['eps5c'])
        CP('dve', ident_b, cst[:, C_ID:C_ID + 128], ['cst'], ['ident_b'])
        CP('dve', trip_b, cst[:, C_TRIP:C_TRIP + 128], ['cst'], ['trip_b'])
        CP('dve', tris_b, cst[:, C_TRIS:C_TRIS + 64], ['cst'], ['tris_b'])
        CP('dve', bmT_b, cst[:, C_BMT:C_BMT + 1024], ['cst'], ['bmT_b'])
        ident_f = cst[:, C_ID:C_ID + 128]
        DMA('sp', h, xT.rearrange("(k p) t -> p k t", p=128), [], ['h'])
        MSET('dve', ones_bf, 1.0, ['ones_bf'])
        MSET('dve', epsc, EPS, ['epsc'])

        def vcol(name, idx):
            o = VOFF[name] + idx
            return vecs[:, o:o + 1]

        scr0 = sb.mark()
        S.barrier()

        def rmsnorm(t0, T, gname, gbase, out, okey, sq, rs, ti, hkey=None, sqkey=('sq',)):
            hk = hkey if hkey is not None else ('h', ti)
            ACT(sq[:, :, :T], h[:, :, t0:t0 + T], AF.Square, [hk], [sqkey])
            ps, pk = psum()
            for k in range(KC):
                MM(ps[:, :T], ones_bf, sq[:, k, :T], k == 0, k == KC - 1, [sqkey, 'ones_bf'], [pk])
            ACT(rs[:, :T], ps[:, :T], AF.Ln, [pk, 'epsc'], [('rs',)], bias=epsc[:, 0:1], scale=1.0 / D)
            ACT(rs[:, :T], rs[:, :T], AF.Exp, [('rs',)], [('rs',)], scale=-0.5)
            for k in range(KC):
                STT('dve', out[:, k, :T], h[:, k, t0:t0 + T], vcol(gname, gbase + k), rs[:, :T], ALU.mult, ALU.mult,
                    [hk, ('rs',), 'vecs'], [okey])

        def ffn(l, wgu_d, wd_d, gname):
            m = sb.mark()
            xn = sb.alloc(KC * NT, BF16).rearrange("p (k t) -> p k t", k=KC)
            act = sb.alloc(4 * NT, BF16).rearrange("p (f t) -> p f t", f=4)
            wgu = [sb.alloc(KC * 2 * 512, BF16).rearrange("p (k g n) -> p k g n", k=KC, g=2) for _ in range(2)]
            wdb = [sb.alloc(4 * 1024, BF16).rearrange("p (f n) -> p f n", f=4) for _ in range(2)]
            sq = sb.alloc(KC * 512, BF16).rearrange("p (k t) -> p k t", k=KC)
            rs = sb.alloc(512, F32)
            sg = [sb.alloc(512, F32) for _ in range(2)]
            for ti, (t0, T) in enumerate(TILES):
                rmsnorm(t0, T, gname, l * KC, xn[:, :, t0:t0 + T], ('xn', ti), sq, rs, ti)
            wg_v = wgu_d[l].rearrange("(k p) n -> p k n", p=128)
            wd_v = wd_d[l].rearrange("(f p) n -> p f n", p=128)
            cnt = 0
            for gi, (f0, nf) in enumerate(FGROUPS):
                wb = wgu[gi % 2]
                wdd = wdb[gi % 2]
                DMA('pool', wb[:, :, 0, :nf * 128], wg_v[:, :, f0 * 128:(f0 + nf) * 128], [], [('wgu', gi % 2)])
                DMA('pool', wb[:, :, 1, :nf * 128], wg_v[:, :, DFF + f0 * 128:DFF + (f0 + nf) * 128], [], [('wgu', gi % 2)])
                DMA('pool', wdd[:, :nf, :], wd_v[:, f0:f0 + nf, :], [], [('wd', gi % 2)])
                for ti, (t0, T) in enumerate(TILES):
                    for f in range(nf):
                        pg, pgk = psum()
                        pu, puk = psum()
                        for k in range(KC):
                            MM(pg[:, :T], wb[:, k, 0, f * 128:(f + 1) * 128], xn[:, k, t0:t0 + T], k == 0, k == KC - 1,
                               [('wgu', gi % 2), ('xn', ti)], [pgk])
                        for k in range(KC):
                            MM(pu[:, :T], wb[:, k, 1, f * 128:(f + 1) * 128], xn[:, k, t0:t0 + T], k == 0, k == KC - 1,
                               [('wgu', gi % 2), ('xn', ti)], [puk])
                        sgb = sg[cnt % 2]
                        sk = ('sg', cnt % 2)
                        cnt += 1
                        ACT(sgb[:, :T], pg[:, :T], AF.Silu, [pgk], [sk])
                        TT('dve', act[:, f, t0:t0 + T], sgb[:, :T], pu[:, :T], ALU.mult, [sk, puk], [('act', f, ti)])
                for ti, (t0, T) in enumerate(TILES):
                    for d in range(KC):
                        po, pok = psum()
                        for f in range(nf):
                            MM(po[:, :T], wdd[:, f, d * 128:(d + 1) * 128], act[:, f, t0:t0 + T], f == 0, f == nf - 1,
                               [('wd', gi % 2), ('act', f, ti)], [pok])
                        STT('dve', h[:, d, t0:t0 + T], po[:, :T], 0.5, h[:, d, t0:t0 + T], ALU.mult, ALU.add,
                            [pok, ('h', ti)], [('h', ti)])
            S.barrier()
            sb.reset(m)

        def ple(l):
            m = sb.mark()
            xn = sb.alloc(KC * NT, BF16).rearrange("p (k t) -> p k t", k=KC)
            wg = sb.alloc(KC * 1024, BF16).rearrange("p (k n) -> p k n", k=KC)
            wpi = sb.alloc(2 * 1024, BF16).rearrange("p (k n) -> p k n", k=2)
            ptb = sb.alloc(2 * NT, BF16).rearrange("p (k t) -> p k t", k=2)
            sq = sb.alloc(KC * 512, BF16).rearrange("p (k t) -> p k t", k=KC)
            rs = sb.alloc(512, F32)
            sg = [sb.alloc(512, F32) for _ in range(2)]
            DMA('pool', wg, Wd_['ple_gate'][l].rearrange("(k p) n -> p k n", p=128), [], ['wg'])
            DMA('pool', wpi, Wd_['ple_in'][l].rearrange("(k p) n -> p k n", p=128), [], ['wpi'])
            DMA('pool', ptb, pT[l].rearrange("(k p) t -> p k t", p=128), [], ['ptb'])
            for ti, (t0, T) in enumerate(TILES):
                rmsnorm(t0, T, 'norm_ple', l * KC, xn[:, :, t0:t0 + T], ('xn', ti), sq, rs, ti)
            cnt = 0
            for ti, (t0, T) in enumerate(TILES):
                for d in range(KC):
                    p1, p1k = psum()
                    p2, p2k = psum()
                    for k in range(KC):
                        MM(p1[:, :T], wg[:, k, d * 128:(d + 1) * 128], xn[:, k, t0:t0 + T], k == 0, k == KC - 1,
                           ['wg', ('xn', ti)], [p1k])
                    for k in range(2):
                        MM(p2[:, :T], wpi[:, k, d * 128:(d + 1) * 128], ptb[:, k, t0:t0 + T], k == 0, k == 1,
                           ['wpi', 'ptb'], [p2k])
                    sgb = sg[cnt % 2]
                    sk = ('sg', cnt % 2)
                    cnt += 1
                    ACT(sgb[:, :T], p1[:, :T], AF.Sigmoid, [p1k], [sk])
                    TT('dve', sgb[:, :T], sgb[:, :T], p2[:, :T], ALU.mult, [sk, p2k], [sk])
                    TT('pool', h[:, d, t0:t0 + T], h[:, d, t0:t0 + T], sgb[:, :T], ALU.add, [sk, ('h', ti)], [('h', ti)])
            S.barrier()
            sb.reset(m)

        def final():
            m = sb.mark()
            sq = sb.alloc(KC * 512, BF16).rearrange("p (k t) -> p k t", k=KC)
            rs = sb.alloc(512, F32)
            yo = [sb.alloc(KC * 512, F32).rearrange("p (k t) -> p k t", k=KC) for _ in range(2)]
            yv = yT.rearrange("(k p) t -> p k t", p=128)
            for ti, (t0, T) in enumerate(TILES):
                yb = yo[ti % 2]
                rmsnorm(t0, T, 'norm_final', 0, yb, ('yo', ti % 2), sq, rs, ti)
                DMA('sp', yv[:, :, t0:t0 + T], yb[:, :, :T], [('yo', ti % 2)], [('yT', ti)])
            sb.reset(m)

        def mamba(l):
            j = l // 2
            Win = Wd_['m_in_proj'][j].rearrange("(k p) n -> p k n", p=128)
            Wout = Wd_['m_out_proj'][j].rearrange("(c p) n -> p c n", p=128)
            R0 = ['cst', 'vecs', 'hv', 'dbc', 'onec', 'eps5c', 'ident_b', 'trip_b', 'tris_b', 'bmT_b']
            cw = [0]

            def run(smp):
                m = sb.mark()
                T = 64 if smp else 256
                L = 64 if smp else 128
                nch = T // L
                tiles = [(2048, 64)] if smp else [(i * 256, 256) for i in range(8)]
                tri = cst[:L, C_TRIS:C_TRIS + 64] if smp else cst[:, C_TRIP:C_TRIP + 128]
                onesm = cst[:L, C_BLKS:C_BLKS + 64] if smp else cst[:, C_ONES:C_ONES + 128]
                cmask = tris_b[:L, :] if smp else trip_b
                u = sb.alloc(KC * T, BF16).rearrange("p (k t) -> p k t", k=KC)
                rs = sb.alloc(T, F32)
                NWB = 4
                wblk = [sb.alloc(KC * 256, BF16).rearrange("p (k n) -> p k n", k=KC) for _ in range(NWB)]
                wdt = sb.alloc(KC * 32, BF16).rearrange("p (k n) -> p k n", k=KC)
                zs = sb.alloc(nch * 2048, BF16).rearrange("p (c n) -> p c n", c=nch)
                PW = 112 if smp else 3 + T
                preb = [sb.alloc(PW, F32) for _ in range(2)]
                accb = [sb.alloc(T, F32) for _ in range(2)]
                xa = sb.alloc(32 * T, BF16).rearrange("p (c t) -> p c t", c=32)
                dtT = sb.alloc(T, F32, parts=32)
                dtAT = sb.alloc(T, F32, parts=32)
                expA = sb.alloc(1, F32, parts=32)
                xtok = sb.alloc(2048, BF16)
                Btok = sb.alloc(1024, BF16)
                dtk = sb.alloc(64, F32)
                nak = sb.alloc(64, F32)
                ena = sb.alloc(32, F32)
                dend = sb.alloc(32, F32)
                xdt = sb.alloc(2048, BF16)
                sq = xdt[:, 0:KC * T].rearrange("p (k t) -> p k t", k=KC)
                NSET = 1 if smp else 4
                gsets = []
                for _si in range(NSET):
                    gsets.append(dict(
                        cbm=sb.alloc(128, BF16),
                        dexp=sb.alloc(4 * 128, F32).rearrange("p (a b) -> p a b", a=4),
                        dec=[sb.alloc(128, F32) for _ in range(2)],
                        MT=sb.alloc(4 * 128, BF16).rearrange("p (a b) -> p a b", a=4),
                        tmp=sb.alloc(256, F32), yg=sb.alloc(256, F32), ssq=sb.alloc(1, F32),
                        ynk=sb.alloc(256, BF16), xdd=sb.alloc(256, BF16)))
                print('MAMBA sets alloc off', sb.off, sb.nbytes, smp)
                ynT = sb.alloc(16 * T, BF16).rearrange("p (c t) -> p c t", c=16)
                wob = [sb.alloc(16 * 128, BF16).rearrange("p (c n) -> p c n", c=16) for _ in range(2)]
                if smp:
                    cstate = sb.alloc(32 * 48, F32).rearrange("p (c b r) -> p c b r", c=32, b=16)
                    cso = sb.alloc(32 * 48, F32).rearrange("p (c b r) -> p c b r", c=32, b=16)
                    nat = sb.alloc(16 * 256, F32).rearrange("p (b q n) -> p b q n", b=16, q=2)
                    STs = sb.alloc(16 * 256, F32).rearrange("p (b n) -> p b n", b=16)
                    STsb = sb.alloc(16 * 256, BF16).rearrange("p (b n) -> p b n", b=16)
                    CTm = sb.alloc(16 * 64, BF16).rearrange("p (b t) -> p b t", b=16)
                    xddm = sb.alloc(16 * 256, BF16).rearrange("p (b n) -> p b n", b=16)
                    edCs = sb.alloc(512, F32).rearrange("p (b h) -> p b h", b=16)
                    dtAe = sb.alloc(512, F32).rearrange("p (b h) -> p b h", b=16)
                    DMA('sp', cstate, conv_in[j].rearrange("(c p) b r -> p c b r", p=128), [], ['cstate'])
                else:
                    carry = sb.alloc(96, F32).rearrange("p (c r) -> p c r", c=32)
                    ST = sb.alloc(2048, F32)
                    STb = sb.alloc(2048, BF16)
                    natp = sb.alloc(512, F32).rearrange("p (q n) -> p q n", q=4)
                    edC = sb.alloc(32, F32)
                    MSET('dve', carry, 0.0, ['carry'])
                    MSET('dve', ST, 0.0, ['ST'])
                    MSET('pool', STb, 0.0, ['STb'])
                ACT(expA, hv[:, 2 * j + 1:2 * j + 2], AF.Exp, ['hv'], ['expA'])

                mcols = [zb * 256 for zb in range(8)] + [2048 + xb * 256 for xb in range(16)]
                mseq = [i for _t in tiles for i in range(24)]
                mnext = [0]
                mq = []

                def missue(i):
                    wb = wblk[cw[0] % NWB]
                    wk = ('wblk', cw[0] % NWB)
                    cw[0] += 1
                    DMA('pool', wb, Win[:, :, mcols[i]:mcols[i] + 256], [], [wk])
                    return wb, wk

                def mfill():
                    while len(mq) < NWB - 1 and mnext[0] < len(mseq):
                        mq.append(missue(mseq[mnext[0]]))
                        mnext[0] += 1

                def mget():
                    mfill()
                    x = mq.pop(0)
                    mfill()
                    return x

                for (t0, T_) in tiles:
                    DMA('pool', wdt, Win[:, :, 6144:6176], [], ['wdt'])
                    mfill()
                    rmsnorm(t0, T, 'norm_mix', l * KC, u, ('u',), sq, rs, 0, hkey='h', sqkey='xdt')
                    for zb in range(8):
                        wb, wk = mget()
                        for tc in range(nch):
                            ps, pk = psum()
                            for k in range(KC):
                                MM(ps[:L, 0:256], u[:, k, tc * L:(tc + 1) * L], wb[:, k, :], k == 0, k == KC - 1, [('u',), wk], [pk])
                            ACT(zs[:L, tc, zb * 256:(zb + 1) * 256], ps[:L, 0:256], AF.Silu, [pk], [('zs', tc, zb)])
                    for xb in range(16):
                        wb, wk = mget()
                        def cc_gen(cc, q):
                            ps, pk = psum()
                            for k in range(KC):
                                MM(ps[:, :T], wb[:, k, q * 128:(q + 1) * 128], u[:, k, :], k == 0, k == KC - 1, [('u',), wk], [pk])
                            pre = preb[cc % 2]
                            prk = ('pre', cc % 2)
                            acc = accb[cc % 2]
                            ack = ('acc', cc % 2)
                            ce = 'dve'
                            if smp:
                                prev = pre.rearrange("p (b c) -> p b c", c=7)
                                ACT(prev[:, :, 0:3], cstate[:, cc], AF.Copy, ['cstate'], [prk])
                                ACT(prev[:, :, 3:7], ps[:, :T].rearrange("p (b t) -> p b t", t=4), AF.Copy, [pk], [prk])
                                win = lambda k_: prev[:, :, k_:k_ + 4]
                                accv = acc.rearrange("p (b t) -> p b t", t=4)
                                xav = xa[:, cc, :].rearrange("p (b t) -> p b t", t=4)
                            else:
                                ACT(pre[:, 0:3], carry[:, cc, :], AF.Copy, [('carry', cc)], [prk])
                                ACT(pre[:, 3:3 + T], ps[:, :T], AF.Copy, [pk], [prk])
                                win = lambda k_: pre[:, k_:k_ + T]
                                accv = acc
                                xav = xa[:, cc, :]
                            yield
                            TS(ce, accv, win(0), vcol('m_conv_w', (j * 4 + 0) * 32 + cc), None, ALU.mult, None, [prk, 'vecs'], [ack])
                            for k_ in range(1, 4):
                                yield
                                STT(ce, accv, win(k_), vcol('m_conv_w', (j * 4 + k_) * 32 + cc), accv, ALU.mult, ALU.add,
                                    [prk, 'vecs', ack], [ack])
                            yield
                            ACT(xav, accv, AF.Silu, [ack, 'vecs'], [('xa', cc)], bias=vcol('m_conv_b', j * 32 + cc))
                            if smp:
                                ACT(cso[:, cc], prev[:, :, 4:7], AF.Copy, [prk], [('cso', cc)])
                            else:
                                ACT(carry[:, cc, :], pre[:, T:T + 3], AF.Copy, [prk], [('carry', cc)])
                            yield
                        cgs = [cc_gen(xb * 2 + q, q) for q in range(2)]
                        alive = True
                        while alive:
                            alive = False
                            for cg in cgs:
                                try:
                                    next(cg)
                                    alive = True
                                except StopIteration:
                                    pass
                    ps, pk = psum()
                    for k in range(KC):
                        MM(ps[:32, :T], wdt[:, k, :], u[:, k, :], k == 0, k == KC - 1, [('u',), 'wdt'], [pk])
                    ACT(dtT, ps[:32, :T], AF.Exp, [pk, 'hv'], ['dtT'], bias=hv[:, 2 * j:2 * j + 1])
                    ACT(dtT, dtT, AF.Ln, ['dtT', 'onec'], ['dtT'], bias=onec[0:32, 0:1])
                    TS('dve', dtAT, dtT, expA[:, 0:1], None, ALU.mult, None, ['dtT', 'expA'], ['dtAT'])

                    for tc in range(nch):
                        c0 = tc * L
                        for half in range(2):
                            ps, pk = psum()
                            psb = ps.bitcast(BF16)
                            for q in range(8):
                                cc = half * 8 + q
                                TR(psb[:L, q * 128:(q + 1) * 128], xa[:, cc, c0:c0 + L], ident_b, [('xa', cc), 'ident_b'], [pk])
                            CP('dve', xtok[:L, half * 1024:(half + 1) * 1024], psb[:L, :], [pk], [('xtok', half)])
                        ps, pk = psum()
                        psb = ps.bitcast(BF16)
                        for g in range(8):
                            TR(psb[:L, g * 128:(g + 1) * 128], xa[:, 16 + g, c0:c0 + L], ident_b, [('xa', 16 + g), 'ident_b'], [pk])
                        CP('dve', Btok[:L, :], psb[:L, :], [pk], ['Btok'])
                        ps, pk = psum()
                        TR(ps[:L, 0:32], dtT[:, c0:c0 + L], ident_f[0:32, 0:32], ['dtT', 'cst'], [pk])
                        TR(ps[:L, 32:64], dtAT[:, c0:c0 + L], ident_f[0:32, 0:32], ['dtAT', 'cst'], [pk])
                        CP('dve', dtk[:L, :], ps[:L, 0:64], [pk], ['dtk'])
                        ps, pk = psum()
                        MM(ps[:L, 0:32], tri, dtk[:L, 32:64], True, True, ['cst', 'dtk'], [pk])
                        MM(ps[:L, 32:64], onesm, dtk[:L, 32:64], True, True, ['cst', 'dtk'], [pk])
                        CP('dve', nak[:L, :], ps[:L, 0:64], [pk], ['nak'])
                        ACT(ena[:L, :], nak[:L, 0:32], AF.Exp, ['nak'], ['ena'], scale=-1.0)
                        TT('dve', dend[:L, :], nak[:L, 0:32], nak[:L, 32:64], ALU.subtract, ['nak'], ['dend'])
                        ACT(dend[:L, :], dend[:L, :], AF.Exp, ['dend'], ['dend'])
                        TT('dve', xdt[:L, :].rearrange("p (h d) -> p h d", h=32), xtok[:L, :].rearrange("p (h d) -> p h d", h=32),
                           dtk[:L, 0:32].rearrange("p (h o) -> p h o", o=1).broadcast_to([L, 32, 64]), ALU.mult, ['xtok', 'dtk'], ['xdt'])
                        if smp:
                            TT('dve', dtAe[:L], dtk[:L, 32:64].rearrange("p (o h) -> p o h", o=1).broadcast_to([L, 16, 32]),
                               cst[:L, C_BM:C_BM + 16].rearrange("p (b o) -> p b o", o=1).broadcast_to([L, 16, 32]), ALU.mult, ['dtk', 'cst'], ['dtAe'])
                            ps, pk = psum()
                            MM(ps[:, 0:512], cst[:L, C_ONES:C_ONES + 128], dtAe[:L].rearrange("p b h -> p (b h)"), True, True, ['cst', 'dtAe'], [pk])
                            ACT(edCs.rearrange("p b h -> p (b h)"), ps[:, 0:512], AF.Exp, [pk], ['edCs'], scale=-1.0)
                        else:
                            ACT(edC, nak[:, 32:64], AF.Exp, ['nak'], ['edC'], scale=-1.0)

                        def group_gen(g, si):
                            gs_ = gsets[si]
                            cbm, dexp, dec, MT, tmp, yg, ssq, ynk, xdd = (gs_['cbm'], gs_['dexp'], gs_['dec'], gs_['MT'], gs_['tmp'],
                                                                          gs_['yg'], gs_['ssq'], gs_['ynk'], gs_['xdd'])
                            BT = xa[:, 16 + g, c0:c0 + L]
                            CT = xa[:, 24 + g, c0:c0 + L]
                            if smp:
                                for q_ in range(2):
                                    DMA('sp', nat[:, :, q_, :], ssm_in[j][:, 4 * g + 2 * q_:4 * g + 2 * q_ + 2].rearrange("b a p n -> (a p) b n"), [], ['nat'])
                                for b4 in range(8):
                                    ps, pk = psum()
                                    for i4 in range(4):
                                        b_ = b4 * 2 + i4 // 2
                                        q_ = i4 % 2
                                        TR(ps[:, i4 * 128:(i4 + 1) * 128], nat[:, b_, q_, :], ident_f, ['nat', 'cst'], [pk])
                                    CP('dve', STs[:, b4 * 2:b4 * 2 + 2, :].rearrange("p b n -> p (b n)"), ps[:, :], [pk], [('STs', b4)])
                                    CP('act' if False else 'pool', STsb[:, b4 * 2:b4 * 2 + 2, :], STs[:, b4 * 2:b4 * 2 + 2, :], [('STs', b4)], [('STsb', b4)])
                            ps, pk = psum()
                            MM(ps[:L, :L], BT, CT, True, True, [('xa', 16 + g), ('xa', 24 + g)], [pk])
                            TT('dve', cbm[:L, :L], ps[:L, :L], cmask[:L, :L], ALU.mult, [pk, 'trip_b', 'tris_b'], [('cbm', si)])
                            CP('pool', dexp[:L, :, :L], dtk[:L, 32 + 4 * g:36 + 4 * g].rearrange("p (a o) -> p a o", o=1).broadcast_to([L, 4, L]), ['dtk'], [('dexp', si)])
                            yield
                            ps2, pk2 = psum()
                            for hh in range(4):
                                MM(ps2[:L, hh * L:(hh + 1) * L], dexp[:L, hh, :L], tri, True, True, [('dexp', si), 'cst'], [pk2])
                            yield
                            for hh in range(4):
                                h_ = 4 * g + hh
                                dc = dec[hh % 2]
                                dk = ('dec', si, hh % 2)
                                ACT(dc[:L, :L], ps2[:L, hh * L:(hh + 1) * L], AF.Exp, [pk2, 'nak'], [dk], bias=nak[:L, h_:h_ + 1], scale=-1.0)
                                STT('dve', MT[:L, hh, :L], dc[:L, :L], 1.0, cbm[:L, :L], ALU.min, ALU.mult, [dk, ('cbm', si)], [('MT', si, hh)])
                            yield
                            psy, pyk = psum()
                            for hh in range(4):
                                h_ = 4 * g + hh
                                MM(psy[:L, hh * 64:(hh + 1) * 64], MT[:L, hh, :L], xdt[:L, h_ * 64:(h_ + 1) * 64], True, True, [('MT', si, hh), 'xdt'], [pyk])
                            if smp:
                                TT('pool', CTm, CT.rearrange("p (o t) -> p o t", o=1).broadcast_to([128, 16, 64]),
                                   bmT_b.rearrange("p (b t) -> p b t", b=16), ALU.mult, [('xa', 24 + g), 'bmT_b'], ['CTm'])
                                for b in range(16):
                                    MM(psy[:L, 256:512], CTm[:, b, :], STsb[:, b, :], b == 0, b == 15, ['CTm', ('STsb', b // 2)], [pyk])
                            else:
                                MM(psy[:L, 256:512], CT, STb[:, g * 256:(g + 1) * 256], True, True, [('xa', 24 + g), ('STb', g)], [pyk])
                            yield
                            t3 = tmp[:L, :].rearrange("p (a d) -> p a d", a=4)
                            TT('dve', t3, psy[:L, 256:512].rearrange("p (a d) -> p a d", a=4),
                               ena[:L, 4 * g:4 * g + 4].rearrange("p (a o) -> p a o", o=1).broadcast_to([L, 4, 64]), ALU.mult, [pyk, 'ena'], [('tmp', si)])
                            TT('dve', yg[:L, :], psy[:L, 0:256], tmp[:L, :], ALU.add, [pyk, ('tmp', si)], [('yg', si)])
                            TT('pool', t3, xtok[:L, g * 256:(g + 1) * 256].rearrange("p (a d) -> p a d", a=4),
                               dbc[:L, j * 32 + 4 * g:j * 32 + 4 * g + 4].rearrange("p (a o) -> p a o", o=1).broadcast_to([L, 4, 64]), ALU.mult,
                               [('xtok', g // 4), 'dbc', ('yg', si)], [('tmp', si)])
                            TT('dve', yg[:L, :], yg[:L, :], tmp[:L, :], ALU.add, [('tmp', si), ('yg', si)], [('yg', si)])
                            yield
                            TT('dve', yg[:L, :], yg[:L, :], zs[:L, tc, g * 256:(g + 1) * 256], ALU.mult, [('yg', si), ('zs', tc, g)], [('yg', si)])
                            TT('pool', tmp[:L, :], yg[:L, :], yg[:L, :], ALU.mult, [('yg', si)], [('tmp', si)])
                            S.add('dve', (lambda e, o=ssq[:L, 0:1], i_=tmp[:L, :]: e.tensor_reduce(out=o, in_=i_, axis=AX.X, op=ALU.add)), reads=[('tmp', si)], writes=[('ssq', si)])
                            yield
                            ACT(ssq[:L, :], ssq[:L, :], AF.Ln, [('ssq', si), 'eps5c'], [('ssq', si)], bias=eps5c[:L, 0:1], scale=1.0 / 256)
                            ACT(ssq[:L, :], ssq[:L, :], AF.Exp, [('ssq', si)], [('ssq', si)], scale=-0.5)
                            TS('dve', ynk[:L, :], yg[:L, :], ssq[:L, 0:1], None, ALU.mult, None, [('yg', si), ('ssq', si)], [('ynk', si)])
                            yield
                            pst, ptk = psum()
                            pstb = pst.bitcast(BF16)
                            for q in range(2):
                                TR(pstb[:, q * L:(q + 1) * L], ynk[:L, q * 128:(q + 1) * 128], ident_b[:L, :L], [('ynk', si), 'ident_b'], [ptk])
                            for q in range(2):
                                cc = 2 * g + q
                                ACT(ynT[:, cc, c0:c0 + L], pstb[:, q * L:(q + 1) * L], AF.Copy, [ptk, 'vecs'], [('ynT', cc)], scale=vcol('m_norm', j * 16 + cc))
                            yield
                            TT('dve', xdd[:L, :].rearrange("p (a d) -> p a d", a=4), xdt[:L, g * 256:(g + 1) * 256].rearrange("p (a d) -> p a d", a=4),
                               dend[:L, 4 * g:4 * g + 4].rearrange("p (a o) -> p a o", o=1).broadcast_to([L, 4, 64]), ALU.mult, ['xdt', 'dend'], [('xdd', si)])
                            if smp:
                                TT('pool', xddm[:L], xdd[:L, :].rearrange("p (o n) -> p o n", o=1).broadcast_to([L, 16, 256]),
                                   cst[:L, C_BM:C_BM + 16].rearrange("p (b o) -> p b o", o=1).broadcast_to([L, 16, 256]), ALU.mult, [('xdd', si), 'cst'], ['xddm'])
                                TT('dve', STs.rearrange("p b (a d) -> p b a d", a=4), STs.rearrange("p b (a d) -> p b a d", a=4),
                                   edCs[:, :, 4 * g:4 * g + 4].rearrange("p b (a o) -> p b a o", o=1).broadcast_to([128, 16, 4, 64]), ALU.mult,
                                   ['STs', 'edCs', 'STsb'], ['STs'])
                                for b2 in range(8):
                                    pss, psk = psum()
                                    for i2 in range(2):
                                        MM(pss[:, i2 * 256:(i2 + 1) * 256], Btok[:L, g * 128:(g + 1) * 128], xddm[:L, b2 * 2 + i2, :], True, True, ['Btok', 'xddm'], [psk])
                                    TT('dve', STs[:, b2 * 2:b2 * 2 + 2, :].rearrange("p b n -> p (b n)"), STs[:, b2 * 2:b2 * 2 + 2, :].rearrange("p b n -> p (b n)"),
                                       pss[:, :], ALU.add, [psk, ('STs', b2)], [('STs', b2)])
                                for b4 in range(8):
                                    ps, pk = psum()
                                    for i4 in range(4):
                                        b_ = b4 * 2 + i4 // 2
                                        q_ = i4 % 2
                                        TR(ps[:, i4 * 128:(i4 + 1) * 128], STs[:, b_, q_ * 128:(q_ + 1) * 128], ident_f, [('STs', b4), 'cst'], [pk])
                                    CP('dve', nat[:, b4 * 2:b4 * 2 + 2].rearrange("p b q n -> p (b q n)"), ps[:, :], [pk], ['nat'])
                                for q_ in range(2):
                                    DMA('sp', ssm_s[j][:, 4 * g + 2 * q_:4 * g + 2 * q_ + 2].rearrange("b a p n -> (a p) b n"), nat[:, :, q_, :], ['nat'], [('ssm_s', j, g, q_)])
                            else:
                                pss, psk = psum()
                                MM(pss[:, 0:256], Btok[:L, g * 128:(g + 1) * 128], xdd[:L, :], True, True, ['Btok', ('xdd', si)], [psk])
                                sg3 = ST[:, g * 256:(g + 1) * 256].rearrange("p (a d) -> p a d", a=4)
                                TT('dve', sg3, sg3, edC[:, 4 * g:4 * g + 4].rearrange("p (a o) -> p a o", o=1).broadcast_to([128, 4, 64]), ALU.mult,
                                   [('ST', g), 'edC', ('STb', g)], [('ST', g)])
                                TT('dve', ST[:, g * 256:(g + 1) * 256], ST[:, g * 256:(g + 1) * 256], pss[:, 0:256], ALU.add, [psk, ('ST', g)], [('ST', g)])
                                CP('pool', STb[:, g * 256:(g + 1) * 256], ST[:, g * 256:(g + 1) * 256], [('ST', g)], [('STb', g)])
                            yield
                        for gb in range(0, 8, NSET):
                            gens_ = [group_gen(g, g % NSET) for g in range(gb, gb + NSET)]
                            alive = True
                            while alive:
                                alive = False
                                for gn_ in gens_:
                                    try:
                                        next(gn_)
                                        alive = True
                                    except StopIteration:
                                        pass
                    DMA('pool', wob[0], Wout[:, :, 0:128], [], [('wob', 0)])
                    for d in range(KC):
                        wo = wob[d % 2]
                        wok = ('wob', d % 2)
                        if d + 1 < KC:
                            DMA('pool', wob[(d + 1) % 2], Wout[:, :, (d + 1) * 128:(d + 2) * 128], [], [('wob', (d + 1) % 2)])
                        ps, pk = psum()
                        for cc in range(16):
                            MM(ps[:, :T], wo[:, cc, :], ynT[:, cc, :], cc == 0, cc == 15, [wok, ('ynT', cc)], [pk])
                        TT('dve', h[:, d, t0:t0 + T], h[:, d, t0:t0 + T], ps[:, :T], ALU.add, [pk, 'h'], ['h'])
                if smp:
                    DMA('sp', conv_s[j].rearrange("(c p) b r -> p c b r", p=128), cso, ['cso'], [('conv_s', j)])
                else:
                    DMA('sp', conv_p[j].rearrange("(c p) r -> p c r", p=128), carry, ['carry'], [('conv_p', j)])
                    for q4 in range(4):
                        ps, pk = psum()
                        for i4 in range(4):
                            q = q4 * 4 + i4
                            TR(ps[:, i4 * 128:(i4 + 1) * 128], ST[:, q * 128:(q + 1) * 128], ident_f, ['ST', 'cst'], [pk])
                        CP('dve', natp.rearrange("p q n -> p (q n)"), ps[:, :], [pk], ['natp'])
                        DMA('sp', ssm_p[j].rearrange("(q a) p n -> (a p) q n", a=2)[:, q4 * 4:q4 * 4 + 4, :], natp, ['natp'], [('ssm_p', j, q4)])
                S.barrier()
                sb.reset(m)

            run(False)
            run(True)


        def rwkv(l):
            j = l // 2
            RDT = BF16
            mu0 = VOFF['r_mu'] + j * 48

            def wview(name, jj=None):
                return Wd_[name][j if jj is None else jj]

            def run(smp):
                m = sb.mark()
                T = 64 if smp else 128
                L = T
                nb = 16 if smp else 1
                lt = L // nb
                nlev = 2 if smp else 7
                tiles = [(2048, 64)] if smp else [(i * 128, 128) for i in range(16)]
                mAB = cst[:L, C_MABS:C_MABS + 128] if smp else cst[:, C_MABP:C_MABP + 256]
                mlow = cst[:L, C_LOWS:C_LOWS + 64] if smp else cst[:, C_LOWP:C_LOWP + 128]
                rst = cst[:, C_RSTS:C_RSTS + 64] if smp else cst[:, C_RSTP:C_RSTP + 128]
                blk2 = cst[:, C_BLK2:C_BLK2 + 128]
                f32a = lambda: sb.alloc(KC * T, F32).rearrange("p (k t) -> p k t", k=KC)
                b16a = lambda: sb.alloc(KC * T, BF16).rearrange("p (k t) -> p k t", k=KC)
                uf, up, r_, k_, v_, nlw, a_, kk, np_, tA, tB = [f32a() for _ in range(11)]
                xm = [b16a() for _ in range(2)]
                g_, BT_, KT_, BH_, KH_ = [b16a() for _ in range(5)]
                V_ = xm[1]
                yg = xm[0]
                ARt = sb.alloc(KC * 2 * T, BF16).rearrange("p (k c t) -> p k c t", k=KC, c=2)
                sq = BH_
                rs = sb.alloc(T, F32)
                NWBR = 3
                wbuf = [sb.alloc(KC * 256, BF16).rearrange("p (k n) -> p k n", k=KC) for _ in range(NWBR)]
                w1b = sb.alloc(KC * 64, BF16).rearrange("p (k n) -> p k n", k=KC)
                a1b = sb.alloc(KC * 64, BF16).rearrange("p (k n) -> p k n", k=KC)
                g1b = sb.alloc(KC * 160, BF16).rearrange("p (k n) -> p k n", k=KC)
                v1b = sb.alloc(KC * 32, BF16).rearrange("p (k n) -> p k n", k=KC)
                w2b = sb.alloc(1024, BF16)
                a2b = sb.alloc(1024, BF16)
                g2a = sb.alloc(1024, BF16)
                g2b = sb.alloc(1024, BF16)
                v2b = sb.alloc(1024, BF16)
                lo1 = sb.alloc(T, BF16)
                lo2 = sb.alloc(T, BF16)
                lo3 = sb.alloc(T, BF16)
                negw0 = sb.alloc(8, F32)
                omka = sb.alloc(8, F32)
                mhalf = sb.alloc(1, F32)
                c24 = sb.alloc(1, F32)
                gnec = sb.alloc(1, F32)
                eC = sb.alloc(KC * nb, F32).rearrange("p (k b) -> p k b", k=KC)
                Vtok = sb.alloc(KC * 128, BF16).rearrange("p (k n) -> p k n", k=KC)
                BHtok = sb.alloc(KC * 128, BF16).rearrange("p (k n) -> p k n", k=KC)
                KHtok = sb.alloc(KC * 128, BF16).rearrange("p (k n) -> p k n", k=KC)
                Ytok = sb.alloc(KC * 128, F32).rearrange("p (k n) -> p k n", k=KC)
                Ysq = up if not smp else sb.alloc(KC * 128, F32).rearrange("p (k n) -> p k n", k=KC)
                YSK = 'Ysq' if smp else 'up'
                Yn = xm[1] if not smp else sb.alloc(KC * 128, BF16).rearrange("p (k n) -> p k n", k=KC)
                YNK = 'Yn' if smp else 'xm'
                st1 = sb.alloc(16, F32)
                st2 = sb.alloc(16, F32)
                NH = 1 if smp else 8
                ABt = [sb.alloc(2 * L, BF16) for _ in range(NH)]
                AKt = [sb.alloc(2 * L, BF16) for _ in range(NH)]
                MMb = [[sb.alloc(2 * L, BF16) for _ in range(2)] for _ in range(NH)]
                Xb = [sb.alloc(64, BF16) for _ in range(NH)]
                S0T = sb.alloc(KC * nb * 64, F32).rearrange("p (k b i) -> p k b i", k=KC, b=nb)
                nbd = min(nb, 4)
                S0bh = [sb.alloc(nb * 64, BF16).rearrange("p (b i) -> p b i", b=nb) for _ in range(NH)]
                bd = sb.alloc(nbd * 128, F32).rearrange("p (b n) -> p b n", b=nbd)
                if smp:
                    sst = sb.alloc(KC * 16, F32).rearrange("p (k b) -> p k b", k=KC)
                    ATm = [sb.alloc(16 * 64, BF16).rearrange("p (b t) -> p b t", b=16) for _ in range(NH)]
                    RTm = [sb.alloc(16 * 64, BF16).rearrange("p (b t) -> p b t", b=16) for _ in range(NH)]
                    Wm = [sb.alloc(16 * 64, BF16).rearrange("p (b t) -> p b t", b=16) for _ in range(NH)]
                    Vm = [sb.alloc(16 * 64, BF16).rearrange("p (b t) -> p b t", b=16) for _ in range(NH)]
                    DMA('sp', sst, shift_in[j].rearrange("(k p) b -> p k b", p=128), [], ['sst'])
                    MSET('pool', BHtok, 0.0, ['BHtok'])
                    MSET('pool', Vtok, 0.0, ['Vtok'])
                    for q_ in range(NH):
                        MSET('pool', AKt[q_], 0.0, [('AKt', q_)])
                        MSET('pool', ABt[q_], 0.0, [('ABt', q_)])
                        MSET('pool', Xb[q_], 0.0, [('Xb', q_)])
                    MSET('pool', KHtok, 0.0, ['KHtok'])
                    for q_ in range(NH):
                        MSET('pool', Wm[q_], 0.0, [('Wm', q_)])
                        MSET('pool', Vm[q_], 0.0, [('Vm', q_)])
                else:
                    carry = sb.alloc(8, F32)
                    MSET('dve', carry, 0.0, ['carry'])
                MSET('dve', mhalf, -0.5, ['mhalf'])
                MSET('dve', c24, 1e-24, ['c24'])
                MSET('dve', gnec, 64e-5, ['gnec'])
                TS('dve', negw0, vecs[:, VOFF['r_w0'] + j * 8:VOFF['r_w0'] + j * 8 + 8], -1.0, None, ALU.mult, None, ['vecs'], ['negw0'])
                TS('dve', omka, vecs[:, VOFF['r_k_a'] + j * 8:VOFF['r_k_a'] + j * 8 + 8], -1.0, 1.0, ALU.mult, ALU.add, ['vecs'], ['omka'])
                DMA('pool', w1b, wview('r_w1').rearrange("(k p) n -> p k n", p=128), [], ['w1b'])
                DMA('pool', a1b, wview('r_a1').rearrange("(k p) n -> p k n", p=128), [], ['a1b'])
                DMA('pool', g1b, wview('r_g1').rearrange("(k p) n -> p k n", p=128), [], ['g1b'])
                DMA('pool', w2b[0:64, :], wview('r_w2'), [], ['w2b'])
                DMA('pool', a2b[0:64, :], wview('r_a2'), [], ['a2b'])
                DMA('pool', g2a, wview('r_g2')[0:128, :], [], ['g2a'])
                DMA('pool', g2b[0:32, :], wview('r_g2')[128:160, :], [], ['g2b'])
                if j == 1:
                    DMA('pool', v1b, wview('r_v1', 0).rearrange("(k p) n -> p k n", p=128), [], ['v1b'])
                    DMA('pool', v2b[0:32, :], wview('r_v2', 0), [], ['v2b'])
                if smp:
                    MSET('pool', bd, 0.0, ['bd'])
                    for d in range(KC):
                        for b4 in range(4):
                            for a in range(2):
                                DMA('sp', bd[a * 64:(a + 1) * 64, :, a * 64:(a + 1) * 64], wkv_in[j][b4 * 4:b4 * 4 + 4, 2 * d + a].rearrange("b i j -> i b j"), [], ['bd'])
                            ps, pk = psum()
                            for i4 in range(4):
                                TR(ps[:, i4 * 128:(i4 + 1) * 128], bd[:, i4, :], ident_f, ['bd', 'cst'], [pk])
                            for a in range(2):
                                CP('dve', S0T[a * 64:(a + 1) * 64, d, b4 * 4:b4 * 4 + 4, :],
                                   ps[a * 64:(a + 1) * 64, :].rearrange("p (b n) -> p b n", b=4)[:, :, a * 64:(a + 1) * 64], [pk], [('S0T', d)])
                else:
                    MSET('dve', S0T, 0.0, ['S0T'])
                wc = [0]

                rseq = [(nm_, hf_) for _t in tiles[:CFG.get('rtiles', 99)] for nm_ in ('r_wr', 'r_wk', 'r_wv', 'r_wo') for hf_ in range(4)]
                rnext = [0]
                rq = []

                def wissue(wname, hf):
                    Wv = wview(wname, None).rearrange("(k p) n -> p k n", p=128)
                    wb = wbuf[wc[0] % NWBR]
                    wk = ('wbuf', wc[0] % NWBR)
                    wc[0] += 1
                    DMA('pool', wb, Wv[:, :, hf * 256:(hf + 1) * 256], [], [wk])
                    return wname, hf, wb, wk

                def rfill():
                    while len(rq) < NWBR - 1 and rnext[0] < len(rseq):
                        rq.append(wissue(*rseq[rnext[0]]))
                        rnext[0] += 1

                def proj(wname, src, evac, jj=None, nxt=None):
                    for hf in range(4):
                        rfill()
                        nm_, hf_, wb, wk = rq.pop(0)
                        assert nm_ == wname and hf_ == hf, (nm_, hf_, wname, hf)
                        rfill()
                        for dd in range(2):
                            d = hf * 2 + dd
                            ps, pk = psum()
                            for k in range(KC):
                                MM(ps[:, :T], wb[:, k, dd * 128:(dd + 1) * 128], src[:, k, :], k == 0, k == KC - 1, [wk, 'xm'], [pk])
                            evac(d, ps, pk)

                try:
                    for (t0, T_) in tiles[:CFG.get('rtiles', 99)]:
                        rmsnorm(t0, T, 'norm_mix', l * KC, uf, ('uf',), sq, rs, 0, hkey='h', sqkey='BH_')
                        if smp:
                            u4 = uf.rearrange("p k (b t) -> p k b t", t=4)
                            p4 = up.rearrange("p k (b t) -> p k b t", t=4)
                            for k in range(KC):
                                CP('pool', p4[:, k, :, 1:4], u4[:, k, :, 0:3], ['uf'], ['up'])
                                CP('pool', p4[:, k, :, 0:1], sst[:, k, :].rearrange("p (b o) -> p b o", o=1), ['sst'], ['up'])
                                CP('pool', sst[:, k, :].rearrange("p (b o) -> p b o", o=1), u4[:, k, :, 3:4], ['uf', 'up'], ['sst'])
                        else:
                            CP('pool', up[:, :, 1:T], uf[:, :, 0:T - 1], ['uf'], ['up'])
                            CP('pool', up[:, :, 0:1], carry.rearrange("p (k o) -> p k o", o=1), ['carry'], ['up'])
                            CP('pool', carry.rearrange("p (k o) -> p k o", o=1), uf[:, :, T - 1:T], ['uf', 'up'], ['carry'])
                        TT('dve', up, up, uf, ALU.subtract, ['up', 'uf'], ['up'])

                        def mix(i, dst):
                            muv = vecs[:, mu0 + i * 8:mu0 + i * 8 + 8].rearrange("p (k o) -> p k o", o=1).broadcast_to([128, KC, T])
                            TT('dve', tA, up, muv, ALU.mult, ['up', 'vecs'], ['tA'])
                            TT('dve', dst, tA, uf, ALU.add, ['tA', 'uf'], ['xm'])
                        mix(0, xm[0])
                        proj('r_wr', xm[0], lambda d, ps, pk: ACT(r_[:, d, :], ps[:, :T], AF.Copy, [pk], [('r_', d)]), nxt='r_wk')
                        mix(2, xm[1])
                        proj('r_wk', xm[1], lambda d, ps, pk: ACT(k_[:, d, :], ps[:, :T], AF.Copy, [pk], [('k_', d)]), nxt='r_wv')
                        mix(3, xm[0])
                        proj('r_wv', xm[0], lambda d, ps, pk: ACT(v_[:, d, :], ps[:, :T], AF.Copy, [pk], [('v_', d)]), nxt='r_wo')
                        if j == 1:
                            ps, pk = psum()
                            for k in range(KC):
                                MM(ps[:32, :T], v1b[:, k, :], xm[0][:, k, :], k == 0, k == KC - 1, ['v1b', 'xm'], [pk])
                            CP('dve', lo3[0:32, :], ps[:32, :T], [pk], ['lo3'])
                            DMA('sp', tB, vfirst_d.rearrange("(k p) t -> p k t", p=128)[:, :, t0:t0 + T], ['vf_dram'], ['tB'])
                            for d in range(KC):
                                ps, pk = psum()
                                MM(ps[:, :T], v2b[0:32, d * 128:(d + 1) * 128], lo3[0:32, :], True, True, ['v2b', 'lo3'], [pk])
                                ACT(tA[:, d, :], ps[:, :T], AF.Sigmoid, [pk, 'vecs'], ['tA'], bias=vcol('r_v0', d))
                            TT('dve', tB, tB, v_, ALU.subtract, ['tB', 'v_'], ['tB'])
                            TT('dve', tB, tB, tA, ALU.mult, ['tB', 'tA'], ['tB'])
                            TT('dve', v_, v_, tB, ALU.add, ['tB', 'v_'], ['v_'])
                        else:
                            DMA('sp', vfirst_d.rearrange("(k p) t -> p k t", p=128)[:, :, t0:t0 + T], v_, ['v_'], [('vf_dram', t0)])
                        _stg(1)
                        mix(1, xm[1])
                        ps, pk = psum()
                        for k in range(KC):
                            MM(ps[:64, :T], w1b[:, k, :], xm[1][:, k, :], k == 0, k == KC - 1, ['w1b', 'xm'], [pk])
                        ACT(lo1[0:64, :], ps[:64, :T], AF.Tanh, [pk], ['lo1'])
                        for d in range(KC):
                            ps, pk = psum()
                            MM(ps[:, :T], w2b[0:64, d * 128:(d + 1) * 128], lo1[0:64, :], True, True, ['w2b', 'lo1'], [pk])
                            ACT(nlw[:, d, :], ps[:, :T], AF.Exp, [pk, 'negw0'], [('nlw', d)], bias=negw0[:, d:d + 1], scale=-1.0)
                            ACT(nlw[:, d, :], nlw[:, d, :], AF.Ln, [('nlw', d), 'onec'], [('nlw', d)], bias=onec[:, 0:1])
                            ACT(nlw[:, d, :], nlw[:, d, :], AF.Exp, [('nlw', d), 'mhalf'], [('nlw', d)], bias=mhalf[:, 0:1], scale=-1.0)
                        mix(4, xm[0])
                        ps, pk = psum()
                        for k in range(KC):
                            MM(ps[:64, :T], a1b[:, k, :], xm[0][:, k, :], k == 0, k == KC - 1, ['a1b', 'xm'], [pk])
                        CP('dve', lo2[0:64, :], ps[:64, :T], [pk], ['lo2'])
                        for d in range(KC):
                            ps, pk = psum()
                            MM(ps[:, :T], a2b[0:64, d * 128:(d + 1) * 128], lo2[0:64, :], True, True, ['a2b', 'lo2'], [pk])
                            ACT(a_[:, d, :], ps[:, :T], AF.Sigmoid, [pk, 'vecs'], [('a_', d)], bias=vcol('r_a0', j * 8 + d))
                        mix(5, xm[1])
                        ps, pk = psum()
                        for k in range(KC):
                            MM(ps[:, :T], g1b[:, k, 0:128], xm[1][:, k, :], k == 0, k == KC - 1, ['g1b', 'xm'], [pk])
                        ACT(lo1, ps[:, :T], AF.Sigmoid, [pk], ['lo1'])
                        ps, pk = psum()
                        for k in range(KC):
                            MM(ps[:32, :T], g1b[:, k, 128:160], xm[1][:, k, :], k == 0, k == KC - 1, ['g1b', 'xm'], [pk])
                        ACT(lo2[0:32, :], ps[:32, :T], AF.Sigmoid, [pk], ['lo2'])
                        for d in range(KC):
                            ps, pk = psum()
                            MM(ps[:, :T], g2a[:, d * 128:(d + 1) * 128], lo1, True, False, ['g2a', 'lo1'], [pk])
                            MM(ps[:, :T], g2b[0:32, d * 128:(d + 1) * 128], lo2[0:32, :], False, True, ['g2b', 'lo2'], [pk])
                            ACT(g_[:, d, :], ps[:, :T], AF.Copy, [pk], [('g_', d)])
                        _stg(2)
                        kkv = vecs[:, VOFF['r_k_k'] + j * 8:VOFF['r_k_k'] + j * 8 + 8].rearrange("p (k o) -> p k o", o=1).broadcast_to([128, KC, T])
                        TT('dve', kk, k_, kkv, ALU.mult, ['k_', 'vecs'], ['kk'])
                        TT('dve', tA, kk, kk, ALU.mult, ['kk'], ['tA'])
                        for d in range(KC):
                            ps, pk = psum()
                            MM(ps[:, :T], blk2, tA[:, d, :], True, True, ['cst', 'tA'], [pk])
                            TS('dve', tB[:, d, :], ps[:, :T], c24[:, 0:1], None, ALU.max, None, [pk, 'c24'], ['tB'])
                        ACT(tB, tB, AF.Ln, ['tB'], ['tB'])
                        ACT(tB, tB, AF.Exp, ['tB'], ['tB'], scale=-0.5)
                        TT('dve', kk, kk, tB, ALU.mult, ['kk', 'tB'], ['kk'])
                        kav = vecs[:, VOFF['r_k_a'] + j * 8:VOFF['r_k_a'] + j * 8 + 8].rearrange("p (k o) -> p k o", o=1).broadcast_to([128, KC, T])
                        omv = omka.rearrange("p (k o) -> p k o", o=1).broadcast_to([128, KC, T])
                        TT('dve', tA, a_, kav, ALU.mult, ['a_', 'vecs'], ['tA'])
                        TT('dve', tA, tA, omv, ALU.add, ['tA', 'omka'], ['tA'])
                        TT('dve', k_, k_, tA, ALU.mult, ['k_', 'tA'], ['k_'])
                        TT('dve', tA, kk, a_, ALU.mult, ['kk', 'a_'], ['tA'])
                        rkv = vecs[:, VOFF['r_r_k'] + j * 8:VOFF['r_r_k'] + j * 8 + 8].rearrange("p (k o) -> p k o", o=1).broadcast_to([128, KC, T])
                        TT('dve', np_, r_, k_, ALU.mult, ['r_', 'k_'], ['np_'])
                        TT('dve', np_, np_, rkv, ALU.mult, ['np_', 'vecs'], ['np_'])
                        for d in range(KC):
                            ps, pk = psum()
                            MM(ps[:, :T], blk2, np_[:, d, :], True, True, ['cst', 'np_'], [pk])
                            TT('dve', tB[:, d, :], ps[:, :T], v_[:, d, :], ALU.mult, [pk, 'v_'], ['tB'])
                        _stg(3)
                        for d in range(KC):
                            S.add('dve', (lambda e, o=np_[:, d, :], d1=nlw[:, d, :]: e.tensor_tensor_scan(out=o, data0=rst[:, :T], data1=d1, initial=0.0, op0=ALU.mult, op1=ALU.add)),
                                  reads=['nlw', 'cst', 'np_'], writes=['np_'])
                        npv = np_.rearrange("p k (b t) -> p k b t", b=nb)
                        npE = npv[:, :, :, lt - 1:lt].broadcast_to([128, KC, nb, lt])
                        ACT(eC, npv[:, :, :, lt - 1:lt].rearrange("p k b o -> p k (b o)"), AF.Exp, ['np_'], ['eC'], scale=-1.0)
                        TT('dve', uf, np_, nlw, ALU.subtract, ['np_', 'nlw'], ['uf'])
                        ACT(uf, uf, AF.Exp, ['uf'], ['uf'], scale=-1.0)
                        STT('dve', ARt[:, :, 0, :], kk, -1.0, uf, ALU.mult, ALU.mult, ['kk', 'uf'], ['ARt'])
                        ACT(uf, np_, AF.Exp, ['np_', 'ARt'], ['uf'], scale=-1.0)
                        TT('dve', ARt[:, :, 1, :], r_, uf, ALU.mult, ['r_', 'uf'], ['ARt'])
                        ACT(uf, np_, AF.Exp, ['np_', 'ARt'], ['uf'])
                        TT('dve', BT_, tA, uf, ALU.mult, ['tA', 'uf'], ['BT_'])
                        TT('dve', KT_, k_, uf, ALU.mult, ['k_', 'uf'], ['KT_'])
                        TT('dve', uf.rearrange("p k (b t) -> p k b t", b=nb), npv, npE, ALU.subtract, ['np_', 'BT_', 'KT_'], ['uf'])
                        ACT(uf, uf, AF.Exp, ['uf'], ['uf'])
                        TT('dve', BH_, tA, uf, ALU.mult, ['tA', 'uf'], ['BH_'])
                        TT('dve', KH_, k_, uf, ALU.mult, ['k_', 'uf'], ['KH_'])
                        CP('pool', V_, v_, ['v_'], ['xm'])
                        _stg(4)
                        for (src, dst, nm) in ((V_, Vtok, 'Vtok'), (BH_, BHtok, 'BHtok'), (KH_, KHtok, 'KHtok')):
                            ps, pk = psum()
                            psb = ps.bitcast(BF16)
                            for d in range(KC):
                                TR(psb[:L, d * 128:(d + 1) * 128], src[:, d, :], ident_b, [nm[:-3] + '_' if nm != 'Vtok' else 'xm', 'ident_b'], [pk])
                            CP('dve', dst[:L].rearrange("p k n -> p (k n)"), psb[:L, :], [pk], [nm])
                        _stg(5)
                        for hg in range(16 // NH):
                            heads = [(hg * NH + q) for q in range(NH)]
                            if CFG.get('rheads') is not None and heads[0] not in CFG['rheads']:
                                continue
                            HD = [(hh // 2, hh % 2) for hh in heads]
                            for q, (d, a) in enumerate(HD):
                                sl = slice(a * 64, (a + 1) * 64)
                                AR = ARt[sl, d].rearrange("p c t -> p (c t)")
                                ps, pk = psum()
                                MM(ps[:L, 0:2 * L], BT_[sl, d, :], AR, True, True, ['BT_', 'ARt'], [pk])
                                TT('dve', ABt[q][:L, :], ps[:L, 0:2 * L], mAB, ALU.mult, [pk, 'cst'], [('ABt', q)])
                                ps, pk = psum()
                                MM(ps[:L, 0:2 * L], KT_[sl, d, :], AR, True, True, ['KT_', 'ARt'], [pk])
                                TT('dve', AKt[q][:L, :], ps[:L, 0:2 * L], mAB, ALU.mult, [pk, 'cst'], [('AKt', q)])
                                ps, pk = psum()
                                MM(ps[:L, 0:L], ARt[sl, d, 0, :], BT_[sl, d, :], True, True, ['BT_', 'ARt'], [pk])
                                TT('dve', MMb[q][0][:L, 0:L], ps[:L, 0:L], mlow, ALU.mult, [pk, 'cst'], [('MMb', q, 0)])
                                CP('pool', MMb[q][0][:L, L:2 * L], ABt[q][:L, 0:L], [('ABt', q)], [('MMb', q, 0)])
                            _stg(6)
                            for q, (d, a) in enumerate(HD):
                                sl = slice(a * 64, (a + 1) * 64)
                                CP('dve', S0bh[q][sl], S0T[sl, d], [('S0T', d)], [('S0bh', q)])
                                ps, pk = psum()
                                if smp:
                                    TT('dve', ATm[q][sl], ARt[sl, d, 0, :].rearrange("p (o t) -> p o t", o=1).broadcast_to([64, 16, 64]),
                                       bmT_b[sl].rearrange("p (b t) -> p b t", b=16), ALU.mult, ['ARt', 'bmT_b'], [('ATm', q)])
                                    TT('dve', RTm[q][sl], ARt[sl, d, 1, :].rearrange("p (o t) -> p o t", o=1).broadcast_to([64, 16, 64]),
                                       bmT_b[sl].rearrange("p (b t) -> p b t", b=16), ALU.mult, ['ARt', 'bmT_b'], [('RTm', q)])
                                    for b in range(16):
                                        MM(ps[:L, 0:64], ATm[q][sl, b, :], S0bh[q][sl, b, :], b == 0, False, [('ATm', q), ('S0bh', q)], [pk])
                                else:
                                    MM(ps[:L, 0:64], ARt[sl, d, 0, :], S0bh[q][sl, 0, :], True, False, ['ARt', ('S0bh', q)], [pk])
                                MM(ps[:L, 0:64], AKt[q][:, 0:L], Vtok[:, d, sl], False, True, [('AKt', q), 'Vtok'], [pk])
                                CP('dve', Xb[q][:L, :], ps[:L, 0:64], [pk], [('Xb', q)])
                            _stg(7)
                            for lev in range(nlev):
                                cur = lev % 2
                                for q, (d, a) in enumerate(HD):
                                    M_ = MMb[q][cur][:L, 0:L]
                                    Mt_ = MMb[q][cur][:L, L:2 * L]
                                    ps, pk = psum()
                                    MM(ps[:L, 0:64], Mt_, Xb[q][:L, :], True, True, [('MMb', q, cur), ('Xb', q)], [pk])
                                    TT('dve', Xb[q][:L, :], Xb[q][:L, :], ps[:L, 0:64], ALU.add, [pk, ('Xb', q)], [('Xb', q)])
                                    if lev < nlev - 1:
                                        ps2, pk2 = psum()
                                        MM(ps2[:L, 0:L], Mt_, M_, True, True, [('MMb', q, cur)], [pk2])
                                        MM(ps2[:L, L:2 * L], M_, Mt_, True, True, [('MMb', q, cur)], [pk2])
                                        ACT(MMb[q][1 - cur][:L, :], ps2[:L, 0:2 * L], AF.Copy, [pk2], [('MMb', q, 1 - cur)])
                            _stg(8)
                            for q, (d, a) in enumerate(HD):
                                sl = slice(a * 64, (a + 1) * 64)
                                ps, pk = psum()
                                if smp:
                                    for b in range(16):
                                        MM(ps[:L, 0:64], RTm[q][sl, b, :], S0bh[q][sl, b, :], b == 0, False, [('RTm', q), ('S0bh', q)], [pk])
                                else:
                                    MM(ps[:L, 0:64], ARt[sl, d, 1, :], S0bh[q][sl, 0, :], True, False, ['ARt', ('S0bh', q)], [pk])
                                MM(ps[:L, 0:64], ABt[q][:, L:2 * L], Xb[q][:, :], False, False, [('ABt', q), ('Xb', q)], [pk])
                                MM(ps[:L, 0:64], AKt[q][:, L:2 * L], Vtok[:, d, sl], False, True, [('AKt', q), 'Vtok'], [pk])
                                ACT(Ytok[:L, d, sl], ps[:L, 0:64], AF.Copy, [pk], [('Ytok', d, a)])
                                _stg(8.2)
                                if smp:
                                    bmv = cst[:L, C_BM:C_BM + 16].rearrange("p (b o) -> p b o", o=1).broadcast_to([L, 16, 64])
                                    TT('pool', Wm[q][:L], Xb[q][:L, :].rearrange("p (o i) -> p o i", o=1).broadcast_to([L, 16, 64]), bmv, ALU.mult, [('Xb', q), 'cst'], [('Wm', q)])
                                    TT('pool', Vm[q][:L], Vtok[:L, d, sl].rearrange("p (o i) -> p o i", o=1).broadcast_to([L, 16, 64]), bmv, ALU.mult, ['Vtok', 'cst'], [('Vm', q)])
                                    _stg(8.4)
                                    S3 = S0T[sl, d]
                                    TT('dve', S3, S3, eC[sl, d, :].rearrange("p (b o) -> p b o", o=1).broadcast_to([64, 16, 64]), ALU.mult, [('S0T', d), 'eC'], [('S0T', d)])
                                    _stg(8.6)
                                    for b8 in range(2):
                                        ps, pk = psum()
                                        for bi in range(8):
                                            b = b8 * 8 + bi
                                            MM(ps[sl, bi * 64:(bi + 1) * 64], BHtok[:, d, sl], Wm[q][:, b, :], True, False, ['BHtok', ('Wm', q)], [pk])
                                            MM(ps[sl, bi * 64:(bi + 1) * 64], KHtok[:, d, sl], Vm[q][:, b, :], False, True, ['KHtok', ('Vm', q)], [pk])
                                        TT('dve', S0T[sl, d, b8 * 8:b8 * 8 + 8, :], S0T[sl, d, b8 * 8:b8 * 8 + 8, :], ps[sl, :].rearrange("p (b i) -> p b i", b=8), ALU.add,
                                           [pk, ('S0T', d)], [('S0T', d)])
                                else:
                                    ps, pk = psum()
                                    MM(ps[sl, 0:64], BHtok[:L, d, sl], Xb[q][:L, :], True, False, ['BHtok', ('Xb', q)], [pk])
                                    MM(ps[sl, 0:64], KHtok[:L, d, sl], Vtok[:L, d, sl], False, True, ['KHtok', 'Vtok'], [pk])
                                    STT('dve', S0T[sl, d, 0, :], S0T[sl, d, 0, :], eC[sl, d, 0:1], ps[sl, 0:64], ALU.mult, ALU.add, [pk, ('S0T', d), 'eC'], [('S0T', d)])
                        _stg(9)
                        Y3 = Ytok[:L].rearrange("p k (a i) -> p (k a) i", a=2)
                        S.add('dve', (lambda e, o=st1[:L, :]: e.tensor_reduce(out=o, in_=Y3, axis=AX.X, op=ALU.add)), reads=['Ytok'], writes=['st1'])
                        TT('pool', Ysq[:L], Ytok[:L], Ytok[:L], ALU.mult, ['Ytok'], [YSK])
                        S.add('dve', (lambda e, o=st2[:L, :]: e.tensor_reduce(out=o, in_=Ysq[:L].rearrange("p k (a i) -> p (k a) i", a=2), axis=AX.X, op=ALU.add)), reads=[YSK], writes=['st2'])
                        TS('dve', st1[:L, :], st1[:L, :], 1.0 / 64, None, ALU.mult, None, ['st1'], ['st1'])
                        TS('dve', st2[:L, :], st2[:L, :], 1.0 / 64, None, ALU.mult, None, ['st2'], ['st2'])
                        TT('dve', Ysq[:L, 0, 0:16], st1[:L, :], st1[:L, :], ALU.mult, ['st1', YSK], [YSK])
                        TT('dve', st2[:L, :], st2[:L, :], Ysq[:L, 0, 0:16], ALU.subtract, ['st2', YSK], ['st2'])
                        ACT(st2[:L, :], st2[:L, :], AF.Ln, ['st2', 'gnec'], ['st2'], bias=gnec[:L, 0:1])
                        ACT(st2[:L, :], st2[:L, :], AF.Exp, ['st2'], ['st2'], scale=-0.5)
                        TT('dve', Y3, Y3, st1[:L, :].rearrange("p (h o) -> p h o", o=1).broadcast_to([L, 16, 64]), ALU.subtract, ['Ytok', 'st1'], ['Ytok'])
                        TT('dve', Yn[:L].rearrange("p k (a i) -> p (k a) i", a=2), Y3, st2[:L, :].rearrange("p (h o) -> p h o", o=1).broadcast_to([L, 16, 64]), ALU.mult,
                           ['Ytok', 'st2'], [YNK])
                        for hf in range(2):
                            ps, pk = psum()
                            psb = ps.bitcast(BF16)
                            for dd in range(4):
                                d = hf * 4 + dd
                                TR(psb[:, dd * L:(dd + 1) * L], Yn[:L, d, :], ident_b[:L, :L], [YNK, 'ident_b'], [pk])
                            for dd in range(4):
                                d = hf * 4 + dd
                                TS('dve', tA[:, d, :], psb[:, dd * L:(dd + 1) * L], vcol('r_gn_w', j * 8 + d), vcol('r_gn_b', j * 8 + d), ALU.mult, ALU.add, [pk, 'vecs'], ['tA'])
                        TT('dve', tA, tA, tB, ALU.add, ['tA', 'tB'], ['tA'])
                        TT('dve', yg, tA, g_, ALU.mult, ['tA', 'g_'], ['xm'])
                        proj('r_wo', yg, lambda d, ps, pk: TT('dve', h[:, d, t0:t0 + T], h[:, d, t0:t0 + T], ps[:, :T], ALU.add, [pk, 'h'], ['h']), nxt=('r_wr' if t0 != tiles[-1][0] else None))
                except _Stop:
                    pass
                if CFG.get('rstage', 99) < 10:
                    S.barrier()
                    sb.reset(m)
                    return
                if smp:
                    DMA('sp', shift_s[j].rearrange("(k p) b -> p k b", p=128), sst, ['sst'], [('shift_s', j)])
                else:
                    DMA('sp', shift_p[j], carry, ['carry'], [('shift_p', j)])
                MSET('pool', bd, 0.0, ['bd'])
                for d in range(KC):
                    for b4 in range((nb + 3) // 4):
                        n4 = min(4, nb - b4 * 4)
                        for a in range(2):
                            sl = slice(a * 64, (a + 1) * 64)
                            CP('dve', bd[sl, 0:n4, a * 64:(a + 1) * 64], S0T[sl, d, b4 * 4:b4 * 4 + n4, :], [('S0T', d)], ['bd'])
                        ps, pk = psum()
                        for i4 in range(n4):
                            TR(ps[:, i4 * 128:(i4 + 1) * 128], bd[:, i4, :], ident_f, ['bd', 'cst'], [pk])
                        for a in range(2):
                            sl = slice(a * 64, (a + 1) * 64)
                            CP('dve', Ysq[sl, 0:n4, 0:64], ps[sl, :].rearrange("p (b n) -> p b n", b=4)[:, 0:n4, a * 64:(a + 1) * 64], [pk], [YSK])
                            if smp:
                                DMA('sp', wkv_s[j][b4 * 4:b4 * 4 + n4, 2 * d + a].rearrange("b i j -> i b j"), Ysq[sl, 0:n4, 0:64], [YSK], [('wkv_s', j, d, a, b4)])
                            else:
                                DMA('sp', wkv_p[j][2 * d + a], Ysq[sl, 0, 0:64], [YSK], [('wkv_p', j, d, a)])
                S.barrier()
                sb.reset(m)

            run(False)
            if CFG.get('rwkv_sample', True):
                run(True)


        for l in range(CFG['depth']):
            if CFG.get('dense', True):
                ffn(l, Wd_['ffn1_gate_up'], Wd_['ffn1_down'], 'norm_ffn1')
            if CFG['mixers']:
                if l % 2 == 0:
                    if CFG.get('mamba', True):
                        mamba(l)
                elif CFG.get('rwkv', False):
                    rwkv(l)
            if CFG.get('dense', True):
                ffn(l, Wd_['ffn2_gate_up'], Wd_['ffn2_down'], 'norm_ffn2')
                ple(l)
        final()
        S.add('sp', lambda e: e.nop(), reads=['yT', 'ssm_s', 'ssm_p', 'conv_s', 'conv_p', 'wkv_p', 'wkv_s', 'shift_p', 'shift_s'])
        S.emit()
    return nc


def pack_vecs(inp):
    cols = []
    for n in VECS:
        a = np.asarray(inp[n], dtype=np.float32)
        cols.append(np.ascontiguousarray(a.reshape(-1, 128).T))
    return np.ascontiguousarray(np.concatenate(cols, axis=1))


def kernel(**inp):
    inp = {k: np.asarray(v) for k, v in inp.items()}
    nc = build_program()
    vecs = pack_vecs(inp)
    consts = make_consts()
    hv = np.zeros((32, 4), np.float32)
    dbc = np.zeros((128, 64), np.float32)
    for j in range(2):
        hv[:, 2 * j] = inp['m_dt_bias'][j]
        hv[:, 2 * j + 1] = inp['m_A_log'][j]
        dbc[:, j * 32:(j + 1) * 32] = inp['m_D'][j][None, :]
    ncores = CFG['cores']
    in_maps = []
    for c in range(ncores):
        xs = inp['x_sample'][16 * c:16 * c + 16].reshape(NS, D)
        xc = np.concatenate([inp['x_prompt'][c], xs], axis=0)
        ps_ = inp['p_sample'][:, 16 * c:16 * c + 16].reshape(4, NS, 256)
        pc = np.concatenate([inp['p_prompt'][:, c], ps_], axis=1)
        m = {'xT': np.ascontiguousarray(xc.T), 'pT': np.ascontiguousarray(pc.transpose(0, 2, 1)), 'vecs': vecs,
             'consts': consts, 'hv': hv, 'dbc': dbc,
             'ssm_in': np.ascontiguousarray(inp['state_ssm'][:, 16 * c:16 * c + 16]),
             'conv_in': np.ascontiguousarray(inp['state_conv'][:, 16 * c:16 * c + 16].transpose(0, 3, 1, 2)),
             'wkv_in': np.ascontiguousarray(inp['state_wkv'][:, 16 * c:16 * c + 16]),
             'shift_in': np.ascontiguousarray(inp['state_shift'][:, 16 * c:16 * c + 16].transpose(0, 2, 1))}
        for n in WSHAPES:
            m[n] = np.ascontiguousarray(inp[n], dtype=np.float32)
        in_maps.append(m)
    res = run_bass_kernel_spmd(nc, in_maps, core_ids=list(range(ncores)))
    B = 8
    yp = np.zeros((B, NP, D), np.float32)
    ys = np.zeros((128, 4, D), np.float32)
    ssm_p = np.zeros((2, B, 32, 64, 128), np.float32)
    conv_p = np.zeros((2, B, 3, 4096), np.float32)
    wkv_p = np.zeros((2, B, 16, 64, 64), np.float32)
    shift_p = np.zeros((2, B, D), np.float32)
    ssm_s = np.zeros((2, 128, 32, 64, 128), np.float32)
    conv_s = np.zeros((2, 128, 3, 4096), np.float32)
    wkv_s = np.zeros((2, 128, 16, 64, 64), np.float32)
    shift_s = np.zeros((2, 128, D), np.float32)
    for c in range(ncores):
        r = res.results[c]
        y = r['yT'].T
        yp[c] = y[:NP]
        ys[16 * c:16 * c + 16] = y[NP:].reshape(16, 4, D)
        ssm_p[:, c] = r['ssm_p']
        conv_p[:, c] = r['conv_p'].transpose(0, 2, 1)
        wkv_p[:, c] = r['wkv_p']
        shift_p[:, c] = r['shift_p'].transpose(0, 2, 1).reshape(2, D)
        ssm_s[:, 16 * c:16 * c + 16] = r['ssm_s']
        conv_s[:, 16 * c:16 * c + 16] = r['conv_s'].transpose(0, 2, 3, 1)
        wkv_s[:, 16 * c:16 * c + 16] = r['wkv_s']
        shift_s[:, 16 * c:16 * c + 16] = r['shift_s'].transpose(0, 2, 1)
    return (yp, ys, ssm_p, conv_p, wkv_p, shift_p, ssm_s, conv_s, wkv_s, shift_s)
```
